# Optimizing a Trainium2 kernel written in Bass

```python
import math
import jax, jax.numpy as jnp
from jax import lax
import numpy as np

D_MODEL = 1024
BATCH = 2
SEQ = 8192
DEPTH = 4

N_MIXERS = 3
PLE_DIM = 256
EPS = 1e-6
N_NORMS = 5

RET_HEADS = 4
RET_QK_DIM = D_MODEL // RET_HEADS
RET_V_DIM = 2 * RET_QK_DIM
RET_CHUNK = 128
ROPE_BASE = 10000.0
RET_IN_WIDTH = 2 * RET_HEADS * RET_QK_DIM + 2 * RET_HEADS * RET_V_DIM

ATT_HEADS = 16
ATT_HEAD_DIM = D_MODEL // ATT_HEADS
DILATION_PAIRS = ((128, 1), (512, 4), (2048, 16))
N_DIL_GROUPS = len(DILATION_PAIRS)
ATT_IN_WIDTH = N_DIL_GROUPS * 3 * ATT_HEADS * ATT_HEAD_DIM

REL_BUCKETS = 32
REL_MAX_EXACT = REL_BUCKETS // 2
REL_MAX_DIST = 2048

GM_CHUNK = 128
GM_WIDTH = 2 * D_MODEL
GM_GROUPS = 8
GM_GROUP_DIM = GM_WIDTH // GM_GROUPS

FFN_HIDDEN = -(-(8 * D_MODEL) // (3 * 256)) * 256

N_A = (DEPTH + 2) // 3
N_B = (DEPTH + 1) // 3
N_C = DEPTH // 3

kernel_name = "hybrid_retention_dilated_gmlp_trunk"


def _rmsnorm(x, g):
    xf = x.astype(jnp.float32)
    y = xf * lax.rsqrt(jnp.mean(xf * xf, axis=-1, keepdims=True) + EPS)
    return (y * g.astype(jnp.float32)).astype(x.dtype)


def _rope(t, seq_len):
    d = t.shape[-1]
    inv_freq = ROPE_BASE ** (-jnp.arange(0, d, 2, dtype=jnp.float32) / d)
    ang = jnp.arange(seq_len, dtype=jnp.float32)[:, None] * inv_freq[None, :]
    cos = jnp.cos(ang)[None, :, None, :]
    sin = jnp.sin(ang)[None, :, None, :]
    t1, t2 = jnp.split(t, 2, axis=-1)
    return jnp.concatenate([t1 * cos - t2 * sin, t2 * cos + t1 * sin], axis=-1)


def _retention_mixer(h, w_in, w_out):
    B, S, _ = h.shape
    H, dk, dv, C = RET_HEADS, RET_QK_DIM, RET_V_DIM, RET_CHUNK
    proj = h @ w_in
    q, k, v, g = jnp.split(proj, [H * dk, 2 * H * dk, 2 * H * dk + H * dv], axis=-1)
    q = _rope(q.reshape(B, S, H, dk).astype(jnp.float32), S)
    k = _rope(k.reshape(B, S, H, dk).astype(jnp.float32), S) * (dk ** -0.5)
    v = v.reshape(B, S, H, dv).astype(jnp.float32)

    log_gamma = jnp.log1p(-jnp.exp2(-5.0 - jnp.arange(H, dtype=jnp.float32)))
    idx = jnp.arange(C, dtype=jnp.float32)
    diff = idx[:, None] - idx[None, :]
    decay_mask = jnp.where(diff >= 0, jnp.exp(log_gamma[:, None, None] * jnp.maximum(diff, 0.0)), 0.0)
    q_decay = jnp.exp(log_gamma[None, :] * (idx[:, None] + 1.0))
    k_decay = jnp.exp(log_gamma[None, :] * (C - 1.0 - idx[:, None]))
    chunk_decay = jnp.exp(log_gamma * C)

    n_chunks = S // C

    def to_chunks(t):
        return t.reshape(B, n_chunks, C, H, t.shape[-1]).transpose(1, 0, 2, 3, 4)

    def step(state, qkv):
        qc, kc, vc = qkv
        scores = jnp.einsum('bnhk,bmhk->bhnm', qc, kc) * decay_mask[None]
        inner = jnp.einsum('bhnm,bmhv->bnhv', scores, vc)
        cross = jnp.einsum('bnhk,bhkv->bnhv', qc, state) * q_decay[None, :, :, None]
        new_state = state * chunk_decay[None, :, None, None] + jnp.einsum(
            'bmhk,bmhv->bhkv', kc * k_decay[None, :, :, None], vc)
        return new_state, inner + cross

    state0 = jnp.zeros((B, H, dk, dv), jnp.float32)
    _, ys = lax.scan(step, state0, (to_chunks(q), to_chunks(k), to_chunks(v)))
    y = ys.transpose(1, 0, 2, 3, 4).reshape(B, S, H, dv)
    y = y * lax.rsqrt(jnp.mean(y * y, axis=-1, keepdims=True) + EPS)
    y = y.reshape(B, S, H * dv).astype(h.dtype)
    return (jax.nn.silu(g) * y) @ w_out


def _t5_bucket(dist):
    d = jnp.maximum(dist, 1).astype(jnp.float32)
    large = REL_MAX_EXACT + (jnp.log(d / REL_MAX_EXACT) / math.log(REL_MAX_DIST / REL_MAX_EXACT)
                             * (REL_BUCKETS - REL_MAX_EXACT)).astype(jnp.int32)
    large = jnp.minimum(large, REL_BUCKETS - 1)
    return jnp.where(dist < REL_MAX_EXACT, dist, large)


def _dilated_group(q, k, v, bias_table, window, dilation):
    B, S, H, Dh = q.shape
    r = dilation
    blk = window // dilation
    L = -(-S // (r * blk)) * r * blk
    pad = ((0, 0), (0, L - S), (0, 0), (0, 0))
    nb = L // r // blk

    def to_sub(t):
        t = jnp.pad(t, pad).reshape(B, L // r, r, H, Dh).transpose(0, 2, 1, 3, 4)
        return t.reshape(B, r, nb, blk, H, Dh)

    qb, kb, vb = to_sub(q), to_sub(k), to_sub(v)
    shift = ((0, 0), (0, 0), (1, 0), (0, 0), (0, 0), (0, 0))
    kk = jnp.concatenate([jnp.pad(kb, shift)[:, :, :-1], kb], axis=3)
    vv = jnp.concatenate([jnp.pad(vb, shift)[:, :, :-1], vb], axis=3)

    qi = jnp.arange(blk)[:, None]
    kj = jnp.arange(2 * blk)[None, :]
    dist_sub = blk + qi - kj
    band = (dist_sub >= 0) & (dist_sub <= blk)
    mask = band[None] & ((jnp.arange(nb)[:, None, None] > 0) | (kj >= blk)[None])
    bias = bias_table[_t5_bucket(jnp.maximum(dist_sub, 0) * r)].astype(jnp.float32)
    bias = bias.transpose(2, 0, 1)

    logits = jnp.einsum('brnqhd,brnkhd->brnhqk', qb, kk) * (Dh ** -0.5) + bias
    logits = jnp.where(mask[:, None], logits, -jnp.inf)
    m = jnp.max(logits, axis=-1, keepdims=True)
    e = jnp.exp(logits - m)
    s = jnp.sum(e, axis=-1)
    o = jnp.einsum('brnhqk,brnkhd->brnqhd', e, vv) / s.transpose(0, 1, 2, 4, 3)[..., None]
    lse = (m[..., 0] + jnp.log(s)).transpose(0, 1, 2, 4, 3)

    o = o.reshape(B, r, L // r, H, Dh).transpose(0, 2, 1, 3, 4).reshape(B, L, H, Dh)[:, :S]
    lse = lse.reshape(B, r, L // r, H).transpose(0, 2, 1, 3).reshape(B, L, H)[:, :S]
    return o, lse


def _dilated_attention_mixer(h, w_in, w_out, rel_bias):
    B, S, _ = h.shape
    H = ATT_HEADS
    qkv = (h @ w_in).reshape(B, S, N_DIL_GROUPS, 3, H, ATT_HEAD_DIM).astype(jnp.float32)
    outs, lses = [], []
    for gi, (window, dil) in enumerate(DILATION_PAIRS):
        o, lse = _dilated_group(qkv[:, :, gi, 0], qkv[:, :, gi, 1], qkv[:, :, gi, 2],
                                rel_bias[:, gi * H:(gi + 1) * H], window, dil)
        outs.append(o)
        lses.append(lse)
    wts = jax.nn.softmax(jnp.stack(lses, axis=0), axis=0)
    o = jnp.sum(wts[..., None] * jnp.stack(outs, axis=0), axis=0)
    return o.reshape(B, S, H * ATT_HEAD_DIM).astype(h.dtype) @ w_out


def _gmlp_mixer(h, w_in, ln_g, ln_b, w_s, b_s, w_out):
    B, S, _ = h.shape
    C, G = GM_CHUNK, GM_GROUPS
    z = jax.nn.gelu(h @ w_in, approximate=False)
    u, v = jnp.split(z, 2, axis=-1)
    vf = v.astype(jnp.float32)
    mu = jnp.mean(vf, axis=-1, keepdims=True)
    var = jnp.mean(jnp.square(vf - mu), axis=-1, keepdims=True)
    vn = (vf - mu) * lax.rsqrt(var + EPS) * ln_g.astype(jnp.float32) + ln_b.astype(jnp.float32)
    vn = vn.reshape(B, S // C, C, G, GM_GROUP_DIM)
    causal = jnp.tril(jnp.ones((C, C), jnp.float32))
    s = jnp.einsum('gnm,bcmgd->bcngd', w_s.astype(jnp.float32) * causal[None], vn)
    s = s + b_s.astype(jnp.float32).T[None, None, :, :, None]
    s = s.reshape(B, S, GM_WIDTH).astype(h.dtype)
    return (u * s) @ w_out


def _swiglu(h, w_in, w_out):
    gate, up = jnp.split(h @ w_in, 2, axis=-1)
    return (jax.nn.silu(gate) * up) @ w_out


def setup_inputs(seed: int = 0) -> dict:
    key = jax.random.key(seed)
    ks = jax.random.split(key, 18)

    def nrm(k, shape, scale):
        return jax.random.normal(k, shape, jnp.float32) * scale

    D = D_MODEL
    return {
        "x": nrm(ks[0], (BATCH, SEQ, D), 1.0),
        "p": nrm(ks[1], (DEPTH, BATCH, SEQ, PLE_DIM), 1.0),
        "norm_g": 1.0 + nrm(ks[2], (DEPTH, N_NORMS, D), 0.1),
        "ret_w_in": nrm(ks[3], (N_A, D, RET_IN_WIDTH), D ** -0.5),
        "ret_w_out": nrm(ks[4], (N_A, RET_HEADS * RET_V_DIM, D), (RET_HEADS * RET_V_DIM) ** -0.5),
        "attn_w_in": nrm(ks[5], (N_B, D, ATT_IN_WIDTH), D ** -0.5),
        "attn_w_out": nrm(ks[6], (N_B, ATT_HEADS * ATT_HEAD_DIM, D), (ATT_HEADS * ATT_HEAD_DIM) ** -0.5),
        "rel_bias": nrm(ks[7], (REL_BUCKETS, N_DIL_GROUPS * ATT_HEADS), 0.3),
        "gm_w_in": nrm(ks[8], (N_C, D, 2 * GM_WIDTH), D ** -0.5),
        "gm_ln_g": 1.0 + nrm(ks[9], (N_C, GM_WIDTH), 0.1),
        "gm_ln_b": nrm(ks[10], (N_C, GM_WIDTH), 0.02),
        "gm_w_s": nrm(ks[11], (N_C, GM_GROUPS, GM_CHUNK, GM_CHUNK), GM_CHUNK ** -0.5),
        "gm_b_s": 1.0 + nrm(ks[12], (N_C, GM_GROUPS, GM_CHUNK), 0.1),
        "gm_w_out": nrm(ks[13], (N_C, GM_WIDTH, D), GM_WIDTH ** -0.5),
        "ffn_w_in": nrm(ks[14], (DEPTH, D, 2 * FFN_HIDDEN), D ** -0.5),
        "ffn_w_out": nrm(ks[15], (DEPTH, FFN_HIDDEN, D), FFN_HIDDEN ** -0.5),
        "ple_w_proj": nrm(ks[16], (DEPTH, PLE_DIM, D), PLE_DIM ** -0.5),
        "ple_w_gate": nrm(ks[17], (DEPTH, D, D), D ** -0.5),
    }


def reference(x, p, norm_g, ret_w_in, ret_w_out, attn_w_in, attn_w_out, rel_bias,
              gm_w_in, gm_ln_g, gm_ln_b, gm_w_s, gm_b_s, gm_w_out,
              ffn_w_in, ffn_w_out, ple_w_proj, ple_w_gate):
    h = x
    for i in range(DEPTH):
        kind, j = i % N_MIXERS, i // N_MIXERS
        a = _rmsnorm(h, norm_g[i, 0])
        if kind == 0:
            mix = _retention_mixer(a, ret_w_in[j], ret_w_out[j])
        elif kind == 1:
            mix = _dilated_attention_mixer(a, attn_w_in[j], attn_w_out[j], rel_bias)
        else:
            mix = _gmlp_mixer(a, gm_w_in[j], gm_ln_g[j], gm_ln_b[j], gm_w_s[j], gm_b_s[j], gm_w_out[j])
        h = h + _rmsnorm(mix.astype(h.dtype), norm_g[i, 1])
        f = _swiglu(_rmsnorm(h, norm_g[i, 2]), ffn_w_in[i], ffn_w_out[i])
        h = h + _rmsnorm(f.astype(h.dtype), norm_g[i, 3])
        emb = p[i].astype(h.dtype) @ ple_w_proj[i]
        gate = jax.nn.sigmoid(_rmsnorm(h, norm_g[i, 4]) @ ple_w_gate[i])
        h = h + gate * emb
    return h
```

```python
import contextlib
import math
import numpy as np
import concourse.bass as bass
import concourse.mybir as mybir
from concourse.bass_utils import run_bass_kernel_spmd

F32 = mybir.dt.float32
BF16 = mybir.dt.bfloat16
I32 = mybir.dt.int32
AF = mybir.ActivationFunctionType
ALU = mybir.AluOpType
AX = mybir.AxisListType

SEM_ROLL = 30000
import os as _os
_SEQ_DEBUG = bool(_os.environ.get('KSEQ'))

D = 1024
KC = 8
BATCH = 2
SEQ = 8192
DEPTH = 4
NCORES = 8
T = 2048
NT = 16
EPS = 1e-6
PLE_DIM = 256
FFN_H = 2816
RET_H, RET_DK, RET_DV = 4, 256, 512
L_RANKS = 3
ATT_H, ATT_DH = 16, 64
DIL = ((128, 1), (512, 4), (2048, 16))
GM_W = 2048
GM_G = 8


class Buf:
    __slots__ = ("name", "w", "r", "dsem", "dcount")

    def __init__(self, name):
        self.name = name
        self.w = None
        self.r = []
        self.dsem = None
        self.dcount = 0


class Prog:
    ENGS = ("pe", "act", "dve", "pool", "sp")

    def __init__(self, nc):
        self.nc = nc
        self.stack = contextlib.ExitStack()
        self.ops = {e: [] for e in self.ENGS}
        self.sems = {}
        self.esem = {}
        self.ecount = {e: 0 for e in self.ENGS}
        self.eroll = {e: 0 for e in self.ENGS}
        self.known = {e: {} for e in self.ENGS}
        self.nsem = 0
        self.final_tokens = []
        self.chans = []
        self.free_sems = []
        for e in self.ENGS:
            self._new_esem(e)

    def sem(self, name):
        h = self.stack.enter_context(self.nc.semaphore(name))
        self.sems[name] = h
        self.nsem += 1
        return name

    def _new_esem(self, e):
        k = self.sem(f"s_{e}_{self.eroll[e]}")
        self.eroll[e] += 1
        self.esem[e] = k
        self.ecount[e] = 0

    def sbuf(self, name, shape, dtype):
        return self.stack.enter_context(self.nc.sbuf_tensor("sb_" + name, list(shape), dtype))

    def psum(self, name, shape, dtype):
        return self.stack.enter_context(self.nc.psum_tensor("ps_" + name, list(shape), dtype))

    def _deps(self, eng, reads, writes, skip_same_pe=False):
        waits = {}

        def need(tok):
            if tok is None:
                return
            k, v = tok
            if skip_same_pe and eng == "pe" and k == self.esem["pe"]:
                return
            if self.known[eng].get(k, 0) >= v:
                return
            if waits.get(k, 0) < v:
                waits[k] = v

        for b in reads:
            need(b.w)
        for b in writes:
            need(b.w)
            for t in b.r:
                need(t)
        for k, v in waits.items():
            self.known[eng][k] = v
        return list(waits.items())

    def _commit(self, tok, reads, writes):
        for b in writes:
            b.w = tok
            b.r = []
        for b in reads:
            b.r.append(tok)
            if len(b.r) > 48:
                best = {}
                for k, v in b.r:
                    if best.get(k, 0) < v:
                        best[k] = v
                b.r = list(best.items())

    def op(self, eng, fn, reads=(), writes=()):
        if self.ecount[eng] >= SEM_ROLL:
            self._new_esem(eng)
        waits = self._deps(eng, reads, writes, skip_same_pe=True)
        self.ecount[eng] += 1
        tok = (self.esem[eng], self.ecount[eng])
        self.ops[eng].append((fn, waits, (self.esem[eng], 1)))
        self._commit(tok, reads, writes)
        return tok

    def dma(self, q, out_ap, in_ap, reads=(), writes=(), chan=None, **kw):
        if chan is None:
            chan = writes[0] if writes else reads[0]
        if chan.dsem is None:
            self.free_sems.sort(key=lambda x: x[1])
            if self.free_sems and self.free_sems[0][1] < 16000:
                chan.dsem, chan.dcount = self.free_sems.pop(0)
            else:
                chan.dsem = self.sem(f"d_{chan.name}")
                chan.dcount = 0
            self.chans.append(chan)
        waits = self._deps(q, reads, writes)
        if chan.dcount > 0 and self.known[q].get(chan.dsem, 0) < chan.dcount:
            waits = [w for w in waits if w[0] != chan.dsem] + [(chan.dsem, chan.dcount)]
            self.known[q][chan.dsem] = chan.dcount
        chan.dcount += 16
        tok = (chan.dsem, chan.dcount)

        def fn(e, out_ap=out_ap, in_ap=in_ap, kw=kw):
            return e.dma_start(out=out_ap, in_=in_ap, **kw)

        self.ops[q].append((fn, waits, (chan.dsem, 16)))
        self._commit(tok, reads, writes)
        return tok

    def barrier(self):
        toks = {}
        for e in self.ENGS:
            if self.ecount[e] > 0:
                toks[self.esem[e]] = self.ecount[e]
        for c in self.chans:
            if c.dcount > 0:
                toks[c.dsem] = c.dcount
        for e in self.ENGS:
            waits = []
            for k, v in toks.items():
                if self.known[e].get(k, 0) < v:
                    waits.append((k, v))
                    self.known[e][k] = v
            if waits:
                self.ops[e].append((None, waits, None))

    def retire(self, bufs):
        for c in bufs:
            if c.dsem is not None:
                self.free_sems.append((c.dsem, c.dcount))
                self.chans.remove(c)
                c.dsem = None
                c.dcount = 0

    def finish_on(self, eng, toks):
        self.final_tokens.append((eng, list(toks)))

    def emit(self):
        nc = self.nc
        for eng, toks in self.final_tokens:
            best = {}
            for k, v in toks:
                if best.get(k, 0) < v:
                    best[k] = v
            self.ops[eng].append((None, list(best.items()), None))
        handles = {"pe": "tensor", "act": "scalar", "dve": "vector", "pool": "gpsimd", "sp": "sync"}
        with nc.Block() as block:
            for eng in self.ENGS:
                ops = self.ops[eng]
                if not ops:
                    continue

                def body(e, ops=ops):
                    for fn, waits, inc in ops:
                        for k, v in waits:
                            e.wait_ge(self.sems[k], v)
                        if fn is not None:
                            ins = fn(e)
                            ins.then_inc(self.sems[inc[0]], inc[1])

                getattr(block, handles[eng])(body)
        self.stack.close()

    def stats(self):
        d = {e: len(v) for e, v in self.ops.items()}
        d["nsem"] = self.nsem
        return d


class Builder:
    ST_BYTES = 84 * 1024

    def __init__(self, nc, dr):
        self.nc = nc
        self.dr = dr
        P = self.P = Prog(nc)
        self.aT = P.sbuf("aT", [128, KC, T], BF16)
        self.aT_b = [Buf(f"aT{t}") for t in range(NT)]
        self.big = P.sbuf("big", [128, NT * D], F32)
        self.acc = self.big[:].rearrange("p (t n) -> p t n", t=NT)
        self.acc_b = [Buf(f"acc{t}") for t in range(NT)]
        self.st = P.sbuf("st", [128, self.ST_BYTES // 2], BF16)
        self.ident = P.sbuf("ident", [128, 128], BF16)
        self.ident_b = Buf("ident")
        self.hring = P.sbuf("hring", [128, 3, D], F32)
        self.hr_b = [Buf("hr0"), Buf("hr1"), Buf("hr2")]
        self.hextra = []
        self.gpost = P.sbuf("gpost", [128, D], F32)
        self.gpost_b = Buf("gpost")
        self.gpre = P.sbuf("gpre", [128, D], F32)
        self.gpre_b = Buf("gpre")
        self.gpre2 = None
        self.gpre2_b = None
        self.abf_n = 2
        self.abf = P.sbuf("abf", [128, 2, D], BF16)
        self.abf_b = [Buf("abf0"), Buf("abf1")]
        self.junk = P.sbuf("junk", [128, D], BF16)
        self.junk_b = Buf("junk")
        self.stat = P.sbuf("stat", [128, 64], F32)
        self.stat_b = [Buf(f"stat{i}") for i in range(64)]
        self.stat_i = 0
        self.psum_all = P.psum("all", [128, 8, 512], F32)
        self.bank = [self.psum_all[:, i, :] for i in range(8)]
        self.bank_b = [Buf(f"bank{i}") for i in range(8)]
        self.hD_b = [Buf(f"hD{t}") for t in range(NT)]
        self.h_rd = None
        self.h_wr = None
        self.hcnt = 0
        P.dma("pool", self.ident[:], dr["ident"][:, :], writes=[self.ident_b])
        _orig_barrier = P.barrier

        def _barrier():
            _orig_barrier()
            P.retire([b for _, b in self.hextra])
            self.hextra = []
        P.barrier = _barrier
        self.cm05 = P.sbuf("cm05", [128, 1], F32)
        self.cm05_b = Buf("cm05")
        P.op("pool", lambda e: e.memset(self.cm05[:], -0.5), writes=[self.cm05_b])

    def st_view(self, off_bytes, shape, dtype):
        n = int(np.prod(shape))
        if dtype == BF16:
            a = self.st[:, off_bytes // 2: off_bytes // 2 + n]
            nb = n * 2
        else:
            a = self.st[:, off_bytes // 2: off_bytes // 2 + 2 * n].bitcast(F32)
            nb = n * 4
        assert off_bytes + nb <= self.ST_BYTES, (off_bytes, nb)
        if len(shape) == 2:
            v = a.rearrange("p (a b) -> p a b", a=shape[0])
        elif len(shape) == 3:
            v = a.rearrange("p (a b c) -> p a b c", a=shape[0], b=shape[1])
        else:
            v = a
        return v, off_bytes + nb

    def set_hextra(self, off_bytes, n):
        self.hextra = []
        for k in range(n):
            v, off_bytes = self.st_view(off_bytes, [1, D], F32)
            self.hextra.append((v[:, 0, :], Buf(f"hx{self.hcnt}_{k}")))

    def new_stat(self):
        i = self.stat_i % 64
        self.stat_i += 1
        return self.stat[:, i:i + 1], self.stat_b[i]

    def load_g(self, which, idx):
        tile, b = {"post": (self.gpost, self.gpost_b), "pre": (self.gpre, self.gpre_b), "pre2": (self.gpre2, self.gpre2_b)}[which]
        i, j = idx
        self.P.dma("sp", tile[:], self.dr["norm_g"][i, j, :].partition_broadcast(128), writes=[b])

    def rstd_of(self, src_ap, src_bufs, n):
        P = self.P
        ss, ss_b = self.new_stat()
        P.op("act", lambda e: e.activation(out=self.junk[:, 0:n], in_=src_ap, func=AF.Square, scale=float(n) ** -0.5, accum_out=ss),
             reads=list(src_bufs), writes=[self.junk_b, ss_b])
        r, r_b = self.new_stat()
        P.op("pool", lambda e: e.tensor_scalar(out=r, in0=ss, scalar1=EPS, scalar2=None, op0=ALU.add),
             reads=[ss_b], writes=[r_b])
        r2, r2_b = self.new_stat()
        P.op("pool", lambda e: e.tensor_tensor(out=r2, in0=r, in1=self.cm05[:], op=ALU.pow),
             reads=[r_b, self.cm05_b], writes=[r2_b])
        return r2, r2_b

    def h_load(self, t, src=None):
        P = self.P
        slots = [(self.hring[:, k, :], self.hr_b[k]) for k in range(3)] + list(self.hextra)
        hs, hb = slots[self.hcnt % len(slots)]
        self.hcnt += 1
        src = self.h_rd if src is None else src
        P.dma("sp", hs, src[t * 128:(t + 1) * 128, :], reads=[self.hD_b[t]], writes=[hb])
        return hs, hb

    def h_store(self, t, hs, hb, dst=None):
        dst = self.h_wr if dst is None else dst
        return self.P.dma("sp", dst[t * 128:(t + 1) * 128, :], hs, reads=[hb], writes=[self.hD_b[t]], chan=hb)

    def pre_norm(self, t, hs, hb, tbank=7):
        P = self.P
        r, r_b = self.rstd_of(hs, [hb], D)
        s = t % 2
        ab, ab_b = self.abf[:, s, :], self.abf_b[s]
        P.op("dve", lambda e: e.scalar_tensor_tensor(out=ab, in0=hs, scalar=r, in1=self.gpre[:], op0=ALU.mult, op1=ALU.mult),
             reads=[hb, r_b, self.gpre_b], writes=[ab_b])
        pst = self.bank[tbank][:].bitcast(BF16).rearrange("p (k n) -> p k n", k=KC)
        for kc in range(KC):
            P.op("pe", lambda e, kc=kc: e.transpose(out=pst[:, kc, :], in_=self.abf[:, s, kc * 128:(kc + 1) * 128], identity=self.ident[:]),
                 reads=[ab_b, self.ident_b], writes=[self.bank_b[tbank]])
        P.op("act", lambda e: e.copy(out=self.aT[:, :, t * 128:(t + 1) * 128], in_=pst),
             reads=[self.bank_b[tbank]], writes=[self.aT_b[t]])

    def boundary(self, t, src_ap, src_b, dst_ap, dst_b, do_pre=True, tbank=7):
        P = self.P
        r, r_b = self.rstd_of(src_ap, src_b, D)
        P.op("dve", lambda e: e.scalar_tensor_tensor(out=dst_ap, in0=src_ap, scalar=r, in1=self.gpost[:], op0=ALU.mult, op1=ALU.mult),
             reads=list(src_b) + [r_b, self.gpost_b], writes=[dst_b])
        hs, hb = self.h_load(t)
        P.op("pool", lambda e: e.tensor_tensor(out=hs, in0=hs, in1=dst_ap, op=ALU.add), reads=[dst_b], writes=[hb])
        tok = self.h_store(t, hs, hb)
        if do_pre:
            self.pre_norm(t, hs, hb, tbank=tbank)
        return tok

    def stage_barrier(self):
        self.P.barrier()
        self.hextra = []

    def first_norm(self):
        for t in range(NT):
            hs, hb = self.h_load(t)
            self.pre_norm(t, hs, hb)

    def wload(self, dst_ap, src_ap, buf):
        return self.P.dma("pool", dst_ap, src_ap, writes=[buf])

    def ffn(self, i):
        P = self.P
        P.barrier()
        w_in = self.dr["ffn_w_in"][i]
        w_out = self.dr["ffn_w_out"][i]
        off = 0
        winA, winA_b = [], []
        for s in range(3):
            v, off = self.st_view(off, [KC, 512], BF16)
            winA.append(v)
            winA_b.append(Buf(f"f{i}winA{s}"))
        woutB, woutB_b = [], []
        for s in range(2):
            v, off = self.st_view(off, [4, D], BF16)
            woutB.append(v)
            woutB_b.append(Buf(f"f{i}woutB{s}"))
        hidT, hidT_b = [], []
        for s in range(2):
            v, off = self.st_view(off, [4, T], BF16)
            hidT.append(v)
            hidT_b.append([[Buf(f"f{i}hid{s}_{jl}_{tg}") for tg in range(4)] for jl in range(4)])
        sil, sil_b = [], []
        for s in range(2):
            v, off = self.st_view(off, [1, 512], F32)
            sil.append(v)
            sil_b.append(Buf(f"f{i}sil{s}"))
        nchunk = FFN_H // 128
        ngrp = (nchunk + 3) // 4
        unit = 0
        for pi in range(nchunk // 2):
            ws, wb = winA[pi % 3], winA_b[pi % 3]
            self.wload(ws[:, :, 0:256], w_in[:, pi * 256:(pi + 1) * 256].rearrange("(kc p) n -> p kc n", p=128), wb)
            self.wload(ws[:, :, 256:512], w_in[:, FFN_H + pi * 256:FFN_H + (pi + 1) * 256].rearrange("(kc p) n -> p kc n", p=128), wb)
            for jj in range(2):
                j = 2 * pi + jj
                grp, jl = j // 4, j % 4
                hs_, hs_b = hidT[grp % 2], hidT_b[grp % 2]
                for tg in range(4):
                    bg, bu = 2 * (unit % 2), 2 * (unit % 2) + 1
                    sl, slb = sil[unit % 2], sil_b[unit % 2]
                    unit += 1
                    for (bk, coff) in ((bg, jj * 128), (bu, 256 + jj * 128)):
                        for kc in range(KC):
                            P.op("pe", lambda e, bk=bk, coff=coff, kc=kc, tg=tg, ws=ws: e.matmul(
                                self.bank[bk][:], lhsT=ws[:, kc, coff:coff + 128], rhs=self.aT[:, kc, tg * 512:(tg + 1) * 512],
                                start=(kc == 0), stop=(kc == KC - 1)),
                                reads=[wb] + self.aT_b[tg * 4:(tg + 1) * 4], writes=[self.bank_b[bk]])
                    P.op("act", lambda e, bg=bg, sl=sl: e.activation(out=sl[:, 0, :], in_=self.bank[bg][:], func=AF.Silu),
                         reads=[self.bank_b[bg]], writes=[slb])
                    P.op("dve", lambda e, bu=bu, sl=sl, hs_=hs_, jl=jl, tg=tg: e.tensor_tensor(
                        out=hs_[:, jl, tg * 512:(tg + 1) * 512], in0=sl[:, 0, :], in1=self.bank[bu][:], op=ALU.mult),
                        reads=[slb, self.bank_b[bu]], writes=[hs_b[jl][tg]])
                if jl == 3 or j == nchunk - 1:
                    n = jl + 1
                    wo, wob = woutB[grp % 2], woutB_b[grp % 2]
                    self.wload(wo[:, 0:n, :], w_out[grp * 512:grp * 512 + n * 128, :].rearrange("(jl p) n -> p jl n", p=128), wob)
                    for t in range(NT):
                        for half in range(2):
                            bk = 4 + 2 * (t % 2) + half
                            for q in range(n):
                                P.op("pe", lambda e, bk=bk, q=q, t=t, half=half, hs_=hs_, wo=wo, n=n: e.matmul(
                                    self.bank[bk][:], lhsT=hs_[:, q, t * 128:(t + 1) * 128], rhs=wo[:, q, half * 512:(half + 1) * 512],
                                    start=(q == 0), stop=(q == n - 1)),
                                    reads=[wob, hs_b[q][t // 4]], writes=[self.bank_b[bk]])
                            dst = self.acc[:, t, half * 512:(half + 1) * 512]
                            if grp == 0:
                                P.op("act", lambda e, bk=bk, dst=dst: e.copy(out=dst, in_=self.bank[bk][:]),
                                     reads=[self.bank_b[bk]], writes=[self.acc_b[t]])
                            else:
                                P.op("dve", lambda e, bk=bk, dst=dst: e.tensor_tensor(out=dst, in0=dst, in1=self.bank[bk][:], op=ALU.add),
                                     reads=[self.bank_b[bk]], writes=[self.acc_b[t]])
        assert grp == ngrp - 1

    def ple(self, i, do_pre=True):
        P = self.P
        P.barrier()
        off = 0
        Wg, off = self.st_view(off, [KC, D], BF16)
        Wg_b = Buf(f"p{i}Wg")
        Wp, off = self.st_view(off, [2, D], BF16)
        Wp_b = Buf(f"p{i}Wp")
        pb, pb_b, pT, pT_b, sg, sg_b = [], [], [], [], [], []
        for s in range(2):
            v, off = self.st_view(off, [1, 256], BF16)
            pb.append(v)
            pb_b.append(Buf(f"p{i}pb{s}"))
            v, off = self.st_view(off, [2, 128], BF16)
            pT.append(v)
            pT_b.append(Buf(f"p{i}pT{s}"))
            v, off = self.st_view(off, [2, 512], F32)
            sg.append(v)
            sg_b.append(Buf(f"p{i}sg{s}"))
        self.wload(Wg, self.dr["ple_w_gate"][i].rearrange("(kc p) n -> p kc n", p=128), Wg_b)
        self.wload(Wp, self.dr["ple_w_proj"][i].rearrange("(kc p) n -> p kc n", p=128), Wp_b)
        toks = []
        for t in range(NT):
            s = t % 2
            self.wload(pb[s][:, 0, :], self.dr["p"][i, t * 128:(t + 1) * 128, :], pb_b[s])
            ptb = 6
            pst = self.bank[ptb][:].bitcast(BF16)[:, 0:256].rearrange("p (k n) -> p k n", k=2)
            for kc in range(2):
                P.op("pe", lambda e, kc=kc, s=s, pst=pst: e.transpose(out=pst[:, kc, :], in_=pb[s][:, 0, kc * 128:(kc + 1) * 128], identity=self.ident[:]),
                     reads=[pb_b[s], self.ident_b], writes=[self.bank_b[ptb]])
            P.op("dve", lambda e, s=s, pst=pst: e.tensor_copy(out=pT[s], in_=pst), reads=[self.bank_b[ptb]], writes=[pT_b[s]])
            for half in range(2):
                u = 2 * t + half
                bG, bE = u % 3, 3 + u % 3
                for kc in range(KC):
                    P.op("pe", lambda e, kc=kc, half=half, bG=bG, t=t: e.matmul(
                        self.bank[bG][:], lhsT=self.aT[:, kc, t * 128:(t + 1) * 128], rhs=Wg[:, kc, half * 512:(half + 1) * 512],
                        start=(kc == 0), stop=(kc == KC - 1)), reads=[Wg_b, self.aT_b[t]], writes=[self.bank_b[bG]])
                for kc in range(2):
                    P.op("pe", lambda e, kc=kc, half=half, bE=bE, s=s: e.matmul(
                        self.bank[bE][:], lhsT=pT[s][:, kc, :], rhs=Wp[:, kc, half * 512:(half + 1) * 512],
                        start=(kc == 0), stop=(kc == 1)), reads=[Wp_b, pT_b[s]], writes=[self.bank_b[bE]])
                P.op("act", lambda e, half=half, bG=bG, s=s: e.activation(out=sg[s][:, half, :], in_=self.bank[bG][:], func=AF.Sigmoid),
                     reads=[self.bank_b[bG]], writes=[sg_b[s]])
                P.op("dve", lambda e, half=half, bE=bE, s=s: e.tensor_tensor(out=sg[s][:, half, :], in0=sg[s][:, half, :], in1=self.bank[bE][:], op=ALU.mult),
                     reads=[self.bank_b[bE]], writes=[sg_b[s]])
            hs, hb = self.h_load(t)
            P.op("pool", lambda e, hs=hs, s=s: e.tensor_tensor(out=hs, in0=hs, in1=sg[s].rearrange("p a b -> p (a b)"), op=ALU.add),
                 reads=[sg_b[s]], writes=[hb])
            toks.append(self.h_store(t, hs, hb))
            if do_pre:
                self.pre_norm(t, hs, hb)
        return toks

    def ffn_stage(self, i):
        self.load_g("post", (i, 3))
        self.load_g("pre", (i, 4))
        self.ffn(i)
        toks = []
        for t in range(NT):
            a = self.acc[:, t, :]
            toks.append(self.boundary(t, a, [self.acc_b[t]], a, self.acc_b[t]))
        return toks

    def ple_stage(self, i, last):
        if not last:
            self.load_g("pre", (i + 1, 0))
        return self.ple(i, do_pre=not last)


def _ret_emit(self, j, pre):
    P = self.P
    P.barrier()
    dr = self.dr
    w_in = dr["ret_w_in"][j]
    w_out = dr["ret_w_out"][j]
    off = 0
    W, off = self.st_view(off, [KC, 1536], BF16)
    W_b = {k: Buf(f"r{j}{pre}W{k}") for k in "qkvg"}
    wo, off = self.st_view(off, [4, D], BF16)
    wo_b = Buf(f"r{j}{pre}wo")
    cs, cs_b = [], []
    for s in range(2):
        v, off = self.st_view(off, [2, 512], F32)
        cs.append(v)
        cs_b.append(Buf(f"r{j}{pre}cs{s}"))
    qT, off = self.st_view(off, [2, 512], BF16)
    kT, off = self.st_view(off, [2, 512], BF16)
    qdT, off = self.st_view(off, [2, 512], BF16)
    qT_b, kT_b, qdT_b = Buf("qT"), Buf("kT"), Buf("qdT")
    rt, off = self.st_view(off, [4, 512], F32)
    rt_b = [Buf(f"rt{x}") for x in range(4)]
    qdec, off = self.st_view(off, [1, 512], F32)
    qdec_b = Buf("qdec")
    v_sb, sg, sm, kd, gy, gyT = [], [], [], [], [], []
    v_b, sg_b, sm_b, kd_b, gy_b, gyT_b = [], [], [], [], [], []
    for s in range(2):
        a, off = self.st_view(off, [1, 512], BF16); v_sb.append(a); v_b.append(Buf(f"v{s}"))
        a, off = self.st_view(off, [1, 512], BF16); sg.append(a); sg_b.append(Buf(f"sg{s}"))
        a, off = self.st_view(off, [1, 128], BF16); sm.append(a); sm_b.append(Buf(f"sm{s}"))
        a, off = self.st_view(off, [1, 256], BF16); kd.append(a); kd_b.append(Buf(f"kd{s}"))
        a, off = self.st_view(off, [1, 512], BF16); gy.append(a); gy_b.append(Buf(f"gy{s}"))
        a, off = self.st_view(off, [4, 128], BF16); gyT.append(a); gyT_b.append(Buf(f"gyT{s}"))
    maskT, off = self.st_view(off, [RET_H, 128], F32)
    maskT_b = Buf("maskT")
    kdec, off = self.st_view(off, [1, RET_H], F32)
    kdec_b = Buf("kdec")
    coef, off = self.st_view(off, [1, NCORES * RET_H], F32)
    coef_b = Buf("coef")
    state, off = self.st_view(off, [2, 512], F32)
    state_b = Buf("state")
    sbf, off = self.st_view(off, [2, 512], BF16)
    sbf_b = Buf("sbf")
    P.dma("sp", maskT, dr["ret_maskT"].rearrange("h m n -> m h n"), writes=[maskT_b])
    P.dma("sp", kdec[:, 0, :], dr["ret_kdec"][:, :], writes=[kdec_b])
    P.dma("sp", coef[:, 0, :], dr["ret_coef"][:, :], writes=[coef_b])
    bk = self.bank
    bb = self.bank_b
    b6 = bk[6][:]
    sT_ps = b6[:, 0:128]
    sT_pb = Buf("b6a")
    kdT_ps = b6[:, 128:256].bitcast(BF16).rearrange("p (a b) -> p a b", a=2)
    kdT_pb = Buf("b6b")
    gyT_ps = b6[:, 256:512].bitcast(BF16).rearrange("p (a b) -> p a b", a=4)
    gyT_pb = Buf("b6c")
    gam = [1.0 - 2.0 ** (-5.0 - h) for h in range(RET_H)]

    for hd in range(RET_H):
        cd = gam[hd] ** 128
        cols = {"q": (hd * 256, 256, 0), "k": (1024 + hd * 256, 256, 256), "v": (2048 + hd * 512, 512, 512), "g": (4096 + hd * 512, 512, 1024)}
        for k in ("k", "v") if pre else ("q", "k", "v", "g"):
            c0, n, o = cols[k]
            self.wload(W[:, :, o:o + n], w_in[:, c0:c0 + n].rearrange("(kc p) n -> p kc n", p=128), W_b[k])
        if not pre:
            self.wload(wo, w_out[hd * 512:(hd + 1) * 512, :].rearrange("(q p) n -> p q n", p=128), wo_b)
            P.dma("sp", qdec[:, 0, :], dr["ret_qdec"][hd], writes=[qdec_b])
        if pre:
            P.op("pool", lambda e: e.memset(state, 0.0), writes=[state_b])
        else:
            tmp = rt[:, 0:2, :]
            for r in range(NCORES):
                P.dma("sp", tmp, dr["L_all"][r, hd].rearrange("(fc p) n -> p fc n", p=128), writes=[rt_b[0], rt_b[1]])
                cf = coef[:, 0, r * RET_H + hd:r * RET_H + hd + 1]
                if r == 0:
                    P.op("dve", lambda e, cf=cf: e.tensor_scalar(out=state, in0=tmp, scalar1=cf, scalar2=None, op0=ALU.mult),
                         reads=[rt_b[0], rt_b[1], coef_b], writes=[state_b])
                else:
                    P.op("dve", lambda e, cf=cf: e.scalar_tensor_tensor(out=state, in0=tmp, scalar=cf, in1=state, op0=ALU.mult, op1=ALU.add),
                         reads=[rt_b[0], rt_b[1], coef_b], writes=[state_b])
        P.op("act", lambda e: e.copy(out=sbf, in_=state), reads=[state_b], writes=[sbf_b])

        for tg in range(4):
            cst, cst_b = cs[tg % 2], cs_b[tg % 2]
            P.dma("sp", cst[:, 0, :], dr["rope_cos"][:, tg * 512:(tg + 1) * 512], writes=[cst_b])
            P.dma("sp", cst[:, 1, :], dr["rope_sin"][:, tg * 512:(tg + 1) * 512], writes=[cst_b])
            for which in ("k",) if pre else ("q", "k"):
                o = cols[which][2]
                b0 = 0 if which == "q" else 2
                for fc in range(2):
                    for kc in range(KC):
                        P.op("pe", lambda e, fc=fc, kc=kc, o=o, b0=b0, tg=tg: e.matmul(
                            bk[b0 + fc][:], lhsT=W[:, kc, o + fc * 128:o + (fc + 1) * 128], rhs=self.aT[:, kc, tg * 512:(tg + 1) * 512],
                            start=(kc == 0), stop=(kc == KC - 1)),
                            reads=[W_b[which]] + self.aT_b[tg * 4:(tg + 1) * 4], writes=[bb[b0 + fc]])
                dstT, dst_b = (qT, qT_b) if which == "q" else (kT, kT_b)
                t1, t2 = bk[b0][:], bk[b0 + 1][:]
                P.op("dve", lambda e, t1=t1, cst=cst: e.tensor_tensor(out=rt[:, 0, :], in0=t1, in1=cst[:, 0, :], op=ALU.mult),
                     reads=[bb[b0], cst_b], writes=[rt_b[0]])
                P.op("dve", lambda e, t2=t2, cst=cst: e.tensor_tensor(out=rt[:, 1, :], in0=t2, in1=cst[:, 1, :], op=ALU.mult),
                     reads=[bb[b0 + 1], cst_b], writes=[rt_b[1]])
                P.op("dve", lambda e, t2=t2, cst=cst: e.tensor_tensor(out=rt[:, 2, :], in0=t2, in1=cst[:, 0, :], op=ALU.mult),
                     reads=[bb[b0 + 1], cst_b], writes=[rt_b[2]])
                P.op("dve", lambda e, t1=t1, cst=cst: e.tensor_tensor(out=rt[:, 3, :], in0=t1, in1=cst[:, 1, :], op=ALU.mult),
                     reads=[bb[b0], cst_b], writes=[rt_b[3]])
                P.op("pool", lambda e, dstT=dstT: e.tensor_tensor(out=dstT[:, 0, :], in0=rt[:, 0, :], in1=rt[:, 1, :], op=ALU.subtract),
                     reads=[rt_b[0], rt_b[1]], writes=[dst_b])
                P.op("pool", lambda e, dstT=dstT: e.tensor_tensor(out=dstT[:, 1, :], in0=rt[:, 2, :], in1=rt[:, 3, :], op=ALU.add),
                     reads=[rt_b[2], rt_b[3]], writes=[dst_b])
                if which == "q":
                    for fc in range(2):
                        P.op("pool", lambda e, fc=fc: e.tensor_tensor(out=qdT[:, fc, :], in0=qT[:, fc, :], in1=qdec[:, 0, :], op=ALU.mult),
                             reads=[qT_b, qdec_b], writes=[qdT_b])
            for cl in range(4):
                c = tg * 4 + cl
                s = c % 2
                tsl = slice(cl * 128, (cl + 1) * 128)
                csl = slice(c * 128, (c + 1) * 128)
                for kc in range(KC):
                    P.op("pe", lambda e, kc=kc, csl=csl: e.matmul(bk[4][:], lhsT=self.aT[:, kc, csl], rhs=W[:, kc, 512:1024],
                                                               start=(kc == 0), stop=(kc == KC - 1)),
                         reads=[W_b["v"], self.aT_b[c]], writes=[bb[4]])
                P.op("act", lambda e, s=s: e.copy(out=v_sb[s][:, 0, :], in_=bk[4][:]), reads=[bb[4]], writes=[v_b[s]])
                if not pre:
                    for kc in range(KC):
                        P.op("pe", lambda e, kc=kc, csl=csl: e.matmul(bk[5][:], lhsT=self.aT[:, kc, csl], rhs=W[:, kc, 1024:1536],
                                                                   start=(kc == 0), stop=(kc == KC - 1)),
                             reads=[W_b["g"], self.aT_b[c]], writes=[bb[5]])
                    P.op("act", lambda e, s=s: e.activation(out=sg[s][:, 0, :], in_=bk[5][:], func=AF.Silu), reads=[bb[5]], writes=[sg_b[s]])
                    for fc in range(2):
                        P.op("pe", lambda e, fc=fc, tsl=tsl: e.matmul(sT_ps, lhsT=kT[:, fc, tsl], rhs=qT[:, fc, tsl], start=(fc == 0), stop=(fc == 1)),
                             reads=[kT_b, qT_b], writes=[sT_pb])
                    P.op("dve", lambda e, s=s, hd=hd: e.tensor_tensor(out=sm[s][:, 0, :], in0=sT_ps, in1=maskT[:, hd, :], op=ALU.mult),
                         reads=[sT_pb, maskT_b], writes=[sm_b[s]])
                for fc in range(2):
                    P.op("pe", lambda e, fc=fc, tsl=tsl: e.transpose(out=kdT_ps[:, fc, :], in_=kT[:, fc, tsl], identity=self.ident[:]),
                         reads=[kT_b, self.ident_b], writes=[kdT_pb])
                P.op("dve", lambda e, s=s, hd=hd: e.tensor_scalar(out=kd[s][:, 0, :], in0=kdT_ps.rearrange("p a b -> p (a b)"),
                                                                 scalar1=kdec[:, 0, hd:hd + 1], scalar2=None, op0=ALU.mult),
                     reads=[kdT_pb, kdec_b], writes=[kd_b[s]])
                if not pre:
                    P.op("pe", lambda e, s=s: e.matmul(bk[7][:], lhsT=sm[s][:, 0, :], rhs=v_sb[s][:, 0, :], start=True, stop=False),
                         reads=[sm_b[s], v_b[s]], writes=[bb[7]])
                    for fc in range(2):
                        P.op("pe", lambda e, fc=fc, tsl=tsl: e.matmul(bk[7][:], lhsT=qdT[:, fc, tsl], rhs=sbf[:, fc, :], start=False, stop=(fc == 1)),
                             reads=[qdT_b, sbf_b], writes=[bb[7]])
                for fc in range(2):
                    P.op("pe", lambda e, fc=fc, s=s: e.matmul(bk[fc][:], lhsT=kd[s][:, 0, fc * 128:(fc + 1) * 128], rhs=v_sb[s][:, 0, :], start=True, stop=True),
                         reads=[kd_b[s], v_b[s]], writes=[bb[fc]])
                for fc in range(2):
                    P.op("dve", lambda e, fc=fc, cd=cd: e.scalar_tensor_tensor(out=state[:, fc, :], in0=state[:, fc, :], scalar=float(cd), in1=bk[fc][:],
                                                                           op0=ALU.mult, op1=ALU.add),
                         reads=[bb[fc]], writes=[state_b])
                P.op("act", lambda e: e.copy(out=sbf, in_=state), reads=[state_b], writes=[sbf_b])
                if not pre:
                    r, r_b = self.rstd_of(bk[7][:], [bb[7]], 512)
                    P.op("dve", lambda e, s=s, r=r: e.scalar_tensor_tensor(out=gy[s][:, 0, :], in0=bk[7][:], scalar=r, in1=sg[s][:, 0, :],
                                                                         op0=ALU.mult, op1=ALU.mult),
                         reads=[bb[7], r_b, sg_b[s]], writes=[gy_b[s]])
                    for q in range(4):
                        P.op("pe", lambda e, q=q, s=s: e.transpose(out=gyT_ps[:, q, :], in_=gy[s][:, 0, q * 128:(q + 1) * 128], identity=self.ident[:]),
                             reads=[gy_b[s], self.ident_b], writes=[gyT_pb])
                    P.op("act", lambda e, s=s: e.copy(out=gyT[s], in_=gyT_ps), reads=[gyT_pb], writes=[gyT_b[s]])
                    for half in range(2):
                        for q in range(4):
                            P.op("pe", lambda e, q=q, s=s, half=half: e.matmul(bk[2 + half][:], lhsT=gyT[s][:, q, :], rhs=wo[:, q, half * 512:(half + 1) * 512],
                                                                              start=(q == 0), stop=(q == 3)),
                                 reads=[gyT_b[s], wo_b], writes=[bb[2 + half]])
                        dst = self.acc[:, c, half * 512:(half + 1) * 512]
                        if hd == 0:
                            P.op("act", lambda e, dst=dst, half=half: e.copy(out=dst, in_=bk[2 + half][:]), reads=[bb[2 + half]], writes=[self.acc_b[c]])
                        else:
                            P.op("dve", lambda e, dst=dst, half=half: e.tensor_tensor(out=dst, in0=dst, in1=bk[2 + half][:], op=ALU.add),
                                 reads=[bb[2 + half]], writes=[self.acc_b[c]])
        if pre:
            P.dma("sp", dr["L_out"][hd].rearrange("(fc p) n -> p fc n", p=128), state, reads=[state_b], chan=state_b)
    if pre:
        return [(state_b.dsem, state_b.dcount)]
    return []


Builder.ret_emit = _ret_emit


def _ret_emit2(self, j, pre, boundary=False):
    P = self.P
    P.barrier()
    dr = self.dr
    w_in = dr["ret_w_in"][j]
    w_out = dr["ret_w_out"][j]
    off = 0
    W, off = self.st_view(off, [KC, 1536], BF16)
    W_b = {k: Buf(f"r{j}{pre}W{k}") for k in "qkvg"}
    wo, off = self.st_view(off, [4, D], BF16)
    wo_b = Buf(f"r{j}{pre}wo")
    cs, cs_b = [], []
    for s in range(2):
        v, off = self.st_view(off, [2, 512], F32)
        cs.append(v)
        cs_b.append(Buf(f"r{j}{pre}cs{s}"))
    qT, off = self.st_view(off, [2, 512], BF16)
    kT, off = self.st_view(off, [2, 512], BF16)
    qdT, off = self.st_view(off, [2, 512], BF16)
    qT_b, kT_b, qdT_b = Buf("qT"), Buf("kT"), Buf("qdT")
    rt, off = self.st_view(off, [4, 512], F32)
    rt_b = [Buf(f"rt{x}") for x in range(4)]
    qdec, off = self.st_view(off, [1, 512], F32)
    qdec_b = Buf("qdec")
    v_sb, sg, sm, kd, gy, gyT = [], [], [], [], [], []
    v_b, sg_b, sm_b, kd_b, gy_b, gyT_b = [], [], [], [], [], []
    for s in range(2):
        a, off = self.st_view(off, [1, 512], BF16); v_sb.append(a); v_b.append(Buf(f"v{s}"))
        a, off = self.st_view(off, [1, 512], BF16); sg.append(a); sg_b.append(Buf(f"sg{s}"))
        a, off = self.st_view(off, [1, 128], BF16); sm.append(a); sm_b.append(Buf(f"sm{s}"))
        a, off = self.st_view(off, [1, 256], BF16); kd.append(a); kd_b.append(Buf(f"kd{s}"))
        a, off = self.st_view(off, [1, 512], BF16); gy.append(a); gy_b.append(Buf(f"gy{s}"))
        a, off = self.st_view(off, [4, 128], BF16); gyT.append(a); gyT_b.append(Buf(f"gyT{s}"))
    maskT, off = self.st_view(off, [RET_H, 128], F32)
    maskT_b = Buf("maskT")
    kdec, off = self.st_view(off, [1, RET_H], F32)
    kdec_b = Buf("kdec")
    coef, off = self.st_view(off, [1, L_RANKS * RET_H], F32)
    coef_b = Buf("coef")
    state, off = self.st_view(off, [2, 512], F32)
    state_b = Buf("state")
    sbf, off = self.st_view(off, [2, 512], BF16)
    sbf_b = Buf("sbf")
    self.set_hextra(off, max(0, (self.ST_BYTES - off) // 4096))
    P.dma("sp", maskT, dr["ret_maskT"].rearrange("h m n -> m h n"), writes=[maskT_b])
    P.dma("sp", kdec[:, 0, :], dr["ret_kdec"][:, :], writes=[kdec_b])
    P.dma("sp", coef[:, 0, :], dr["ret_coef"][:, :], writes=[coef_b])
    bk = self.bank
    bb = self.bank_b
    b6 = bk[6][:]
    sT_ps = b6[:, 0:128]
    sT_pb = bb[6]
    kdT_ps = b6[:, 128:256].bitcast(BF16).rearrange("p (a b) -> p a b", a=2)
    kdT_pb = bb[6]
    gyT_ps = bk[7][:, 256:512].bitcast(BF16).rearrange("p (a b) -> p a b", a=4)
    gyT_pb = bb[7]
    gam = [1.0 - 2.0 ** (-5.0 - h) for h in range(RET_H)]
    btoks = []

    for hd in range(RET_H):
        cd = gam[hd] ** 128
        cols = {"q": (hd * 256, 256, 0), "k": (1024 + hd * 256, 256, 256), "v": (2048 + hd * 512, 512, 512), "g": (4096 + hd * 512, 512, 1024)}
        for k in ("k", "v") if pre else ("q", "k", "v", "g"):
            c0, n, o = cols[k]
            self.wload(W[:, :, o:o + n], w_in[:, c0:c0 + n].rearrange("(kc p) n -> p kc n", p=128), W_b[k])
        if not pre:
            self.wload(wo, w_out[hd * 512:(hd + 1) * 512, :].rearrange("(q p) n -> p q n", p=128), wo_b)
            P.dma("sp", qdec[:, 0, :], dr["ret_qdec"][hd], writes=[qdec_b])
        if pre:
            P.op("pool", lambda e: e.memset(state, 0.0), writes=[state_b])
        else:
            tmp = rt[:, 0:2, :]
            for r in range(L_RANKS):
                P.dma("sp", tmp, dr["L_all"][r, hd].rearrange("(fc p) n -> p fc n", p=128), writes=[rt_b[0], rt_b[1]])
                cf = coef[:, 0, r * RET_H + hd:r * RET_H + hd + 1]
                if r == 0:
                    P.op("dve", lambda e, cf=cf: e.tensor_scalar(out=state, in0=tmp, scalar1=cf, scalar2=None, op0=ALU.mult),
                         reads=[rt_b[0], rt_b[1], coef_b], writes=[state_b])
                else:
                    P.op("dve", lambda e, cf=cf: e.scalar_tensor_tensor(out=state, in0=tmp, scalar=cf, in1=state, op0=ALU.mult, op1=ALU.add),
                         reads=[rt_b[0], rt_b[1], coef_b], writes=[state_b])
        P.op("act", lambda e: e.copy(out=sbf, in_=state), reads=[state_b], writes=[sbf_b])

        def qk_proj(tg, hd=hd, cols=cols):
            cst, cst_b = cs[tg % 2], cs_b[tg % 2]
            P.dma("sp", cst[:, 0, :], dr["rope_cos"][:, tg * 512:(tg + 1) * 512], writes=[cst_b])
            P.dma("sp", cst[:, 1, :], dr["rope_sin"][:, tg * 512:(tg + 1) * 512], writes=[cst_b])
            for which in ("k",) if pre else ("q", "k"):
                o = cols[which][2]
                b0 = 0 if which == "q" else 2
                for fc in range(2):
                    for kc in range(KC):
                        P.op("pe", lambda e, fc=fc, kc=kc, o=o, b0=b0, tg=tg: e.matmul(
                            bk[b0 + fc][:], lhsT=W[:, kc, o + fc * 128:o + (fc + 1) * 128], rhs=self.aT[:, kc, tg * 512:(tg + 1) * 512],
                            start=(kc == 0), stop=(kc == KC - 1)),
                            reads=[W_b[which]] + self.aT_b[tg * 4:(tg + 1) * 4], writes=[bb[b0 + fc]])
                dstT, dst_b = (qT, qT_b) if which == "q" else (kT, kT_b)
                t1, t2 = bk[b0][:], bk[b0 + 1][:]
                P.op("dve", lambda e, t1=t1, cst=cst: e.tensor_tensor(out=rt[:, 0, :], in0=t1, in1=cst[:, 0, :], op=ALU.mult),
                     reads=[bb[b0], cst_b], writes=[rt_b[0]])
                P.op("dve", lambda e, t2=t2, cst=cst: e.tensor_tensor(out=rt[:, 1, :], in0=t2, in1=cst[:, 1, :], op=ALU.mult),
                     reads=[bb[b0 + 1], cst_b], writes=[rt_b[1]])
                P.op("dve", lambda e, t2=t2, cst=cst: e.tensor_tensor(out=rt[:, 2, :], in0=t2, in1=cst[:, 0, :], op=ALU.mult),
                     reads=[bb[b0 + 1], cst_b], writes=[rt_b[2]])
                P.op("dve", lambda e, t1=t1, cst=cst: e.tensor_tensor(out=rt[:, 3, :], in0=t1, in1=cst[:, 1, :], op=ALU.mult),
                     reads=[bb[b0], cst_b], writes=[rt_b[3]])
                P.op("pool", lambda e, dstT=dstT: e.tensor_tensor(out=dstT[:, 0, :], in0=rt[:, 0, :], in1=rt[:, 1, :], op=ALU.subtract),
                     reads=[rt_b[0], rt_b[1]], writes=[dst_b])
                P.op("pool", lambda e, dstT=dstT: e.tensor_tensor(out=dstT[:, 1, :], in0=rt[:, 2, :], in1=rt[:, 3, :], op=ALU.add),
                     reads=[rt_b[2], rt_b[3]], writes=[dst_b])
                if which == "q":
                    for fc in range(2):
                        P.op("pool", lambda e, fc=fc: e.tensor_tensor(out=qdT[:, fc, :], in0=qT[:, fc, :], in1=qdec[:, 0, :], op=ALU.mult),
                             reads=[qT_b, qdec_b], writes=[qdT_b])
        def ph0(c, hd=hd, cd=cd, qk_proj=qk_proj):
            if True:
                tg, cl = c // 4, c % 4
                if cl == 0:
                    qk_proj(tg)
                s = c % 2
                tsl = slice(cl * 128, (cl + 1) * 128)
                csl = slice(c * 128, (c + 1) * 128)
                for kc in range(KC):
                    P.op("pe", lambda e, kc=kc, csl=csl: e.matmul(bk[4][:], lhsT=self.aT[:, kc, csl], rhs=W[:, kc, 512:1024],
                                                               start=(kc == 0), stop=(kc == KC - 1)),
                         reads=[W_b["v"], self.aT_b[c]], writes=[bb[4]])
                P.op("act", lambda e, s=s: e.copy(out=v_sb[s][:, 0, :], in_=bk[4][:]), reads=[bb[4]], writes=[v_b[s]])
                if not pre:
                    for kc in range(KC):
                        P.op("pe", lambda e, kc=kc, csl=csl: e.matmul(bk[5][:], lhsT=self.aT[:, kc, csl], rhs=W[:, kc, 1024:1536],
                                                                   start=(kc == 0), stop=(kc == KC - 1)),
                             reads=[W_b["g"], self.aT_b[c]], writes=[bb[5]])
                    P.op("act", lambda e, s=s: e.activation(out=sg[s][:, 0, :], in_=bk[5][:], func=AF.Silu), reads=[bb[5]], writes=[sg_b[s]])
                    for fc in range(2):
                        P.op("pe", lambda e, fc=fc, tsl=tsl: e.matmul(sT_ps, lhsT=kT[:, fc, tsl], rhs=qT[:, fc, tsl], start=(fc == 0), stop=(fc == 1)),
                             reads=[kT_b, qT_b], writes=[sT_pb])
                    P.op("dve", lambda e, s=s, hd=hd: e.tensor_tensor(out=sm[s][:, 0, :], in0=sT_ps, in1=maskT[:, hd, :], op=ALU.mult),
                         reads=[sT_pb, maskT_b], writes=[sm_b[s]])
                for fc in range(2):
                    P.op("pe", lambda e, fc=fc, tsl=tsl: e.transpose(out=kdT_ps[:, fc, :], in_=kT[:, fc, tsl], identity=self.ident[:]),
                         reads=[kT_b, self.ident_b], writes=[kdT_pb])
                P.op("dve", lambda e, s=s, hd=hd: e.tensor_scalar(out=kd[s][:, 0, :], in0=kdT_ps.rearrange("p a b -> p (a b)"),
                                                                 scalar1=kdec[:, 0, hd:hd + 1], scalar2=None, op0=ALU.mult),
                     reads=[kdT_pb, kdec_b], writes=[kd_b[s]])
        def ph1(c, hd=hd, cd=cd):
            if True:
                tg, cl = c // 4, c % 4
                s = c % 2
                tsl = slice(cl * 128, (cl + 1) * 128)
                csl = slice(c * 128, (c + 1) * 128)
                if not pre:
                    P.op("pe", lambda e, s=s: e.matmul(bk[7][:], lhsT=sm[s][:, 0, :], rhs=v_sb[s][:, 0, :], start=True, stop=False),
                         reads=[sm_b[s], v_b[s]], writes=[bb[7]])
                    for fc in range(2):
                        P.op("pe", lambda e, fc=fc, tsl=tsl: e.matmul(bk[7][:], lhsT=qdT[:, fc, tsl], rhs=sbf[:, fc, :], start=False, stop=(fc == 1)),
                             reads=[qdT_b, sbf_b], writes=[bb[7]])
                for fc in range(2):
                    P.op("pe", lambda e, fc=fc, s=s: e.matmul(bk[fc][:], lhsT=kd[s][:, 0, fc * 128:(fc + 1) * 128], rhs=v_sb[s][:, 0, :], start=True, stop=True),
                         reads=[kd_b[s], v_b[s]], writes=[bb[fc]])
                for fc in range(2):
                    P.op("dve", lambda e, fc=fc, cd=cd: e.scalar_tensor_tensor(out=state[:, fc, :], in0=state[:, fc, :], scalar=float(cd), in1=bk[fc][:],
                                                                           op0=ALU.mult, op1=ALU.add),
                         reads=[bb[fc]], writes=[state_b])
                if not pre:
                    P.op("act", lambda e: e.copy(out=sbf, in_=state), reads=[state_b], writes=[sbf_b])
                if not pre:
                    r, r_b = self.rstd_of(bk[7][:], [bb[7]], 512)
                    P.op("dve", lambda e, s=s, r=r: e.scalar_tensor_tensor(out=gy[s][:, 0, :], in0=bk[7][:], scalar=r, in1=sg[s][:, 0, :],
                                                                         op0=ALU.mult, op1=ALU.mult),
                         reads=[bb[7], r_b, sg_b[s]], writes=[gy_b[s]])
        def ph2(c, hd=hd, cd=cd):
            if True:
                tg, cl = c // 4, c % 4
                s = c % 2
                if not pre:
                    for q in range(4):
                        P.op("pe", lambda e, q=q, s=s: e.transpose(out=gyT_ps[:, q, :], in_=gy[s][:, 0, q * 128:(q + 1) * 128], identity=self.ident[:]),
                             reads=[gy_b[s], self.ident_b], writes=[gyT_pb])
                    P.op("act", lambda e, s=s: e.copy(out=gyT[s], in_=gyT_ps), reads=[gyT_pb], writes=[gyT_b[s]])
                    for half in range(2):
                        for q in range(4):
                            P.op("pe", lambda e, q=q, s=s, half=half: e.matmul(bk[2 + half][:], lhsT=gyT[s][:, q, :], rhs=wo[:, q, half * 512:(half + 1) * 512],
                                                                              start=(q == 0), stop=(q == 3)),
                                 reads=[gyT_b[s], wo_b], writes=[bb[2 + half]])
                        dst = self.acc[:, c, half * 512:(half + 1) * 512]
                        if hd == 0:
                            P.op("act", lambda e, dst=dst, half=half: e.copy(out=dst, in_=bk[2 + half][:]), reads=[bb[2 + half]], writes=[self.acc_b[c]])
                        else:
                            P.op("dve", lambda e, dst=dst, half=half: e.tensor_tensor(out=dst, in0=dst, in1=bk[2 + half][:], op=ALU.add),
                                 reads=[bb[2 + half]], writes=[self.acc_b[c]])
        phases = [ph0, ph1] if pre else [ph0, ph1, ph2]
        if boundary and (not pre) and hd == RET_H - 1:
            phases = phases + self.boundary_phases_f(lambda t: (self.acc[:, t, :], [self.acc_b[t]]), lambda t: (self.acc[:, t, :], self.acc_b[t]),
                                                   toks=btoks, tbanks=(5,), merge_b=True)
        pipeline(NT, phases)
        if pre:
            P.dma("sp", dr["L_out"][hd].rearrange("(fc p) n -> p fc n", p=128), state, reads=[state_b], chan=state_b)
    if pre:
        return [(state_b.dsem, state_b.dcount)]
    return btoks


Builder.ret_emit = _ret_emit2


def _mixer_boundaries(self, i):
    toks = []
    for t in range(NT):
        a = self.acc[:, t, :]
        toks.append(self.boundary(t, a, [self.acc_b[t]], a, self.acc_b[t]))
    return toks


Builder.mixer_boundaries = _mixer_boundaries


def ret_tables():
    gam = np.array([1.0 - 2.0 ** (-5.0 - h) for h in range(RET_H)], np.float64)
    idx = np.arange(128, dtype=np.float64)
    diff = idx[None, :] - idx[:, None]
    maskT = np.where(diff >= 0, gam[:, None, None] ** np.maximum(diff, 0.0), 0.0) / 16.0
    qdec = gam[:, None] ** (np.tile(idx, 4) + 1.0)[None, :]
    qdec = np.broadcast_to(qdec[:, None, :], (RET_H, 128, 512))
    kdec = (gam[None, :] ** (127.0 - idx[:, None])) / 16.0
    return (np.ascontiguousarray(maskT, dtype=np.float32), np.ascontiguousarray(qdec, dtype=np.float32),
            np.ascontiguousarray(kdec, dtype=np.float32))


def ret_coef(core):
    gam = np.array([1.0 - 2.0 ** (-5.0 - h) for h in range(RET_H)], np.float64)
    b, s = core // 4, core % 4
    c = np.zeros((L_RANKS, RET_H), np.float64)
    for rs in range(L_RANKS):
        if rs < s:
            c[rs] = gam ** (float(T) * (s - 1 - rs))
    return np.ascontiguousarray(np.broadcast_to(c.reshape(1, -1), (128, L_RANKS * RET_H)), dtype=np.float32)


def rope_tables(core):
    s = core % 4
    inv_freq = (np.float32(10000.0) ** (-(np.arange(0, RET_DK, 2, dtype=np.float32) / np.float32(RET_DK)))).astype(np.float32)
    pos = (np.arange(T, dtype=np.float32) + np.float32(s * T)).astype(np.float32)
    ang = (inv_freq[:, None] * pos[None, :]).astype(np.float32)
    return np.cos(ang).astype(np.float32), np.sin(ang).astype(np.float32)


def _gmlp_stage(self, i, j):
    P = self.P
    dr = self.dr
    P.barrier()
    bigb = self.big[:].bitcast(BF16)
    Wv = bigb[:, 0:16384].rearrange("p (k n) -> p k n", k=KC)
    Wu = bigb[:, 16384:32768].rearrange("p (k n) -> p k n", k=KC)
    Wv_b, Wu_b = Buf("gWv"), Buf("gWu")
    off = 0
    Wo, off = self.st_view(off, [16, D], BF16)
    Wo_b = Buf("gWo")
    wsT, off = self.st_view(off, [GM_G, 128], BF16)
    wsT_b = Buf("gwsT")
    gtab, off = self.st_view(off, [1, GM_W], F32)
    btab, off = self.st_view(off, [1, GM_W], F32)
    gtab_b, btab_b = Buf("ggtab"), Buf("gbtab")
    bsT, off = self.st_view(off, [1, GM_G], F32)
    bsT_b = Buf("gbsT")
    cmask, off = self.st_view(off, [1, 128], F32)
    cmask_b = Buf("gcmask")
    off_vf = off
    vf, off = self.st_view(off, [1, GM_W], F32)
    vf_b = Buf("gvf")
    vn, off = self.st_view(off, [1, GM_W], BF16)
    vn_b = Buf("gvn")
    ssb, off = self.st_view(off, [1, GM_W], F32)
    ssb_b = Buf("gssb")
    us, off = self.st_view(off, [1, GM_W], BF16)
    us_b = Buf("gus")
    usT, off = self.st_view(off, [16, 128], BF16)
    usT_b = Buf("gusT")
    gu, gu_b = [], []
    for s in range(2):
        a, off = self.st_view(off, [1, 512], F32)
        gu.append(a)
        gu_b.append(Buf(f"ggu{s}"))
    wsl, off2 = self.st_view(off_vf, [GM_G, 128], BF16)
    wsl_b = vf_b
    w_in = dr["gm_w_in"][j]
    for cb in range(4):
        self.P.dma("pool", Wv[:, :, cb * 512:(cb + 1) * 512], w_in[:, GM_W + cb * 512:GM_W + (cb + 1) * 512].rearrange("(kc p) n -> p kc n", p=128),
                   writes=[Wv_b] + self.acc_b[0:8])
    for cb in range(4):
        self.P.dma("pool", Wu[:, :, cb * 512:(cb + 1) * 512], w_in[:, cb * 512:(cb + 1) * 512].rearrange("(kc p) n -> p kc n", p=128),
                   writes=[Wu_b] + self.acc_b[8:16])
    for q in range(4):
        self.wload(Wo[:, q * 4:(q + 1) * 4, :], dr["gm_w_out"][j][q * 512:(q + 1) * 512, :].rearrange("(kc p) n -> p kc n", p=128), Wo_b)
    self.wload(wsl, dr["gm_w_s"][j].rearrange("g n m -> n g m"), wsl_b)
    P.dma("sp", gtab[:, 0, :], dr["gm_ln_g"][j, :].partition_broadcast(128), writes=[gtab_b])
    P.dma("sp", btab[:, 0, :], dr["gm_ln_b"][j, :].partition_broadcast(128), writes=[btab_b])
    P.dma("sp", bsT[:, 0, :], dr["gm_b_s"][j].rearrange("g n -> n g"), writes=[bsT_b], allow_slow_non_contiguous=True)
    P.dma("sp", cmask[:, 0, :], dr["causalT"][:, :], writes=[cmask_b])
    bk, bb = self.bank, self.bank_b
    tps = bk[0][:].bitcast(BF16)[:, 0:1024].rearrange("p (g n) -> p g n", g=GM_G)
    for g in range(GM_G):
        P.op("pe", lambda e, g=g: e.transpose(out=tps[:, g, :], in_=wsl[:, g, :], identity=self.ident[:]),
             reads=[wsl_b, self.ident_b], writes=[bb[0]])
    for g in range(GM_G):
        P.op("dve", lambda e, g=g: e.tensor_tensor(out=wsT[:, g, :], in0=tps[:, g, :], in1=cmask[:, 0, :], op=ALU.mult),
             reads=[bb[0], cmask_b], writes=[wsT_b])
    toks = []
    for t in range(NT):
        tsl = slice(t * 128, (t + 1) * 128)
        sv, sv_b = [], []
        for cb in range(4):
            for kc in range(KC):
                P.op("pe", lambda e, cb=cb, kc=kc, tsl=tsl: e.matmul(bk[cb][:], lhsT=self.aT[:, kc, tsl], rhs=Wv[:, kc, cb * 512:(cb + 1) * 512],
                                                                     start=(kc == 0), stop=(kc == KC - 1)),
                     reads=[Wv_b, self.aT_b[t]], writes=[bb[cb]])
            a, a_b = self.new_stat()
            sv.append(a)
            sv_b.append(a_b)
            P.op("act", lambda e, cb=cb, a=a: e.activation(out=vf[:, 0, cb * 512:(cb + 1) * 512], in_=bk[cb][:], func=AF.Gelu, accum_out=a),
                 reads=[bb[cb]], writes=[vf_b, a_b])
        sq, sq_b = [], []
        for hh in range(2):
            a, a_b = self.new_stat()
            sq.append(a)
            sq_b.append(a_b)
            P.op("act", lambda e, hh=hh, a=a: e.activation(out=self.junk[:, :], in_=vf[:, 0, hh * 1024:(hh + 1) * 1024], func=AF.Square, accum_out=a),
                 reads=[vf_b], writes=[self.junk_b, a_b])
        m1, m1_b = self.new_stat()
        P.op("pool", lambda e, sv=sv, m1=m1: e.tensor_tensor(out=m1, in0=sv[0], in1=sv[1], op=ALU.add), reads=[sv_b[0], sv_b[1]], writes=[m1_b])
        m2, m2_b = self.new_stat()
        P.op("pool", lambda e, sv=sv, m2=m2: e.tensor_tensor(out=m2, in0=sv[2], in1=sv[3], op=ALU.add), reads=[sv_b[2], sv_b[3]], writes=[m2_b])
        mu, mu_b = self.new_stat()
        P.op("pool", lambda e, m1=m1, m2=m2, mu=mu: e.tensor_tensor(out=mu, in0=m1, in1=m2, op=ALU.add), reads=[m1_b, m2_b], writes=[mu_b])
        P.op("pool", lambda e, mu=mu: e.tensor_scalar(out=mu, in0=mu, scalar1=1.0 / GM_W, scalar2=None, op0=ALU.mult), reads=[], writes=[mu_b])
        e2, e2_b = self.new_stat()
        P.op("pool", lambda e, sq=sq, e2=e2: e.tensor_tensor(out=e2, in0=sq[0], in1=sq[1], op=ALU.add), reads=[sq_b[0], sq_b[1]], writes=[e2_b])
        mm, mm_b = self.new_stat()
        P.op("pool", lambda e, mu=mu, mm=mm: e.tensor_tensor(out=mm, in0=mu, in1=mu, op=ALU.mult), reads=[mu_b], writes=[mm_b])
        var, var_b = self.new_stat()
        P.op("pool", lambda e, e2=e2, var=var: e.tensor_scalar(out=var, in0=e2, scalar1=1.0 / GM_W, scalar2=EPS, op0=ALU.mult, op1=ALU.add),
             reads=[e2_b], writes=[var_b])
        P.op("pool", lambda e, var=var, mm=mm: e.tensor_tensor(out=var, in0=var, in1=mm, op=ALU.subtract), reads=[mm_b], writes=[var_b])
        rs, rs_b = self.new_stat()
        P.op("pool", lambda e, var=var, rs=rs: e.tensor_tensor(out=rs, in0=var, in1=self.cm05[:], op=ALU.pow), reads=[var_b, self.cm05_b], writes=[rs_b])
        P.op("dve", lambda e, mu=mu, rs=rs: e.tensor_scalar(out=vf[:, 0, :], in0=vf[:, 0, :], scalar1=mu, scalar2=rs, op0=ALU.subtract, op1=ALU.mult),
             reads=[mu_b, rs_b], writes=[vf_b])
        P.op("pool", lambda e: e.tensor_tensor(out=vf[:, 0, :], in0=vf[:, 0, :], in1=gtab[:, 0, :], op=ALU.mult), reads=[gtab_b], writes=[vf_b])
        P.op("dve", lambda e: e.tensor_tensor(out=vn[:, 0, :], in0=vf[:, 0, :], in1=btab[:, 0, :], op=ALU.add), reads=[vf_b, btab_b], writes=[vn_b])
        for g in range(GM_G):
            bkx = 4 + g // 2
            P.op("pe", lambda e, g=g, bkx=bkx: e.matmul(bk[bkx][:, (g % 2) * 256:(g % 2 + 1) * 256], lhsT=wsT[:, g, :], rhs=vn[:, 0, g * 256:(g + 1) * 256],
                                                        start=True, stop=True),
                 reads=[wsT_b, vn_b], writes=[bb[bkx]])
        for g in range(GM_G):
            bkx = 4 + g // 2
            P.op("dve", lambda e, g=g, bkx=bkx: e.tensor_scalar(out=ssb[:, 0, g * 256:(g + 1) * 256], in0=bk[bkx][:, (g % 2) * 256:(g % 2 + 1) * 256],
                                                                scalar1=bsT[:, 0, g:g + 1], scalar2=None, op0=ALU.add),
                 reads=[bb[bkx], bsT_b], writes=[ssb_b])
        for cb in range(4):
            for kc in range(KC):
                P.op("pe", lambda e, cb=cb, kc=kc, tsl=tsl: e.matmul(bk[cb][:], lhsT=self.aT[:, kc, tsl], rhs=Wu[:, kc, cb * 512:(cb + 1) * 512],
                                                                     start=(kc == 0), stop=(kc == KC - 1)),
                     reads=[Wu_b, self.aT_b[t]], writes=[bb[cb]])
            s = cb % 2
            P.op("act", lambda e, cb=cb, s=s: e.activation(out=gu[s][:, 0, :], in_=bk[cb][:], func=AF.Gelu), reads=[bb[cb]], writes=[gu_b[s]])
            P.op("dve", lambda e, cb=cb, s=s: e.tensor_tensor(out=us[:, 0, cb * 512:(cb + 1) * 512], in0=gu[s][:, 0, :], in1=ssb[:, 0, cb * 512:(cb + 1) * 512], op=ALU.mult),
                 reads=[gu_b[s], ssb_b], writes=[us_b])
        for half in range(2):
            tp = bk[4 + half][:].bitcast(BF16).rearrange("p (k n) -> p k n", k=8)
            for q in range(8):
                kc = half * 8 + q
                P.op("pe", lambda e, tp=tp, q=q, kc=kc: e.transpose(out=tp[:, q, :], in_=us[:, 0, kc * 128:(kc + 1) * 128], identity=self.ident[:]),
                     reads=[us_b, self.ident_b], writes=[bb[4 + half]])
            P.op("act", lambda e, tp=tp, half=half: e.copy(out=usT[:, half * 8:(half + 1) * 8, :], in_=tp), reads=[bb[4 + half]], writes=[usT_b])
        for half in range(2):
            for kc in range(16):
                P.op("pe", lambda e, half=half, kc=kc: e.matmul(bk[6 + half][:], lhsT=usT[:, kc, :], rhs=Wo[:, kc, half * 512:(half + 1) * 512],
                                                                start=(kc == 0), stop=(kc == 15)),
                     reads=[usT_b, Wo_b], writes=[bb[6 + half]])
        src = self.psum_all[:, 6:8, :].rearrange("p a b -> p (a b)")
        toks.append(self.boundary(t, src, [bb[6], bb[7]], vf[:, 0, 0:D], vf_b, tbank=5))
    for t in range(NT):
        src_b = Wv_b if t < 8 else Wu_b
        self.acc_b[t].r = list(self.acc_b[t].r) + list(src_b.r) + ([src_b.w] if src_b.w else [])
    return toks


Builder.gmlp_stage = _gmlp_stage


def att_tables(rel_bias):
    p = np.arange(128)[:, None, None]
    kb = np.arange(2)[None, :, None]
    q = np.arange(128)[None, None, :]
    dist = 128 * (1 - kb) + q - p
    mask01 = ((dist >= 0) & (dist <= 128)).astype(np.float32).reshape(128, 256)
    dcl = np.maximum(dist, 0)
    out = np.zeros((3, ATT_H, 128, 256), np.float32)
    for g, (w, r) in enumerate(DIL):
        dd = (dcl * r).astype(np.int32)
        df = np.maximum(dd, 1).astype(np.float32)
        large = 16 + (np.log(df / np.float32(16)) / np.float32(math.log(2048 / 16)) * np.float32(16)).astype(np.int32)
        large = np.minimum(large, 31)
        bucket = np.where(dd < 16, dd, large).reshape(128, 256)
        tab = rel_bias[:, g * ATT_H:(g + 1) * ATT_H]
        out[g] = np.transpose(tab[bucket], (2, 0, 1))
    return out, mask01


def att_prevmask(core, mask01):
    m = mask01.copy()
    if core % 4 == 0:
        m[:, 0:128] = 0.0
    return m


def _att_kv_pass(self, j):
    P = self.P
    dr = self.dr
    P.barrier()
    w_in = dr["attn_w_in"][j]
    off = 0
    ring, ring_b = [], []
    for s in range(3):
        v, off = self.st_view(off, [KC, 512], BF16)
        ring.append(v)
        ring_b.append(Buf(f"akv_w{s}"))
    kst, kst_b = [], []
    for s in range(2):
        v, off = self.st_view(off, [1, T], BF16)
        kst.append(v)
        kst_b.append(Buf(f"akv_k{s}"))
    vst, vst_b = [], []
    for s in range(2):
        v, off = self.st_view(off, [1, D], BF16)
        vst.append(v)
        vst_b.append(Buf(f"akv_v{s}"))
    bk, bb = self.bank, self.bank_b
    self.KT_b = [[Buf(f"KT{g}_{fb}") for fb in range(8)] for g in range(3)]
    self.V_b = [[Buf(f"V{g}_{blk}") for blk in range(NT)] for g in range(3)]
    ri = 0
    ui = 0
    for g, (w, r) in enumerate(DIL):
        nbc = 16 // r
        for hf in range(2):
            ws, wb = ring[ri % 3], ring_b[ri % 3]
            ri += 1
            c0 = g * 3072 + 1024 + hf * 512
            self.wload(ws, w_in[:, c0:c0 + 512].rearrange("(kc p) n -> p kc n", p=128), wb)
            for fq in range(4):
                fb = hf * 4 + fq
                ks, ksb = kst[fb % 2], kst_b[fb % 2]
                for tg in range(4):
                    b = ui % 4
                    ui += 1
                    for kc in range(KC):
                        P.op("pe", lambda e, b=b, kc=kc, fq=fq, tg=tg, ws=ws: e.matmul(
                            bk[b][:], lhsT=ws[:, kc, fq * 128:(fq + 1) * 128], rhs=self.aT[:, kc, tg * 512:(tg + 1) * 512],
                            start=(kc == 0), stop=(kc == KC - 1)), reads=[wb] + self.aT_b[tg * 4:(tg + 1) * 4], writes=[bb[b]])
                    jn = 512 // r
                    dst = ks[:, 0, :].rearrange("p (c j) -> p c j", c=r)[:, :, tg * jn:(tg + 1) * jn]
                    src = bk[b][:].rearrange("p (j c) -> p c j", c=r)
                    eng = "act" if (ui % 2 == 0) else "dve"
                    if eng == "act":
                        P.op("act", lambda e, dst=dst, src=src: e.copy(out=dst, in_=src), reads=[bb[b]], writes=[ksb])
                    else:
                        P.op("dve", lambda e, dst=dst, src=src: e.tensor_copy(out=dst, in_=src), reads=[bb[b]], writes=[ksb])
                P.dma("sp", dr["KT_own"][g, fb * 128:(fb + 1) * 128, :], ks[:, 0, :], reads=[ksb], writes=[self.KT_b[g][fb]], chan=ksb)
        wsl = []
        for hf in range(2):
            ws, wb = ring[ri % 3], ring_b[ri % 3]
            ri += 1
            c0 = g * 3072 + 2048 + hf * 512
            self.wload(ws, w_in[:, c0:c0 + 512].rearrange("(kc p) n -> p kc n", p=128), wb)
            wsl.append((ws, wb))
        for blk in range(NT):
            c, n = blk // nbc, blk % nbc
            start = n * 128 * r + c
            vs, vsb = vst[blk % 2], vst_b[blk % 2]
            toks_t = sorted(set((start + i * r) // 128 for i in range(128)))
            for hf in range(2):
                ws, wb = wsl[hf]
                b = ui % 4
                ui += 1
                for kc in range(KC):
                    P.op("pe", lambda e, b=b, kc=kc, ws=ws, start=start, r=r: e.matmul(
                        bk[b][:], lhsT=self.aT[:, kc, start:start + 127 * r + 1:r], rhs=ws[:, kc, :],
                        start=(kc == 0), stop=(kc == KC - 1)), reads=[wb] + [self.aT_b[t] for t in toks_t], writes=[bb[b]])
                if hf == 0:
                    P.op("act", lambda e, b=b, vs=vs: e.copy(out=vs[:, 0, 0:512], in_=bk[b][:]), reads=[bb[b]], writes=[vsb])
                else:
                    P.op("dve", lambda e, b=b, vs=vs: e.tensor_copy(out=vs[:, 0, 512:1024], in_=bk[b][:]), reads=[bb[b]], writes=[vsb])
            P.dma("sp", dr["V_own"][g, blk * 128:(blk + 1) * 128, :], vs[:, 0, :], reads=[vsb], writes=[self.V_b[g][blk]], chan=vsb)


Builder.att_kv_pass = _att_kv_pass


def _att_main(self, i, j):
    P = self.P
    dr = self.dr
    P.barrier()
    w_in = dr["attn_w_in"][j]
    off = 0
    Wq, off = self.st_view(off, [KC, 384], BF16)
    Wq_b = Buf("aWq")
    QT, off = self.st_view(off, [1, T], BF16)
    QT_b = Buf("aQT")
    KT, off = self.st_view(off, [1, T], BF16)
    KT_sb = Buf("aKT")
    KH, off = self.st_view(off, [1, T], BF16)
    KH_b = Buf("aKH")
    Vt, off = self.st_view(off, [NT, 128], BF16)
    Vt_b = Buf("aVt")
    VH, off = self.st_view(off, [NT, 128], BF16)
    VH_b = Buf("aVH")
    Nacc, off = self.st_view(off, [2, T], F32)
    Dacc, off = self.st_view(off, [2, T], F32)
    ND_b = [Buf("aND0"), Buf("aND1")]
    bst, off = self.st_view(off, [2, 256], F32)
    bst_b = Buf("abst")
    Et, off = self.st_view(off, [2, 256], F32)
    Et0, off = self.st_view(off, [2, 256], F32)
    Et_b = Buf("aEt")
    m01, off = self.st_view(off, [1, 256], F32)
    m01p, off = self.st_view(off, [1, 256], F32)
    m_b = Buf("am01")
    ones, off = self.st_view(off, [1, 64], BF16)
    ones_b = Buf("aones")
    ex, ex_b, pt, pt_b = [], [], [], []
    for s in range(2):
        a, off = self.st_view(off, [1, 256], F32); ex.append(a); ex_b.append(Buf(f"aex{s}"))
        a, off = self.st_view(off, [2, 128], BF16); pt.append(a); pt_b.append(Buf(f"apt{s}"))
    rc, off = self.st_view(off, [1, T], F32)
    rc_b = Buf("arc")
    bk, bb = self.bank, self.bank_b
    if getattr(self, "KT_b", None) is None:
        self.KT_b = [[Buf(f"KT{g}_{fb}") for fb in range(8)] for g in range(3)]
        self.V_b = [[Buf(f"V{g}_{blk}") for blk in range(NT)] for g in range(3)]
    oT = self.big[:].bitcast(BF16).rearrange("p (h t) -> p h t", h=ATT_H)
    oT_b = [Buf(f"aoT{h}") for h in range(ATT_H)]
    P.dma("sp", m01[:, 0, :], dr["att_mask01"][:, :], writes=[m_b])
    P.dma("sp", m01p[:, 0, :], dr["att_prevmask"][:, :], writes=[m_b])
    P.op("pool", lambda e: e.memset(ones, 1.0), writes=[ones_b])
    sT_ps = [self.psum_all[:, 2 + s // 2, (s % 2) * 256:(s % 2 + 1) * 256].rearrange("p (a b) -> p a b", a=2) for s in range(4)]
    sT_pb = [Buf(f"asT{s}") for s in range(4)]
    nd_ps = [self.psum_all[0:64, 4, s * 256:(s + 1) * 256].rearrange("p (a b) -> p a b", a=2) for s in range(2)]
    nd_pb = [Buf(f"and{s}") for s in range(2)]
    ui = 0
    si = 0
    for hp in range(8):
        for g in range(3):
            c0 = g * 3072 + hp * 128
            self.wload(Wq[:, :, g * 128:(g + 1) * 128], w_in[:, c0:c0 + 128].rearrange("(kc p) n -> p kc n", p=128), Wq_b)
        for g, (w, r) in enumerate(DIL):
            nbc = 16 // r
            P.dma("sp", KT[:, 0, :], dr["KT_own"][g, hp * 128:(hp + 1) * 128, :], reads=self.KT_b[g], writes=[KT_sb])
            for c in range(r):
                p0 = c * (T // r) + (nbc - 1) * 128
                P.dma("sp", KH[:, 0, c * 128:(c + 1) * 128], dr["KT_prev"][g, hp * 128:(hp + 1) * 128, p0:p0 + 128], writes=[KH_b])
            P.dma("sp", Vt, dr["V_own"][g, :, hp * 128:(hp + 1) * 128].rearrange("(b p) f -> p b f", p=128), reads=self.V_b[g], writes=[Vt_b])
            for c in range(r):
                p0 = c * (T // r) + (nbc - 1) * 128
                P.dma("sp", VH[:, c, :], dr["V_prev"][g, p0:p0 + 128, hp * 128:(hp + 1) * 128], writes=[VH_b])
            P.dma("sp", bst, dr["att_bias"][g, 2 * hp:2 * hp + 2].rearrange("h p n -> p h n"), writes=[bst_b])
            P.op("act", lambda e: e.activation(out=bst, in_=bst, func=AF.Exp), reads=[], writes=[bst_b])
            for hh in range(2):
                P.op("dve", lambda e, hh=hh: e.tensor_tensor(out=Et[:, hh, :], in0=bst[:, hh, :], in1=m01[:, 0, :], op=ALU.mult),
                     reads=[bst_b, m_b], writes=[Et_b])
                P.op("dve", lambda e, hh=hh: e.tensor_tensor(out=Et0[:, hh, :], in0=bst[:, hh, :], in1=m01p[:, 0, :], op=ALU.mult),
                     reads=[bst_b, m_b], writes=[Et_b])
            for tg in range(4):
                b = ui % 2
                ui += 1
                for kc in range(KC):
                    P.op("pe", lambda e, b=b, kc=kc, g=g, tg=tg: e.matmul(
                        bk[b][:], lhsT=Wq[:, kc, g * 128:(g + 1) * 128], rhs=self.aT[:, kc, tg * 512:(tg + 1) * 512],
                        start=(kc == 0), stop=(kc == KC - 1)), reads=[Wq_b] + self.aT_b[tg * 4:(tg + 1) * 4], writes=[bb[b]])
                jn = 512 // r
                dst = QT[:, 0, :].rearrange("p (c j) -> p c j", c=r)[:, :, tg * jn:(tg + 1) * jn]
                src = bk[b][:].rearrange("p (j c) -> p c j", c=r)
                P.op("act", lambda e, dst=dst, src=src: e.copy(out=dst, in_=src), reads=[bb[b]], writes=[QT_b])
            for blk in range(NT):
                c, n = blk // nbc, blk % nbc
                qs = slice(blk * 128, (blk + 1) * 128)
                first = (n == 0)
                for hh in range(2):
                    ps = slice(hh * 64, (hh + 1) * 64)
                    s4 = si % 4
                    s2 = si % 2
                    si += 1
                    kprev = KH[ps, 0, c * 128:(c + 1) * 128] if first else KT[ps, 0, (blk - 1) * 128:blk * 128]
                    P.op("pe", lambda e, s4=s4, kprev=kprev, ps=ps, qs=qs: e.matmul(sT_ps[s4][:, 0, :], lhsT=kprev, rhs=QT[ps, 0, qs], start=True, stop=True),
                         reads=[KH_b if first else KT_sb, QT_b], writes=[sT_pb[s4]])
                    P.op("pe", lambda e, s4=s4, ps=ps, qs=qs: e.matmul(sT_ps[s4][:, 1, :], lhsT=KT[ps, 0, qs], rhs=QT[ps, 0, qs], start=True, stop=True),
                         reads=[KT_sb, QT_b], writes=[sT_pb[s4]])
                    P.op("act", lambda e, s4=s4, s2=s2: e.activation(out=ex[s2][:, 0, :], in_=sT_ps[s4].rearrange("p a b -> p (a b)"), func=AF.Exp, scale=0.125),
                         reads=[sT_pb[s4]], writes=[ex_b[s2]])
                    tab = Et0 if first else Et
                    P.op("dve", lambda e, s2=s2, tab=tab, hh=hh: e.tensor_tensor(out=pt[s2].rearrange("p a b -> p (a b)"), in0=ex[s2][:, 0, :], in1=tab[:, hh, :], op=ALU.mult),
                         reads=[ex_b[s2], Et_b], writes=[pt_b[s2]])
                    vprev = VH[:, c, ps] if first else Vt[:, blk - 1, ps]
                    P.op("pe", lambda e, s2=s2, vprev=vprev: e.matmul(nd_ps[s2][:, 0, :], lhsT=vprev, rhs=pt[s2][:, 0, :], start=True, stop=False),
                         reads=[VH_b if first else Vt_b, pt_b[s2]], writes=[nd_pb[s2]])
                    P.op("pe", lambda e, s2=s2, blk=blk, ps=ps: e.matmul(nd_ps[s2][:, 0, :], lhsT=Vt[:, blk, ps], rhs=pt[s2][:, 1, :], start=False, stop=True),
                         reads=[Vt_b, pt_b[s2]], writes=[nd_pb[s2]])
                    P.op("pe", lambda e, s2=s2: e.matmul(nd_ps[s2][:, 1, :], lhsT=ones[:, 0, :], rhs=pt[s2][:, 0, :], start=True, stop=False),
                         reads=[ones_b, pt_b[s2]], writes=[nd_pb[s2]])
                    P.op("pe", lambda e, s2=s2: e.matmul(nd_ps[s2][:, 1, :], lhsT=ones[:, 0, :], rhs=pt[s2][:, 1, :], start=False, stop=True),
                         reads=[ones_b, pt_b[s2]], writes=[nd_pb[s2]])
                    t0 = n * 128 * r + c
                    tsl = slice(t0, t0 + 127 * r + 1, r)
                    if g == 0:
                        P.op("act", lambda e, s2=s2, hh=hh, tsl=tsl: e.copy(out=Nacc[0:64, hh, tsl], in_=nd_ps[s2][:, 0, :]), reads=[nd_pb[s2]], writes=[ND_b[hh]])
                        P.op("dve", lambda e, s2=s2, hh=hh, tsl=tsl: e.tensor_copy(out=Dacc[0:64, hh, tsl], in_=nd_ps[s2][:, 1, :]), reads=[nd_pb[s2]], writes=[ND_b[hh]])
                    else:
                        P.op("dve", lambda e, s2=s2, hh=hh, tsl=tsl: e.tensor_tensor(out=Nacc[0:64, hh, tsl], in0=Nacc[0:64, hh, tsl], in1=nd_ps[s2][:, 0, :], op=ALU.add),
                             reads=[nd_pb[s2]], writes=[ND_b[hh]])
                        P.op("dve", lambda e, s2=s2, hh=hh, tsl=tsl: e.tensor_tensor(out=Dacc[0:64, hh, tsl], in0=Dacc[0:64, hh, tsl], in1=nd_ps[s2][:, 1, :], op=ALU.add),
                             reads=[nd_pb[s2]], writes=[ND_b[hh]])
        for hh in range(2):
            h = 2 * hp + hh
            P.op("dve", lambda e, hh=hh: e.reciprocal(out=rc[0:64, 0, :], in_=Dacc[0:64, hh, :]), reads=[ND_b[hh]], writes=[rc_b])
            P.op("pool", lambda e, hh=hh, h=h: e.tensor_tensor(out=oT[0:64, h, :], in0=Nacc[0:64, hh, :], in1=rc[0:64, 0, :], op=ALU.mult),
                 reads=[ND_b[hh], rc_b], writes=[oT_b[h]] + self.acc_b)
    P.barrier()
    off = 0
    Wo, off = self.st_view(off, [ATT_H, D], BF16)
    Wo_b = Buf("aWo")
    nrm, off = self.st_view(off, [2, D], F32)
    nrm_b = [Buf("anrm0"), Buf("anrm1")]
    self.wload(Wo[0:64], dr["attn_w_out"][j].rearrange("(h p) n -> p h n", p=64), Wo_b)
    toks = []
    for t in range(NT):
        tsl = slice(t * 128, (t + 1) * 128)
        for half in range(2):
            for h in range(ATT_H):
                P.op("pe", lambda e, h=h, half=half, tsl=tsl: e.matmul(bk[6 + half][:], lhsT=oT[0:64, h, tsl], rhs=Wo[0:64, h, half * 512:(half + 1) * 512],
                                                                        start=(h == 0), stop=(h == ATT_H - 1)),
                     reads=[oT_b[h], Wo_b], writes=[bb[6 + half]])
        src = self.psum_all[:, 6:8, :].rearrange("p a b -> p (a b)")
        toks.append(self.boundary(t, src, [bb[6], bb[7]], nrm[:, t % 2, :], nrm_b[t % 2], tbank=5))
    for t in range(NT):
        for h in range(ATT_H):
            self.acc_b[t].r = list(self.acc_b[t].r) + list(oT_b[h].r)
    return toks


Builder.att_main = _att_main


W_SHAPES = {
    "norm_g": [DEPTH, 5, D], "ret_w_in": [2, D, 6144], "ret_w_out": [2, 2048, D],
    "attn_w_in": [1, D, 9216], "attn_w_out": [1, D, D],
    "gm_w_in": [1, D, 4096], "gm_ln_g": [1, 2048], "gm_ln_b": [1, 2048], "gm_w_s": [1, 8, 128, 128], "gm_b_s": [1, 8, 128],
    "gm_w_out": [1, 2048, D], "ffn_w_in": [DEPTH, D, 2 * FFN_H], "ffn_w_out": [DEPTH, FFN_H, D],
    "ple_w_proj": [DEPTH, PLE_DIM, D], "ple_w_gate": [DEPTH, D, D],
}
T_SHAPES = {
    "ident": [128, 128], "causalT": [128, 128],
    "ret_maskT": [RET_H, 128, 128], "ret_qdec": [RET_H, 128, 512], "ret_kdec": [128, RET_H], "ret_coef": [128, L_RANKS * RET_H],
    "rope_cos": [128, T], "rope_sin": [128, T],
    "att_bias": [3, ATT_H, 128, 256], "att_mask01": [128, 256], "att_prevmask": [128, 256],
}
RET_IN = ["ret_w_in", "ret_w_out", "ret_maskT", "ret_qdec", "ret_kdec", "ret_coef", "rope_cos", "rope_sin"]
FFN_IN = ["ffn_w_in", "ffn_w_out", "ple_w_proj", "ple_w_gate", "p"]
ATT_IN = ["attn_w_in", "attn_w_out", "att_bias", "att_mask01", "att_prevmask"]
GM_IN = ["gm_w_in", "gm_ln_g", "gm_ln_b", "gm_w_s", "gm_b_s", "gm_w_out", "causalT"]
SEG_IN = {
    1: ["hin", "norm_g", "ident"] + RET_IN,
    2: ["hin", "norm_g", "ident", "L_all"] + RET_IN + FFN_IN + ["attn_w_in"],
    3: ["hin", "norm_g", "ident", "KT_own", "V_own", "KT_prev", "V_prev"] + ATT_IN + FFN_IN + GM_IN + RET_IN,
    4: ["hin", "norm_g", "ident", "L_all"] + RET_IN + FFN_IN,
}
SEG_OUT = {1: ["L_out"], 2: ["hout", "KT_own", "V_own"], 3: ["hout", "L_out"], 4: ["hout"]}
IO_SHAPES = {
    "hin": ([T, D], F32), "hout": ([T, D], F32), "p": ([DEPTH, T, PLE_DIM], F32),
    "L_all": ([L_RANKS, RET_H, RET_DK, RET_DV], F32), "L_out": ([RET_H, RET_DK, RET_DV], F32),
    "KT_own": ([3, D, T], BF16), "V_own": ([3, T, D], BF16), "KT_prev": ([3, D, T], BF16), "V_prev": ([3, T, D], BF16),
}


def build_segment(seg):
    nc = bass.Bass("TRN2", target_bir_lowering=False)
    dr = {}
    for name in dict.fromkeys(SEG_IN[seg]):
        if name in SEG_OUT[seg]:
            continue
        if name in W_SHAPES:
            shape, dt = W_SHAPES[name], F32
        elif name in T_SHAPES:
            shape, dt = T_SHAPES[name], F32
        else:
            shape, dt = IO_SHAPES[name]
        dr[name] = nc.dram_tensor(name, list(shape), dt, kind="ExternalInput").ap()
    for name in SEG_OUT[seg]:
        shape, dt = IO_SHAPES[name]
        dr[name] = nc.dram_tensor(name, list(shape), dt, kind="ExternalOutput").ap()
    hD = nc.dram_tensor("hD", [T, D], F32, kind="Internal").ap()
    B = Builder(nc, dr)
    P = B.P
    fin = []
    if seg == 1:
        B.h_rd = dr["hin"]
        B.load_g("pre", (0, 0))
        B.first_norm()
        fin += B.ret_emit(0, True)
    elif seg == 2:
        B.h_rd, B.h_wr = dr["hin"], hD
        B.load_g("pre", (0, 0))
        B.first_norm()
        B.load_g("post", (0, 1))
        B.load_g("pre", (0, 2))
        B.ret_emit(0, False, boundary=True)
        B.h_rd = hD
        B.ffn_stage(0)
        B.h_wr = dr["hout"]
        fin += B.ple_stage(0, last=False)
        B.att_kv_pass(0)
        for g in range(3):
            for b in B.KT_b[g] + B.V_b[g]:
                fin.append(b.w)
    elif seg == 3:
        B.h_rd, B.h_wr = dr["hin"], hD
        B.load_g("pre", (1, 0))
        B.first_norm()
        B.load_g("post", (1, 1))
        B.load_g("pre", (1, 2))
        B.att_main(1, 0)
        B.h_rd = hD
        B.ffn_stage(1)
        B.ple_stage(1, last=False)
        B.load_g("post", (2, 1))
        B.load_g("pre", (2, 2))
        B.gmlp_stage(2, 0)
        B.ffn_stage(2)
        B.h_wr = dr["hout"]
        fin += B.ple_stage(2, last=False)
        fin += B.ret_emit(1, True)
    elif seg == 4:
        B.h_rd, B.h_wr = dr["hin"], hD
        B.load_g("pre", (3, 0))
        B.first_norm()
        B.load_g("post", (3, 1))
        B.load_g("pre", (3, 2))
        B.ret_emit(1, False, boundary=True)
        B.h_rd = hD
        B.ffn_stage(3)
        B.h_wr = dr["hout"]
        fin += B.ple_stage(3, last=True)
    P.finish_on("sp", [t for t in fin if t is not None])
    P.emit()
    return nc


_PROG_CACHE = {}


def _prog(seg):
    if seg not in _PROG_CACHE:
        _PROG_CACHE[seg] = build_segment(seg)
    return _PROG_CACHE[seg]


def _run(seg, maps):
    nc = build_segment(seg)
    res = run_bass_kernel_spmd(nc, maps, core_ids=list(range(NCORES)))
    return res.results


def kernel(x, p, norm_g, ret_w_in, ret_w_out, attn_w_in, attn_w_out, rel_bias,
           gm_w_in, gm_ln_g, gm_ln_b, gm_w_s, gm_b_s, gm_w_out,
           ffn_w_in, ffn_w_out, ple_w_proj, ple_w_gate):
    import ml_dtypes
    f32 = lambda a: np.ascontiguousarray(np.asarray(a), dtype=np.float32)
    wts = {"norm_g": f32(norm_g), "ret_w_in": f32(ret_w_in), "ret_w_out": f32(ret_w_out), "attn_w_in": f32(attn_w_in),
           "attn_w_out": f32(attn_w_out), "gm_w_in": f32(gm_w_in), "gm_ln_g": f32(gm_ln_g), "gm_ln_b": f32(gm_ln_b),
           "gm_w_s": f32(gm_w_s), "gm_b_s": f32(gm_b_s), "gm_w_out": f32(gm_w_out), "ffn_w_in": f32(ffn_w_in),
           "ffn_w_out": f32(ffn_w_out), "ple_w_proj": f32(ple_w_proj), "ple_w_gate": f32(ple_w_gate)}
    x = f32(x)
    p = f32(p)
    maskT, qdec, kdec = ret_tables()
    bias_full, mask01 = att_tables(f32(rel_bias))
    idx = np.arange(128)
    consts = {"ident": np.eye(128, dtype=np.float32), "causalT": (idx[None, :] >= idx[:, None]).astype(np.float32),
              "ret_maskT": maskT, "ret_qdec": qdec, "ret_kdec": kdec, "att_bias": bias_full, "att_mask01": mask01}
    per_core = []
    for c in range(NCORES):
        b, s = c // 4, c % 4
        cos, sin = rope_tables(c)
        per_core.append({"ret_coef": ret_coef(c), "rope_cos": cos, "rope_sin": sin, "att_prevmask": att_prevmask(c, mask01),
                         "p": np.ascontiguousarray(p[:, b, s * T:(s + 1) * T, :])})

    def maps(seg, extra):
        out = []
        for c in range(NCORES):
            m = {}
            for name in dict.fromkeys(SEG_IN[seg]):
                if name in SEG_OUT[seg]:
                    continue
                if name in wts:
                    m[name] = wts[name]
                elif name in consts:
                    m[name] = consts[name]
                elif name in per_core[c]:
                    m[name] = per_core[c][name]
                else:
                    m[name] = extra[c][name]
            out.append(m)
        return out

    h = [np.ascontiguousarray(x[c // 4, (c % 4) * T:(c % 4 + 1) * T, :]) for c in range(NCORES)]
    r1 = _run(1, maps(1, [{"hin": h[c]} for c in range(NCORES)]))
    L_all = [np.ascontiguousarray(np.stack([r1[(c // 4) * 4 + rs]["L_out"] for rs in range(L_RANKS)], 0)) for c in range(NCORES)]
    r2 = _run(2, maps(2, [{"hin": h[c], "L_all": L_all[c]} for c in range(NCORES)]))
    h = [r2[c]["hout"] for c in range(NCORES)]
    zk = np.zeros((3, D, T), ml_dtypes.bfloat16)
    zv = np.zeros((3, T, D), ml_dtypes.bfloat16)
    ex3 = []
    for c in range(NCORES):
        prev = c - 1 if c % 4 > 0 else None
        ex3.append({"hin": h[c], "KT_own": r2[c]["KT_own"], "V_own": r2[c]["V_own"],
                    "KT_prev": r2[prev]["KT_own"] if prev is not None else zk,
                    "V_prev": r2[prev]["V_own"] if prev is not None else zv})
    r3 = _run(3, maps(3, ex3))
    h = [r3[c]["hout"] for c in range(NCORES)]
    L_all = [np.ascontiguousarray(np.stack([r3[(c // 4) * 4 + rs]["L_out"] for rs in range(L_RANKS)], 0)) for c in range(NCORES)]
    r4 = _run(4, maps(4, [{"hin": h[c], "L_all": L_all[c]} for c in range(NCORES)]))
    out = np.empty((BATCH, SEQ, D), np.float32)
    for c in range(NCORES):
        out[c // 4, (c % 4) * T:(c % 4 + 1) * T, :] = r4[c]["hout"]
    return out


def pipeline(n, phases):
    nph = len(phases)
    if _SEQ_DEBUG:
        for t in range(n):
            for ph in phases:
                ph(t)
        return
    for step in range(n + nph - 1):
        for ph in reversed(range(nph)):
            t = step - ph
            if 0 <= t < n:
                phases[ph](t)


def _pre_norm_phases(self, get_h, tbanks=(7,), gain=None):
    P = self.P
    stt = {}
    gt, gt_b = gain if gain is not None else (self.gpre, self.gpre_b)
    slots = (0, 1) if gain is None else (2, 3)

    def pa(t):
        hs, hb = get_h(t)
        r, r_b = self.rstd_of(hs, [hb], D)
        s = slots[t % 2] if self.abf_n == 4 else t % 2
        ab, ab_b = self.abf[:, s, :], self.abf_b[s]
        P.op("dve", lambda e: e.scalar_tensor_tensor(out=ab, in0=hs, scalar=r, in1=gt[:], op0=ALU.mult, op1=ALU.mult),
             reads=[hb, r_b, gt_b], writes=[ab_b])
        stt[t] = (s, ab_b)

    def pb(t):
        s, ab_b = stt.pop(t)
        tbank = tbanks[t % len(tbanks)]
        pst = self.bank[tbank][:].bitcast(BF16).rearrange("p (k n) -> p k n", k=KC)
        for kc in range(KC):
            P.op("pe", lambda e, kc=kc: e.transpose(out=pst[:, kc, :], in_=self.abf[:, s, kc * 128:(kc + 1) * 128], identity=self.ident[:]),
                 reads=[ab_b, self.ident_b], writes=[self.bank_b[tbank]])
        P.op("act", lambda e: e.copy(out=self.aT[:, :, t * 128:(t + 1) * 128], in_=pst),
             reads=[self.bank_b[tbank]], writes=[self.aT_b[t]])

    return [pa, pb]


def _boundary_phases(self, src_fn, dst_fn, do_pre=True, tbanks=(7,), toks=None):
    P = self.P
    st1 = {}
    st2 = {}

    def p0(t):
        src_ap, src_b = src_fn(t)
        dst_ap, dst_b = dst_fn(t)
        r, r_b = self.rstd_of(src_ap, src_b, D)
        P.op("dve", lambda e: e.scalar_tensor_tensor(out=dst_ap, in0=src_ap, scalar=r, in1=self.gpost[:], op0=ALU.mult, op1=ALU.mult),
             reads=list(src_b) + [r_b, self.gpost_b], writes=[dst_b])
        hs, hb = self.h_load(t)
        st1[t] = (dst_ap, dst_b, hs, hb)

    def p1(t):
        dst_ap, dst_b, hs, hb = st1.pop(t)
        P.op("pool", lambda e: e.tensor_tensor(out=hs, in0=hs, in1=dst_ap, op=ALU.add), reads=[dst_b], writes=[hb])
        tok = self.h_store(t, hs, hb)
        if toks is not None:
            toks.append(tok)
        st2[t] = (hs, hb)

    phases = [p0, p1]
    if do_pre:
        pn = self.pre_norm_phases(lambda t: st2.pop(t), tbanks=tbanks)
        def p1b(t, p1=p1, pa=pn[0]):
            p1(t)
            pa(t)
        phases = [p0, p1b, pn[1]]
    return phases


def _first_norm(self):
    self.set_hextra(0, 5)
    pipeline(NT, self.pre_norm_phases(lambda t: self.h_load(t), tbanks=(6, 7)))


def _mixer_boundaries2(self, i):
    toks = []
    ph = self.boundary_phases(lambda t: (self.acc[:, t, :], [self.acc_b[t]]), lambda t: (self.acc[:, t, :], self.acc_b[t]), toks=toks, tbanks=(6, 7))
    pipeline(NT, ph)
    return toks


Builder.pre_norm_phases = _pre_norm_phases
Builder.boundary_phases = _boundary_phases
Builder.first_norm = _first_norm
Builder.mixer_boundaries = _mixer_boundaries2


def _ffn_stage2(self, i):
    P = self.P
    self.load_g("post", (i, 3))
    self.load_g("pre", (i, 4))
    P.barrier()
    w_in = self.dr["ffn_w_in"][i]
    w_out = self.dr["ffn_w_out"][i]
    off = 0
    winA, winA_b = [], []
    for s in range(3):
        v, off = self.st_view(off, [KC, 512], BF16)
        winA.append(v)
        winA_b.append(Buf(f"f{i}winA{s}"))
    woutB, woutB_b = [], []
    for s in range(2):
        v, off = self.st_view(off, [4, D], BF16)
        woutB.append(v)
        woutB_b.append(Buf(f"f{i}woutB{s}"))
    hidT, hidT_b = [], []
    for s in range(2):
        v, off = self.st_view(off, [4, T], BF16)
        hidT.append(v)
        hidT_b.append([[Buf(f"f{i}hid{s}_{jl}_{tg}") for tg in range(4)] for jl in range(4)])
    sil, sil_b = [], []
    for s in range(2):
        v, off = self.st_view(off, [1, 512], F32)
        sil.append(v)
        sil_b.append(Buf(f"f{i}sil{s}"))
    nchunk = FFN_H // 128
    ngrp = (nchunk + 3) // 4
    unit = [0]
    self.set_hextra(off, (self.ST_BYTES - off) // 4096)

    def in_phase(pi):
        ws, wb = winA[pi % 3], winA_b[pi % 3]
        self.wload(ws[:, :, 0:256], w_in[:, pi * 256:(pi + 1) * 256].rearrange("(kc p) n -> p kc n", p=128), wb)
        self.wload(ws[:, :, 256:512], w_in[:, FFN_H + pi * 256:FFN_H + (pi + 1) * 256].rearrange("(kc p) n -> p kc n", p=128), wb)
        for jj in range(2):
            j = 2 * pi + jj
            grp, jl = j // 4, j % 4
            hs_, hs_b = hidT[grp % 2], hidT_b[grp % 2]
            for tg in range(4):
                u = unit[0]
                unit[0] += 1
                bg, bu = 2 * (u % 2), 2 * (u % 2) + 1
                sl, slb = sil[u % 2], sil_b[u % 2]
                for (bk, coff) in ((bg, jj * 128), (bu, 256 + jj * 128)):
                    for kc in range(KC):
                        P.op("pe", lambda e, bk=bk, coff=coff, kc=kc, tg=tg, ws=ws: e.matmul(
                            self.bank[bk][:], lhsT=ws[:, kc, coff:coff + 128], rhs=self.aT[:, kc, tg * 512:(tg + 1) * 512],
                            start=(kc == 0), stop=(kc == KC - 1)),
                            reads=[wb] + self.aT_b[tg * 4:(tg + 1) * 4], writes=[self.bank_b[bk]])
                P.op("act", lambda e, bg=bg, sl=sl: e.activation(out=sl[:, 0, :], in_=self.bank[bg][:], func=AF.Silu),
                     reads=[self.bank_b[bg]], writes=[slb])
                P.op("dve", lambda e, bu=bu, sl=sl, hs_=hs_, jl=jl, tg=tg: e.tensor_tensor(
                    out=hs_[:, jl, tg * 512:(tg + 1) * 512], in0=sl[:, 0, :], in1=self.bank[bu][:], op=ALU.mult),
                    reads=[slb, self.bank_b[bu]], writes=[hs_b[jl][tg]])

    def out_load(grp):
        n = min(4, nchunk - grp * 4)
        wo, wob = woutB[grp % 2], woutB_b[grp % 2]
        self.wload(wo[:, 0:n, :], w_out[grp * 512:grp * 512 + n * 128, :].rearrange("(jl p) n -> p jl n", p=128), wob)

    def out_tile(grp, t):
        n = min(4, nchunk - grp * 4)
        wo, wob = woutB[grp % 2], woutB_b[grp % 2]
        hs_, hs_b = hidT[grp % 2], hidT_b[grp % 2]
        for half in range(2):
            bk = 4 + 2 * (t % 2) + half
            for q in range(n):
                P.op("pe", lambda e, bk=bk, q=q, half=half: e.matmul(
                    self.bank[bk][:], lhsT=hs_[:, q, t * 128:(t + 1) * 128], rhs=wo[:, q, half * 512:(half + 1) * 512],
                    start=(q == 0), stop=(q == n - 1)),
                    reads=[wob, hs_b[q][t // 4]], writes=[self.bank_b[bk]])
            dst = self.acc[:, t, half * 512:(half + 1) * 512]
            if grp == 0:
                P.op("act", lambda e, bk=bk, dst=dst: e.copy(out=dst, in_=self.bank[bk][:]),
                     reads=[self.bank_b[bk]], writes=[self.acc_b[t]])
            else:
                P.op("dve", lambda e, bk=bk, dst=dst: e.tensor_tensor(out=dst, in0=dst, in1=self.bank[bk][:], op=ALU.add),
                     reads=[self.bank_b[bk]], writes=[self.acc_b[t]])

    toks = []
    pairs_of = lambda grp: range(grp * 2, min(grp * 2 + 2, nchunk // 2))
    for pi in pairs_of(0):
        in_phase(pi)
    for grp in range(ngrp):
        out_load(grp)
        if grp + 1 < ngrp:
            for pi in pairs_of(grp + 1):
                in_phase(pi)
            for t in range(NT):
                out_tile(grp, t)
        else:
            ph = self.boundary_phases_f(lambda t: (self.acc[:, t, :], [self.acc_b[t]]), lambda t: (self.acc[:, t, :], self.acc_b[t]),
                                        toks=toks, tbanks=(0, 1))
            pipeline(NT, [lambda t, grp=grp: out_tile(grp, t)] + ph)
    return toks


Builder.ffn_stage = _ffn_stage2


def _ple2(self, i, do_pre=True):
    P = self.P
    P.barrier()
    off = 0
    Wg, off = self.st_view(off, [KC, D], BF16)
    Wg_b = Buf(f"p{i}Wg")
    Wp, off = self.st_view(off, [2, D], BF16)
    Wp_b = Buf(f"p{i}Wp")
    pb, pb_b, pT, pT_b, sg, sg_b = [], [], [], [], [], []
    for s in range(3):
        v, off = self.st_view(off, [1, 256], BF16)
        pb.append(v)
        pb_b.append(Buf(f"p{i}pb{s}"))
        v, off = self.st_view(off, [2, 128], BF16)
        pT.append(v)
        pT_b.append(Buf(f"p{i}pT{s}"))
        v, off = self.st_view(off, [2, 512], F32)
        sg.append(v)
        sg_b.append(Buf(f"p{i}sg{s}"))
    self.set_hextra(off, 6)
    self.wload(Wg[:, :, 0:512], self.dr["ple_w_gate"][i][:, 0:512].rearrange("(kc p) n -> p kc n", p=128), Wg_b)
    self.wload(Wp, self.dr["ple_w_proj"][i].rearrange("(kc p) n -> p kc n", p=128), Wp_b)
    Wg2_b = Buf(f"p{i}Wg2")
    self.wload(Wg[:, :, 512:1024], self.dr["ple_w_gate"][i][:, 512:1024].rearrange("(kc p) n -> p kc n", p=128), Wg2_b)
    toks = []
    hst = {}

    def mm(t):
        s = t % 3
        self.wload(pb[s][:, 0, :], self.dr["p"][i, t * 128:(t + 1) * 128, :], pb_b[s])
        ptb = 6
        pst = self.bank[ptb][:].bitcast(BF16)[:, (t % 2) * 256:(t % 2) * 256 + 256].rearrange("p (k n) -> p k n", k=2)
        for kc in range(2):
            P.op("pe", lambda e, kc=kc: e.transpose(out=pst[:, kc, :], in_=pb[s][:, 0, kc * 128:(kc + 1) * 128], identity=self.ident[:]),
                 reads=[pb_b[s], self.ident_b], writes=[self.bank_b[ptb]])
        P.op("dve", lambda e: e.tensor_copy(out=pT[s], in_=pst), reads=[self.bank_b[ptb]], writes=[pT_b[s]])
        for half in range(2):
            u = 2 * t + half
            bG, bE = u % 3, 3 + u % 3
            wgb = Wg_b if half == 0 else Wg2_b
            for kc in range(KC):
                P.op("pe", lambda e, kc=kc, half=half, bG=bG: e.matmul(
                    self.bank[bG][:], lhsT=self.aT[:, kc, t * 128:(t + 1) * 128], rhs=Wg[:, kc, half * 512:(half + 1) * 512],
                    start=(kc == 0), stop=(kc == KC - 1)), reads=[wgb, self.aT_b[t]], writes=[self.bank_b[bG]])
            for kc in range(2):
                P.op("pe", lambda e, kc=kc, half=half, bE=bE: e.matmul(
                    self.bank[bE][:], lhsT=pT[s][:, kc, :], rhs=Wp[:, kc, half * 512:(half + 1) * 512],
                    start=(kc == 0), stop=(kc == 1)), reads=[Wp_b, pT_b[s]], writes=[self.bank_b[bE]])

    def post(t):
        s = t % 3
        for half in range(2):
            u = 2 * t + half
            bG, bE = u % 3, 3 + u % 3
            P.op("act", lambda e, half=half, bG=bG: e.activation(out=sg[s][:, half, :], in_=self.bank[bG][:], func=AF.Sigmoid),
                 reads=[self.bank_b[bG]], writes=[sg_b[s]])
            P.op("dve", lambda e, half=half, bE=bE: e.tensor_tensor(out=sg[s][:, half, :], in0=sg[s][:, half, :], in1=self.bank[bE][:], op=ALU.mult),
                 reads=[self.bank_b[bE]], writes=[sg_b[s]])
        hs, hb = self.h_load(t)
        hst[t] = (hs, hb)

    def resid(t):
        s = t % 3
        hs, hb = hst[t]
        P.op("dve", lambda e: e.tensor_tensor(out=hs, in0=hs, in1=sg[s].rearrange("p a b -> p (a b)"), op=ALU.add),
             reads=[sg_b[s]], writes=[hb])
        toks.append(self.h_store(t, hs, hb))

    phases = [mm, post]
    if do_pre:
        pn = self.pre_norm_phases_f(lambda t: hst.pop(t), tbanks=(7,))

        def resid_pa(t):
            resid(t)
            pn[0](t)
        phases += [resid_pa] + pn[1:]
    else:
        phases += [resid]
    pipeline(NT, phases)
    return toks


Builder.ple = _ple2


def _att_main2(self, i, j):
    P = self.P
    dr = self.dr
    P.barrier()
    w_in = dr["attn_w_in"][j]
    if getattr(self, "KT_b", None) is None:
        self.KT_b = [[Buf(f"KT{g}_{fb}") for fb in range(8)] for g in range(3)]
        self.V_b = [[Buf(f"V{g}_{blk}") for blk in range(NT)] for g in range(3)]
    off = 0
    Wq, off = self.st_view(off, [KC, 384], BF16)
    Wq_b = [Buf(f"aWq{g}") for g in range(3)]
    QT, KT, Vt, KH, VH, Et, Et0 = [], [], [], [], [], [], []
    QT_b, KT_sb, Vt_b, KH_b, VH_b, Et_b = [], [], [], [], [], []
    for g, (w, r) in enumerate(DIL):
        a, off = self.st_view(off, [1, T], BF16); QT.append(a); QT_b.append(Buf(f"aQT{g}"))
        a, off = self.st_view(off, [1, T], BF16); KT.append(a); KT_sb.append(Buf(f"aKT{g}"))
        a, off = self.st_view(off, [NT, 128], BF16); Vt.append(a); Vt_b.append(Buf(f"aVt{g}"))
        a, off = self.st_view(off, [1, r * 128], BF16); KH.append(a); KH_b.append(Buf(f"aKH{g}"))
        a, off = self.st_view(off, [r, 128], BF16); VH.append(a); VH_b.append(Buf(f"aVH{g}"))
        a, off = self.st_view(off, [2, 256], F32); Et.append(a)
        a, off = self.st_view(off, [2, 256], F32); Et0.append(a); Et_b.append(Buf(f"aEt{g}"))
    bst, off = self.st_view(off, [2, 256], F32)
    bst_b = Buf("abst")
    m01, off = self.st_view(off, [1, 256], F32)
    m01p, off = self.st_view(off, [1, 256], F32)
    m_b = Buf("am01")
    ones, off = self.st_view(off, [1, 64], BF16)
    ones_b = Buf("aones")
    ex, ex_b, pt, pt_b = [], [], [], []
    for s in range(2):
        a, off = self.st_view(off, [1, 1024], F32); ex.append(a); ex_b.append(Buf(f"aex{s}"))
        a, off = self.st_view(off, [8, 128], BF16); pt.append(a); pt_b.append(Buf(f"apt{s}"))
    bk, bb = self.bank, self.bank_b
    bigb = self.big[:].bitcast(BF16)
    oT = bigb[:, 0:16384].rearrange("p (h t) -> p h t", h=8)
    oT_b = [Buf(f"aoT{h}") for h in range(8)]
    NDacc = self.big[:, 8192:12288].rearrange("p (a t) -> p a t", a=2)
    ND_b = Buf("aND")
    P.dma("sp", m01[:, 0, :], dr["att_mask01"][:, :], writes=[m_b])
    P.dma("sp", m01p[:, 0, :], dr["att_prevmask"][:, :], writes=[m_b])
    P.op("pool", lambda e: e.memset(ones, 1.0), writes=[ones_b])
    uq = [0]
    for hp in range(8):
        for g, (w, r) in enumerate(DIL):
            nbc = 16 // r
            c0 = g * 3072 + hp * 128
            self.wload(Wq[:, :, g * 128:(g + 1) * 128], w_in[:, c0:c0 + 128].rearrange("(kc p) n -> p kc n", p=128), Wq_b[g])
            P.dma("sp", KT[g][:, 0, :], dr["KT_own"][g, hp * 128:(hp + 1) * 128, :], reads=[self.KT_b[g][hp]], writes=[KT_sb[g]])
            P.dma("sp", Vt[g], dr["V_own"][g, :, hp * 128:(hp + 1) * 128].rearrange("(b p) f -> p b f", p=128), reads=self.V_b[g], writes=[Vt_b[g]])
            for c in range(r):
                p0 = c * (T // r) + (nbc - 1) * 128
                P.dma("sp", KH[g][:, 0, c * 128:(c + 1) * 128], dr["KT_prev"][g, hp * 128:(hp + 1) * 128, p0:p0 + 128], writes=[KH_b[g]])
            for c in range(r):
                p0 = c * (T // r) + (nbc - 1) * 128
                P.dma("sp", VH[g][:, c, :], dr["V_prev"][g, p0:p0 + 128, hp * 128:(hp + 1) * 128], writes=[VH_b[g]])
            P.dma("sp", bst, dr["att_bias"][g, 2 * hp:2 * hp + 2].rearrange("h p n -> p h n"), writes=[bst_b])
            P.op("act", lambda e: e.activation(out=bst, in_=bst, func=AF.Exp), reads=[], writes=[bst_b])
            for hh in range(2):
                P.op("pool", lambda e, hh=hh, g=g: e.tensor_tensor(out=Et[g][:, hh, :], in0=bst[:, hh, :], in1=m01[:, 0, :], op=ALU.mult),
                     reads=[bst_b, m_b], writes=[Et_b[g]])
                P.op("pool", lambda e, hh=hh, g=g: e.tensor_tensor(out=Et0[g][:, hh, :], in0=bst[:, hh, :], in1=m01p[:, 0, :], op=ALU.mult),
                     reads=[bst_b, m_b], writes=[Et_b[g]])

        def q_proj(g):
            r = DIL[g][1]
            for tg in range(4):
                b = uq[0] % 2
                uq[0] += 1
                for kc in range(KC):
                    P.op("pe", lambda e, b=b, kc=kc, tg=tg: e.matmul(
                        bk[b][:], lhsT=Wq[:, kc, g * 128:(g + 1) * 128], rhs=self.aT[:, kc, tg * 512:(tg + 1) * 512],
                        start=(kc == 0), stop=(kc == KC - 1)), reads=[Wq_b[g]] + self.aT_b[tg * 4:(tg + 1) * 4], writes=[bb[b]])
                jn = 512 // r
                dst = QT[g][:, 0, :].rearrange("p (c j) -> p c j", c=r)[:, :, tg * jn:(tg + 1) * jn]
                src = bk[b][:].rearrange("p (j c) -> p c j", c=r)
                P.op("act", lambda e, dst=dst, src=src: e.copy(out=dst, in_=src), reads=[bb[b]], writes=[QT_b[g]])

        def geom(it, d):
            g, bp = it // 8, it % 8
            blk = 2 * bp + d
            r = DIL[g][1]
            nbc = 16 // r
            c, n = blk // nbc, blk % nbc
            return g, blk, r, c, n

        def ph0(it):
            if it % 8 == 0:
                q_proj(it // 8)
            sb0 = 2 if it % 2 == 0 else 6
            for d in range(2):
                g, blk, r, c, n = geom(it, d)
                qs = slice(blk * 128, (blk + 1) * 128)
                first = (n == 0)
                for hh in range(2):
                    ps = slice(hh * 64, (hh + 1) * 64)
                    sT = bk[sb0 + hh][:].rearrange("p (d k q) -> p d k q", d=2, k=2)
                    kprev = KH[g][ps, 0, c * 128:(c + 1) * 128] if first else KT[g][ps, 0, (blk - 1) * 128:blk * 128]
                    P.op("pe", lambda e, sT=sT, kprev=kprev, ps=ps, d=d, qs=qs, g=g: e.matmul(sT[:, d, 0, :], lhsT=kprev, rhs=QT[g][ps, 0, qs], start=True, stop=True),
                         reads=[KH_b[g] if first else KT_sb[g], QT_b[g]], writes=[bb[sb0 + hh]])
                    P.op("pe", lambda e, sT=sT, ps=ps, d=d, qs=qs, g=g: e.matmul(sT[:, d, 1, :], lhsT=KT[g][ps, 0, qs], rhs=QT[g][ps, 0, qs], start=True, stop=True),
                         reads=[KT_sb[g], QT_b[g]], writes=[bb[sb0 + hh]])

        def ph1(it):
            sb0 = 2 if it % 2 == 0 else 6
            s2 = it % 2
            P.op("act", lambda e: e.activation(out=ex[s2][:, 0, :].rearrange("p (h x) -> p h x", h=2), in_=self.psum_all[:, sb0:sb0 + 2, :], func=AF.Exp, scale=0.125),
                 reads=[bb[sb0], bb[sb0 + 1]], writes=[ex_b[s2]])
            exv = ex[s2][:, 0, :].rearrange("p (h d x) -> p h d x", h=2, d=2)
            ptv = pt[s2].rearrange("p (h d k) q -> p h d (k q)", h=2, d=2)
            for d in range(2):
                g, blk, r, c, n = geom(it, d)
                tab = Et0[g] if n == 0 else Et[g]
                P.op("dve", lambda e, d=d, tab=tab: e.tensor_tensor(out=ptv[:, :, d, :], in0=exv[:, :, d, :], in1=tab, op=ALU.mult),
                     reads=[ex_b[s2], Et_b[g]], writes=[pt_b[s2]])

        def ph2(it):
            s2 = it % 2
            nb_ = 4 + it % 2
            ndb = bk[nb_][:].rearrange("p (d a q) -> p d a q", d=2, a=2)
            for d in range(2):
                g, blk, r, c, n = geom(it, d)
                first = (n == 0)
                for hh in range(2):
                    ps = slice(hh * 64, (hh + 1) * 64)
                    i0 = hh * 4 + d * 2
                    vprev = VH[g][:, c, ps] if first else Vt[g][:, blk - 1, ps]
                    P.op("pe", lambda e, d=d, vprev=vprev, ps=ps, i0=i0: e.matmul(ndb[ps, d, 0, :], lhsT=vprev, rhs=pt[s2][:, i0, :], start=True, stop=False),
                         reads=[VH_b[g] if first else Vt_b[g], pt_b[s2]], writes=[bb[nb_]])
                    P.op("pe", lambda e, d=d, ps=ps, i0=i0, g=g, blk=blk: e.matmul(ndb[ps, d, 0, :], lhsT=Vt[g][:, blk, ps], rhs=pt[s2][:, i0 + 1, :], start=False, stop=True),
                         reads=[Vt_b[g], pt_b[s2]], writes=[bb[nb_]])
                    P.op("pe", lambda e, d=d, ps=ps, i0=i0: e.matmul(ndb[ps, d, 1, :], lhsT=ones[:, 0, :], rhs=pt[s2][:, i0, :], start=True, stop=False),
                         reads=[ones_b, pt_b[s2]], writes=[bb[nb_]])
                    P.op("pe", lambda e, d=d, ps=ps, i0=i0: e.matmul(ndb[ps, d, 1, :], lhsT=ones[:, 0, :], rhs=pt[s2][:, i0 + 1, :], start=False, stop=True),
                         reads=[ones_b, pt_b[s2]], writes=[bb[nb_]])

        def ph3(it):
            nb_ = 4 + it % 2
            ndb = bk[nb_][:].rearrange("p (d a q) -> p d a q", d=2, a=2)
            for d in range(2):
                g, blk, r, c, n = geom(it, d)
                t0 = n * 128 * r + c
                tsl = slice(t0, t0 + 127 * r + 1, r)
                if g == 0:
                    P.op("dve", lambda e, d=d, tsl=tsl: e.tensor_copy(out=NDacc[:, :, tsl], in_=ndb[:, d, :, :]), reads=[bb[nb_]], writes=[ND_b] + self.acc_b[8:12])
                else:
                    P.op("dve", lambda e, d=d, tsl=tsl: e.tensor_tensor(out=NDacc[:, :, tsl], in0=NDacc[:, :, tsl], in1=ndb[:, d, :, :], op=ALU.add), reads=[bb[nb_]], writes=[ND_b])

        _lvl = int(_os.environ.get('ADBG', '9'))
        pipeline(3 * 8, [ph0, ph1, ph2, ph3][:min(_lvl, 4)])
        if _lvl < 5:
            continue
        P.op("dve", lambda e: e.reciprocal(out=NDacc[:, 1, :], in_=NDacc[:, 1, :]), reads=[], writes=[ND_b])
        P.op("pool", lambda e, hp=hp: e.tensor_tensor(out=oT[:, hp, :], in0=NDacc[:, 0, :], in1=NDacc[:, 1, :], op=ALU.mult),
             reads=[ND_b], writes=[oT_b[hp]] + self.acc_b[hp:hp + 1])
    P.barrier()
    off = 0
    Wo, off = self.st_view(off, [KC, D], BF16)
    Wo_b = Buf("aWo")
    nrm, off = self.st_view(off, [2, D], F32)
    nrm_b = [Buf("anrm0"), Buf("anrm1")]
    self.wload(Wo, dr["attn_w_out"][j].rearrange("(kc p) n -> p kc n", p=128), Wo_b)
    self.set_hextra(off, 6)
    toks = []

    def outp(t):
        tsl = slice(t * 128, (t + 1) * 128)
        for half in range(2):
            for h in range(8):
                P.op("pe", lambda e, h=h, half=half: e.matmul(bk[6 + half][:], lhsT=oT[:, h, tsl], rhs=Wo[:, h, half * 512:(half + 1) * 512],
                                                             start=(h == 0), stop=(h == 7)),
                     reads=[oT_b[h], Wo_b], writes=[bb[6 + half]])

    src = self.psum_all[:, 6:8, :].rearrange("p a b -> p (a b)")
    ph = self.boundary_phases(lambda t: (src, [bb[6], bb[7]]), lambda t: (nrm[:, t % 2, :], nrm_b[t % 2]), toks=toks, tbanks=(0, 1))
    pipeline(NT, [outp] + ph)
    for t in range(NT):
        for h in range(8):
            self.acc_b[t].r = list(self.acc_b[t].r) + list(oT_b[h].r)
        self.acc_b[t].r = list(self.acc_b[t].r) + list(ND_b.r) + ([ND_b.w] if ND_b.w else [])
    return toks


Builder.att_main = _att_main2


def _gmlp_stage2(self, i, j):
    P = self.P
    dr = self.dr
    P.barrier()
    bigb = self.big[:].bitcast(BF16)
    Wv = bigb[:, 0:16384].rearrange("p (k n) -> p k n", k=KC)
    Wu = bigb[:, 16384:32768].rearrange("p (k n) -> p k n", k=KC)
    Wv_b = [Buf(f"gWv{c}") for c in range(4)]
    Wu_b = [Buf(f"gWu{c}") for c in range(4)]
    off = 0
    Wo, off = self.st_view(off, [16, D], BF16)
    Wo_b = [Buf(f"gWo{q}") for q in range(4)]
    wsT, off = self.st_view(off, [GM_G, 128], BF16)
    wsT_b = Buf("gwsT")
    gtab, off = self.st_view(off, [1, GM_W], F32)
    btab, off = self.st_view(off, [1, GM_W], BF16)
    gtab_b, btab_b = Buf("ggtab"), Buf("gbtab")
    bsT, off = self.st_view(off, [1, GM_G], F32)
    bsT_b = Buf("gbsT")
    cmask, off = self.st_view(off, [1, 128], F32)
    cmask_b = Buf("gcmask")
    off_vf = off
    vf, off = self.st_view(off, [1, GM_W], F32)
    vf_b = Buf("gvf")
    vn, off = self.st_view(off, [1, GM_W], BF16)
    vn_b = Buf("gvn")
    ssb, off = self.st_view(off, [1, GM_W], F32)
    ssb_b = Buf("gssb")
    us, off = self.st_view(off, [1, GM_W], BF16)
    us_b = Buf("gus")
    usT, off = self.st_view(off, [16, 128], BF16)
    usT_b = Buf("gusT")
    gu, gu_b = [], []
    for s in range(2):
        a, off = self.st_view(off, [1, 512], F32)
        gu.append(a)
        gu_b.append(Buf(f"ggu{s}"))
    nrm, off = self.st_view(off, [1, D], F32)
    nrm_b = Buf("gnrm")
    wsl, off2 = self.st_view(off_vf, [GM_G, 128], BF16)
    wsl_b = vf_b
    w_in = dr["gm_w_in"][j]
    self.wload(wsl, dr["gm_w_s"][j].rearrange("g n m -> n g m"), wsl_b)
    self.wload(btab[:, 0, :], dr["gm_ln_b"][j, :].partition_broadcast(128), btab_b)
    P.dma("sp", gtab[:, 0, :], dr["gm_ln_g"][j, :].partition_broadcast(128), writes=[gtab_b])
    for cb in range(4):
        self.P.dma("pool", Wv[:, :, cb * 512:(cb + 1) * 512], w_in[:, GM_W + cb * 512:GM_W + (cb + 1) * 512].rearrange("(kc p) n -> p kc n", p=128),
                   writes=[Wv_b[cb]])
    for cb in range(4):
        self.P.dma("pool", Wu[:, :, cb * 512:(cb + 1) * 512], w_in[:, cb * 512:(cb + 1) * 512].rearrange("(kc p) n -> p kc n", p=128),
                   writes=[Wu_b[cb]])
    for q in range(4):
        self.wload(Wo[:, q * 4:(q + 1) * 4, :], dr["gm_w_out"][j][q * 512:(q + 1) * 512, :].rearrange("(kc p) n -> p kc n", p=128), Wo_b[q])
    P.dma("sp", bsT[:, 0, :], dr["gm_b_s"][j].rearrange("g n -> n g"), writes=[bsT_b], allow_slow_non_contiguous=True)
    P.dma("sp", cmask[:, 0, :], dr["causalT"][:, :], writes=[cmask_b])
    bk, bb = self.bank, self.bank_b
    tps = bk[0][:].bitcast(BF16)[:, 0:1024].rearrange("p (g n) -> p g n", g=GM_G)
    for g in range(GM_G):
        P.op("pe", lambda e, g=g: e.transpose(out=tps[:, g, :], in_=wsl[:, g, :], identity=self.ident[:]),
             reads=[wsl_b, self.ident_b], writes=[bb[0]])
    for g in range(GM_G):
        P.op("dve", lambda e, g=g: e.tensor_tensor(out=wsT[:, g, :], in0=tps[:, g, :], in1=cmask[:, 0, :], op=ALU.mult),
             reads=[bb[0], cmask_b], writes=[wsT_b])
    toks = []
    stt = {}
    uu = [0]

    def ph0(t):
        tsl = slice(t * 128, (t + 1) * 128)
        sv, sv_b = [], []
        for cb in range(4):
            b = uu[0] % 2
            uu[0] += 1
            for kc in range(KC):
                P.op("pe", lambda e, cb=cb, kc=kc, b=b: e.matmul(bk[b][:], lhsT=self.aT[:, kc, tsl], rhs=Wv[:, kc, cb * 512:(cb + 1) * 512],
                                                               start=(kc == 0), stop=(kc == KC - 1)),
                     reads=[Wv_b[cb], self.aT_b[t]], writes=[bb[b]])
            a, a_b = self.new_stat()
            sv.append(a)
            sv_b.append(a_b)
            P.op("act", lambda e, cb=cb, a=a, b=b: e.activation(out=vf[:, 0, cb * 512:(cb + 1) * 512], in_=bk[b][:], func=AF.Gelu, accum_out=a),
                 reads=[bb[b]], writes=[vf_b, a_b])
        stt[t] = (sv, sv_b)

    def ph1(t):
        sv, sv_b = stt.pop(t)
        sq, sq_b = [], []
        for hh in range(2):
            a, a_b = self.new_stat()
            sq.append(a)
            sq_b.append(a_b)
            P.op("act", lambda e, hh=hh, a=a: e.activation(out=self.junk[:, :], in_=vf[:, 0, hh * 1024:(hh + 1) * 1024], func=AF.Square, accum_out=a),
                 reads=[vf_b], writes=[self.junk_b, a_b])
        m1, m1_b = self.new_stat()
        P.op("pool", lambda e: e.tensor_tensor(out=m1, in0=sv[0], in1=sv[1], op=ALU.add), reads=[sv_b[0], sv_b[1]], writes=[m1_b])
        m2, m2_b = self.new_stat()
        P.op("pool", lambda e: e.tensor_tensor(out=m2, in0=sv[2], in1=sv[3], op=ALU.add), reads=[sv_b[2], sv_b[3]], writes=[m2_b])
        mu, mu_b = self.new_stat()
        P.op("pool", lambda e: e.tensor_tensor(out=mu, in0=m1, in1=m2, op=ALU.add), reads=[m1_b, m2_b], writes=[mu_b])
        P.op("pool", lambda e: e.tensor_scalar(out=mu, in0=mu, scalar1=1.0 / GM_W, scalar2=None, op0=ALU.mult), reads=[], writes=[mu_b])
        e2, e2_b = self.new_stat()
        P.op("pool", lambda e: e.tensor_tensor(out=e2, in0=sq[0], in1=sq[1], op=ALU.add), reads=[sq_b[0], sq_b[1]], writes=[e2_b])
        mm, mm_b = self.new_stat()
        P.op("pool", lambda e: e.tensor_tensor(out=mm, in0=mu, in1=mu, op=ALU.mult), reads=[mu_b], writes=[mm_b])
        var, var_b = self.new_stat()
        P.op("pool", lambda e: e.tensor_scalar(out=var, in0=e2, scalar1=1.0 / GM_W, scalar2=EPS, op0=ALU.mult, op1=ALU.add),
             reads=[e2_b], writes=[var_b])
        P.op("pool", lambda e: e.tensor_tensor(out=var, in0=var, in1=mm, op=ALU.subtract), reads=[mm_b], writes=[var_b])
        rs, rs_b = self.new_stat()
        P.op("pool", lambda e: e.tensor_tensor(out=rs, in0=var, in1=self.cm05[:], op=ALU.pow), reads=[var_b, self.cm05_b], writes=[rs_b])
        P.op("dve", lambda e: e.tensor_scalar(out=vf[:, 0, :], in0=vf[:, 0, :], scalar1=mu, scalar2=rs, op0=ALU.subtract, op1=ALU.mult),
             reads=[mu_b, rs_b], writes=[vf_b])
        P.op("pool", lambda e: e.tensor_tensor(out=vf[:, 0, :], in0=vf[:, 0, :], in1=gtab[:, 0, :], op=ALU.mult), reads=[gtab_b], writes=[vf_b])
        P.op("dve", lambda e: e.tensor_tensor(out=vn[:, 0, :], in0=vf[:, 0, :], in1=btab[:, 0, :], op=ALU.add), reads=[vf_b, btab_b], writes=[vn_b])

    def ph2(t):
        tsl = slice(t * 128, (t + 1) * 128)
        for un in range(4):
            bkx = 2 + un % 2
            for gg in range(2):
                g = un * 2 + gg
                P.op("pe", lambda e, g=g, gg=gg, bkx=bkx: e.matmul(bk[bkx][:, gg * 256:(gg + 1) * 256], lhsT=wsT[:, g, :], rhs=vn[:, 0, g * 256:(g + 1) * 256],
                                                                   start=True, stop=True),
                     reads=[wsT_b, vn_b], writes=[bb[bkx]])
            for gg in range(2):
                g = un * 2 + gg
                P.op("dve", lambda e, g=g, gg=gg, bkx=bkx: e.tensor_scalar(out=ssb[:, 0, g * 256:(g + 1) * 256], in0=bk[bkx][:, gg * 256:(gg + 1) * 256],
                                                                            scalar1=bsT[:, 0, g:g + 1], scalar2=None, op0=ALU.add),
                     reads=[bb[bkx], bsT_b], writes=[ssb_b])
        for cb in range(4):
            b = uu[0] % 2
            uu[0] += 1
            for kc in range(KC):
                P.op("pe", lambda e, cb=cb, kc=kc, b=b: e.matmul(bk[b][:], lhsT=self.aT[:, kc, tsl], rhs=Wu[:, kc, cb * 512:(cb + 1) * 512],
                                                               start=(kc == 0), stop=(kc == KC - 1)),
                     reads=[Wu_b[cb], self.aT_b[t]], writes=[bb[b]])
            s = cb % 2
            P.op("act", lambda e, b=b, s=s: e.activation(out=gu[s][:, 0, :], in_=bk[b][:], func=AF.Gelu), reads=[bb[b]], writes=[gu_b[s]])
            P.op("dve", lambda e, cb=cb, s=s: e.tensor_tensor(out=us[:, 0, cb * 512:(cb + 1) * 512], in0=gu[s][:, 0, :], in1=ssb[:, 0, cb * 512:(cb + 1) * 512], op=ALU.mult),
                 reads=[gu_b[s], ssb_b], writes=[us_b])

    def ph3(t):
        for half in range(2):
            tp = bk[4 + half][:].bitcast(BF16).rearrange("p (k n) -> p k n", k=8)
            for q in range(8):
                kc = half * 8 + q
                P.op("pe", lambda e, tp=tp, q=q, kc=kc: e.transpose(out=tp[:, q, :], in_=us[:, 0, kc * 128:(kc + 1) * 128], identity=self.ident[:]),
                     reads=[us_b, self.ident_b], writes=[bb[4 + half]])
            P.op("act", lambda e, tp=tp, half=half: e.copy(out=usT[:, half * 8:(half + 1) * 8, :], in_=tp), reads=[bb[4 + half]], writes=[usT_b])
        for half in range(2):
            for kc in range(16):
                P.op("pe", lambda e, half=half, kc=kc: e.matmul(bk[6 + half][:], lhsT=usT[:, kc, :], rhs=Wo[:, kc, half * 512:(half + 1) * 512],
                                                                start=(kc == 0), stop=(kc == 15)),
                     reads=[usT_b, Wo_b[kc // 4]], writes=[bb[6 + half]])

    src = self.psum_all[:, 6:8, :].rearrange("p a b -> p (a b)")
    bph = self.boundary_phases(lambda t: (src, [bb[6], bb[7]]), lambda t: (nrm[:, 0, :], nrm_b), toks=toks, tbanks=(4, 5))
    def ph3b(t):
        ph3(t)
        bph[0](t)
    pipeline(NT, [ph0, ph1, ph2, ph3b] + bph[1:])
    for t in range(NT):
        for src_b in (Wv_b if t < 8 else Wu_b):
            self.acc_b[t].r = list(self.acc_b[t].r) + list(src_b.r) + ([src_b.w] if src_b.w else [])
    return toks


Builder.gmlp_stage = _gmlp_stage2


def _ffn_ple_stage(self, i, last):
    P = self.P
    self.load_g("post", (i, 3))
    self.load_g("pre", (i, 4))
    P.barrier()
    w_in = self.dr["ffn_w_in"][i]
    w_out = self.dr["ffn_w_out"][i]
    off = 0
    winA, winA_b = [], []
    for s in range(3):
        v, off = self.st_view(off, [KC, 512], BF16)
        winA.append(v)
        winA_b.append(Buf(f"f{i}winA{s}"))
    woutB, woutB_b, woutB_off = [], [], []
    for s in range(2):
        woutB_off.append(off)
        v, off = self.st_view(off, [4, D], BF16)
        woutB.append(v)
        woutB_b.append(Buf(f"f{i}woutB{s}"))
    hidT, hidT_b, hidT_off = [], [], []
    for s in range(2):
        hidT_off.append(off)
        v, off = self.st_view(off, [4, T], BF16)
        hidT.append(v)
        hidT_b.append([[Buf(f"f{i}hid{s}_{jl}_{tg}") for tg in range(4)] for jl in range(4)])
    sil, sil_b = [], []
    for s in range(2):
        v, off = self.st_view(off, [1, 512], F32)
        sil.append(v)
        sil_b.append(Buf(f"f{i}sil{s}"))
    nchunk = FFN_H // 128
    ngrp = (nchunk + 3) // 4
    unit = [0]

    def in_phase(pi):
        ws, wb = winA[pi % 3], winA_b[pi % 3]
        self.wload(ws[:, :, 0:256], w_in[:, pi * 256:(pi + 1) * 256].rearrange("(kc p) n -> p kc n", p=128), wb)
        self.wload(ws[:, :, 256:512], w_in[:, FFN_H + pi * 256:FFN_H + (pi + 1) * 256].rearrange("(kc p) n -> p kc n", p=128), wb)
        for jj in range(2):
            j = 2 * pi + jj
            grp, jl = j // 4, j % 4
            hs_, hs_b = hidT[grp % 2], hidT_b[grp % 2]
            for tg in range(4):
                u = unit[0]
                unit[0] += 1
                bg, bu = 2 * (u % 2), 2 * (u % 2) + 1
                sl, slb = sil[u % 2], sil_b[u % 2]
                for (bk, coff) in ((bg, jj * 128), (bu, 256 + jj * 128)):
                    for kc in range(KC):
                        P.op("pe", lambda e, bk=bk, coff=coff, kc=kc, tg=tg, ws=ws: e.matmul(
                            self.bank[bk][:], lhsT=ws[:, kc, coff:coff + 128], rhs=self.aT[:, kc, tg * 512:(tg + 1) * 512],
                            start=(kc == 0), stop=(kc == KC - 1)),
                            reads=[wb] + self.aT_b[tg * 4:(tg + 1) * 4], writes=[self.bank_b[bk]])
                P.op("act", lambda e, bg=bg, sl=sl: e.activation(out=sl[:, 0, :], in_=self.bank[bg][:], func=AF.Silu),
                     reads=[self.bank_b[bg]], writes=[slb])
                P.op("dve", lambda e, bu=bu, sl=sl, hs_=hs_, jl=jl, tg=tg: e.tensor_tensor(
                    out=hs_[:, jl, tg * 512:(tg + 1) * 512], in0=sl[:, 0, :], in1=self.bank[bu][:], op=ALU.mult),
                    reads=[slb, self.bank_b[bu]], writes=[hs_b[jl][tg]])

    def out_load(grp):
        n = min(4, nchunk - grp * 4)
        wo, wob = woutB[grp % 2], woutB_b[grp % 2]
        self.wload(wo[:, 0:n, :], w_out[grp * 512:grp * 512 + n * 128, :].rearrange("(jl p) n -> p jl n", p=128), wob)

    def out_tile(grp, t, banks):
        n = min(4, nchunk - grp * 4)
        wo, wob = woutB[grp % 2], woutB_b[grp % 2]
        hs_, hs_b = hidT[grp % 2], hidT_b[grp % 2]
        for half in range(2):
            bk = banks[half]
            for q in range(n):
                P.op("pe", lambda e, bk=bk, q=q, half=half: e.matmul(
                    self.bank[bk][:], lhsT=hs_[:, q, t * 128:(t + 1) * 128], rhs=wo[:, q, half * 512:(half + 1) * 512],
                    start=(q == 0), stop=(q == n - 1)),
                    reads=[wob, hs_b[q][t // 4]], writes=[self.bank_b[bk]])
            dst = self.acc[:, t, half * 512:(half + 1) * 512]
            if grp == 0:
                P.op("act", lambda e, bk=bk, dst=dst: e.copy(out=dst, in_=self.bank[bk][:]),
                     reads=[self.bank_b[bk]], writes=[self.acc_b[t]])
            else:
                P.op("dve", lambda e, bk=bk, dst=dst: e.tensor_tensor(out=dst, in0=dst, in1=self.bank[bk][:], op=ALU.add),
                     reads=[self.bank_b[bk]], writes=[self.acc_b[t]])

    pairs_of = lambda grp: range(grp * 2, min(grp * 2 + 2, nchunk // 2))
    for pi in pairs_of(0):
        in_phase(pi)
    for grp in range(ngrp - 1):
        out_load(grp)
        for pi in pairs_of(grp + 1):
            in_phase(pi)
        for t in range(NT):
            out_tile(grp, t, (4 + 2 * (t % 2), 5 + 2 * (t % 2)))
    lg = ngrp - 1
    out_load(lg)
    oth = (lg + 1) % 2
    dead = list(winA_b) + [woutB_b[oth]] + [b for row in hidT_b[oth] for b in row] + list(sil_b)
    assert woutB_off[oth] == 24 * 1024 or woutB_off[oth] == 32 * 1024
    po = 0
    Wg, po = self.st_view(po, [KC, D], BF16)
    Wp, po = self.st_view(po, [2, D], BF16)
    Wg_b, Wg2_b, Wp_b = Buf(f"p{i}Wg"), Buf(f"p{i}Wg2"), Buf(f"p{i}Wp")
    pb, pb_b, pT, pT_b = [], [], [], []
    for s in range(3):
        v, po = self.st_view(po, [1, 256], BF16)
        pb.append(v)
        pb_b.append(Buf(f"p{i}pb{s}"))
        v, po = self.st_view(po, [2, 128], BF16)
        pT.append(v)
        pT_b.append(Buf(f"p{i}pT{s}"))
    lim = 24 * 1024 if woutB_off[oth] != 24 * 1024 else 32 * 1024
    assert po <= lim, (po, lim)
    so = hidT_off[oth]
    sg, sg_b = [], []
    for s in range(3):
        v, so = self.st_view(so, [2, 512], F32)
        sg.append(v)
        sg_b.append(Buf(f"p{i}sg{s}"))
    assert so <= hidT_off[oth] + 16 * 1024
    if lim == 24 * 1024:
        dead = list(winA_b) + [b for row in hidT_b[oth] for b in row]
    P.dma("pool", Wg[:, :, 0:512], self.dr["ple_w_gate"][i][:, 0:512].rearrange("(kc p) n -> p kc n", p=128), writes=[Wg_b] + dead)
    P.dma("pool", Wp, self.dr["ple_w_proj"][i].rearrange("(kc p) n -> p kc n", p=128), writes=[Wp_b])
    P.dma("pool", Wg[:, :, 512:1024], self.dr["ple_w_gate"][i][:, 512:1024].rearrange("(kc p) n -> p kc n", p=128), writes=[Wg2_b])
    if not last:
        self.load_g("pre2", (i + 1, 0))
    toks_b, toks = [], []
    hst = {}

    def mm(t):
        s = t % 3
        P.dma("pool", pb[s][:, 0, :], self.dr["p"][i, t * 128:(t + 1) * 128, :], writes=[pb_b[s]])
        pst = self.bank[7][:].bitcast(BF16)[:, 0:256].rearrange("p (k n) -> p k n", k=2)
        for kc in range(2):
            P.op("pe", lambda e, kc=kc: e.transpose(out=pst[:, kc, :], in_=pb[s][:, 0, kc * 128:(kc + 1) * 128], identity=self.ident[:]),
                 reads=[pb_b[s], self.ident_b], writes=[self.bank_b[7]])
        P.op("dve", lambda e: e.tensor_copy(out=pT[s], in_=pst), reads=[self.bank_b[7]], writes=[pT_b[s]])
        for half in range(2):
            u = 2 * t + half
            bG, bE = u % 2, 2 + u % 2
            wgb = Wg_b if half == 0 else Wg2_b
            for kc in range(KC):
                P.op("pe", lambda e, kc=kc, half=half, bG=bG: e.matmul(
                    self.bank[bG][:], lhsT=self.aT[:, kc, t * 128:(t + 1) * 128], rhs=Wg[:, kc, half * 512:(half + 1) * 512],
                    start=(kc == 0), stop=(kc == KC - 1)), reads=[wgb, self.aT_b[t]], writes=[self.bank_b[bG]])
            for kc in range(2):
                P.op("pe", lambda e, kc=kc, half=half, bE=bE: e.matmul(
                    self.bank[bE][:], lhsT=pT[s][:, kc, :], rhs=Wp[:, kc, half * 512:(half + 1) * 512],
                    start=(kc == 0), stop=(kc == 1)), reads=[Wp_b, pT_b[s]], writes=[self.bank_b[bE]])
            P.op("act", lambda e, half=half, bG=bG: e.activation(out=sg[s][:, half, :], in_=self.bank[bG][:], func=AF.Sigmoid),
                 reads=[self.bank_b[bG]], writes=[sg_b[s]])
            P.op("dve", lambda e, half=half, bE=bE: e.tensor_tensor(out=sg[s][:, half, :], in0=sg[s][:, half, :], in1=self.bank[bE][:], op=ALU.mult),
                 reads=[self.bank_b[bE]], writes=[sg_b[s]])

    def post(t):
        hs, hb = self.h_load(t, src=getattr(self, "h_rd_ple", None))
        hst[t] = (hs, hb)

    def resid(t):
        s = t % 3
        hs, hb = hst[t]
        P.op("pool", lambda e: e.tensor_tensor(out=hs, in0=hs, in1=sg[s].rearrange("p a b -> p (a b)"), op=ALU.add),
             reads=[sg_b[s]], writes=[hb])
        toks.append(self.h_store(t, hs, hb, dst=getattr(self, "h_wr_ple", None)))

    bph = self.boundary_phases(lambda t: (self.acc[:, t, :], [self.acc_b[t]]), lambda t: (self.acc[:, t, :], self.acc_b[t]),
                               toks=toks_b, tbanks=(6,))
    phases = [lambda t: out_tile(lg, t, (4, 5))] + bph + [mm, post]
    if not last:
        pn = self.pre_norm_phases(lambda t: hst.pop(t), tbanks=(7,), gain=(self.gpre2, self.gpre2_b))

        def resid_pa(t):
            resid(t)
            pn[0](t)
        phases += [resid_pa, pn[1]]
    else:
        phases += [resid]
    pipeline(NT, phases)
    return toks


Builder.ffn_ple_stage = _ffn_ple_stage


def _rstd_split(self, src_ap, src_bufs, n):
    P = self.P
    st = {}

    def sq():
        ss, ss_b = self.new_stat()
        P.op("act", lambda e: e.activation(out=self.junk[:, 0:n], in_=src_ap, func=AF.Square, scale=float(n) ** -0.5, accum_out=ss),
             reads=list(src_bufs), writes=[self.junk_b, ss_b])
        st["ss"] = (ss, ss_b)

    def pw():
        ss, ss_b = st["ss"]
        r, r_b = self.new_stat()
        P.op("pool", lambda e: e.tensor_scalar(out=r, in0=ss, scalar1=EPS, scalar2=None, op0=ALU.add), reads=[ss_b], writes=[r_b])
        r2, r2_b = self.new_stat()
        P.op("pool", lambda e: e.tensor_tensor(out=r2, in0=r, in1=self.cm05[:], op=ALU.pow), reads=[r_b, self.cm05_b], writes=[r2_b])
        st["r"] = (r2, r2_b)

    return sq, pw, (lambda: st["r"])


def _pre_norm_phases_f(self, get_h, tbanks=(6, 7), merge_b=False):
    P = self.P
    S = {}

    def a1(t):
        hs, hb = get_h(t)
        sq, pw, get = self.rstd_split(hs, [hb], D)
        sq()
        S[t] = [hs, hb, pw, get]

    def a2(t):
        S[t][2]()

    def a3(t):
        hs, hb, pw, get = S[t]
        r, r_b = get()
        s = t % 2
        ab, ab_b = self.abf[:, s, :], self.abf_b[s]
        P.op("dve", lambda e: e.scalar_tensor_tensor(out=ab, in0=hs, scalar=r, in1=self.gpre[:], op0=ALU.mult, op1=ALU.mult),
             reads=[hb, r_b, self.gpre_b], writes=[ab_b])
        S[t] = (s, ab_b)

    def b1(t):
        s, ab_b = S[t]
        tbank = tbanks[t % len(tbanks)]
        pst = self.bank[tbank][:].bitcast(BF16).rearrange("p (k n) -> p k n", k=KC)
        for kc in range(KC):
            P.op("pe", lambda e, kc=kc: e.transpose(out=pst[:, kc, :], in_=self.abf[:, s, kc * 128:(kc + 1) * 128], identity=self.ident[:]),
                 reads=[ab_b, self.ident_b], writes=[self.bank_b[tbank]])
        S[t] = (tbank, pst)

    def b2(t):
        tbank, pst = S.pop(t)
        P.op("act", lambda e: e.copy(out=self.aT[:, :, t * 128:(t + 1) * 128], in_=pst),
             reads=[self.bank_b[tbank]], writes=[self.aT_b[t]])

    if merge_b:
        def b12(t):
            b1(t)
            b2(t)
        return [a1, a2, a3, b12]
    return [a1, a2, a3, b1, b2]


def _boundary_phases_f(self, src_fn, dst_fn, do_pre=True, tbanks=(6, 7), toks=None, merge_b=False):
    P = self.P
    S = {}
    H = {}

    def p0a(t):
        src_ap, src_b = src_fn(t)
        sq, pw, get = self.rstd_split(src_ap, src_b, D)
        sq()
        S[t] = [src_ap, src_b, pw, get]

    def p0b(t):
        S[t][2]()

    def p0c(t):
        src_ap, src_b, pw, get = S[t]
        dst_ap, dst_b = dst_fn(t)
        r, r_b = get()
        P.op("dve", lambda e: e.scalar_tensor_tensor(out=dst_ap, in0=src_ap, scalar=r, in1=self.gpost[:], op0=ALU.mult, op1=ALU.mult),
             reads=list(src_b) + [r_b, self.gpost_b], writes=[dst_b])
        hs, hb = self.h_load(t)
        S[t] = (dst_ap, dst_b, hs, hb)

    def p1(t):
        dst_ap, dst_b, hs, hb = S.pop(t)
        P.op("dve", lambda e: e.tensor_tensor(out=hs, in0=hs, in1=dst_ap, op=ALU.add), reads=[dst_b], writes=[hb])
        tok = self.h_store(t, hs, hb)
        if toks is not None:
            toks.append(tok)
        H[t] = (hs, hb)

    phases = [p0a, p0b, p0c, p1]
    if do_pre:
        pn = self.pre_norm_phases_f(lambda t: H.pop(t), tbanks=tbanks, merge_b=merge_b)

        def p1a(t):
            p1(t)
            pn[0](t)
        phases = [p0a, p0b, p0c, p1a] + pn[1:]
    return phases


Builder.rstd_split = _rstd_split
Builder.pre_norm_phases_f = _pre_norm_phases_f
Builder.boundary_phases_f = _boundary_phases_f


def _first_norm_f(self):
    self.set_hextra(0, 5)
    pipeline(NT, self.pre_norm_phases_f(lambda t: self.h_load(t), tbanks=(6, 7)))


Builder.first_norm = _first_norm_f
```

```python
import contextlib
import math
import numpy as np
import concourse.bass as bass
import concourse.mybir as mybir
from concourse.bass_utils import run_bass_kernel_spmd

F32 = mybir.dt.float32
BF16 = mybir.dt.bfloat16
I32 = mybir.dt.int32
AF = mybir.ActivationFunctionType
ALU = mybir.AluOpType
AX = mybir.AxisListType

SEM_ROLL = 30000
import os as _os
_SEQ_DEBUG = bool(_os.environ.get('KSEQ'))

D = 1024
KC = 8
BATCH = 2
SEQ = 8192
DEPTH = 4
NCORES = 8
T = 2048
NT = 16
EPS = 1e-6
PLE_DIM = 256
FFN_H = 2816
RET_H, RET_DK, RET_DV = 4, 256, 512
L_RANKS = 3
ATT_H, ATT_DH = 16, 64
DIL = ((128, 1), (512, 4), (2048, 16))
GM_W = 2048
GM_G = 8


class Buf:
    __slots__ = ("name", "w", "r", "dsem", "dcount")

    def __init__(self, name):
        self.name = name
        self.w = None
        self.r = []
        self.dsem = None
        self.dcount = 0


class Prog:
    ENGS = ("pe", "act", "dve", "pool", "sp")

    def __init__(self, nc):
        self.nc = nc
        self.stack = contextlib.ExitStack()
        self.ops = {e: [] for e in self.ENGS}
        self.sems = {}
        self.esem = {}
        self.ecount = {e: 0 for e in self.ENGS}
        self.eroll = {e: 0 for e in self.ENGS}
        self.known = {e: {} for e in self.ENGS}
        self.nsem = 0
        self.final_tokens = []
        self.chans = []
        self.free_sems = []
        for e in self.ENGS:
            self._new_esem(e)

    def sem(self, name):
        h = self.stack.enter_context(self.nc.semaphore(name))
        self.sems[name] = h
        self.nsem += 1
        return name

    def _new_esem(self, e):
        k = self.sem(f"s_{e}_{self.eroll[e]}")
        self.eroll[e] += 1
        self.esem[e] = k
        self.ecount[e] = 0

    def sbuf(self, name, shape, dtype):
        return self.stack.enter_context(self.nc.sbuf_tensor("sb_" + name, list(shape), dtype))

    def psum(self, name, shape, dtype):
        return self.stack.enter_context(self.nc.psum_tensor("ps_" + name, list(shape), dtype))

    def _deps(self, eng, reads, writes, skip_same_pe=False):
        waits = {}

        def need(tok):
            if tok is None:
                return
            k, v = tok
            if skip_same_pe and eng == "pe" and k == self.esem["pe"]:
                return
            if self.known[eng].get(k, 0) >= v:
                return
            if waits.get(k, 0) < v:
                waits[k] = v

        for b in reads:
            need(b.w)
        for b in writes:
            need(b.w)
            for t in b.r:
                need(t)
        for k, v in waits.items():
            self.known[eng][k] = v
        return list(waits.items())

    def _commit(self, tok, reads, writes):
        for b in writes:
            b.w = tok
            b.r = []
        for b in reads:
            b.r.append(tok)
            if len(b.r) > 48:
                best = {}
                for k, v in b.r:
                    if best.get(k, 0) < v:
                        best[k] = v
                b.r = list(best.items())

    def op(self, eng, fn, reads=(), writes=()):
        if self.ecount[eng] >= SEM_ROLL:
            self._new_esem(eng)
        waits = self._deps(eng, reads, writes, skip_same_pe=True)
        self.ecount[eng] += 1
        tok = (self.esem[eng], self.ecount[eng])
        self.ops[eng].append((fn, waits, (self.esem[eng], 1)))
        self._commit(tok, reads, writes)
        return tok

    def dma(self, q, out_ap, in_ap, reads=(), writes=(), chan=None, **kw):
        if chan is None:
            chan = writes[0] if writes else reads[0]
        if chan.dsem is None:
            self.free_sems.sort(key=lambda x: x[1])
            if self.free_sems and self.free_sems[0][1] < 16000:
                chan.dsem, chan.dcount = self.free_sems.pop(0)
            else:
                chan.dsem = self.sem(f"d_{chan.name}")
                chan.dcount = 0
            self.chans.append(chan)
        waits = self._deps(q, reads, writes)
        if chan.dcount > 0 and self.known[q].get(chan.dsem, 0) < chan.dcount:
            waits = [w for w in waits if w[0] != chan.dsem] + [(chan.dsem, chan.dcount)]
            self.known[q][chan.dsem] = chan.dcount
        chan.dcount += 16
        tok = (chan.dsem, chan.dcount)

        def fn(e, out_ap=out_ap, in_ap=in_ap, kw=kw):
            return e.dma_start(out=out_ap, in_=in_ap, **kw)

        self.ops[q].append((fn, waits, (chan.dsem, 16)))
        self._commit(tok, reads, writes)
        return tok

    def barrier(self):
        toks = {}
        for e in self.ENGS:
            if self.ecount[e] > 0:
                toks[self.esem[e]] = self.ecount[e]
        for c in self.chans:
            if c.dcount > 0:
                toks[c.dsem] = c.dcount
        for e in self.ENGS:
            waits = []
            for k, v in toks.items():
                if self.known[e].get(k, 0) < v:
                    waits.append((k, v))
                    self.known[e][k] = v
            if waits:
                self.ops[e].append((None, waits, None))

    def retire(self, bufs):
        for c in bufs:
            if c.dsem is not None:
                self.free_sems.append((c.dsem, c.dcount))
                self.chans.remove(c)
                c.dsem = None
                c.dcount = 0

    def finish_on(self, eng, toks):
        self.final_tokens.append((eng, list(toks)))

    def emit(self):
        nc = self.nc
        for eng, toks in self.final_tokens:
            best = {}
            for k, v in toks:
                if best.get(k, 0) < v:
                    best[k] = v
            self.ops[eng].append((None, list(best.items()), None))
        handles = {"pe": "tensor", "act": "scalar", "dve": "vector", "pool": "gpsimd", "sp": "sync"}
        with nc.Block() as block:
            for eng in self.ENGS:
                ops = self.ops[eng]
                if not ops:
                    continue

                def body(e, ops=ops):
                    for fn, waits, inc in ops:
                        for k, v in waits:
                            e.wait_ge(self.sems[k], v)
                        if fn is not None:
                            ins = fn(e)
                            ins.then_inc(self.sems[inc[0]], inc[1])

                getattr(block, handles[eng])(body)
        self.stack.close()

    def stats(self):
        d = {e: len(v) for e, v in self.ops.items()}
        d["nsem"] = self.nsem
        return d


class Builder:
    ST_BYTES = 84 * 1024

    def __init__(self, nc, dr):
        self.nc = nc
        self.dr = dr
        P = self.P = Prog(nc)
        self.aT = P.sbuf("aT", [128, KC, T], BF16)
        self.aT_b = [Buf(f"aT{t}") for t in range(NT)]
        self.big = P.sbuf("big", [128, NT * D], F32)
        self.acc = self.big[:].rearrange("p (t n) -> p t n", t=NT)
        self.acc_b = [Buf(f"acc{t}") for t in range(NT)]
        self.st = P.sbuf("st", [128, self.ST_BYTES // 2], BF16)
        self.ident = P.sbuf("ident", [128, 128], BF16)
        self.ident_b = Buf("ident")
        self.hring = P.sbuf("hring", [128, 3, D], F32)
        self.hr_b = [Buf("hr0"), Buf("hr1"), Buf("hr2")]
        self.hextra = []
        self.gpost = P.sbuf("gpost", [128, D], F32)
        self.gpost_b = Buf("gpost")
        self.gpre = P.sbuf("gpre", [128, D], F32)
        self.gpre_b = Buf("gpre")
        self.gpre2 = None
        self.gpre2_b = None
        self.abf_n = 2
        self.abf = P.sbuf("abf", [128, 2, D], BF16)
        self.abf_b = [Buf("abf0"), Buf("abf1")]
        self.junk = P.sbuf("junk", [128, D], BF16)
        self.junk_b = Buf("junk")
        self.stat = P.sbuf("stat", [128, 64], F32)
        self.stat_b = [Buf(f"stat{i}") for i in range(64)]
        self.stat_i = 0
        self.psum_all = P.psum("all", [128, 8, 512], F32)
        self.bank = [self.psum_all[:, i, :] for i in range(8)]
        self.bank_b = [Buf(f"bank{i}") for i in range(8)]
        self.hD_b = [Buf(f"hD{t}") for t in range(NT)]
        self.h_rd = None
        self.h_wr = None
        self.hcnt = 0
        P.dma("pool", self.ident[:], dr["ident"][:, :], writes=[self.ident_b])
        _orig_barrier = P.barrier

        def _barrier():
            _orig_barrier()
            P.retire([b for _, b in self.hextra])
            self.hextra = []
        P.barrier = _barrier
        self.cm05 = P.sbuf("cm05", [128, 1], F32)
        self.cm05_b = Buf("cm05")
        P.op("pool", lambda e: e.memset(self.cm05[:], -0.5), writes=[self.cm05_b])

    def st_view(self, off_bytes, shape, dtype):
        n = int(np.prod(shape))
        if dtype == BF16:
            a = self.st[:, off_bytes // 2: off_bytes // 2 + n]
            nb = n * 2
        else:
            a = self.st[:, off_bytes // 2: off_bytes // 2 + 2 * n].bitcast(F32)
            nb = n * 4
        assert off_bytes + nb <= self.ST_BYTES, (off_bytes, nb)
        if len(shape) == 2:
            v = a.rearrange("p (a b) -> p a b", a=shape[0])
        elif len(shape) == 3:
            v = a.rearrange("p (a b c) -> p a b c", a=shape[0], b=shape[1])
        else:
            v = a
        return v, off_bytes + nb

    def set_hextra(self, off_bytes, n):
        self.hextra = []
        for k in range(n):
            v, off_bytes = self.st_view(off_bytes, [1, D], F32)
            self.hextra.append((v[:, 0, :], Buf(f"hx{self.hcnt}_{k}")))

    def new_stat(self):
        i = self.stat_i % 64
        self.stat_i += 1
        return self.stat[:, i:i + 1], self.stat_b[i]

    def load_g(self, which, idx):
        tile, b = {"post": (self.gpost, self.gpost_b), "pre": (self.gpre, self.gpre_b), "pre2": (self.gpre2, self.gpre2_b)}[which]
        i, j = idx
        self.P.dma("sp", tile[:], self.dr["norm_g"][i, j, :].partition_broadcast(128), writes=[b])

    def rstd_of(self, src_ap, src_bufs, n):
        P = self.P
        ss, ss_b = self.new_stat()
        P.op("act", lambda e: e.activation(out=self.junk[:, 0:n], in_=src_ap, func=AF.Square, scale=float(n) ** -0.5, accum_out=ss),
             reads=list(src_bufs), writes=[self.junk_b, ss_b])
        r, r_b = self.new_stat()
        P.op("pool", lambda e: e.tensor_scalar(out=r, in0=ss, scalar1=EPS, scalar2=None, op0=ALU.add),
             reads=[ss_b], writes=[r_b])
        r2, r2_b = self.new_stat()
        P.op("pool", lambda e: e.tensor_tensor(out=r2, in0=r, in1=self.cm05[:], op=ALU.pow),
             reads=[r_b, self.cm05_b], writes=[r2_b])
        return r2, r2_b

    def h_load(self, t, src=None):
        P = self.P
        slots = [(self.hring[:, k, :], self.hr_b[k]) for k in range(3)] + list(self.hextra)
        hs, hb = slots[self.hcnt % len(slots)]
        self.hcnt += 1
        src = self.h_rd if src is None else src
        P.dma("sp", hs, src[t * 128:(t + 1) * 128, :], reads=[self.hD_b[t]], writes=[hb])
        return hs, hb

    def h_store(self, t, hs, hb, dst=None):
        dst = self.h_wr if dst is None else dst
        return self.P.dma("sp", dst[t * 128:(t + 1) * 128, :], hs, reads=[hb], writes=[self.hD_b[t]], chan=hb)

    def pre_norm(self, t, hs, hb, tbank=7):
        P = self.P
        r, r_b = self.rstd_of(hs, [hb], D)
        s = t % 2
        ab, ab_b = self.abf[:, s, :], self.abf_b[s]
        P.op("dve", lambda e: e.scalar_tensor_tensor(out=ab, in0=hs, scalar=r, in1=self.gpre[:], op0=ALU.mult, op1=ALU.mult),
             reads=[hb, r_b, self.gpre_b], writes=[ab_b])
        pst = self.bank[tbank][:].bitcast(BF16).rearrange("p (k n) -> p k n", k=KC)
        for kc in range(KC):
            P.op("pe", lambda e, kc=kc: e.transpose(out=pst[:, kc, :], in_=self.abf[:, s, kc * 128:(kc + 1) * 128], identity=self.ident[:]),
                 reads=[ab_b, self.ident_b], writes=[self.bank_b[tbank]])
        P.op("act", lambda e: e.copy(out=self.aT[:, :, t * 128:(t + 1) * 128], in_=pst),
             reads=[self.bank_b[tbank]], writes=[self.aT_b[t]])

    def boundary(self, t, src_ap, src_b, dst_ap, dst_b, do_pre=True, tbank=7):
        P = self.P
        r, r_b = self.rstd_of(src_ap, src_b, D)
        P.op("dve", lambda e: e.scalar_tensor_tensor(out=dst_ap, in0=src_ap, scalar=r, in1=self.gpost[:], op0=ALU.mult, op1=ALU.mult),
             reads=list(src_b) + [r_b, self.gpost_b], writes=[dst_b])
        hs, hb = self.h_load(t)
        P.op("pool", lambda e: e.tensor_tensor(out=hs, in0=hs, in1=dst_ap, op=ALU.add), reads=[dst_b], writes=[hb])
        tok = self.h_store(t, hs, hb)
        if do_pre:
            self.pre_norm(t, hs, hb, tbank=tbank)
        return tok

    def stage_barrier(self):
        self.P.barrier()
        self.hextra = []

    def first_norm(self):
        for t in range(NT):
            hs, hb = self.h_load(t)
            self.pre_norm(t, hs, hb)

    def wload(self, dst_ap, src_ap, buf):
        return self.P.dma("pool", dst_ap, src_ap, writes=[buf])

    def ffn(self, i):
        P = self.P
        P.barrier()
        w_in = self.dr["ffn_w_in"][i]
        w_out = self.dr["ffn_w_out"][i]
        off = 0
        winA, winA_b = [], []
        for s in range(3):
            v, off = self.st_view(off, [KC, 512], BF16)
            winA.append(v)
            winA_b.append(Buf(f"f{i}winA{s}"))
        woutB, woutB_b = [], []
        for s in range(2):
            v, off = self.st_view(off, [4, D], BF16)
            woutB.append(v)
            woutB_b.append(Buf(f"f{i}woutB{s}"))
        hidT, hidT_b = [], []
        for s in range(2):
            v, off = self.st_view(off, [4, T], BF16)
            hidT.append(v)
            hidT_b.append([[Buf(f"f{i}hid{s}_{jl}_{tg}") for tg in range(4)] for jl in range(4)])
        sil, sil_b = [], []
        for s in range(2):
            v, off = self.st_view(off, [1, 512], F32)
            sil.append(v)
            sil_b.append(Buf(f"f{i}sil{s}"))
        nchunk = FFN_H // 128
        ngrp = (nchunk + 3) // 4
        unit = 0
        for pi in range(nchunk // 2):
            ws, wb = winA[pi % 3], winA_b[pi % 3]
            self.wload(ws[:, :, 0:256], w_in[:, pi * 256:(pi + 1) * 256].rearrange("(kc p) n -> p kc n", p=128), wb)
            self.wload(ws[:, :, 256:512], w_in[:, FFN_H + pi * 256:FFN_H + (pi + 1) * 256].rearrange("(kc p) n -> p kc n", p=128), wb)
            for jj in range(2):
                j = 2 * pi + jj
                grp, jl = j // 4, j % 4
                hs_, hs_b = hidT[grp % 2], hidT_b[grp % 2]
                for tg in range(4):
                    bg, bu = 2 * (unit % 2), 2 * (unit % 2) + 1
                    sl, slb = sil[unit % 2], sil_b[unit % 2]
                    unit += 1
                    for (bk, coff) in ((bg, jj * 128), (bu, 256 + jj * 128)):
                        for kc in range(KC):
                            P.op("pe", lambda e, bk=bk, coff=coff, kc=kc, tg=tg, ws=ws: e.matmul(
                                self.bank[bk][:], lhsT=ws[:, kc, coff:coff + 128], rhs=self.aT[:, kc, tg * 512:(tg + 1) * 512],
                                start=(kc == 0), stop=(kc == KC - 1)),
                                reads=[wb] + self.aT_b[tg * 4:(tg + 1) * 4], writes=[self.bank_b[bk]])
                    P.op("act", lambda e, bg=bg, sl=sl: e.activation(out=sl[:, 0, :], in_=self.bank[bg][:], func=AF.Silu),
                         reads=[self.bank_b[bg]], writes=[slb])
                    P.op("dve", lambda e, bu=bu, sl=sl, hs_=hs_, jl=jl, tg=tg: e.tensor_tensor(
                        out=hs_[:, jl, tg * 512:(tg + 1) * 512], in0=sl[:, 0, :], in1=self.bank[bu][:], op=ALU.mult),
                        reads=[slb, self.bank_b[bu]], writes=[hs_b[jl][tg]])
                if jl == 3 or j == nchunk - 1:
                    n = jl + 1
                    wo, wob = woutB[grp % 2], woutB_b[grp % 2]
                    self.wload(wo[:, 0:n, :], w_out[grp * 512:grp * 512 + n * 128, :].rearrange("(jl p) n -> p jl n", p=128), wob)
                    for t in range(NT):
                        for half in range(2):
                            bk = 4 + 2 * (t % 2) + half
                            for q in range(n):
                                P.op("pe", lambda e, bk=bk, q=q, t=t, half=half, hs_=hs_, wo=wo, n=n: e.matmul(
                                    self.bank[bk][:], lhsT=hs_[:, q, t * 128:(t + 1) * 128], rhs=wo[:, q, half * 512:(half + 1) * 512],
                                    start=(q == 0), stop=(q == n - 1)),
                                    reads=[wob, hs_b[q][t // 4]], writes=[self.bank_b[bk]])
                            dst = self.acc[:, t, half * 512:(half + 1) * 512]
                            if grp == 0:
                                P.op("act", lambda e, bk=bk, dst=dst: e.copy(out=dst, in_=self.bank[bk][:]),
                                     reads=[self.bank_b[bk]], writes=[self.acc_b[t]])
                            else:
                                P.op("dve", lambda e, bk=bk, dst=dst: e.tensor_tensor(out=dst, in0=dst, in1=self.bank[bk][:], op=ALU.add),
                                     reads=[self.bank_b[bk]], writes=[self.acc_b[t]])
        assert grp == ngrp - 1

    def ple(self, i, do_pre=True):
        P = self.P
        P.barrier()
        off = 0
        Wg, off = self.st_view(off, [KC, D], BF16)
        Wg_b = Buf(f"p{i}Wg")
        Wp, off = self.st_view(off, [2, D], BF16)
        Wp_b = Buf(f"p{i}Wp")
        pb, pb_b, pT, pT_b, sg, sg_b = [], [], [], [], [], []
        for s in range(2):
            v, off = self.st_view(off, [1, 256], BF16)
            pb.append(v)
            pb_b.append(Buf(f"p{i}pb{s}"))
            v, off = self.st_view(off, [2, 128], BF16)
            pT.append(v)
            pT_b.append(Buf(f"p{i}pT{s}"))
            v, off = self.st_view(off, [2, 512], F32)
            sg.append(v)
            sg_b.append(Buf(f"p{i}sg{s}"))
        self.wload(Wg, self.dr["ple_w_gate"][i].rearrange("(kc p) n -> p kc n", p=128), Wg_b)
        self.wload(Wp, self.dr["ple_w_proj"][i].rearrange("(kc p) n -> p kc n", p=128), Wp_b)
        toks = []
        for t in range(NT):
            s = t % 2
            self.wload(pb[s][:, 0, :], self.dr["p"][i, t * 128:(t + 1) * 128, :], pb_b[s])
            ptb = 6
            pst = self.bank[ptb][:].bitcast(BF16)[:, 0:256].rearrange("p (k n) -> p k n", k=2)
            for kc in range(2):
                P.op("pe", lambda e, kc=kc, s=s, pst=pst: e.transpose(out=pst[:, kc, :], in_=pb[s][:, 0, kc * 128:(kc + 1) * 128], identity=self.ident[:]),
                     reads=[pb_b[s], self.ident_b], writes=[self.bank_b[ptb]])
            P.op("dve", lambda e, s=s, pst=pst: e.tensor_copy(out=pT[s], in_=pst), reads=[self.bank_b[ptb]], writes=[pT_b[s]])
            for half in range(2):
                u = 2 * t + half
                bG, bE = u % 3, 3 + u % 3
                for kc in range(KC):
                    P.op("pe", lambda e, kc=kc, half=half, bG=bG, t=t: e.matmul(
                        self.bank[bG][:], lhsT=self.aT[:, kc, t * 128:(t + 1) * 128], rhs=Wg[:, kc, half * 512:(half + 1) * 512],
                        start=(kc == 0), stop=(kc == KC - 1)), reads=[Wg_b, self.aT_b[t]], writes=[self.bank_b[bG]])
                for kc in range(2):
                    P.op("pe", lambda e, kc=kc, half=half, bE=bE, s=s: e.matmul(
                        self.bank[bE][:], lhsT=pT[s][:, kc, :], rhs=Wp[:, kc, half * 512:(half + 1) * 512],
                        start=(kc == 0), stop=(kc == 1)), reads=[Wp_b, pT_b[s]], writes=[self.bank_b[bE]])
                P.op("act", lambda e, half=half, bG=bG, s=s: e.activation(out=sg[s][:, half, :], in_=self.bank[bG][:], func=AF.Sigmoid),
                     reads=[self.bank_b[bG]], writes=[sg_b[s]])
                P.op("dve", lambda e, half=half, bE=bE, s=s: e.tensor_tensor(out=sg[s][:, half, :], in0=sg[s][:, half, :], in1=self.bank[bE][:], op=ALU.mult),
                     reads=[self.bank_b[bE]], writes=[sg_b[s]])
            hs, hb = self.h_load(t)
            P.op("pool", lambda e, hs=hs, s=s: e.tensor_tensor(out=hs, in0=hs, in1=sg[s].rearrange("p a b -> p (a b)"), op=ALU.add),
                 reads=[sg_b[s]], writes=[hb])
            toks.append(self.h_store(t, hs, hb))
            if do_pre:
                self.pre_norm(t, hs, hb)
        return toks

    def ffn_stage(self, i):
        self.load_g("post", (i, 3))
        self.load_g("pre", (i, 4))
        self.ffn(i)
        toks = []
        for t in range(NT):
            a = self.acc[:, t, :]
            toks.append(self.boundary(t, a, [self.acc_b[t]], a, self.acc_b[t]))
        return toks

    def ple_stage(self, i, last):
        if not last:
            self.load_g("pre", (i + 1, 0))
        return self.ple(i, do_pre=not last)


def _ret_emit(self, j, pre):
    P = self.P
    P.barrier()
    dr = self.dr
    w_in = dr["ret_w_in"][j]
    w_out = dr["ret_w_out"][j]
    off = 0
    W, off = self.st_view(off, [KC, 1536], BF16)
    W_b = {k: Buf(f"r{j}{pre}W{k}") for k in "qkvg"}
    wo, off = self.st_view(off, [4, D], BF16)
    wo_b = Buf(f"r{j}{pre}wo")
    cs, cs_b = [], []
    for s in range(2):
        v, off = self.st_view(off, [2, 512], F32)
        cs.append(v)
        cs_b.append(Buf(f"r{j}{pre}cs{s}"))
    qT, off = self.st_view(off, [2, 512], BF16)
    kT, off = self.st_view(off, [2, 512], BF16)
    qdT, off = self.st_view(off, [2, 512], BF16)
    qT_b, kT_b, qdT_b = Buf("qT"), Buf("kT"), Buf("qdT")
    rt, off = self.st_view(off, [4, 512], F32)
    rt_b = [Buf(f"rt{x}") for x in range(4)]
    qdec, off = self.st_view(off, [1, 512], F32)
    qdec_b = Buf("qdec")
    v_sb, sg, sm, kd, gy, gyT = [], [], [], [], [], []
    v_b, sg_b, sm_b, kd_b, gy_b, gyT_b = [], [], [], [], [], []
    for s in range(2):
        a, off = self.st_view(off, [1, 512], BF16); v_sb.append(a); v_b.append(Buf(f"v{s}"))
        a, off = self.st_view(off, [1, 512], BF16); sg.append(a); sg_b.append(Buf(f"sg{s}"))
        a, off = self.st_view(off, [1, 128], BF16); sm.append(a); sm_b.append(Buf(f"sm{s}"))
        a, off = self.st_view(off, [1, 256], BF16); kd.append(a); kd_b.append(Buf(f"kd{s}"))
        a, off = self.st_view(off, [1, 512], BF16); gy.append(a); gy_b.append(Buf(f"gy{s}"))
        a, off = self.st_view(off, [4, 128], BF16); gyT.append(a); gyT_b.append(Buf(f"gyT{s}"))
    maskT, off = self.st_view(off, [RET_H, 128], F32)
    maskT_b = Buf("maskT")
    kdec, off = self.st_view(off, [1, RET_H], F32)
    kdec_b = Buf("kdec")
    coef, off = self.st_view(off, [1, NCORES * RET_H], F32)
    coef_b = Buf("coef")
    state, off = self.st_view(off, [2, 512], F32)
    state_b = Buf("state")
    sbf, off = self.st_view(off, [2, 512], BF16)
    sbf_b = Buf("sbf")
    P.dma("sp", maskT, dr["ret_maskT"].rearrange("h m n -> m h n"), writes=[maskT_b])
    P.dma("sp", kdec[:, 0, :], dr["ret_kdec"][:, :], writes=[kdec_b])
    P.dma("sp", coef[:, 0, :], dr["ret_coef"][:, :], writes=[coef_b])
    bk = self.bank
    bb = self.bank_b
    b6 = bk[6][:]
    sT_ps = b6[:, 0:128]
    sT_pb = Buf("b6a")
    kdT_ps = b6[:, 128:256].bitcast(BF16).rearrange("p (a b) -> p a b", a=2)
    kdT_pb = Buf("b6b")
    gyT_ps = b6[:, 256:512].bitcast(BF16).rearrange("p (a b) -> p a b", a=4)
    gyT_pb = Buf("b6c")
    gam = [1.0 - 2.0 ** (-5.0 - h) for h in range(RET_H)]

    for hd in range(RET_H):
        cd = gam[hd] ** 128
        cols = {"q": (hd * 256, 256, 0), "k": (1024 + hd * 256, 256, 256), "v": (2048 + hd * 512, 512, 512), "g": (4096 + hd * 512, 512, 1024)}
        for k in ("k", "v") if pre else ("q", "k", "v", "g"):
            c0, n, o = cols[k]
            self.wload(W[:, :, o:o + n], w_in[:, c0:c0 + n].rearrange("(kc p) n -> p kc n", p=128), W_b[k])
        if not pre:
            self.wload(wo, w_out[hd * 512:(hd + 1) * 512, :].rearrange("(q p) n -> p q n", p=128), wo_b)
            P.dma("sp", qdec[:, 0, :], dr["ret_qdec"][hd], writes=[qdec_b])
        if pre:
            P.op("pool", lambda e: e.memset(state, 0.0), writes=[state_b])
        else:
            tmp = rt[:, 0:2, :]
            for r in range(NCORES):
                P.dma("sp", tmp, dr["L_all"][r, hd].rearrange("(fc p) n -> p fc n", p=128), writes=[rt_b[0], rt_b[1]])
                cf = coef[:, 0, r * RET_H + hd:r * RET_H + hd + 1]
                if r == 0:
                    P.op("dve", lambda e, cf=cf: e.tensor_scalar(out=state, in0=tmp, scalar1=cf, scalar2=None, op0=ALU.mult),
                         reads=[rt_b[0], rt_b[1], coef_b], writes=[state_b])
                else:
                    P.op("dve", lambda e, cf=cf: e.scalar_tensor_tensor(out=state, in0=tmp, scalar=cf, in1=state, op0=ALU.mult, op1=ALU.add),
                         reads=[rt_b[0], rt_b[1], coef_b], writes=[state_b])
        P.op("act", lambda e: e.copy(out=sbf, in_=state), reads=[state_b], writes=[sbf_b])

        for tg in range(4):
            cst, cst_b = cs[tg % 2], cs_b[tg % 2]
            P.dma("sp", cst[:, 0, :], dr["rope_cos"][:, tg * 512:(tg + 1) * 512], writes=[cst_b])
            P.dma("sp", cst[:, 1, :], dr["rope_sin"][:, tg * 512:(tg + 1) * 512], writes=[cst_b])
            for which in ("k",) if pre else ("q", "k"):
                o = cols[which][2]
                b0 = 0 if which == "q" else 2
                for fc in range(2):
                    for kc in range(KC):
                        P.op("pe", lambda e, fc=fc, kc=kc, o=o, b0=b0, tg=tg: e.matmul(
                            bk[b0 + fc][:], lhsT=W[:, kc, o + fc * 128:o + (fc + 1) * 128], rhs=self.aT[:, kc, tg * 512:(tg + 1) * 512],
                            start=(kc == 0), stop=(kc == KC - 1)),
                            reads=[W_b[which]] + self.aT_b[tg * 4:(tg + 1) * 4], writes=[bb[b0 + fc]])
                dstT, dst_b = (qT, qT_b) if which == "q" else (kT, kT_b)
                t1, t2 = bk[b0][:], bk[b0 + 1][:]
                P.op("dve", lambda e, t1=t1, cst=cst: e.tensor_tensor(out=rt[:, 0, :], in0=t1, in1=cst[:, 0, :], op=ALU.mult),
                     reads=[bb[b0], cst_b], writes=[rt_b[0]])
                P.op("dve", lambda e, t2=t2, cst=cst: e.tensor_tensor(out=rt[:, 1, :], in0=t2, in1=cst[:, 1, :], op=ALU.mult),
                     reads=[bb[b0 + 1], cst_b], writes=[rt_b[1]])
                P.op("dve", lambda e, t2=t2, cst=cst: e.tensor_tensor(out=rt[:, 2, :], in0=t2, in1=cst[:, 0, :], op=ALU.mult),
                     reads=[bb[b0 + 1], cst_b], writes=[rt_b[2]])
                P.op("dve", lambda e, t1=t1, cst=cst: e.tensor_tensor(out=rt[:, 3, :], in0=t1, in1=cst[:, 1, :], op=ALU.mult),
                     reads=[bb[b0], cst_b], writes=[rt_b[3]])
                P.op("pool", lambda e, dstT=dstT: e.tensor_tensor(out=dstT[:, 0, :], in0=rt[:, 0, :], in1=rt[:, 1, :], op=ALU.subtract),
                     reads=[rt_b[0], rt_b[1]], writes=[dst_b])
                P.op("pool", lambda e, dstT=dstT: e.tensor_tensor(out=dstT[:, 1, :], in0=rt[:, 2, :], in1=rt[:, 3, :], op=ALU.add),
                     reads=[rt_b[2], rt_b[3]], writes=[dst_b])
                if which == "q":
                    for fc in range(2):
                        P.op("pool", lambda e, fc=fc: e.tensor_tensor(out=qdT[:, fc, :], in0=qT[:, fc, :], in1=qdec[:, 0, :], op=ALU.mult),
                             reads=[qT_b, qdec_b], writes=[qdT_b])
            for cl in range(4):
                c = tg * 4 + cl
                s = c % 2
                tsl = slice(cl * 128, (cl + 1) * 128)
                csl = slice(c * 128, (c + 1) * 128)
                for kc in range(KC):
                    P.op("pe", lambda e, kc=kc, csl=csl: e.matmul(bk[4][:], lhsT=self.aT[:, kc, csl], rhs=W[:, kc, 512:1024],
                                                               start=(kc == 0), stop=(kc == KC - 1)),
                         reads=[W_b["v"], self.aT_b[c]], writes=[bb[4]])
                P.op("act", lambda e, s=s: e.copy(out=v_sb[s][:, 0, :], in_=bk[4][:]), reads=[bb[4]], writes=[v_b[s]])
                if not pre:
                    for kc in range(KC):
                        P.op("pe", lambda e, kc=kc, csl=csl: e.matmul(bk[5][:], lhsT=self.aT[:, kc, csl], rhs=W[:, kc, 1024:1536],
                                                                   start=(kc == 0), stop=(kc == KC - 1)),
                             reads=[W_b["g"], self.aT_b[c]], writes=[bb[5]])
                    P.op("act", lambda e, s=s: e.activation(out=sg[s][:, 0, :], in_=bk[5][:], func=AF.Silu), reads=[bb[5]], writes=[sg_b[s]])
                    for fc in range(2):
                        P.op("pe", lambda e, fc=fc, tsl=tsl: e.matmul(sT_ps, lhsT=kT[:, fc, tsl], rhs=qT[:, fc, tsl], start=(fc == 0), stop=(fc == 1)),
                             reads=[kT_b, qT_b], writes=[sT_pb])
                    P.op("dve", lambda e, s=s, hd=hd: e.tensor_tensor(out=sm[s][:, 0, :], in0=sT_ps, in1=maskT[:, hd, :], op=ALU.mult),
                         reads=[sT_pb, maskT_b], writes=[sm_b[s]])
                for fc in range(2):
                    P.op("pe", lambda e, fc=fc, tsl=tsl: e.transpose(out=kdT_ps[:, fc, :], in_=kT[:, fc, tsl], identity=self.ident[:]),
                         reads=[kT_b, self.ident_b], writes=[kdT_pb])
                P.op("dve", lambda e, s=s, hd=hd: e.tensor_scalar(out=kd[s][:, 0, :], in0=kdT_ps.rearrange("p a b -> p (a b)"),
                                                                 scalar1=kdec[:, 0, hd:hd + 1], scalar2=None, op0=ALU.mult),
                     reads=[kdT_pb, kdec_b], writes=[kd_b[s]])
                if not pre:
                    P.op("pe", lambda e, s=s: e.matmul(bk[7][:], lhsT=sm[s][:, 0, :], rhs=v_sb[s][:, 0, :], start=True, stop=False),
                         reads=[sm_b[s], v_b[s]], writes=[bb[7]])
                    for fc in range(2):
                        P.op("pe", lambda e, fc=fc, tsl=tsl: e.matmul(bk[7][:], lhsT=qdT[:, fc, tsl], rhs=sbf[:, fc, :], start=False, stop=(fc == 1)),
                             reads=[qdT_b, sbf_b], writes=[bb[7]])
                for fc in range(2):
                    P.op("pe", lambda e, fc=fc, s=s: e.matmul(bk[fc][:], lhsT=kd[s][:, 0, fc * 128:(fc + 1) * 128], rhs=v_sb[s][:, 0, :], start=True, stop=True),
                         reads=[kd_b[s], v_b[s]], writes=[bb[fc]])
                for fc in range(2):
                    P.op("dve", lambda e, fc=fc, cd=cd: e.scalar_tensor_tensor(out=state[:, fc, :], in0=state[:, fc, :], scalar=float(cd), in1=bk[fc][:],
                                                                           op0=ALU.mult, op1=ALU.add),
                         reads=[bb[fc]], writes=[state_b])
                P.op("act", lambda e: e.copy(out=sbf, in_=state), reads=[state_b], writes=[sbf_b])
                if not pre:
                    r, r_b = self.rstd_of(bk[7][:], [bb[7]], 512)
                    P.op("dve", lambda e, s=s, r=r: e.scalar_tensor_tensor(out=gy[s][:, 0, :], in0=bk[7][:], scalar=r, in1=sg[s][:, 0, :],
                                                                         op0=ALU.mult, op1=ALU.mult),
                         reads=[bb[7], r_b, sg_b[s]], writes=[gy_b[s]])
                    for q in range(4):
                        P.op("pe", lambda e, q=q, s=s: e.transpose(out=gyT_ps[:, q, :], in_=gy[s][:, 0, q * 128:(q + 1) * 128], identity=self.ident[:]),
                             reads=[gy_b[s], self.ident_b], writes=[gyT_pb])
                    P.op("act", lambda e, s=s: e.copy(out=gyT[s], in_=gyT_ps), reads=[gyT_pb], writes=[gyT_b[s]])
                    for half in range(2):
                        for q in range(4):
                            P.op("pe", lambda e, q=q, s=s, half=half: e.matmul(bk[2 + half][:], lhsT=gyT[s][:, q, :], rhs=wo[:, q, half * 512:(half + 1) * 512],
                                                                              start=(q == 0), stop=(q == 3)),
                                 reads=[gyT_b[s], wo_b], writes=[bb[2 + half]])
                        dst = self.acc[:, c, half * 512:(half + 1) * 512]
                        if hd == 0:
                            P.op("act", lambda e, dst=dst, half=half: e.copy(out=dst, in_=bk[2 + half][:]), reads=[bb[2 + half]], writes=[self.acc_b[c]])
                        else:
                            P.op("dve", lambda e, dst=dst, half=half: e.tensor_tensor(out=dst, in0=dst, in1=bk[2 + half][:], op=ALU.add),
                                 reads=[bb[2 + half]], writes=[self.acc_b[c]])
        if pre:
            P.dma("sp", dr["L_out"][hd].rearrange("(fc p) n -> p fc n", p=128), state, reads=[state_b], chan=state_b)
    if pre:
        return [(state_b.dsem, state_b.dcount)]
    return []


Builder.ret_emit = _ret_emit


def _ret_emit2(self, j, pre, boundary=False):
    P = self.P
    P.barrier()
    dr = self.dr
    w_in = dr["ret_w_in"][j]
    w_out = dr["ret_w_out"][j]
    off = 0
    W, off = self.st_view(off, [KC, 1536], BF16)
    W_b = {k: Buf(f"r{j}{pre}W{k}") for k in "qkvg"}
    wo, off = self.st_view(off, [4, D], BF16)
    wo_b = Buf(f"r{j}{pre}wo")
    cs, cs_b = [], []
    for s in range(2):
        v, off = self.st_view(off, [2, 512], F32)
        cs.append(v)
        cs_b.append(Buf(f"r{j}{pre}cs{s}"))
    qT, off = self.st_view(off, [2, 512], BF16)
    kT, off = self.st_view(off, [2, 512], BF16)
    qdT, off = self.st_view(off, [2, 512], BF16)
    qT_b, kT_b, qdT_b = Buf("qT"), Buf("kT"), Buf("qdT")
    rt, off = self.st_view(off, [4, 512], F32)
    rt_b = [Buf(f"rt{x}") for x in range(4)]
    qdec, off = self.st_view(off, [1, 512], F32)
    qdec_b = Buf("qdec")
    v_sb, sg, sm, kd, gy, gyT = [], [], [], [], [], []
    v_b, sg_b, sm_b, kd_b, gy_b, gyT_b = [], [], [], [], [], []
    for s in range(2):
        a, off = self.st_view(off, [1, 512], BF16); v_sb.append(a); v_b.append(Buf(f"v{s}"))
        a, off = self.st_view(off, [1, 512], BF16); sg.append(a); sg_b.append(Buf(f"sg{s}"))
        a, off = self.st_view(off, [1, 128], BF16); sm.append(a); sm_b.append(Buf(f"sm{s}"))
        a, off = self.st_view(off, [1, 256], BF16); kd.append(a); kd_b.append(Buf(f"kd{s}"))
        a, off = self.st_view(off, [1, 512], BF16); gy.append(a); gy_b.append(Buf(f"gy{s}"))
        a, off = self.st_view(off, [4, 128], BF16); gyT.append(a); gyT_b.append(Buf(f"gyT{s}"))
    maskT, off = self.st_view(off, [RET_H, 128], F32)
    maskT_b = Buf("maskT")
    kdec, off = self.st_view(off, [1, RET_H], F32)
    kdec_b = Buf("kdec")
    coef, off = self.st_view(off, [1, L_RANKS * RET_H], F32)
    coef_b = Buf("coef")
    state, off = self.st_view(off, [2, 512], F32)
    state_b = Buf("state")
    sbf, off = self.st_view(off, [2, 512], BF16)
    sbf_b = Buf("sbf")
    self.set_hextra(off, max(0, (self.ST_BYTES - off) // 4096))
    P.dma("sp", maskT, dr["ret_maskT"].rearrange("h m n -> m h n"), writes=[maskT_b])
    P.dma("sp", kdec[:, 0, :], dr["ret_kdec"][:, :], writes=[kdec_b])
    P.dma("sp", coef[:, 0, :], dr["ret_coef"][:, :], writes=[coef_b])
    bk = self.bank
    bb = self.bank_b
    b6 = bk[6][:]
    sT_ps = b6[:, 0:128]
    sT_pb = bb[6]
    kdT_ps = b6[:, 128:256].bitcast(BF16).rearrange("p (a b) -> p a b", a=2)
    kdT_pb = bb[6]
    gyT_ps = bk[7][:, 256:512].bitcast(BF16).rearrange("p (a b) -> p a b", a=4)
    gyT_pb = bb[7]
    gam = [1.0 - 2.0 ** (-5.0 - h) for h in range(RET_H)]
    btoks = []

    for hd in range(RET_H):
        cd = gam[hd] ** 128
        cols = {"q": (hd * 256, 256, 0), "k": (1024 + hd * 256, 256, 256), "v": (2048 + hd * 512, 512, 512), "g": (4096 + hd * 512, 512, 1024)}
        for k in ("k", "v") if pre else ("q", "k", "v", "g"):
            c0, n, o = cols[k]
            self.wload(W[:, :, o:o + n], w_in[:, c0:c0 + n].rearrange("(kc p) n -> p kc n", p=128), W_b[k])
        if not pre:
            self.wload(wo, w_out[hd * 512:(hd + 1) * 512, :].rearrange("(q p) n -> p q n", p=128), wo_b)
            P.dma("sp", qdec[:, 0, :], dr["ret_qdec"][hd], writes=[qdec_b])
        if pre:
            P.op("pool", lambda e: e.memset(state, 0.0), writes=[state_b])
        else:
            tmp = rt[:, 0:2, :]
            for r in range(L_RANKS):
                P.dma("sp", tmp, dr["L_all"][r, hd].rearrange("(fc p) n -> p fc n", p=128), writes=[rt_b[0], rt_b[1]])
                cf = coef[:, 0, r * RET_H + hd:r * RET_H + hd + 1]
                if r == 0:
                    P.op("dve", lambda e, cf=cf: e.tensor_scalar(out=state, in0=tmp, scalar1=cf, scalar2=None, op0=ALU.mult),
                         reads=[rt_b[0], rt_b[1], coef_b], writes=[state_b])
                else:
                    P.op("dve", lambda e, cf=cf: e.scalar_tensor_tensor(out=state, in0=tmp, scalar=cf, in1=state, op0=ALU.mult, op1=ALU.add),
                         reads=[rt_b[0], rt_b[1], coef_b], writes=[state_b])
        P.op("act", lambda e: e.copy(out=sbf, in_=state), reads=[state_b], writes=[sbf_b])

        def qk_proj(tg, hd=hd, cols=cols):
            cst, cst_b = cs[tg % 2], cs_b[tg % 2]
            P.dma("sp", cst[:, 0, :], dr["rope_cos"][:, tg * 512:(tg + 1) * 512], writes=[cst_b])
            P.dma("sp", cst[:, 1, :], dr["rope_sin"][:, tg * 512:(tg + 1) * 512], writes=[cst_b])
            for which in ("k",) if pre else ("q", "k"):
                o = cols[which][2]
                b0 = 0 if which == "q" else 2
                for fc in range(2):
                    for kc in range(KC):
                        P.op("pe", lambda e, fc=fc, kc=kc, o=o, b0=b0, tg=tg: e.matmul(
                            bk[b0 + fc][:], lhsT=W[:, kc, o + fc * 128:o + (fc + 1) * 128], rhs=self.aT[:, kc, tg * 512:(tg + 1) * 512],
                            start=(kc == 0), stop=(kc == KC - 1)),
                            reads=[W_b[which]] + self.aT_b[tg * 4:(tg + 1) * 4], writes=[bb[b0 + fc]])
                dstT, dst_b = (qT, qT_b) if which == "q" else (kT, kT_b)
                t1, t2 = bk[b0][:], bk[b0 + 1][:]
                P.op("dve", lambda e, t1=t1, cst=cst: e.tensor_tensor(out=rt[:, 0, :], in0=t1, in1=cst[:, 0, :], op=ALU.mult),
                     reads=[bb[b0], cst_b], writes=[rt_b[0]])
                P.op("dve", lambda e, t2=t2, cst=cst: e.tensor_tensor(out=rt[:, 1, :], in0=t2, in1=cst[:, 1, :], op=ALU.mult),
                     reads=[bb[b0 + 1], cst_b], writes=[rt_b[1]])
                P.op("dve", lambda e, t2=t2, cst=cst: e.tensor_tensor(out=rt[:, 2, :], in0=t2, in1=cst[:, 0, :], op=ALU.mult),
                     reads=[bb[b0 + 1], cst_b], writes=[rt_b[2]])
                P.op("dve", lambda e, t1=t1, cst=cst: e.tensor_tensor(out=rt[:, 3, :], in0=t1, in1=cst[:, 1, :], op=ALU.mult),
                     reads=[bb[b0], cst_b], writes=[rt_b[3]])
                P.op("pool", lambda e, dstT=dstT: e.tensor_tensor(out=dstT[:, 0, :], in0=rt[:, 0, :], in1=rt[:, 1, :], op=ALU.subtract),
                     reads=[rt_b[0], rt_b[1]], writes=[dst_b])
                P.op("pool", lambda e, dstT=dstT: e.tensor_tensor(out=dstT[:, 1, :], in0=rt[:, 2, :], in1=rt[:, 3, :], op=ALU.add),
                     reads=[rt_b[2], rt_b[3]], writes=[dst_b])
                if which == "q":
                    for fc in range(2):
                        P.op("pool", lambda e, fc=fc: e.tensor_tensor(out=qdT[:, fc, :], in0=qT[:, fc, :], in1=qdec[:, 0, :], op=ALU.mult),
                             reads=[qT_b, qdec_b], writes=[qdT_b])
        def ph0(c, hd=hd, cd=cd, qk_proj=qk_proj):
            if True:
                tg, cl = c // 4, c % 4
                if cl == 0:
                    qk_proj(tg)
                s = c % 2
                tsl = slice(cl * 128, (cl + 1) * 128)
                csl = slice(c * 128, (c + 1) * 128)
                for kc in range(KC):
                    P.op("pe", lambda e, kc=kc, csl=csl: e.matmul(bk[4][:], lhsT=self.aT[:, kc, csl], rhs=W[:, kc, 512:1024],
                                                               start=(kc == 0), stop=(kc == KC - 1)),
                         reads=[W_b["v"], self.aT_b[c]], writes=[bb[4]])
                P.op("act", lambda e, s=s: e.copy(out=v_sb[s][:, 0, :], in_=bk[4][:]), reads=[bb[4]], writes=[v_b[s]])
                if not pre:
                    for kc in range(KC):
                        P.op("pe", lambda e, kc=kc, csl=csl: e.matmul(bk[5][:], lhsT=self.aT[:, kc, csl], rhs=W[:, kc, 1024:1536],
                                                                   start=(kc == 0), stop=(kc == KC - 1)),
                             reads=[W_b["g"], self.aT_b[c]], writes=[bb[5]])
                    P.op("act", lambda e, s=s: e.activation(out=sg[s][:, 0, :], in_=bk[5][:], func=AF.Silu), reads=[bb[5]], writes=[sg_b[s]])
                    for fc in range(2):
                        P.op("pe", lambda e, fc=fc, tsl=tsl: e.matmul(sT_ps, lhsT=kT[:, fc, tsl], rhs=qT[:, fc, tsl], start=(fc == 0), stop=(fc == 1)),
                             reads=[kT_b, qT_b], writes=[sT_pb])
                    P.op("dve", lambda e, s=s, hd=hd: e.tensor_tensor(out=sm[s][:, 0, :], in0=sT_ps, in1=maskT[:, hd, :], op=ALU.mult),
                         reads=[sT_pb, maskT_b], writes=[sm_b[s]])
                for fc in range(2):
                    P.op("pe", lambda e, fc=fc, tsl=tsl: e.transpose(out=kdT_ps[:, fc, :], in_=kT[:, fc, tsl], identity=self.ident[:]),
                         reads=[kT_b, self.ident_b], writes=[kdT_pb])
                P.op("dve", lambda e, s=s, hd=hd: e.tensor_scalar(out=kd[s][:, 0, :], in0=kdT_ps.rearrange("p a b -> p (a b)"),
                                                                 scalar1=kdec[:, 0, hd:hd + 1], scalar2=None, op0=ALU.mult),
                     reads=[kdT_pb, kdec_b], writes=[kd_b[s]])
        def ph1(c, hd=hd, cd=cd):
            if True:
                tg, cl = c // 4, c % 4
                s = c % 2
                tsl = slice(cl * 128, (cl + 1) * 128)
                csl = slice(c * 128, (c + 1) * 128)
                if not pre:
                    P.op("pe", lambda e, s=s: e.matmul(bk[7][:], lhsT=sm[s][:, 0, :], rhs=v_sb[s][:, 0, :], start=True, stop=False),
                         reads=[sm_b[s], v_b[s]], writes=[bb[7]])
                    for fc in range(2):
                        P.op("pe", lambda e, fc=fc, tsl=tsl: e.matmul(bk[7][:], lhsT=qdT[:, fc, tsl], rhs=sbf[:, fc, :], start=False, stop=(fc == 1)),
                             reads=[qdT_b, sbf_b], writes=[bb[7]])
                for fc in range(2):
                    P.op("pe", lambda e, fc=fc, s=s: e.matmul(bk[fc][:], lhsT=kd[s][:, 0, fc * 128:(fc + 1) * 128], rhs=v_sb[s][:, 0, :], start=True, stop=True),
                         reads=[kd_b[s], v_b[s]], writes=[bb[fc]])
                for fc in range(2):
                    P.op("dve", lambda e, fc=fc, cd=cd: e.scalar_tensor_tensor(out=state[:, fc, :], in0=state[:, fc, :], scalar=float(cd), in1=bk[fc][:],
                                                                           op0=ALU.mult, op1=ALU.add),
                         reads=[bb[fc]], writes=[state_b])
                if not pre:
                    P.op("act", lambda e: e.copy(out=sbf, in_=state), reads=[state_b], writes=[sbf_b])
                if not pre:
                    r, r_b = self.rstd_of(bk[7][:], [bb[7]], 512)
                    P.op("dve", lambda e, s=s, r=r: e.scalar_tensor_tensor(out=gy[s][:, 0, :], in0=bk[7][:], scalar=r, in1=sg[s][:, 0, :],
                                                                         op0=ALU.mult, op1=ALU.mult),
                         reads=[bb[7], r_b, sg_b[s]], writes=[gy_b[s]])
        def ph2(c, hd=hd, cd=cd):
            if True:
                tg, cl = c // 4, c % 4
                s = c % 2
                if not pre:
                    for q in range(4):
                        P.op("pe", lambda e, q=q, s=s: e.transpose(out=gyT_ps[:, q, :], in_=gy[s][:, 0, q * 128:(q + 1) * 128], identity=self.ident[:]),
                             reads=[gy_b[s], self.ident_b], writes=[gyT_pb])
                    P.op("act", lambda e, s=s: e.copy(out=gyT[s], in_=gyT_ps), reads=[gyT_pb], writes=[gyT_b[s]])
                    for half in range(2):
                        for q in range(4):
                            P.op("pe", lambda e, q=q, s=s, half=half: e.matmul(bk[2 + half][:], lhsT=gyT[s][:, q, :], rhs=wo[:, q, half * 512:(half + 1) * 512],
                                                                              start=(q == 0), stop=(q == 3)),
                                 reads=[gyT_b[s], wo_b], writes=[bb[2 + half]])
                        dst = self.acc[:, c, half * 512:(half + 1) * 512]
                        if hd == 0:
                            P.op("act", lambda e, dst=dst, half=half: e.copy(out=dst, in_=bk[2 + half][:]), reads=[bb[2 + half]], writes=[self.acc_b[c]])
                        else:
                            P.op("dve", lambda e, dst=dst, half=half: e.tensor_tensor(out=dst, in0=dst, in1=bk[2 + half][:], op=ALU.add),
                                 reads=[bb[2 + half]], writes=[self.acc_b[c]])
        phases = [ph0, ph1] if pre else [ph0, ph1, ph2]
        if boundary and (not pre) and hd == RET_H - 1:
            phases = phases + self.boundary_phases_f(lambda t: (self.acc[:, t, :], [self.acc_b[t]]), lambda t: (self.acc[:, t, :], self.acc_b[t]),
                                                   toks=btoks, tbanks=(5,), merge_b=True)
        pipeline(NT, phases)
        if pre:
            P.dma("sp", dr["L_out"][hd].rearrange("(fc p) n -> p fc n", p=128), state, reads=[state_b], chan=state_b)
    if pre:
        return [(state_b.dsem, state_b.dcount)]
    return btoks


Builder.ret_emit = _ret_emit2


def _mixer_boundaries(self, i):
    toks = []
    for t in range(NT):
        a = self.acc[:, t, :]
        toks.append(self.boundary(t, a, [self.acc_b[t]], a, self.acc_b[t]))
    return toks


Builder.mixer_boundaries = _mixer_boundaries


def ret_tables():
    gam = np.array([1.0 - 2.0 ** (-5.0 - h) for h in range(RET_H)], np.float64)
    idx = np.arange(128, dtype=np.float64)
    diff = idx[None, :] - idx[:, None]
    maskT = np.where(diff >= 0, gam[:, None, None] ** np.maximum(diff, 0.0), 0.0) / 16.0
    qdec = gam[:, None] ** (np.tile(idx, 4) + 1.0)[None, :]
    qdec = np.broadcast_to(qdec[:, None, :], (RET_H, 128, 512))
    kdec = (gam[None, :] ** (127.0 - idx[:, None])) / 16.0
    return (np.ascontiguousarray(maskT, dtype=np.float32), np.ascontiguousarray(qdec, dtype=np.float32),
            np.ascontiguousarray(kdec, dtype=np.float32))


def ret_coef(core):
    gam = np.array([1.0 - 2.0 ** (-5.0 - h) for h in range(RET_H)], np.float64)
    b, s = core // 4, core % 4
    c = np.zeros((L_RANKS, RET_H), np.float64)
    for rs in range(L_RANKS):
        if rs < s:
            c[rs] = gam ** (float(T) * (s - 1 - rs))
    return np.ascontiguousarray(np.broadcast_to(c.reshape(1, -1), (128, L_RANKS * RET_H)), dtype=np.float32)


def rope_tables(core):
    s = core % 4
    inv_freq = (np.float32(10000.0) ** (-(np.arange(0, RET_DK, 2, dtype=np.float32) / np.float32(RET_DK)))).astype(np.float32)
    pos = (np.arange(T, dtype=np.float32) + np.float32(s * T)).astype(np.float32)
    ang = (inv_freq[:, None] * pos[None, :]).astype(np.float32)
    return np.cos(ang).astype(np.float32), np.sin(ang).astype(np.float32)


def _gmlp_stage(self, i, j):
    P = self.P
    dr = self.dr
    P.barrier()
    bigb = self.big[:].bitcast(BF16)
    Wv = bigb[:, 0:16384].rearrange("p (k n) -> p k n", k=KC)
    Wu = bigb[:, 16384:32768].rearrange("p (k n) -> p k n", k=KC)
    Wv_b, Wu_b = Buf("gWv"), Buf("gWu")
    off = 0
    Wo, off = self.st_view(off, [16, D], BF16)
    Wo_b = Buf("gWo")
    wsT, off = self.st_view(off, [GM_G, 128], BF16)
    wsT_b = Buf("gwsT")
    gtab, off = self.st_view(off, [1, GM_W], F32)
    btab, off = self.st_view(off, [1, GM_W], F32)
    gtab_b, btab_b = Buf("ggtab"), Buf("gbtab")
    bsT, off = self.st_view(off, [1, GM_G], F32)
    bsT_b = Buf("gbsT")
    cmask, off = self.st_view(off, [1, 128], F32)
    cmask_b = Buf("gcmask")
    off_vf = off
    vf, off = self.st_view(off, [1, GM_W], F32)
    vf_b = Buf("gvf")
    vn, off = self.st_view(off, [1, GM_W], BF16)
    vn_b = Buf("gvn")
    ssb, off = self.st_view(off, [1, GM_W], F32)
    ssb_b = Buf("gssb")
    us, off = self.st_view(off, [1, GM_W], BF16)
    us_b = Buf("gus")
    usT, off = self.st_view(off, [16, 128], BF16)
    usT_b = Buf("gusT")
    gu, gu_b = [], []
    for s in range(2):
        a, off = self.st_view(off, [1, 512], F32)
        gu.append(a)
        gu_b.append(Buf(f"ggu{s}"))
    wsl, off2 = self.st_view(off_vf, [GM_G, 128], BF16)
    wsl_b = vf_b
    w_in = dr["gm_w_in"][j]
    for cb in range(4):
        self.P.dma("pool", Wv[:, :, cb * 512:(cb + 1) * 512], w_in[:, GM_W + cb * 512:GM_W + (cb + 1) * 512].rearrange("(kc p) n -> p kc n", p=128),
                   writes=[Wv_b] + self.acc_b[0:8])
    for cb in range(4):
        self.P.dma("pool", Wu[:, :, cb * 512:(cb + 1) * 512], w_in[:, cb * 512:(cb + 1) * 512].rearrange("(kc p) n -> p kc n", p=128),
                   writes=[Wu_b] + self.acc_b[8:16])
    for q in range(4):
        self.wload(Wo[:, q * 4:(q + 1) * 4, :], dr["gm_w_out"][j][q * 512:(q + 1) * 512, :].rearrange("(kc p) n -> p kc n", p=128), Wo_b)
    self.wload(wsl, dr["gm_w_s"][j].rearrange("g n m -> n g m"), wsl_b)
    P.dma("sp", gtab[:, 0, :], dr["gm_ln_g"][j, :].partition_broadcast(128), writes=[gtab_b])
    P.dma("sp", btab[:, 0, :], dr["gm_ln_b"][j, :].partition_broadcast(128), writes=[btab_b])
    P.dma("sp", bsT[:, 0, :], dr["gm_b_s"][j].rearrange("g n -> n g"), writes=[bsT_b], allow_slow_non_contiguous=True)
    P.dma("sp", cmask[:, 0, :], dr["causalT"][:, :], writes=[cmask_b])
    bk, bb = self.bank, self.bank_b
    tps = bk[0][:].bitcast(BF16)[:, 0:1024].rearrange("p (g n) -> p g n", g=GM_G)
    for g in range(GM_G):
        P.op("pe", lambda e, g=g: e.transpose(out=tps[:, g, :], in_=wsl[:, g, :], identity=self.ident[:]),
             reads=[wsl_b, self.ident_b], writes=[bb[0]])
    for g in range(GM_G):
        P.op("dve", lambda e, g=g: e.tensor_tensor(out=wsT[:, g, :], in0=tps[:, g, :], in1=cmask[:, 0, :], op=ALU.mult),
             reads=[bb[0], cmask_b], writes=[wsT_b])
    toks = []
    for t in range(NT):
        tsl = slice(t * 128, (t + 1) * 128)
        sv, sv_b = [], []
        for cb in range(4):
            for kc in range(KC):
                P.op("pe", lambda e, cb=cb, kc=kc, tsl=tsl: e.matmul(bk[cb][:], lhsT=self.aT[:, kc, tsl], rhs=Wv[:, kc, cb * 512:(cb + 1) * 512],
                                                                     start=(kc == 0), stop=(kc == KC - 1)),
                     reads=[Wv_b, self.aT_b[t]], writes=[bb[cb]])
            a, a_b = self.new_stat()
            sv.append(a)
            sv_b.append(a_b)
            P.op("act", lambda e, cb=cb, a=a: e.activation(out=vf[:, 0, cb * 512:(cb + 1) * 512], in_=bk[cb][:], func=AF.Gelu, accum_out=a),
                 reads=[bb[cb]], writes=[vf_b, a_b])
        sq, sq_b = [], []
        for hh in range(2):
            a, a_b = self.new_stat()
            sq.append(a)
            sq_b.append(a_b)
            P.op("act", lambda e, hh=hh, a=a: e.activation(out=self.junk[:, :], in_=vf[:, 0, hh * 1024:(hh + 1) * 1024], func=AF.Square, accum_out=a),
                 reads=[vf_b], writes=[self.junk_b, a_b])
        m1, m1_b = self.new_stat()
        P.op("pool", lambda e, sv=sv, m1=m1: e.tensor_tensor(out=m1, in0=sv[0], in1=sv[1], op=ALU.add), reads=[sv_b[0], sv_b[1]], writes=[m1_b])
        m2, m2_b = self.new_stat()
        P.op("pool", lambda e, sv=sv, m2=m2: e.tensor_tensor(out=m2, in0=sv[2], in1=sv[3], op=ALU.add), reads=[sv_b[2], sv_b[3]], writes=[m2_b])
        mu, mu_b = self.new_stat()
        P.op("pool", lambda e, m1=m1, m2=m2, mu=mu: e.tensor_tensor(out=mu, in0=m1, in1=m2, op=ALU.add), reads=[m1_b, m2_b], writes=[mu_b])
        P.op("pool", lambda e, mu=mu: e.tensor_scalar(out=mu, in0=mu, scalar1=1.0 / GM_W, scalar2=None, op0=ALU.mult), reads=[], writes=[mu_b])
        e2, e2_b = self.new_stat()
        P.op("pool", lambda e, sq=sq, e2=e2: e.tensor_tensor(out=e2, in0=sq[0], in1=sq[1], op=ALU.add), reads=[sq_b[0], sq_b[1]], writes=[e2_b])
        mm, mm_b = self.new_stat()
        P.op("pool", lambda e, mu=mu, mm=mm: e.tensor_tensor(out=mm, in0=mu, in1=mu, op=ALU.mult), reads=[mu_b], writes=[mm_b])
        var, var_b = self.new_stat()
        P.op("pool", lambda e, e2=e2, var=var: e.tensor_scalar(out=var, in0=e2, scalar1=1.0 / GM_W, scalar2=EPS, op0=ALU.mult, op1=ALU.add),
             reads=[e2_b], writes=[var_b])
        P.op("pool", lambda e, var=var, mm=mm: e.tensor_tensor(out=var, in0=var, in1=mm, op=ALU.subtract), reads=[mm_b], writes=[var_b])
        rs, rs_b = self.new_stat()
        P.op("pool", lambda e, var=var, rs=rs: e.tensor_tensor(out=rs, in0=var, in1=self.cm05[:], op=ALU.pow), reads=[var_b, self.cm05_b], writes=[rs_b])
        P.op("dve", lambda e, mu=mu, rs=rs: e.tensor_scalar(out=vf[:, 0, :], in0=vf[:, 0, :], scalar1=mu, scalar2=rs, op0=ALU.subtract, op1=ALU.mult),
             reads=[mu_b, rs_b], writes=[vf_b])
        P.op("pool", lambda e: e.tensor_tensor(out=vf[:, 0, :], in0=vf[:, 0, :], in1=gtab[:, 0, :], op=ALU.mult), reads=[gtab_b], writes=[vf_b])
        P.op("dve", lambda e: e.tensor_tensor(out=vn[:, 0, :], in0=vf[:, 0, :], in1=btab[:, 0, :], op=ALU.add), reads=[vf_b, btab_b], writes=[vn_b])
        for g in range(GM_G):
            bkx = 4 + g // 2
            P.op("pe", lambda e, g=g, bkx=bkx: e.matmul(bk[bkx][:, (g % 2) * 256:(g % 2 + 1) * 256], lhsT=wsT[:, g, :], rhs=vn[:, 0, g * 256:(g + 1) * 256],
                                                        start=True, stop=True),
                 reads=[wsT_b, vn_b], writes=[bb[bkx]])
        for g in range(GM_G):
            bkx = 4 + g // 2
            P.op("dve", lambda e, g=g, bkx=bkx: e.tensor_scalar(out=ssb[:, 0, g * 256:(g + 1) * 256], in0=bk[bkx][:, (g % 2) * 256:(g % 2 + 1) * 256],
                                                                scalar1=bsT[:, 0, g:g + 1], scalar2=None, op0=ALU.add),
                 reads=[bb[bkx], bsT_b], writes=[ssb_b])
        for cb in range(4):
            for kc in range(KC):
                P.op("pe", lambda e, cb=cb, kc=kc, tsl=tsl: e.matmul(bk[cb][:], lhsT=self.aT[:, kc, tsl], rhs=Wu[:, kc, cb * 512:(cb + 1) * 512],
                                                                     start=(kc == 0), stop=(kc == KC - 1)),
                     reads=[Wu_b, self.aT_b[t]], writes=[bb[cb]])
            s = cb % 2
            P.op("act", lambda e, cb=cb, s=s: e.activation(out=gu[s][:, 0, :], in_=bk[cb][:], func=AF.Gelu), reads=[bb[cb]], writes=[gu_b[s]])
            P.op("dve", lambda e, cb=cb, s=s: e.tensor_tensor(out=us[:, 0, cb * 512:(cb + 1) * 512], in0=gu[s][:, 0, :], in1=ssb[:, 0, cb * 512:(cb + 1) * 512], op=ALU.mult),
                 reads=[gu_b[s], ssb_b], writes=[us_b])
        for half in range(2):
            tp = bk[4 + half][:].bitcast(BF16).rearrange("p (k n) -> p k n", k=8)
            for q in range(8):
                kc = half * 8 + q
                P.op("pe", lambda e, tp=tp, q=q, kc=kc: e.transpose(out=tp[:, q, :], in_=us[:, 0, kc * 128:(kc + 1) * 128], identity=self.ident[:]),
                     reads=[us_b, self.ident_b], writes=[bb[4 + half]])
            P.op("act", lambda e, tp=tp, half=half: e.copy(out=usT[:, half * 8:(half + 1) * 8, :], in_=tp), reads=[bb[4 + half]], writes=[usT_b])
        for half in range(2):
            for kc in range(16):
                P.op("pe", lambda e, half=half, kc=kc: e.matmul(bk[6 + half][:], lhsT=usT[:, kc, :], rhs=Wo[:, kc, half * 512:(half + 1) * 512],
                                                                start=(kc == 0), stop=(kc == 15)),
                     reads=[usT_b, Wo_b], writes=[bb[6 + half]])
        src = self.psum_all[:, 6:8, :].rearrange("p a b -> p (a b)")
        toks.append(self.boundary(t, src, [bb[6], bb[7]], vf[:, 0, 0:D], vf_b, tbank=5))
    for t in range(NT):
        src_b = Wv_b if t < 8 else Wu_b
        self.acc_b[t].r = list(self.acc_b[t].r) + list(src_b.r) + ([src_b.w] if src_b.w else [])
    return toks


Builder.gmlp_stage = _gmlp_stage


def att_tables(rel_bias):
    p = np.arange(128)[:, None, None]
    kb = np.arange(2)[None, :, None]
    q = np.arange(128)[None, None, :]
    dist = 128 * (1 - kb) + q - p
    mask01 = ((dist >= 0) & (dist <= 128)).astype(np.float32).reshape(128, 256)
    dcl = np.maximum(dist, 0)
    out = np.zeros((3, ATT_H, 128, 256), np.float32)
    for g, (w, r) in enumerate(DIL):
        dd = (dcl * r).astype(np.int32)
        df = np.maximum(dd, 1).astype(np.float32)
        large = 16 + (np.log(df / np.float32(16)) / np.float32(math.log(2048 / 16)) * np.float32(16)).astype(np.int32)
        large = np.minimum(large, 31)
        bucket = np.where(dd < 16, dd, large).reshape(128, 256)
        tab = rel_bias[:, g * ATT_H:(g + 1) * ATT_H]
        out[g] = np.transpose(tab[bucket], (2, 0, 1))
    return out, mask01


def att_prevmask(core, mask01):
    m = mask01.copy()
    if core % 4 == 0:
        m[:, 0:128] = 0.0
    return m


def _att_kv_pass(self, j):
    P = self.P
    dr = self.dr
    P.barrier()
    w_in = dr["attn_w_in"][j]
    off = 0
    ring, ring_b = [], []
    for s in range(3):
        v, off = self.st_view(off, [KC, 512], BF16)
        ring.append(v)
        ring_b.append(Buf(f"akv_w{s}"))
    kst, kst_b = [], []
    for s in range(2):
        v, off = self.st_view(off, [1, T], BF16)
        kst.append(v)
        kst_b.append(Buf(f"akv_k{s}"))
    vst, vst_b = [], []
    for s in range(2):
        v, off = self.st_view(off, [1, D], BF16)
        vst.append(v)
        vst_b.append(Buf(f"akv_v{s}"))
    bk, bb = self.bank, self.bank_b
    self.KT_b = [[Buf(f"KT{g}_{fb}") for fb in range(8)] for g in range(3)]
    self.V_b = [[Buf(f"V{g}_{blk}") for blk in range(NT)] for g in range(3)]
    ri = 0
    ui = 0
    for g, (w, r) in enumerate(DIL):
        nbc = 16 // r
        for hf in range(2):
            ws, wb = ring[ri % 3], ring_b[ri % 3]
            ri += 1
            c0 = g * 3072 + 1024 + hf * 512
            self.wload(ws, w_in[:, c0:c0 + 512].rearrange("(kc p) n -> p kc n", p=128), wb)
            for fq in range(4):
                fb = hf * 4 + fq
                ks, ksb = kst[fb % 2], kst_b[fb % 2]
                for tg in range(4):
                    b = ui % 4
                    ui += 1
                    for kc in range(KC):
                        P.op("pe", lambda e, b=b, kc=kc, fq=fq, tg=tg, ws=ws: e.matmul(
                            bk[b][:], lhsT=ws[:, kc, fq * 128:(fq + 1) * 128], rhs=self.aT[:, kc, tg * 512:(tg + 1) * 512],
                            start=(kc == 0), stop=(kc == KC - 1)), reads=[wb] + self.aT_b[tg * 4:(tg + 1) * 4], writes=[bb[b]])
                    jn = 512 // r
                    dst = ks[:, 0, :].rearrange("p (c j) -> p c j", c=r)[:, :, tg * jn:(tg + 1) * jn]
                    src = bk[b][:].rearrange("p (j c) -> p c j", c=r)
                    eng = "act" if (ui % 2 == 0) else "dve"
                    if eng == "act":
                        P.op("act", lambda e, dst=dst, src=src: e.copy(out=dst, in_=src), reads=[bb[b]], writes=[ksb])
                    else:
                        P.op("dve", lambda e, dst=dst, src=src: e.tensor_copy(out=dst, in_=src), reads=[bb[b]], writes=[ksb])
                P.dma("sp", dr["KT_own"][g, fb * 128:(fb + 1) * 128, :], ks[:, 0, :], reads=[ksb], writes=[self.KT_b[g][fb]], chan=ksb)
        wsl = []
        for hf in range(2):
            ws, wb = ring[ri % 3], ring_b[ri % 3]
            ri += 1
            c0 = g * 3072 + 2048 + hf * 512
            self.wload(ws, w_in[:, c0:c0 + 512].rearrange("(kc p) n -> p kc n", p=128), wb)
            wsl.append((ws, wb))
        for blk in range(NT):
            c, n = blk // nbc, blk % nbc
            start = n * 128 * r + c
            vs, vsb = vst[blk % 2], vst_b[blk % 2]
            toks_t = sorted(set((start + i * r) // 128 for i in range(128)))
            for hf in range(2):
                ws, wb = wsl[hf]
                b = ui % 4
                ui += 1
                for kc in range(KC):
                    P.op("pe", lambda e, b=b, kc=kc, ws=ws, start=start, r=r: e.matmul(
                        bk[b][:], lhsT=self.aT[:, kc, start:start + 127 * r + 1:r], rhs=ws[:, kc, :],
                        start=(kc == 0), stop=(kc == KC - 1)), reads=[wb] + [self.aT_b[t] for t in toks_t], writes=[bb[b]])
                if hf == 0:
                    P.op("act", lambda e, b=b, vs=vs: e.copy(out=vs[:, 0, 0:512], in_=bk[b][:]), reads=[bb[b]], writes=[vsb])
                else:
                    P.op("dve", lambda e, b=b, vs=vs: e.tensor_copy(out=vs[:, 0, 512:1024], in_=bk[b][:]), reads=[bb[b]], writes=[vsb])
            P.dma("sp", dr["V_own"][g, blk * 128:(blk + 1) * 128, :], vs[:, 0, :], reads=[vsb], writes=[self.V_b[g][blk]], chan=vsb)


Builder.att_kv_pass = _att_kv_pass


def _att_main(self, i, j):
    P = self.P
    dr = self.dr
    P.barrier()
    w_in = dr["attn_w_in"][j]
    off = 0
    Wq, off = self.st_view(off, [KC, 384], BF16)
    Wq_b = Buf("aWq")
    QT, off = self.st_view(off, [1, T], BF16)
    QT_b = Buf("aQT")
    KT, off = self.st_view(off, [1, T], BF16)
    KT_sb = Buf("aKT")
    KH, off = self.st_view(off, [1, T], BF16)
    KH_b = Buf("aKH")
    Vt, off = self.st_view(off, [NT, 128], BF16)
    Vt_b = Buf("aVt")
    VH, off = self.st_view(off, [NT, 128], BF16)
    VH_b = Buf("aVH")
    Nacc, off = self.st_view(off, [2, T], F32)
    Dacc, off = self.st_view(off, [2, T], F32)
    ND_b = [Buf("aND0"), Buf("aND1")]
    bst, off = self.st_view(off, [2, 256], F32)
    bst_b = Buf("abst")
    Et, off = self.st_view(off, [2, 256], F32)
    Et0, off = self.st_view(off, [2, 256], F32)
    Et_b = Buf("aEt")
    m01, off = self.st_view(off, [1, 256], F32)
    m01p, off = self.st_view(off, [1, 256], F32)
    m_b = Buf("am01")
    ones, off = self.st_view(off, [1, 64], BF16)
    ones_b = Buf("aones")
    ex, ex_b, pt, pt_b = [], [], [], []
    for s in range(2):
        a, off = self.st_view(off, [1, 256], F32); ex.append(a); ex_b.append(Buf(f"aex{s}"))
        a, off = self.st_view(off, [2, 128], BF16); pt.append(a); pt_b.append(Buf(f"apt{s}"))
    rc, off = self.st_view(off, [1, T], F32)
    rc_b = Buf("arc")
    bk, bb = self.bank, self.bank_b
    if getattr(self, "KT_b", None) is None:
        self.KT_b = [[Buf(f"KT{g}_{fb}") for fb in range(8)] for g in range(3)]
        self.V_b = [[Buf(f"V{g}_{blk}") for blk in range(NT)] for g in range(3)]
    oT = self.big[:].bitcast(BF16).rearrange("p (h t) -> p h t", h=ATT_H)
    oT_b = [Buf(f"aoT{h}") for h in range(ATT_H)]
    P.dma("sp", m01[:, 0, :], dr["att_mask01"][:, :], writes=[m_b])
    P.dma("sp", m01p[:, 0, :], dr["att_prevmask"][:, :], writes=[m_b])
    P.op("pool", lambda e: e.memset(ones, 1.0), writes=[ones_b])
    sT_ps = [self.psum_all[:, 2 + s // 2, (s % 2) * 256:(s % 2 + 1) * 256].rearrange("p (a b) -> p a b", a=2) for s in range(4)]
    sT_pb = [Buf(f"asT{s}") for s in range(4)]
    nd_ps = [self.psum_all[0:64, 4, s * 256:(s + 1) * 256].rearrange("p (a b) -> p a b", a=2) for s in range(2)]
    nd_pb = [Buf(f"and{s}") for s in range(2)]
    ui = 0
    si = 0
    for hp in range(8):
        for g in range(3):
            c0 = g * 3072 + hp * 128
            self.wload(Wq[:, :, g * 128:(g + 1) * 128], w_in[:, c0:c0 + 128].rearrange("(kc p) n -> p kc n", p=128), Wq_b)
        for g, (w, r) in enumerate(DIL):
            nbc = 16 // r
            P.dma("sp", KT[:, 0, :], dr["KT_own"][g, hp * 128:(hp + 1) * 128, :], reads=self.KT_b[g], writes=[KT_sb])
            for c in range(r):
                p0 = c * (T // r) + (nbc - 1) * 128
                P.dma("sp", KH[:, 0, c * 128:(c + 1) * 128], dr["KT_prev"][g, hp * 128:(hp + 1) * 128, p0:p0 + 128], writes=[KH_b])
            P.dma("sp", Vt, dr["V_own"][g, :, hp * 128:(hp + 1) * 128].rearrange("(b p) f -> p b f", p=128), reads=self.V_b[g], writes=[Vt_b])
            for c in range(r):
                p0 = c * (T // r) + (nbc - 1) * 128
                P.dma("sp", VH[:, c, :], dr["V_prev"][g, p0:p0 + 128, hp * 128:(hp + 1) * 128], writes=[VH_b])
            P.dma("sp", bst, dr["att_bias"][g, 2 * hp:2 * hp + 2].rearrange("h p n -> p h n"), writes=[bst_b])
            P.op("act", lambda e: e.activation(out=bst, in_=bst, func=AF.Exp), reads=[], writes=[bst_b])
            for hh in range(2):
                P.op("dve", lambda e, hh=hh: e.tensor_tensor(out=Et[:, hh, :], in0=bst[:, hh, :], in1=m01[:, 0, :], op=ALU.mult),
                     reads=[bst_b, m_b], writes=[Et_b])
                P.op("dve", lambda e, hh=hh: e.tensor_tensor(out=Et0[:, hh, :], in0=bst[:, hh, :], in1=m01p[:, 0, :], op=ALU.mult),
                     reads=[bst_b, m_b], writes=[Et_b])
            for tg in range(4):
                b = ui % 2
                ui += 1
                for kc in range(KC):
                    P.op("pe", lambda e, b=b, kc=kc, g=g, tg=tg: e.matmul(
                        bk[b][:], lhsT=Wq[:, kc, g * 128:(g + 1) * 128], rhs=self.aT[:, kc, tg * 512:(tg + 1) * 512],
                        start=(kc == 0), stop=(kc == KC - 1)), reads=[Wq_b] + self.aT_b[tg * 4:(tg + 1) * 4], writes=[bb[b]])
                jn = 512 // r
                dst = QT[:, 0, :].rearrange("p (c j) -> p c j", c=r)[:, :, tg * jn:(tg + 1) * jn]
                src = bk[b][:].rearrange("p (j c) -> p c j", c=r)
                P.op("act", lambda e, dst=dst, src=src: e.copy(out=dst, in_=src), reads=[bb[b]], writes=[QT_b])
            for blk in range(NT):
                c, n = blk // nbc, blk % nbc
                qs = slice(blk * 128, (blk + 1) * 128)
                first = (n == 0)
                for hh in range(2):
                    ps = slice(hh * 64, (hh + 1) * 64)
                    s4 = si % 4
                    s2 = si % 2
                    si += 1
                    kprev = KH[ps, 0, c * 128:(c + 1) * 128] if first else KT[ps, 0, (blk - 1) * 128:blk * 128]
                    P.op("pe", lambda e, s4=s4, kprev=kprev, ps=ps, qs=qs: e.matmul(sT_ps[s4][:, 0, :], lhsT=kprev, rhs=QT[ps, 0, qs], start=True, stop=True),
                         reads=[KH_b if first else KT_sb, QT_b], writes=[sT_pb[s4]])
                    P.op("pe", lambda e, s4=s4, ps=ps, qs=qs: e.matmul(sT_ps[s4][:, 1, :], lhsT=KT[ps, 0, qs], rhs=QT[ps, 0, qs], start=True, stop=True),
                         reads=[KT_sb, QT_b], writes=[sT_pb[s4]])
                    P.op("act", lambda e, s4=s4, s2=s2: e.activation(out=ex[s2][:, 0, :], in_=sT_ps[s4].rearrange("p a b -> p (a b)"), func=AF.Exp, scale=0.125),
                         reads=[sT_pb[s4]], writes=[ex_b[s2]])
                    tab = Et0 if first else Et
                    P.op("dve", lambda e, s2=s2, tab=tab, hh=hh: e.tensor_tensor(out=pt[s2].rearrange("p a b -> p (a b)"), in0=ex[s2][:, 0, :], in1=tab[:, hh, :], op=ALU.mult),
                         reads=[ex_b[s2], Et_b], writes=[pt_b[s2]])
                    vprev = VH[:, c, ps] if first else Vt[:, blk - 1, ps]
                    P.op("pe", lambda e, s2=s2, vprev=vprev: e.matmul(nd_ps[s2][:, 0, :], lhsT=vprev, rhs=pt[s2][:, 0, :], start=True, stop=False),
                         reads=[VH_b if first else Vt_b, pt_b[s2]], writes=[nd_pb[s2]])
                    P.op("pe", lambda e, s2=s2, blk=blk, ps=ps: e.matmul(nd_ps[s2][:, 0, :], lhsT=Vt[:, blk, ps], rhs=pt[s2][:, 1, :], start=False, stop=True),
                         reads=[Vt_b, pt_b[s2]], writes=[nd_pb[s2]])
                    P.op("pe", lambda e, s2=s2: e.matmul(nd_ps[s2][:, 1, :], lhsT=ones[:, 0, :], rhs=pt[s2][:, 0, :], start=True, stop=False),
                         reads=[ones_b, pt_b[s2]], writes=[nd_pb[s2]])
                    P.op("pe", lambda e, s2=s2: e.matmul(nd_ps[s2][:, 1, :], lhsT=ones[:, 0, :], rhs=pt[s2][:, 1, :], start=False, stop=True),
                         reads=[ones_b, pt_b[s2]], writes=[nd_pb[s2]])
                    t0 = n * 128 * r + c
                    tsl = slice(t0, t0 + 127 * r + 1, r)
                    if g == 0:
                        P.op("act", lambda e, s2=s2, hh=hh, tsl=tsl: e.copy(out=Nacc[0:64, hh, tsl], in_=nd_ps[s2][:, 0, :]), reads=[nd_pb[s2]], writes=[ND_b[hh]])
                        P.op("dve", lambda e, s2=s2, hh=hh, tsl=tsl: e.tensor_copy(out=Dacc[0:64, hh, tsl], in_=nd_ps[s2][:, 1, :]), reads=[nd_pb[s2]], writes=[ND_b[hh]])
                    else:
                        P.op("dve", lambda e, s2=s2, hh=hh, tsl=tsl: e.tensor_tensor(out=Nacc[0:64, hh, tsl], in0=Nacc[0:64, hh, tsl], in1=nd_ps[s2][:, 0, :], op=ALU.add),
                             reads=[nd_pb[s2]], writes=[ND_b[hh]])
                        P.op("dve", lambda e, s2=s2, hh=hh, tsl=tsl: e.tensor_tensor(out=Dacc[0:64, hh, tsl], in0=Dacc[0:64, hh, tsl], in1=nd_ps[s2][:, 1, :], op=ALU.add),
                             reads=[nd_pb[s2]], writes=[ND_b[hh]])
        for hh in range(2):
            h = 2 * hp + hh
            P.op("dve", lambda e, hh=hh: e.reciprocal(out=rc[0:64, 0, :], in_=Dacc[0:64, hh, :]), reads=[ND_b[hh]], writes=[rc_b])
            P.op("pool", lambda e, hh=hh, h=h: e.tensor_tensor(out=oT[0:64, h, :], in0=Nacc[0:64, hh, :], in1=rc[0:64, 0, :], op=ALU.mult),
                 reads=[ND_b[hh], rc_b], writes=[oT_b[h]] + self.acc_b)
    P.barrier()
    off = 0
    Wo, off = self.st_view(off, [ATT_H, D], BF16)
    Wo_b = Buf("aWo")
    nrm, off = self.st_view(off, [2, D], F32)
    nrm_b = [Buf("anrm0"), Buf("anrm1")]
    self.wload(Wo[0:64], dr["attn_w_out"][j].rearrange("(h p) n -> p h n", p=64), Wo_b)
    toks = []
    for t in range(NT):
        tsl = slice(t * 128, (t + 1) * 128)
        for half in range(2):
            for h in range(ATT_H):
                P.op("pe", lambda e, h=h, half=half, tsl=tsl: e.matmul(bk[6 + half][:], lhsT=oT[0:64, h, tsl], rhs=Wo[0:64, h, half * 512:(half + 1) * 512],
                                                                        start=(h == 0), stop=(h == ATT_H - 1)),
                     reads=[oT_b[h], Wo_b], writes=[bb[6 + half]])
        src = self.psum_all[:, 6:8, :].rearrange("p a b -> p (a b)")
        toks.append(self.boundary(t, src, [bb[6], bb[7]], nrm[:, t % 2, :], nrm_b[t % 2], tbank=5))
    for t in range(NT):
        for h in range(ATT_H):
            self.acc_b[t].r = list(self.acc_b[t].r) + list(oT_b[h].r)
    return toks


Builder.att_main = _att_main


W_SHAPES = {
    "norm_g": [DEPTH, 5, D], "ret_w_in": [2, D, 6144], "ret_w_out": [2, 2048, D],
    "attn_w_in": [1, D, 9216], "attn_w_out": [1, D, D],
    "gm_w_in": [1, D, 4096], "gm_ln_g": [1, 2048], "gm_ln_b": [1, 2048], "gm_w_s": [1, 8, 128, 128], "gm_b_s": [1, 8, 128],
    "gm_w_out": [1, 2048, D], "ffn_w_in": [DEPTH, D, 2 * FFN_H], "ffn_w_out": [DEPTH, FFN_H, D],
    "ple_w_proj": [DEPTH, PLE_DIM, D], "ple_w_gate": [DEPTH, D, D],
}
T_SHAPES = {
    "ident": [128, 128], "causalT": [128, 128],
    "ret_maskT": [RET_H, 128, 128], "ret_qdec": [RET_H, 128, 512], "ret_kdec": [128, RET_H], "ret_coef": [128, L_RANKS * RET_H],
    "rope_cos": [128, T], "rope_sin": [128, T],
    "att_bias": [3, ATT_H, 128, 256], "att_mask01": [128, 256], "att_prevmask": [128, 256],
}
RET_IN = ["ret_w_in", "ret_w_out", "ret_maskT", "ret_qdec", "ret_kdec", "ret_coef", "rope_cos", "rope_sin"]
FFN_IN = ["ffn_w_in", "ffn_w_out", "ple_w_proj", "ple_w_gate", "p"]
ATT_IN = ["attn_w_in", "attn_w_out", "att_bias", "att_mask01", "att_prevmask"]
GM_IN = ["gm_w_in", "gm_ln_g", "gm_ln_b", "gm_w_s", "gm_b_s", "gm_w_out", "causalT"]
SEG_IN = {
    1: ["hin", "norm_g", "ident"] + RET_IN,
    2: ["hin", "norm_g", "ident", "L_all"] + RET_IN + FFN_IN + ["attn_w_in"],
    3: ["hin", "norm_g", "ident", "KT_own", "V_own", "KT_prev", "V_prev"] + ATT_IN + FFN_IN + GM_IN + RET_IN,
    4: ["hin", "norm_g", "ident", "L_all"] + RET_IN + FFN_IN,
}
SEG_OUT = {1: ["L_out"], 2: ["hout", "KT_own", "V_own"], 3: ["hout", "L_out"], 4: ["hout"]}
IO_SHAPES = {
    "hin": ([T, D], F32), "hout": ([T, D], F32), "p": ([DEPTH, T, PLE_DIM], F32),
    "L_all": ([L_RANKS, RET_H, RET_DK, RET_DV], F32), "L_out": ([RET_H, RET_DK, RET_DV], F32),
    "KT_own": ([3, D, T], BF16), "V_own": ([3, T, D], BF16), "KT_prev": ([3, D, T], BF16), "V_prev": ([3, T, D], BF16),
}


def build_segment(seg):
    nc = bass.Bass("TRN2", target_bir_lowering=False)
    dr = {}
    for name in dict.fromkeys(SEG_IN[seg]):
        if name in SEG_OUT[seg]:
            continue
        if name in W_SHAPES:
            shape, dt = W_SHAPES[name], F32
        elif name in T_SHAPES:
            shape, dt = T_SHAPES[name], F32
        else:
            shape, dt = IO_SHAPES[name]
        dr[name] = nc.dram_tensor(name, list(shape), dt, kind="ExternalInput").ap()
    for name in SEG_OUT[seg]:
        shape, dt = IO_SHAPES[name]
        dr[name] = nc.dram_tensor(name, list(shape), dt, kind="ExternalOutput").ap()
    hD = nc.dram_tensor("hD", [T, D], F32, kind="Internal").ap()
    B = Builder(nc, dr)
    P = B.P
    fin = []
    if seg == 1:
        B.h_rd = dr["hin"]
        B.load_g("pre", (0, 0))
        B.first_norm()
        fin += B.ret_emit(0, True)
    elif seg == 2:
        B.h_rd, B.h_wr = dr["hin"], hD
        B.load_g("pre", (0, 0))
        B.first_norm()
        B.load_g("post", (0, 1))
        B.load_g("pre", (0, 2))
        B.ret_emit(0, False, boundary=True)
        B.h_rd = hD
        B.ffn_stage(0)
        B.h_wr = dr["hout"]
        fin += B.ple_stage(0, last=False)
        B.att_kv_pass(0)
        for g in range(3):
            for b in B.KT_b[g] + B.V_b[g]:
                fin.append(b.w)
    elif seg == 3:
        B.h_rd, B.h_wr = dr["hin"], hD
        B.load_g("pre", (1, 0))
        B.first_norm()
        B.load_g("post", (1, 1))
        B.load_g("pre", (1, 2))
        B.att_main(1, 0)
        B.h_rd = hD
        B.ffn_stage(1)
        B.ple_stage(1, last=False)
        B.load_g("post", (2, 1))
        B.load_g("pre", (2, 2))
        B.gmlp_stage(2, 0)
        B.ffn_stage(2)
        B.h_wr = dr["hout"]
        fin += B.ple_stage(2, last=False)
        fin += B.ret_emit(1, True)
    elif seg == 4:
        B.h_rd, B.h_wr = dr["hin"], hD
        B.load_g("pre", (3, 0))
        B.first_norm()
        B.load_g("post", (3, 1))
        B.load_g("pre", (3, 2))
        B.ret_emit(1, False, boundary=True)
        B.h_rd = hD
        B.ffn_stage(3)
        B.h_wr = dr["hout"]
        fin += B.ple_stage(3, last=True)
    P.finish_on("sp", [t for t in fin if t is not None])
    P.emit()
    return nc


_PROG_CACHE = {}


def _prog(seg):
    if seg not in _PROG_CACHE:
        _PROG_CACHE[seg] = build_segment(seg)
    return _PROG_CACHE[seg]


def _run(seg, maps):
    nc = build_segment(seg)
    res = run_bass_kernel_spmd(nc, maps, core_ids=list(range(NCORES)))
    return res.results


def kernel(x, p, norm_g, ret_w_in, ret_w_out, attn_w_in, attn_w_out, rel_bias,
           gm_w_in, gm_ln_g, gm_ln_b, gm_w_s, gm_b_s, gm_w_out,
           ffn_w_in, ffn_w_out, ple_w_proj, ple_w_gate):
    import ml_dtypes
    f32 = lambda a: np.ascontiguousarray(np.asarray(a), dtype=np.float32)
    wts = {"norm_g": f32(norm_g), "ret_w_in": f32(ret_w_in), "ret_w_out": f32(ret_w_out), "attn_w_in": f32(attn_w_in),
           "attn_w_out": f32(attn_w_out), "gm_w_in": f32(gm_w_in), "gm_ln_g": f32(gm_ln_g), "gm_ln_b": f32(gm_ln_b),
           "gm_w_s": f32(gm_w_s), "gm_b_s": f32(gm_b_s), "gm_w_out": f32(gm_w_out), "ffn_w_in": f32(ffn_w_in),
           "ffn_w_out": f32(ffn_w_out), "ple_w_proj": f32(ple_w_proj), "ple_w_gate": f32(ple_w_gate)}
    x = f32(x)
    p = f32(p)
    maskT, qdec, kdec = ret_tables()
    bias_full, mask01 = att_tables(f32(rel_bias))
    idx = np.arange(128)
    consts = {"ident": np.eye(128, dtype=np.float32), "causalT": (idx[None, :] >= idx[:, None]).astype(np.float32),
              "ret_maskT": maskT, "ret_qdec": qdec, "ret_kdec": kdec, "att_bias": bias_full, "att_mask01": mask01}
    per_core = []
    for c in range(NCORES):
        b, s = c // 4, c % 4
        cos, sin = rope_tables(c)
        per_core.append({"ret_coef": ret_coef(c), "rope_cos": cos, "rope_sin": sin, "att_prevmask": att_prevmask(c, mask01),
                         "p": np.ascontiguousarray(p[:, b, s * T:(s + 1) * T, :])})

    def maps(seg, extra):
        out = []
        for c in range(NCORES):
            m = {}
            for name in dict.fromkeys(SEG_IN[seg]):
                if name in SEG_OUT[seg]:
                    continue
                if name in wts:
                    m[name] = wts[name]
                elif name in consts:
                    m[name] = consts[name]
                elif name in per_core[c]:
                    m[name] = per_core[c][name]
                else:
                    m[name] = extra[c][name]
            out.append(m)
        return out

    h = [np.ascontiguousarray(x[c // 4, (c % 4) * T:(c % 4 + 1) * T, :]) for c in range(NCORES)]
    r1 = _run(1, maps(1, [{"hin": h[c]} for c in range(NCORES)]))
    L_all = [np.ascontiguousarray(np.stack([r1[(c // 4) * 4 + rs]["L_out"] for rs in range(L_RANKS)], 0)) for c in range(NCORES)]
    r2 = _run(2, maps(2, [{"hin": h[c], "L_all": L_all[c]} for c in range(NCORES)]))
    h = [r2[c]["hout"] for c in range(NCORES)]
    zk = np.zeros((3, D, T), ml_dtypes.bfloat16)
    zv = np.zeros((3, T, D), ml_dtypes.bfloat16)
    ex3 = []
    for c in range(NCORES):
        prev = c - 1 if c % 4 > 0 else None
        ex3.append({"hin": h[c], "KT_own": r2[c]["KT_own"], "V_own": r2[c]["V_own"],
                    "KT_prev": r2[prev]["KT_own"] if prev is not None else zk,
                    "V_prev": r2[prev]["V_own"] if prev is not None else zv})
    r3 = _run(3, maps(3, ex3))
    h = [r3[c]["hout"] for c in range(NCORES)]
    L_all = [np.ascontiguousarray(np.stack([r3[(c // 4) * 4 + rs]["L_out"] for rs in range(L_RANKS)], 0)) for c in range(NCORES)]
    r4 = _run(4, maps(4, [{"hin": h[c], "L_all": L_all[c]} for c in range(NCORES)]))
    out = np.empty((BATCH, SEQ, D), np.float32)
    for c in range(NCORES):
        out[c // 4, (c % 4) * T:(c % 4 + 1) * T, :] = r4[c]["hout"]
    return out


def pipeline(n, phases):
    nph = len(phases)
    if _SEQ_DEBUG:
        for t in range(n):
            for ph in phases:
                ph(t)
        return
    for step in range(n + nph - 1):
        for ph in reversed(range(nph)):
            t = step - ph
            if 0 <= t < n:
                phases[ph](t)


def _pre_norm_phases(self, get_h, tbanks=(7,), gain=None):
    P = self.P
    stt = {}
    gt, gt_b = gain if gain is not None else (self.gpre, self.gpre_b)
    slots = (0, 1) if gain is None else (2, 3)

    def pa(t):
        hs, hb = get_h(t)
        r, r_b = self.rstd_of(hs, [hb], D)
        s = slots[t % 2] if self.abf_n == 4 else t % 2
        ab, ab_b = self.abf[:, s, :], self.abf_b[s]
        P.op("dve", lambda e: e.scalar_tensor_tensor(out=ab, in0=hs, scalar=r, in1=gt[:], op0=ALU.mult, op1=ALU.mult),
             reads=[hb, r_b, gt_b], writes=[ab_b])
        stt[t] = (s, ab_b)

    def pb(t):
        s, ab_b = stt.pop(t)
        tbank = tbanks[t % len(tbanks)]
        pst = self.bank[tbank][:].bitcast(BF16).rearrange("p (k n) -> p k n", k=KC)
        for kc in range(KC):
            P.op("pe", lambda e, kc=kc: e.transpose(out=pst[:, kc, :], in_=self.abf[:, s, kc * 128:(kc + 1) * 128], identity=self.ident[:]),
                 reads=[ab_b, self.ident_b], writes=[self.bank_b[tbank]])
        P.op("act", lambda e: e.copy(out=self.aT[:, :, t * 128:(t + 1) * 128], in_=pst),
             reads=[self.bank_b[tbank]], writes=[self.aT_b[t]])

    return [pa, pb]


def _boundary_phases(self, src_fn, dst_fn, do_pre=True, tbanks=(7,), toks=None):
    P = self.P
    st1 = {}
    st2 = {}

    def p0(t):
        src_ap, src_b = src_fn(t)
        dst_ap, dst_b = dst_fn(t)
        r, r_b = self.rstd_of(src_ap, src_b, D)
        P.op("dve", lambda e: e.scalar_tensor_tensor(out=dst_ap, in0=src_ap, scalar=r, in1=self.gpost[:], op0=ALU.mult, op1=ALU.mult),
             reads=list(src_b) + [r_b, self.gpost_b], writes=[dst_b])
        hs, hb = self.h_load(t)
        st1[t] = (dst_ap, dst_b, hs, hb)

    def p1(t):
        dst_ap, dst_b, hs, hb = st1.pop(t)
        P.op("pool", lambda e: e.tensor_tensor(out=hs, in0=hs, in1=dst_ap, op=ALU.add), reads=[dst_b], writes=[hb])
        tok = self.h_store(t, hs, hb)
        if toks is not None:
            toks.append(tok)
        st2[t] = (hs, hb)

    phases = [p0, p1]
    if do_pre:
        pn = self.pre_norm_phases(lambda t: st2.pop(t), tbanks=tbanks)
        def p1b(t, p1=p1, pa=pn[0]):
            p1(t)
            pa(t)
        phases = [p0, p1b, pn[1]]
    return phases


def _first_norm(self):
    self.set_hextra(0, 5)
    pipeline(NT, self.pre_norm_phases(lambda t: self.h_load(t), tbanks=(6, 7)))


def _mixer_boundaries2(self, i):
    toks = []
    ph = self.boundary_phases(lambda t: (self.acc[:, t, :], [self.acc_b[t]]), lambda t: (self.acc[:, t, :], self.acc_b[t]), toks=toks, tbanks=(6, 7))
    pipeline(NT, ph)
    return toks


Builder.pre_norm_phases = _pre_norm_phases
Builder.boundary_phases = _boundary_phases
Builder.first_norm = _first_norm
Builder.mixer_boundaries = _mixer_boundaries2


def _ffn_stage2(self, i):
    P = self.P
    self.load_g("post", (i, 3))
    self.load_g("pre", (i, 4))
    P.barrier()
    w_in = self.dr["ffn_w_in"][i]
    w_out = self.dr["ffn_w_out"][i]
    off = 0
    winA, winA_b = [], []
    for s in range(3):
        v, off = self.st_view(off, [KC, 512], BF16)
        winA.append(v)
        winA_b.append(Buf(f"f{i}winA{s}"))
    woutB, woutB_b = [], []
    for s in range(2):
        v, off = self.st_view(off, [4, D], BF16)
        woutB.append(v)
        woutB_b.append(Buf(f"f{i}woutB{s}"))
    hidT, hidT_b = [], []
    for s in range(2):
        v, off = self.st_view(off, [4, T], BF16)
        hidT.append(v)
        hidT_b.append([[Buf(f"f{i}hid{s}_{jl}_{tg}") for tg in range(4)] for jl in range(4)])
    sil, sil_b = [], []
    for s in range(2):
        v, off = self.st_view(off, [1, 512], F32)
        sil.append(v)
        sil_b.append(Buf(f"f{i}sil{s}"))
    nchunk = FFN_H // 128
    ngrp = (nchunk + 3) // 4
    unit = [0]
    self.set_hextra(off, (self.ST_BYTES - off) // 4096)

    def in_phase(pi):
        ws, wb = winA[pi % 3], winA_b[pi % 3]
        self.wload(ws[:, :, 0:256], w_in[:, pi * 256:(pi + 1) * 256].rearrange("(kc p) n -> p kc n", p=128), wb)
        self.wload(ws[:, :, 256:512], w_in[:, FFN_H + pi * 256:FFN_H + (pi + 1) * 256].rearrange("(kc p) n -> p kc n", p=128), wb)
        for jj in range(2):
            j = 2 * pi + jj
            grp, jl = j // 4, j % 4
            hs_, hs_b = hidT[grp % 2], hidT_b[grp % 2]
            for tg in range(4):
                u = unit[0]
                unit[0] += 1
                bg, bu = 2 * (u % 2), 2 * (u % 2) + 1
                sl, slb = sil[u % 2], sil_b[u % 2]
                for (bk, coff) in ((bg, jj * 128), (bu, 256 + jj * 128)):
                    for kc in range(KC):
                        P.op("pe", lambda e, bk=bk, coff=coff, kc=kc, tg=tg, ws=ws: e.matmul(
                            self.bank[bk][:], lhsT=ws[:, kc, coff:coff + 128], rhs=self.aT[:, kc, tg * 512:(tg + 1) * 512],
                            start=(kc == 0), stop=(kc == KC - 1)),
                            reads=[wb] + self.aT_b[tg * 4:(tg + 1) * 4], writes=[self.bank_b[bk]])
                P.op("act", lambda e, bg=bg, sl=sl: e.activation(out=sl[:, 0, :], in_=self.bank[bg][:], func=AF.Silu),
                     reads=[self.bank_b[bg]], writes=[slb])
                P.op("dve", lambda e, bu=bu, sl=sl, hs_=hs_, jl=jl, tg=tg: e.tensor_tensor(
                    out=hs_[:, jl, tg * 512:(tg + 1) * 512], in0=sl[:, 0, :], in1=self.bank[bu][:], op=ALU.mult),
                    reads=[slb, self.bank_b[bu]], writes=[hs_b[jl][tg]])

    def out_load(grp):
        n = min(4, nchunk - grp * 4)
        wo, wob = woutB[grp % 2], woutB_b[grp % 2]
        self.wload(wo[:, 0:n, :], w_out[grp * 512:grp * 512 + n * 128, :].rearrange("(jl p) n -> p jl n", p=128), wob)

    def out_tile(grp, t):
        n = min(4, nchunk - grp * 4)
        wo, wob = woutB[grp % 2], woutB_b[grp % 2]
        hs_, hs_b = hidT[grp % 2], hidT_b[grp % 2]
        for half in range(2):
            bk = 4 + 2 * (t % 2) + half
            for q in range(n):
                P.op("pe", lambda e, bk=bk, q=q, half=half: e.matmul(
                    self.bank[bk][:], lhsT=hs_[:, q, t * 128:(t + 1) * 128], rhs=wo[:, q, half * 512:(half + 1) * 512],
                    start=(q == 0), stop=(q == n - 1)),
                    reads=[wob, hs_b[q][t // 4]], writes=[self.bank_b[bk]])
            dst = self.acc[:, t, half * 512:(half + 1) * 512]
            if grp == 0:
                P.op("act", lambda e, bk=bk, dst=dst: e.copy(out=dst, in_=self.bank[bk][:]),
                     reads=[self.bank_b[bk]], writes=[self.acc_b[t]])
            else:
                P.op("dve", lambda e, bk=bk, dst=dst: e.tensor_tensor(out=dst, in0=dst, in1=self.bank[bk][:], op=ALU.add),
                     reads=[self.bank_b[bk]], writes=[self.acc_b[t]])

    toks = []
    pairs_of = lambda grp: range(grp * 2, min(grp * 2 + 2, nchunk // 2))
    for pi in pairs_of(0):
        in_phase(pi)
    for grp in range(ngrp):
        out_load(grp)
        if grp + 1 < ngrp:
            for pi in pairs_of(grp + 1):
                in_phase(pi)
            for t in range(NT):
                out_tile(grp, t)
        else:
            ph = self.boundary_phases_f(lambda t: (self.acc[:, t, :], [self.acc_b[t]]), lambda t: (self.acc[:, t, :], self.acc_b[t]),
                                        toks=toks, tbanks=(0, 1))
            pipeline(NT, [lambda t, grp=grp: out_tile(grp, t)] + ph)
    return toks


Builder.ffn_stage = _ffn_stage2


def _ple2(self, i, do_pre=True):
    P = self.P
    P.barrier()
    off = 0
    Wg, off = self.st_view(off, [KC, D], BF16)
    Wg_b = Buf(f"p{i}Wg")
    Wp, off = self.st_view(off, [2, D], BF16)
    Wp_b = Buf(f"p{i}Wp")
    pb, pb_b, pT, pT_b, sg, sg_b = [], [], [], [], [], []
    for s in range(3):
        v, off = self.st_view(off, [1, 256], BF16)
        pb.append(v)
        pb_b.append(Buf(f"p{i}pb{s}"))
        v, off = self.st_view(off, [2, 128], BF16)
        pT.append(v)
        pT_b.append(Buf(f"p{i}pT{s}"))
        v, off = self.st_view(off, [2, 512], F32)
        sg.append(v)
        sg_b.append(Buf(f"p{i}sg{s}"))
    self.set_hextra(off, 6)
    self.wload(Wg[:, :, 0:512], self.dr["ple_w_gate"][i][:, 0:512].rearrange("(kc p) n -> p kc n", p=128), Wg_b)
    self.wload(Wp, self.dr["ple_w_proj"][i].rearrange("(kc p) n -> p kc n", p=128), Wp_b)
    Wg2_b = Buf(f"p{i}Wg2")
    self.wload(Wg[:, :, 512:1024], self.dr["ple_w_gate"][i][:, 512:1024].rearrange("(kc p) n -> p kc n", p=128), Wg2_b)
    toks = []
    hst = {}

    def mm(t):
        s = t % 3
        self.wload(pb[s][:, 0, :], self.dr["p"][i, t * 128:(t + 1) * 128, :], pb_b[s])
        ptb = 6
        pst = self.bank[ptb][:].bitcast(BF16)[:, (t % 2) * 256:(t % 2) * 256 + 256].rearrange("p (k n) -> p k n", k=2)
        for kc in range(2):
            P.op("pe", lambda e, kc=kc: e.transpose(out=pst[:, kc, :], in_=pb[s][:, 0, kc * 128:(kc + 1) * 128], identity=self.ident[:]),
                 reads=[pb_b[s], self.ident_b], writes=[self.bank_b[ptb]])
        P.op("dve", lambda e: e.tensor_copy(out=pT[s], in_=pst), reads=[self.bank_b[ptb]], writes=[pT_b[s]])
        for half in range(2):
            u = 2 * t + half
            bG, bE = u % 3, 3 + u % 3
            wgb = Wg_b if half == 0 else Wg2_b
            for kc in range(KC):
                P.op("pe", lambda e, kc=kc, half=half, bG=bG: e.matmul(
                    self.bank[bG][:], lhsT=self.aT[:, kc, t * 128:(t + 1) * 128], rhs=Wg[:, kc, half * 512:(half + 1) * 512],
                    start=(kc == 0), stop=(kc == KC - 1)), reads=[wgb, self.aT_b[t]], writes=[self.bank_b[bG]])
            for kc in range(2):
                P.op("pe", lambda e, kc=kc, half=half, bE=bE: e.matmul(
                    self.bank[bE][:], lhsT=pT[s][:, kc, :], rhs=Wp[:, kc, half * 512:(half + 1) * 512],
                    start=(kc == 0), stop=(kc == 1)), reads=[Wp_b, pT_b[s]], writes=[self.bank_b[bE]])

    def post(t):
        s = t % 3
        for half in range(2):
            u = 2 * t + half
            bG, bE = u % 3, 3 + u % 3
            P.op("act", lambda e, half=half, bG=bG: e.activation(out=sg[s][:, half, :], in_=self.bank[bG][:], func=AF.Sigmoid),
                 reads=[self.bank_b[bG]], writes=[sg_b[s]])
            P.op("dve", lambda e, half=half, bE=bE: e.tensor_tensor(out=sg[s][:, half, :], in0=sg[s][:, half, :], in1=self.bank[bE][:], op=ALU.mult),
                 reads=[self.bank_b[bE]], writes=[sg_b[s]])
        hs, hb = self.h_load(t)
        hst[t] = (hs, hb)

    def resid(t):
        s = t % 3
        hs, hb = hst[t]
        P.op("dve", lambda e: e.tensor_tensor(out=hs, in0=hs, in1=sg[s].rearrange("p a b -> p (a b)"), op=ALU.add),
             reads=[sg_b[s]], writes=[hb])
        toks.append(self.h_store(t, hs, hb))

    phases = [mm, post]
    if do_pre:
        pn = self.pre_norm_phases_f(lambda t: hst.pop(t), tbanks=(7,))

        def resid_pa(t):
            resid(t)
            pn[0](t)
        phases += [resid_pa] + pn[1:]
    else:
        phases += [resid]
    pipeline(NT, phases)
    return toks


Builder.ple = _ple2


def _att_main2(self, i, j):
    P = self.P
    dr = self.dr
    P.barrier()
    w_in = dr["attn_w_in"][j]
    if getattr(self, "KT_b", None) is None:
        self.KT_b = [[Buf(f"KT{g}_{fb}") for fb in range(8)] for g in range(3)]
        self.V_b = [[Buf(f"V{g}_{blk}") for blk in range(NT)] for g in range(3)]
    off = 0
    Wq, off = self.st_view(off, [KC, 384], BF16)
    Wq_b = [Buf(f"aWq{g}") for g in range(3)]
    QT, KT, Vt, KH, VH, Et, Et0 = [], [], [], [], [], [], []
    QT_b, KT_sb, Vt_b, KH_b, VH_b, Et_b = [], [], [], [], [], []
    for g, (w, r) in enumerate(DIL):
        a, off = self.st_view(off, [1, T], BF16); QT.append(a); QT_b.append(Buf(f"aQT{g}"))
        a, off = self.st_view(off, [1, T], BF16); KT.append(a); KT_sb.append(Buf(f"aKT{g}"))
        a, off = self.st_view(off, [NT, 128], BF16); Vt.append(a); Vt_b.append(Buf(f"aVt{g}"))
        a, off = self.st_view(off, [1, r * 128], BF16); KH.append(a); KH_b.append(Buf(f"aKH{g}"))
        a, off = self.st_view(off, [r, 128], BF16); VH.append(a); VH_b.append(Buf(f"aVH{g}"))
        a, off = self.st_view(off, [2, 256], F32); Et.append(a)
        a, off = self.st_view(off, [2, 256], F32); Et0.append(a); Et_b.append(Buf(f"aEt{g}"))
    bst, off = self.st_view(off, [2, 256], F32)
    bst_b = Buf("abst")
    m01, off = self.st_view(off, [1, 256], F32)
    m01p, off = self.st_view(off, [1, 256], F32)
    m_b = Buf("am01")
    ones, off = self.st_view(off, [1, 64], BF16)
    ones_b = Buf("aones")
    ex, ex_b, pt, pt_b = [], [], [], []
    for s in range(2):
        a, off = self.st_view(off, [1, 1024], F32); ex.append(a); ex_b.append(Buf(f"aex{s}"))
        a, off = self.st_view(off, [8, 128], BF16); pt.append(a); pt_b.append(Buf(f"apt{s}"))
    bk, bb = self.bank, self.bank_b
    bigb = self.big[:].bitcast(BF16)
    oT = bigb[:, 0:16384].rearrange("p (h t) -> p h t", h=8)
    oT_b = [Buf(f"aoT{h}") for h in range(8)]
    NDacc = self.big[:, 8192:12288].rearrange("p (a t) -> p a t", a=2)
    ND_b = Buf("aND")
    P.dma("sp", m01[:, 0, :], dr["att_mask01"][:, :], writes=[m_b])
    P.dma("sp", m01p[:, 0, :], dr["att_prevmask"][:, :], writes=[m_b])
    P.op("pool", lambda e: e.memset(ones, 1.0), writes=[ones_b])
    uq = [0]
    for hp in range(8):
        for g, (w, r) in enumerate(DIL):
            nbc = 16 // r
            c0 = g * 3072 + hp * 128
            self.wload(Wq[:, :, g * 128:(g + 1) * 128], w_in[:, c0:c0 + 128].rearrange("(kc p) n -> p kc n", p=128), Wq_b[g])
            P.dma("sp", KT[g][:, 0, :], dr["KT_own"][g, hp * 128:(hp + 1) * 128, :], reads=[self.KT_b[g][hp]], writes=[KT_sb[g]])
            P.dma("sp", Vt[g], dr["V_own"][g, :, hp * 128:(hp + 1) * 128].rearrange("(b p) f -> p b f", p=128), reads=self.V_b[g], writes=[Vt_b[g]])
            P.dma("sp", KH[g][:, 0, :].rearrange("p (c j) -> p c j", c=r),
                  dr["KT_prev"][g, hp * 128:(hp + 1) * 128, :].rearrange("f (c n j) -> f c n j", c=r, n=nbc)[:, :, nbc - 1, :], writes=[KH_b[g]])
            P.dma("sp", VH[g],
                  dr["V_prev"][g, :, hp * 128:(hp + 1) * 128].rearrange("(c n p) f -> p c n f", c=r, n=nbc)[:, :, nbc - 1, :], writes=[VH_b[g]])
            P.dma("sp", bst, dr["att_bias"][g, 2 * hp:2 * hp + 2].rearrange("h p n -> p h n"), writes=[bst_b])
            P.op("act", lambda e: e.activation(out=bst, in_=bst, func=AF.Exp), reads=[], writes=[bst_b])
            for hh in range(2):
                P.op("pool", lambda e, hh=hh, g=g: e.tensor_tensor(out=Et[g][:, hh, :], in0=bst[:, hh, :], in1=m01[:, 0, :], op=ALU.mult),
                     reads=[bst_b, m_b], writes=[Et_b[g]])
                P.op("pool", lambda e, hh=hh, g=g: e.tensor_tensor(out=Et0[g][:, hh, :], in0=bst[:, hh, :], in1=m01p[:, 0, :], op=ALU.mult),
                     reads=[bst_b, m_b], writes=[Et_b[g]])

        def q_proj(g):
            r = DIL[g][1]
            for tg in range(4):
                b = uq[0] % 2
                uq[0] += 1
                for kc in range(KC):
                    P.op("pe", lambda e, b=b, kc=kc, tg=tg: e.matmul(
                        bk[b][:], lhsT=Wq[:, kc, g * 128:(g + 1) * 128], rhs=self.aT[:, kc, tg * 512:(tg + 1) * 512],
                        start=(kc == 0), stop=(kc == KC - 1)), reads=[Wq_b[g]] + self.aT_b[tg * 4:(tg + 1) * 4], writes=[bb[b]])
                jn = 512 // r
                dst = QT[g][:, 0, :].rearrange("p (c j) -> p c j", c=r)[:, :, tg * jn:(tg + 1) * jn]
                src = bk[b][:].rearrange("p (j c) -> p c j", c=r)
                P.op("act", lambda e, dst=dst, src=src: e.copy(out=dst, in_=src), reads=[bb[b]], writes=[QT_b[g]])

        def geom(it, d):
            g, bp = it // 8, it % 8
            blk = 2 * bp + d
            r = DIL[g][1]
            nbc = 16 // r
            c, n = blk // nbc, blk % nbc
            return g, blk, r, c, n

        def ph0(it):
            if it % 8 == 0:
                q_proj(it // 8)
            sb0 = 2 if it % 2 == 0 else 6
            for d in range(2):
                g, blk, r, c, n = geom(it, d)
                qs = slice(blk * 128, (blk + 1) * 128)
                first = (n == 0)
                for hh in range(2):
                    ps = slice(hh * 64, (hh + 1) * 64)
                    sT = bk[sb0 + hh][:].rearrange("p (d k q) -> p d k q", d=2, k=2)
                    kprev = KH[g][ps, 0, c * 128:(c + 1) * 128] if first else KT[g][ps, 0, (blk - 1) * 128:blk * 128]
                    P.op("pe", lambda e, sT=sT, kprev=kprev, ps=ps, d=d, qs=qs, g=g: e.matmul(sT[:, d, 0, :], lhsT=kprev, rhs=QT[g][ps, 0, qs], start=True, stop=True),
                         reads=[KH_b[g] if first else KT_sb[g], QT_b[g]], writes=[bb[sb0 + hh]])
                    P.op("pe", lambda e, sT=sT, ps=ps, d=d, qs=qs, g=g: e.matmul(sT[:, d, 1, :], lhsT=KT[g][ps, 0, qs], rhs=QT[g][ps, 0, qs], start=True, stop=True),
                         reads=[KT_sb[g], QT_b[g]], writes=[bb[sb0 + hh]])

        def ph1(it):
            sb0 = 2 if it % 2 == 0 else 6
            s2 = it % 2
            P.op("act", lambda e: e.activation(out=ex[s2][:, 0, :].rearrange("p (h x) -> p h x", h=2), in_=self.psum_all[:, sb0:sb0 + 2, :], func=AF.Exp, scale=0.125),
                 reads=[bb[sb0], bb[sb0 + 1]], writes=[ex_b[s2]])
            exv = ex[s2][:, 0, :].rearrange("p (h d x) -> p h d x", h=2, d=2)
            ptv = pt[s2].rearrange("p (h d k) q -> p h d (k q)", h=2, d=2)
            for d in range(2):
                g, blk, r, c, n = geom(it, d)
                tab = Et0[g] if n == 0 else Et[g]
                P.op("dve", lambda e, d=d, tab=tab: e.tensor_tensor(out=ptv[:, :, d, :], in0=exv[:, :, d, :], in1=tab, op=ALU.mult),
                     reads=[ex_b[s2], Et_b[g]], writes=[pt_b[s2]])

        def ph2(it):
            s2 = it % 2
            nb_ = 4 + it % 2
            ndb = bk[nb_][:].rearrange("p (d a q) -> p d a q", d=2, a=2)
            for d in range(2):
                g, blk, r, c, n = geom(it, d)
                first = (n == 0)
                for hh in range(2):
                    ps = slice(hh * 64, (hh + 1) * 64)
                    i0 = hh * 4 + d * 2
                    vprev = VH[g][:, c, ps] if first else Vt[g][:, blk - 1, ps]
                    P.op("pe", lambda e, d=d, vprev=vprev, ps=ps, i0=i0: e.matmul(ndb[ps, d, 0, :], lhsT=vprev, rhs=pt[s2][:, i0, :], start=True, stop=False),
                         reads=[VH_b[g] if first else Vt_b[g], pt_b[s2]], writes=[bb[nb_]])
                    P.op("pe", lambda e, d=d, ps=ps, i0=i0, g=g, blk=blk: e.matmul(ndb[ps, d, 0, :], lhsT=Vt[g][:, blk, ps], rhs=pt[s2][:, i0 + 1, :], start=False, stop=True),
                         reads=[Vt_b[g], pt_b[s2]], writes=[bb[nb_]])
                    P.op("pe", lambda e, d=d, ps=ps, i0=i0: e.matmul(ndb[ps, d, 1, :], lhsT=ones[:, 0, :], rhs=pt[s2][:, i0, :], start=True, stop=False),
                         reads=[ones_b, pt_b[s2]], writes=[bb[nb_]])
                    P.op("pe", lambda e, d=d, ps=ps, i0=i0: e.matmul(ndb[ps, d, 1, :], lhsT=ones[:, 0, :], rhs=pt[s2][:, i0 + 1, :], start=False, stop=True),
                         reads=[ones_b, pt_b[s2]], writes=[bb[nb_]])

        def ph3(it):
            nb_ = 4 + it % 2
            ndb = bk[nb_][:].rearrange("p (d a q) -> p d a q", d=2, a=2)
            for d in range(2):
                g, blk, r, c, n = geom(it, d)
                t0 = n * 128 * r + c
                tsl = slice(t0, t0 + 127 * r + 1, r)
                if g == 0:
                    P.op("dve", lambda e, d=d, tsl=tsl: e.tensor_copy(out=NDacc[:, :, tsl], in_=ndb[:, d, :, :]), reads=[bb[nb_]], writes=[ND_b] + self.acc_b[8:12])
                else:
                    P.op("dve", lambda e, d=d, tsl=tsl: e.tensor_tensor(out=NDacc[:, :, tsl], in0=NDacc[:, :, tsl], in1=ndb[:, d, :, :], op=ALU.add), reads=[bb[nb_]], writes=[ND_b])

        _lvl = int(_os.environ.get('ADBG', '9'))
        pipeline(3 * 8, [ph0, ph1, ph2, ph3][:min(_lvl, 4)])
        if _lvl < 5:
            continue
        P.op("dve", lambda e: e.reciprocal(out=NDacc[:, 1, :], in_=NDacc[:, 1, :]), reads=[], writes=[ND_b])
        P.op("pool", lambda e, hp=hp: e.tensor_tensor(out=oT[:, hp, :], in0=NDacc[:, 0, :], in1=NDacc[:, 1, :], op=ALU.mult),
             reads=[ND_b], writes=[oT_b[hp]] + self.acc_b[hp:hp + 1])
    P.barrier()
    off = 0
    Wo, off = self.st_view(off, [KC, D], BF16)
    Wo_b = Buf("aWo")
    nrm, off = self.st_view(off, [2, D], F32)
    nrm_b = [Buf("anrm0"), Buf("anrm1")]
    self.wload(Wo, dr["attn_w_out"][j].rearrange("(kc p) n -> p kc n", p=128), Wo_b)
    self.set_hextra(off, 6)
    toks = []

    def outp(t):
        tsl = slice(t * 128, (t + 1) * 128)
        for half in range(2):
            for h in range(8):
                P.op("pe", lambda e, h=h, half=half: e.matmul(bk[6 + half][:], lhsT=oT[:, h, tsl], rhs=Wo[:, h, half * 512:(half + 1) * 512],
                                                             start=(h == 0), stop=(h == 7)),
                     reads=[oT_b[h], Wo_b], writes=[bb[6 + half]])

    src = self.psum_all[:, 6:8, :].rearrange("p a b -> p (a b)")
    ph = self.boundary_phases(lambda t: (src, [bb[6], bb[7]]), lambda t: (nrm[:, t % 2, :], nrm_b[t % 2]), toks=toks, tbanks=(0, 1))
    pipeline(NT, [outp] + ph)
    for t in range(NT):
        for h in range(8):
            self.acc_b[t].r = list(self.acc_b[t].r) + list(oT_b[h].r)
        self.acc_b[t].r = list(self.acc_b[t].r) + list(ND_b.r) + ([ND_b.w] if ND_b.w else [])
    return toks


Builder.att_main = _att_main2


def _gmlp_stage2(self, i, j):
    P = self.P
    dr = self.dr
    P.barrier()
    bigb = self.big[:].bitcast(BF16)
    Wv = bigb[:, 0:16384].rearrange("p (k n) -> p k n", k=KC)
    Wu = bigb[:, 16384:32768].rearrange("p (k n) -> p k n", k=KC)
    Wv_b = [Buf(f"gWv{c}") for c in range(4)]
    Wu_b = [Buf(f"gWu{c}") for c in range(4)]
    off = 0
    Wo, off = self.st_view(off, [16, D], BF16)
    Wo_b = [Buf(f"gWo{q}") for q in range(4)]
    wsT, off = self.st_view(off, [GM_G, 128], BF16)
    wsT_b = Buf("gwsT")
    gtab, off = self.st_view(off, [1, GM_W], F32)
    btab, off = self.st_view(off, [1, GM_W], BF16)
    gtab_b, btab_b = Buf("ggtab"), Buf("gbtab")
    bsT, off = self.st_view(off, [1, GM_G], F32)
    bsT_b = Buf("gbsT")
    cmask, off = self.st_view(off, [1, 128], F32)
    cmask_b = Buf("gcmask")
    off_vf = off
    vf, off = self.st_view(off, [1, GM_W], F32)
    vf_b = Buf("gvf")
    vn, off = self.st_view(off, [1, GM_W], BF16)
    vn_b = Buf("gvn")
    ssb, off = self.st_view(off, [1, GM_W], F32)
    ssb_b = Buf("gssb")
    us, off = self.st_view(off, [1, GM_W], BF16)
    us_b = Buf("gus")
    usT, off = self.st_view(off, [16, 128], BF16)
    usT_b = Buf("gusT")
    gu, gu_b = [], []
    for s in range(2):
        a, off = self.st_view(off, [1, 512], F32)
        gu.append(a)
        gu_b.append(Buf(f"ggu{s}"))
    nrm, off = self.st_view(off, [1, D], F32)
    nrm_b = Buf("gnrm")
    wsl, off2 = self.st_view(off_vf, [GM_G, 128], BF16)
    wsl_b = vf_b
    w_in = dr["gm_w_in"][j]
    self.wload(wsl, dr["gm_w_s"][j].rearrange("g n m -> n g m"), wsl_b)
    self.wload(btab[:, 0, :], dr["gm_ln_b"][j, :].partition_broadcast(128), btab_b)
    P.dma("sp", gtab[:, 0, :], dr["gm_ln_g"][j, :].partition_broadcast(128), writes=[gtab_b])
    for cb in range(4):
        self.P.dma("pool", Wv[:, :, cb * 512:(cb + 1) * 512], w_in[:, GM_W + cb * 512:GM_W + (cb + 1) * 512].rearrange("(kc p) n -> p kc n", p=128),
                   writes=[Wv_b[cb]])
    for cb in range(4):
        self.P.dma("pool", Wu[:, :, cb * 512:(cb + 1) * 512], w_in[:, cb * 512:(cb + 1) * 512].rearrange("(kc p) n -> p kc n", p=128),
                   writes=[Wu_b[cb]])
    for q in range(4):
        self.wload(Wo[:, q * 4:(q + 1) * 4, :], dr["gm_w_out"][j][q * 512:(q + 1) * 512, :].rearrange("(kc p) n -> p kc n", p=128), Wo_b[q])
    P.dma("sp", bsT[:, 0, :], dr["gm_b_s"][j].rearrange("g n -> n g"), writes=[bsT_b], allow_slow_non_contiguous=True)
    P.dma("sp", cmask[:, 0, :], dr["causalT"][:, :], writes=[cmask_b])
    bk, bb = self.bank, self.bank_b
    tps = bk[0][:].bitcast(BF16)[:, 0:1024].rearrange("p (g n) -> p g n", g=GM_G)
    for g in range(GM_G):
        P.op("pe", lambda e, g=g: e.transpose(out=tps[:, g, :], in_=wsl[:, g, :], identity=self.ident[:]),
             reads=[wsl_b, self.ident_b], writes=[bb[0]])
    for g in range(GM_G):
        P.op("dve", lambda e, g=g: e.tensor_tensor(out=wsT[:, g, :], in0=tps[:, g, :], in1=cmask[:, 0, :], op=ALU.mult),
             reads=[bb[0], cmask_b], writes=[wsT_b])
    toks = []
    stt = {}
    uu = [0]

    def ph0(t):
        tsl = slice(t * 128, (t + 1) * 128)
        sv, sv_b = [], []
        for cb in range(4):
            b = uu[0] % 2
            uu[0] += 1
            for kc in range(KC):
                P.op("pe", lambda e, cb=cb, kc=kc, b=b: e.matmul(bk[b][:], lhsT=self.aT[:, kc, tsl], rhs=Wv[:, kc, cb * 512:(cb + 1) * 512],
                                                               start=(kc == 0), stop=(kc == KC - 1)),
                     reads=[Wv_b[cb], self.aT_b[t]], writes=[bb[b]])
            a, a_b = self.new_stat()
            sv.append(a)
            sv_b.append(a_b)
            P.op("act", lambda e, cb=cb, a=a, b=b: e.activation(out=vf[:, 0, cb * 512:(cb + 1) * 512], in_=bk[b][:], func=AF.Gelu, accum_out=a),
                 reads=[bb[b]], writes=[vf_b, a_b])
        stt[t] = (sv, sv_b)

    def ph1(t):
        sv, sv_b = stt.pop(t)
        sq, sq_b = [], []
        for hh in range(2):
            a, a_b = self.new_stat()
            sq.append(a)
            sq_b.append(a_b)
            P.op("act", lambda e, hh=hh, a=a: e.activation(out=self.junk[:, :], in_=vf[:, 0, hh * 1024:(hh + 1) * 1024], func=AF.Square, accum_out=a),
                 reads=[vf_b], writes=[self.junk_b, a_b])
        m1, m1_b = self.new_stat()
        P.op("pool", lambda e: e.tensor_tensor(out=m1, in0=sv[0], in1=sv[1], op=ALU.add), reads=[sv_b[0], sv_b[1]], writes=[m1_b])
        m2, m2_b = self.new_stat()
        P.op("pool", lambda e: e.tensor_tensor(out=m2, in0=sv[2], in1=sv[3], op=ALU.add), reads=[sv_b[2], sv_b[3]], writes=[m2_b])
        mu, mu_b = self.new_stat()
        P.op("pool", lambda e: e.tensor_tensor(out=mu, in0=m1, in1=m2, op=ALU.add), reads=[m1_b, m2_b], writes=[mu_b])
        P.op("pool", lambda e: e.tensor_scalar(out=mu, in0=mu, scalar1=1.0 / GM_W, scalar2=None, op0=ALU.mult), reads=[], writes=[mu_b])
        e2, e2_b = self.new_stat()
        P.op("pool", lambda e: e.tensor_tensor(out=e2, in0=sq[0], in1=sq[1], op=ALU.add), reads=[sq_b[0], sq_b[1]], writes=[e2_b])
        mm, mm_b = self.new_stat()
        P.op("pool", lambda e: e.tensor_tensor(out=mm, in0=mu, in1=mu, op=ALU.mult), reads=[mu_b], writes=[mm_b])
        var, var_b = self.new_stat()
        P.op("pool", lambda e: e.tensor_scalar(out=var, in0=e2, scalar1=1.0 / GM_W, scalar2=EPS, op0=ALU.mult, op1=ALU.add),
             reads=[e2_b], writes=[var_b])
        P.op("pool", lambda e: e.tensor_tensor(out=var, in0=var, in1=mm, op=ALU.subtract), reads=[mm_b], writes=[var_b])
        rs, rs_b = self.new_stat()
        P.op("pool", lambda e: e.tensor_tensor(out=rs, in0=var, in1=self.cm05[:], op=ALU.pow), reads=[var_b, self.cm05_b], writes=[rs_b])
        P.op("dve", lambda e: e.tensor_scalar(out=vf[:, 0, :], in0=vf[:, 0, :], scalar1=mu, scalar2=rs, op0=ALU.subtract, op1=ALU.mult),
             reads=[mu_b, rs_b], writes=[vf_b])
        P.op("pool", lambda e: e.tensor_tensor(out=vf[:, 0, :], in0=vf[:, 0, :], in1=gtab[:, 0, :], op=ALU.mult), reads=[gtab_b], writes=[vf_b])
        P.op("dve", lambda e: e.tensor_tensor(out=vn[:, 0, :], in0=vf[:, 0, :], in1=btab[:, 0, :], op=ALU.add), reads=[vf_b, btab_b], writes=[vn_b])

    def ph2(t):
        tsl = slice(t * 128, (t + 1) * 128)
        for un in range(4):
            bkx = 2 + un % 2
            for gg in range(2):
                g = un * 2 + gg
                P.op("pe", lambda e, g=g, gg=gg, bkx=bkx: e.matmul(bk[bkx][:, gg * 256:(gg + 1) * 256], lhsT=wsT[:, g, :], rhs=vn[:, 0, g * 256:(g + 1) * 256],
                                                                   start=True, stop=True),
                     reads=[wsT_b, vn_b], writes=[bb[bkx]])
            for gg in range(2):
                g = un * 2 + gg
                P.op("dve", lambda e, g=g, gg=gg, bkx=bkx: e.tensor_scalar(out=ssb[:, 0, g * 256:(g + 1) * 256], in0=bk[bkx][:, gg * 256:(gg + 1) * 256],
                                                                            scalar1=bsT[:, 0, g:g + 1], scalar2=None, op0=ALU.add),
                     reads=[bb[bkx], bsT_b], writes=[ssb_b])
        for cb in range(4):
            b = uu[0] % 2
            uu[0] += 1
            for kc in range(KC):
                P.op("pe", lambda e, cb=cb, kc=kc, b=b: e.matmul(bk[b][:], lhsT=self.aT[:, kc, tsl], rhs=Wu[:, kc, cb * 512:(cb + 1) * 512],
                                                               start=(kc == 0), stop=(kc == KC - 1)),
                     reads=[Wu_b[cb], self.aT_b[t]], writes=[bb[b]])
            s = cb % 2
            P.op("act", lambda e, b=b, s=s: e.activation(out=gu[s][:, 0, :], in_=bk[b][:], func=AF.Gelu), reads=[bb[b]], writes=[gu_b[s]])
            P.op("dve", lambda e, cb=cb, s=s: e.tensor_tensor(out=us[:, 0, cb * 512:(cb + 1) * 512], in0=gu[s][:, 0, :], in1=ssb[:, 0, cb * 512:(cb + 1) * 512], op=ALU.mult),
                 reads=[gu_b[s], ssb_b], writes=[us_b])

    def ph3(t):
        for half in range(2):
            tp = bk[4 + half][:].bitcast(BF16).rearrange("p (k n) -> p k n", k=8)
            for q in range(8):
                kc = half * 8 + q
                P.op("pe", lambda e, tp=tp, q=q, kc=kc: e.transpose(out=tp[:, q, :], in_=us[:, 0, kc * 128:(kc + 1) * 128], identity=self.ident[:]),
                     reads=[us_b, self.ident_b], writes=[bb[4 + half]])
            P.op("act", lambda e, tp=tp, half=half: e.copy(out=usT[:, half * 8:(half + 1) * 8, :], in_=tp), reads=[bb[4 + half]], writes=[usT_b])
        for half in range(2):
            for kc in range(16):
                P.op("pe", lambda e, half=half, kc=kc: e.matmul(bk[6 + half][:], lhsT=usT[:, kc, :], rhs=Wo[:, kc, half * 512:(half + 1) * 512],
                                                                start=(kc == 0), stop=(kc == 15)),
                     reads=[usT_b, Wo_b[kc // 4]], writes=[bb[6 + half]])

    src = self.psum_all[:, 6:8, :].rearrange("p a b -> p (a b)")
    bph = self.boundary_phases(lambda t: (src, [bb[6], bb[7]]), lambda t: (nrm[:, 0, :], nrm_b), toks=toks, tbanks=(4, 5))
    def ph3b(t):
        ph3(t)
        bph[0](t)
    pipeline(NT, [ph0, ph1, ph2, ph3b] + bph[1:])
    for t in range(NT):
        for src_b in (Wv_b if t < 8 else Wu_b):
            self.acc_b[t].r = list(self.acc_b[t].r) + list(src_b.r) + ([src_b.w] if src_b.w else [])
    return toks


Builder.gmlp_stage = _gmlp_stage2


def _ffn_ple_stage(self, i, last):
    P = self.P
    self.load_g("post", (i, 3))
    self.load_g("pre", (i, 4))
    P.barrier()
    w_in = self.dr["ffn_w_in"][i]
    w_out = self.dr["ffn_w_out"][i]
    off = 0
    winA, winA_b = [], []
    for s in range(3):
        v, off = self.st_view(off, [KC, 512], BF16)
        winA.append(v)
        winA_b.append(Buf(f"f{i}winA{s}"))
    woutB, woutB_b, woutB_off = [], [], []
    for s in range(2):
        woutB_off.append(off)
        v, off = self.st_view(off, [4, D], BF16)
        woutB.append(v)
        woutB_b.append(Buf(f"f{i}woutB{s}"))
    hidT, hidT_b, hidT_off = [], [], []
    for s in range(2):
        hidT_off.append(off)
        v, off = self.st_view(off, [4, T], BF16)
        hidT.append(v)
        hidT_b.append([[Buf(f"f{i}hid{s}_{jl}_{tg}") for tg in range(4)] for jl in range(4)])
    sil, sil_b = [], []
    for s in range(2):
        v, off = self.st_view(off, [1, 512], F32)
        sil.append(v)
        sil_b.append(Buf(f"f{i}sil{s}"))
    nchunk = FFN_H // 128
    ngrp = (nchunk + 3) // 4
    unit = [0]

    def in_phase(pi):
        ws, wb = winA[pi % 3], winA_b[pi % 3]
        self.wload(ws[:, :, 0:256], w_in[:, pi * 256:(pi + 1) * 256].rearrange("(kc p) n -> p kc n", p=128), wb)
        self.wload(ws[:, :, 256:512], w_in[:, FFN_H + pi * 256:FFN_H + (pi + 1) * 256].rearrange("(kc p) n -> p kc n", p=128), wb)
        for jj in range(2):
            j = 2 * pi + jj
            grp, jl = j // 4, j % 4
            hs_, hs_b = hidT[grp % 2], hidT_b[grp % 2]
            for tg in range(4):
                u = unit[0]
                unit[0] += 1
                bg, bu = 2 * (u % 2), 2 * (u % 2) + 1
                sl, slb = sil[u % 2], sil_b[u % 2]
                for (bk, coff) in ((bg, jj * 128), (bu, 256 + jj * 128)):
                    for kc in range(KC):
                        P.op("pe", lambda e, bk=bk, coff=coff, kc=kc, tg=tg, ws=ws: e.matmul(
                            self.bank[bk][:], lhsT=ws[:, kc, coff:coff + 128], rhs=self.aT[:, kc, tg * 512:(tg + 1) * 512],
                            start=(kc == 0), stop=(kc == KC - 1)),
                            reads=[wb] + self.aT_b[tg * 4:(tg + 1) * 4], writes=[self.bank_b[bk]])
                P.op("act", lambda e, bg=bg, sl=sl: e.activation(out=sl[:, 0, :], in_=self.bank[bg][:], func=AF.Silu),
                     reads=[self.bank_b[bg]], writes=[slb])
                P.op("dve", lambda e, bu=bu, sl=sl, hs_=hs_, jl=jl, tg=tg: e.tensor_tensor(
                    out=hs_[:, jl, tg * 512:(tg + 1) * 512], in0=sl[:, 0, :], in1=self.bank[bu][:], op=ALU.mult),
                    reads=[slb, self.bank_b[bu]], writes=[hs_b[jl][tg]])

    def out_load(grp):
        n = min(4, nchunk - grp * 4)
        wo, wob = woutB[grp % 2], woutB_b[grp % 2]
        self.wload(wo[:, 0:n, :], w_out[grp * 512:grp * 512 + n * 128, :].rearrange("(jl p) n -> p jl n", p=128), wob)

    def out_tile(grp, t, banks):
        n = min(4, nchunk - grp * 4)
        wo, wob = woutB[grp % 2], woutB_b[grp % 2]
        hs_, hs_b = hidT[grp % 2], hidT_b[grp % 2]
        for half in range(2):
            bk = banks[half]
            for q in range(n):
                P.op("pe", lambda e, bk=bk, q=q, half=half: e.matmul(
                    self.bank[bk][:], lhsT=hs_[:, q, t * 128:(t + 1) * 128], rhs=wo[:, q, half * 512:(half + 1) * 512],
                    start=(q == 0), stop=(q == n - 1)),
                    reads=[wob, hs_b[q][t // 4]], writes=[self.bank_b[bk]])
            dst = self.acc[:, t, half * 512:(half + 1) * 512]
            if grp == 0:
                P.op("act", lambda e, bk=bk, dst=dst: e.copy(out=dst, in_=self.bank[bk][:]),
                     reads=[self.bank_b[bk]], writes=[self.acc_b[t]])
            else:
                P.op("dve", lambda e, bk=bk, dst=dst: e.tensor_tensor(out=dst, in0=dst, in1=self.bank[bk][:], op=ALU.add),
                     reads=[self.bank_b[bk]], writes=[self.acc_b[t]])

    pairs_of = lambda grp: range(grp * 2, min(grp * 2 + 2, nchunk // 2))
    for pi in pairs_of(0):
        in_phase(pi)
    for grp in range(ngrp - 1):
        out_load(grp)
        for pi in pairs_of(grp + 1):
            in_phase(pi)
        for t in range(NT):
            out_tile(grp, t, (4 + 2 * (t % 2), 5 + 2 * (t % 2)))
    lg = ngrp - 1
    out_load(lg)
    oth = (lg + 1) % 2
    dead = list(winA_b) + [woutB_b[oth]] + [b for row in hidT_b[oth] for b in row] + list(sil_b)
    assert woutB_off[oth] == 24 * 1024 or woutB_off[oth] == 32 * 1024
    po = 0
    Wg, po = self.st_view(po, [KC, D], BF16)
    Wp, po = self.st_view(po, [2, D], BF16)
    Wg_b, Wg2_b, Wp_b = Buf(f"p{i}Wg"), Buf(f"p{i}Wg2"), Buf(f"p{i}Wp")
    pb, pb_b, pT, pT_b = [], [], [], []
    for s in range(3):
        v, po = self.st_view(po, [1, 256], BF16)
        pb.append(v)
        pb_b.append(Buf(f"p{i}pb{s}"))
        v, po = self.st_view(po, [2, 128], BF16)
        pT.append(v)
        pT_b.append(Buf(f"p{i}pT{s}"))
    lim = 24 * 1024 if woutB_off[oth] != 24 * 1024 else 32 * 1024
    assert po <= lim, (po, lim)
    so = hidT_off[oth]
    sg, sg_b = [], []
    for s in range(3):
        v, so = self.st_view(so, [2, 512], F32)
        sg.append(v)
        sg_b.append(Buf(f"p{i}sg{s}"))
    assert so <= hidT_off[oth] + 16 * 1024
    if lim == 24 * 1024:
        dead = list(winA_b) + [b for row in hidT_b[oth] for b in row]
    P.dma("pool", Wg[:, :, 0:512], self.dr["ple_w_gate"][i][:, 0:512].rearrange("(kc p) n -> p kc n", p=128), writes=[Wg_b] + dead)
    P.dma("pool", Wp, self.dr["ple_w_proj"][i].rearrange("(kc p) n -> p kc n", p=128), writes=[Wp_b])
    P.dma("pool", Wg[:, :, 512:1024], self.dr["ple_w_gate"][i][:, 512:1024].rearrange("(kc p) n -> p kc n", p=128), writes=[Wg2_b])
    if not last:
        self.load_g("pre2", (i + 1, 0))
    toks_b, toks = [], []
    hst = {}

    def mm(t):
        s = t % 3
        P.dma("pool", pb[s][:, 0, :], self.dr["p"][i, t * 128:(t + 1) * 128, :], writes=[pb_b[s]])
        pst = self.bank[7][:].bitcast(BF16)[:, 0:256].rearrange("p (k n) -> p k n", k=2)
        for kc in range(2):
            P.op("pe", lambda e, kc=kc: e.transpose(out=pst[:, kc, :], in_=pb[s][:, 0, kc * 128:(kc + 1) * 128], identity=self.ident[:]),
                 reads=[pb_b[s], self.ident_b], writes=[self.bank_b[7]])
        P.op("dve", lambda e: e.tensor_copy(out=pT[s], in_=pst), reads=[self.bank_b[7]], writes=[pT_b[s]])
        for half in range(2):
            u = 2 * t + half
            bG, bE = u % 2, 2 + u % 2
            wgb = Wg_b if half == 0 else Wg2_b
            for kc in range(KC):
                P.op("pe", lambda e, kc=kc, half=half, bG=bG: e.matmul(
                    self.bank[bG][:], lhsT=self.aT[:, kc, t * 128:(t + 1) * 128], rhs=Wg[:, kc, half * 512:(half + 1) * 512],
                    start=(kc == 0), stop=(kc == KC - 1)), reads=[wgb, self.aT_b[t]], writes=[self.bank_b[bG]])
            for kc in range(2):
                P.op("pe", lambda e, kc=kc, half=half, bE=bE: e.matmul(
                    self.bank[bE][:], lhsT=pT[s][:, kc, :], rhs=Wp[:, kc, half * 512:(half + 1) * 512],
                    start=(kc == 0), stop=(kc == 1)), reads=[Wp_b, pT_b[s]], writes=[self.bank_b[bE]])
            P.op("act", lambda e, half=half, bG=bG: e.activation(out=sg[s][:, half, :], in_=self.bank[bG][:], func=AF.Sigmoid),
                 reads=[self.bank_b[bG]], writes=[sg_b[s]])
            P.op("dve", lambda e, half=half, bE=bE: e.tensor_tensor(out=sg[s][:, half, :], in0=sg[s][:, half, :], in1=self.bank[bE][:], op=ALU.mult),
                 reads=[self.bank_b[bE]], writes=[sg_b[s]])

    def post(t):
        hs, hb = self.h_load(t, src=getattr(self, "h_rd_ple", None))
        hst[t] = (hs, hb)

    def resid(t):
        s = t % 3
        hs, hb = hst[t]
        P.op("pool", lambda e: e.tensor_tensor(out=hs, in0=hs, in1=sg[s].rearrange("p a b -> p (a b)"), op=ALU.add),
             reads=[sg_b[s]], writes=[hb])
        toks.append(self.h_store(t, hs, hb, dst=getattr(self, "h_wr_ple", None)))

    bph = self.boundary_phases(lambda t: (self.acc[:, t, :], [self.acc_b[t]]), lambda t: (self.acc[:, t, :], self.acc_b[t]),
                               toks=toks_b, tbanks=(6,))
    phases = [lambda t: out_tile(lg, t, (4, 5))] + bph + [mm, post]
    if not last:
        pn = self.pre_norm_phases(lambda t: hst.pop(t), tbanks=(7,), gain=(self.gpre2, self.gpre2_b))

        def resid_pa(t):
            resid(t)
            pn[0](t)
        phases += [resid_pa, pn[1]]
    else:
        phases += [resid]
    pipeline(NT, phases)
    return toks


Builder.ffn_ple_stage = _ffn_ple_stage


def _rstd_split(self, src_ap, src_bufs, n):
    P = self.P
    st = {}

    def sq():
        ss, ss_b = self.new_stat()
        P.op("act", lambda e: e.activation(out=self.junk[:, 0:n], in_=src_ap, func=AF.Square, scale=float(n) ** -0.5, accum_out=ss),
             reads=list(src_bufs), writes=[self.junk_b, ss_b])
        st["ss"] = (ss, ss_b)

    def pw():
        ss, ss_b = st["ss"]
        r, r_b = self.new_stat()
        P.op("pool", lambda e: e.tensor_scalar(out=r, in0=ss, scalar1=EPS, scalar2=None, op0=ALU.add), reads=[ss_b], writes=[r_b])
        r2, r2_b = self.new_stat()
        P.op("pool", lambda e: e.tensor_tensor(out=r2, in0=r, in1=self.cm05[:], op=ALU.pow), reads=[r_b, self.cm05_b], writes=[r2_b])
        st["r"] = (r2, r2_b)

    return sq, pw, (lambda: st["r"])


def _pre_norm_phases_f(self, get_h, tbanks=(6, 7), merge_b=False):
    P = self.P
    S = {}

    def a1(t):
        hs, hb = get_h(t)
        sq, pw, get = self.rstd_split(hs, [hb], D)
        sq()
        S[t] = [hs, hb, pw, get]

    def a2(t):
        S[t][2]()

    def a3(t):
        hs, hb, pw, get = S[t]
        r, r_b = get()
        s = t % 2
        ab, ab_b = self.abf[:, s, :], self.abf_b[s]
        P.op("dve", lambda e: e.scalar_tensor_tensor(out=ab, in0=hs, scalar=r, in1=self.gpre[:], op0=ALU.mult, op1=ALU.mult),
             reads=[hb, r_b, self.gpre_b], writes=[ab_b])
        S[t] = (s, ab_b)

    def b1(t):
        s, ab_b = S[t]
        tbank = tbanks[t % len(tbanks)]
        pst = self.bank[tbank][:].bitcast(BF16).rearrange("p (k n) -> p k n", k=KC)
        for kc in range(KC):
            P.op("pe", lambda e, kc=kc: e.transpose(out=pst[:, kc, :], in_=self.abf[:, s, kc * 128:(kc + 1) * 128], identity=self.ident[:]),
                 reads=[ab_b, self.ident_b], writes=[self.bank_b[tbank]])
        S[t] = (tbank, pst)

    def b2(t):
        tbank, pst = S.pop(t)
        P.op("act", lambda e: e.copy(out=self.aT[:, :, t * 128:(t + 1) * 128], in_=pst),
             reads=[self.bank_b[tbank]], writes=[self.aT_b[t]])

    if merge_b:
        def b12(t):
            b1(t)
            b2(t)
        return [a1, a2, a3, b12]
    return [a1, a2, a3, b1, b2]


def _boundary_phases_f(self, src_fn, dst_fn, do_pre=True, tbanks=(6, 7), toks=None, merge_b=False):
    P = self.P
    S = {}
    H = {}

    def p0a(t):
        src_ap, src_b = src_fn(t)
        sq, pw, get = self.rstd_split(src_ap, src_b, D)
        sq()
        S[t] = [src_ap, src_b, pw, get]

    def p0b(t):
        S[t][2]()

    def p0c(t):
        src_ap, src_b, pw, get = S[t]
        dst_ap, dst_b = dst_fn(t)
        r, r_b = get()
        P.op("dve", lambda e: e.scalar_tensor_tensor(out=dst_ap, in0=src_ap, scalar=r, in1=self.gpost[:], op0=ALU.mult, op1=ALU.mult),
             reads=list(src_b) + [r_b, self.gpost_b], writes=[dst_b])
        hs, hb = self.h_load(t)
        S[t] = (dst_ap, dst_b, hs, hb)

    def p1(t):
        dst_ap, dst_b, hs, hb = S.pop(t)
        P.op("dve", lambda e: e.tensor_tensor(out=hs, in0=hs, in1=dst_ap, op=ALU.add), reads=[dst_b], writes=[hb])
        tok = self.h_store(t, hs, hb)
        if toks is not None:
            toks.append(tok)
        H[t] = (hs, hb)

    phases = [p0a, p0b, p0c, p1]
    if do_pre:
        pn = self.pre_norm_phases_f(lambda t: H.pop(t), tbanks=tbanks, merge_b=merge_b)

        def p1a(t):
            p1(t)
            pn[0](t)
        phases = [p0a, p0b, p0c, p1a] + pn[1:]
    return phases


Builder.rstd_split = _rstd_split
Builder.pre_norm_phases_f = _pre_norm_phases_f
Builder.boundary_phases_f = _boundary_phases_f


def _first_norm_f(self):
    self.set_hextra(0, 5)
    pipeline(NT, self.pre_norm_phases_f(lambda t: self.h_load(t), tbanks=(6, 7)))


Builder.first_norm = _first_norm_f
```

```python
import contextlib
import math
import numpy as np
import concourse.bass as bass
import concourse.mybir as mybir
from concourse.bass_utils import run_bass_kernel_spmd

F32 = mybir.dt.float32
BF16 = mybir.dt.bfloat16
I32 = mybir.dt.int32
AF = mybir.ActivationFunctionType
ALU = mybir.AluOpType
AX = mybir.AxisListType

SEM_ROLL = 30000
import os as _os
_SEQ_DEBUG = bool(_os.environ.get('KSEQ'))

D = 1024
KC = 8
BATCH = 2
SEQ = 8192
DEPTH = 4
NCORES = 8
T = 2048
NT = 16
EPS = 1e-6
PLE_DIM = 256
FFN_H = 2816
RET_H, RET_DK, RET_DV = 4, 256, 512
L_RANKS = 3
ATT_H, ATT_DH = 16, 64
DIL = ((128, 1), (512, 4), (2048, 16))
GM_W = 2048
GM_G = 8


class Buf:
    __slots__ = ("name", "w", "r", "dsem", "dcount")

    def __init__(self, name):
        self.name = name
        self.w = None
        self.r = []
        self.dsem = None
        self.dcount = 0


class Prog:
    ENGS = ("pe", "act", "dve", "pool", "sp")

    def __init__(self, nc):
        self.nc = nc
        self.stack = contextlib.ExitStack()
        self.ops = {e: [] for e in self.ENGS}
        self.sems = {}
        self.esem = {}
        self.ecount = {e: 0 for e in self.ENGS}
        self.eroll = {e: 0 for e in self.ENGS}
        self.known = {e: {} for e in self.ENGS}
        self.nsem = 0
        self.final_tokens = []
        self.chans = []
        self.free_sems = []
        for e in self.ENGS:
            self._new_esem(e)

    def sem(self, name):
        h = self.stack.enter_context(self.nc.semaphore(name))
        self.sems[name] = h
        self.nsem += 1
        return name

    def _new_esem(self, e):
        k = self.sem(f"s_{e}_{self.eroll[e]}")
        self.eroll[e] += 1
        self.esem[e] = k
        self.ecount[e] = 0

    def sbuf(self, name, shape, dtype):
        return self.stack.enter_context(self.nc.sbuf_tensor("sb_" + name, list(shape), dtype))

    def psum(self, name, shape, dtype):
        return self.stack.enter_context(self.nc.psum_tensor("ps_" + name, list(shape), dtype))

    def _deps(self, eng, reads, writes, skip_same_pe=False):
        waits = {}

        def need(tok):
            if tok is None:
                return
            k, v = tok
            if skip_same_pe and eng == "pe" and k == self.esem["pe"]:
                return
            if self.known[eng].get(k, 0) >= v:
                return
            if waits.get(k, 0) < v:
                waits[k] = v

        for b in reads:
            need(b.w)
        for b in writes:
            need(b.w)
            for t in b.r:
                need(t)
        for k, v in waits.items():
            self.known[eng][k] = v
        return list(waits.items())

    def _commit(self, tok, reads, writes):
        for b in writes:
            b.w = tok
            b.r = []
        for b in reads:
            b.r.append(tok)
            if len(b.r) > 48:
                best = {}
                for k, v in b.r:
                    if best.get(k, 0) < v:
                        best[k] = v
                b.r = list(best.items())

    def op(self, eng, fn, reads=(), writes=()):
        if self.ecount[eng] >= SEM_ROLL:
            self._new_esem(eng)
        waits = self._deps(eng, reads, writes, skip_same_pe=True)
        self.ecount[eng] += 1
        tok = (self.esem[eng], self.ecount[eng])
        self.ops[eng].append((fn, waits, (self.esem[eng], 1)))
        self._commit(tok, reads, writes)
        return tok

    def dma(self, q, out_ap, in_ap, reads=(), writes=(), chan=None, **kw):
        if chan is None:
            chan = writes[0] if writes else reads[0]
        if chan.dsem is None:
            self.free_sems.sort(key=lambda x: x[1])
            if self.free_sems and self.free_sems[0][1] < 16000:
                chan.dsem, chan.dcount = self.free_sems.pop(0)
            else:
                chan.dsem = self.sem(f"d_{chan.name}")
                chan.dcount = 0
            self.chans.append(chan)
        waits = self._deps(q, reads, writes)
        if chan.dcount > 0 and self.known[q].get(chan.dsem, 0) < chan.dcount:
            waits = [w for w in waits if w[0] != chan.dsem] + [(chan.dsem, chan.dcount)]
            self.known[q][chan.dsem] = chan.dcount
        chan.dcount += 16
        tok = (chan.dsem, chan.dcount)

        def fn(e, out_ap=out_ap, in_ap=in_ap, kw=kw):
            return e.dma_start(out=out_ap, in_=in_ap, **kw)

        self.ops[q].append((fn, waits, (chan.dsem, 16)))
        self._commit(tok, reads, writes)
        return tok

    def barrier(self):
        toks = {}
        for e in self.ENGS:
            if self.ecount[e] > 0:
                toks[self.esem[e]] = self.ecount[e]
        for c in self.chans:
            if c.dcount > 0:
                toks[c.dsem] = c.dcount
        for e in self.ENGS:
            waits = []
            for k, v in toks.items():
                if self.known[e].get(k, 0) < v:
                    waits.append((k, v))
                    self.known[e][k] = v
            if waits:
                self.ops[e].append((None, waits, None))

    def retire(self, bufs):
        for c in bufs:
            if c.dsem is not None:
                self.free_sems.append((c.dsem, c.dcount))
                self.chans.remove(c)
                c.dsem = None
                c.dcount = 0

    def finish_on(self, eng, toks):
        self.final_tokens.append((eng, list(toks)))

    def emit(self):
        nc = self.nc
        for eng, toks in self.final_tokens:
            best = {}
            for k, v in toks:
                if best.get(k, 0) < v:
                    best[k] = v
            self.ops[eng].append((None, list(best.items()), None))
        handles = {"pe": "tensor", "act": "scalar", "dve": "vector", "pool": "gpsimd", "sp": "sync"}
        with nc.Block() as block:
            for eng in self.ENGS:
                ops = self.ops[eng]
                if not ops:
                    continue

                def body(e, ops=ops):
                    for fn, waits, inc in ops:
                        for k, v in waits:
                            e.wait_ge(self.sems[k], v)
                        if fn is not None:
                            ins = fn(e)
                            ins.then_inc(self.sems[inc[0]], inc[1])

                getattr(block, handles[eng])(body)
        self.stack.close()

    def stats(self):
        d = {e: len(v) for e, v in self.ops.items()}
        d["nsem"] = self.nsem
        return d


class Builder:
    ST_BYTES = 84 * 1024

    def __init__(self, nc, dr):
        self.nc = nc
        self.dr = dr
        P = self.P = Prog(nc)
        self.aT = P.sbuf("aT", [128, KC, T], BF16)
        self.aT_b = [Buf(f"aT{t}") for t in range(NT)]
        self.big = P.sbuf("big", [128, NT * D], F32)
        self.acc = self.big[:].rearrange("p (t n) -> p t n", t=NT)
        self.acc_b = [Buf(f"acc{t}") for t in range(NT)]
        self.st = P.sbuf("st", [128, self.ST_BYTES // 2], BF16)
        self.ident = P.sbuf("ident", [128, 128], BF16)
        self.ident_b = Buf("ident")
        self.hring = P.sbuf("hring", [128, 3, D], F32)
        self.hr_b = [Buf("hr0"), Buf("hr1"), Buf("hr2")]
        self.hextra = []
        self.gpost = P.sbuf("gpost", [128, D], F32)
        self.gpost_b = Buf("gpost")
        self.gpre = P.sbuf("gpre", [128, D], F32)
        self.gpre_b = Buf("gpre")
        self.gpre2 = None
        self.gpre2_b = None
        self.abf_n = 2
        self.abf = P.sbuf("abf", [128, 2, D], BF16)
        self.abf_b = [Buf("abf0"), Buf("abf1")]
        self.junk = P.sbuf("junk", [128, D], BF16)
        self.junk_b = Buf("junk")
        self.stat = P.sbuf("stat", [128, 64], F32)
        self.stat_b = [Buf(f"stat{i}") for i in range(64)]
        self.stat_i = 0
        self.psum_all = P.psum("all", [128, 8, 512], F32)
        self.bank = [self.psum_all[:, i, :] for i in range(8)]
        self.bank_b = [Buf(f"bank{i}") for i in range(8)]
        self.hD_b = [Buf(f"hD{t}") for t in range(NT)]
        self.h_rd = None
        self.h_wr = None
        self.hcnt = 0
        P.dma("pool", self.ident[:], dr["ident"][:, :], writes=[self.ident_b])
        _orig_barrier = P.barrier

        def _barrier():
            _orig_barrier()
            P.retire([b for _, b in self.hextra])
            self.hextra = []
        P.barrier = _barrier
        self.cm05 = P.sbuf("cm05", [128, 1], F32)
        self.cm05_b = Buf("cm05")
        P.op("pool", lambda e: e.memset(self.cm05[:], -0.5), writes=[self.cm05_b])

    def st_view(self, off_bytes, shape, dtype):
        n = int(np.prod(shape))
        if dtype == BF16:
            a = self.st[:, off_bytes // 2: off_bytes // 2 + n]
            nb = n * 2
        else:
            a = self.st[:, off_bytes // 2: off_bytes // 2 + 2 * n].bitcast(F32)
            nb = n * 4
        assert off_bytes + nb <= self.ST_BYTES, (off_bytes, nb)
        if len(shape) == 2:
            v = a.rearrange("p (a b) -> p a b", a=shape[0])
        elif len(shape) == 3:
            v = a.rearrange("p (a b c) -> p a b c", a=shape[0], b=shape[1])
        else:
            v = a
        return v, off_bytes + nb

    def set_hextra(self, off_bytes, n):
        self.hextra = []
        for k in range(n):
            v, off_bytes = self.st_view(off_bytes, [1, D], F32)
            self.hextra.append((v[:, 0, :], Buf(f"hx{self.hcnt}_{k}")))

    def new_stat(self):
        i = self.stat_i % 64
        self.stat_i += 1
        return self.stat[:, i:i + 1], self.stat_b[i]

    def load_g(self, which, idx):
        tile, b = {"post": (self.gpost, self.gpost_b), "pre": (self.gpre, self.gpre_b), "pre2": (self.gpre2, self.gpre2_b)}[which]
        i, j = idx
        self.P.dma("sp", tile[:], self.dr["norm_g"][i, j, :].partition_broadcast(128), writes=[b])

    def rstd_of(self, src_ap, src_bufs, n):
        P = self.P
        ss, ss_b = self.new_stat()
        P.op("act", lambda e: e.activation(out=self.junk[:, 0:n], in_=src_ap, func=AF.Square, scale=float(n) ** -0.5, accum_out=ss),
             reads=list(src_bufs), writes=[self.junk_b, ss_b])
        r, r_b = self.new_stat()
        P.op("pool", lambda e: e.tensor_scalar(out=r, in0=ss, scalar1=EPS, scalar2=None, op0=ALU.add),
             reads=[ss_b], writes=[r_b])
        r2, r2_b = self.new_stat()
        P.op("pool", lambda e: e.tensor_tensor(out=r2, in0=r, in1=self.cm05[:], op=ALU.pow),
             reads=[r_b, self.cm05_b], writes=[r2_b])
        return r2, r2_b

    def h_load(self, t, src=None):
        P = self.P
        slots = [(self.hring[:, k, :], self.hr_b[k]) for k in range(3)] + list(self.hextra)
        hs, hb = slots[self.hcnt % len(slots)]
        self.hcnt += 1
        src = self.h_rd if src is None else src
        P.dma("sp", hs, src[t * 128:(t + 1) * 128, :], reads=[self.hD_b[t]], writes=[hb])
        return hs, hb

    def h_store(self, t, hs, hb, dst=None):
        dst = self.h_wr if dst is None else dst
        return self.P.dma("sp", dst[t * 128:(t + 1) * 128, :], hs, reads=[hb], writes=[self.hD_b[t]], chan=hb)

    def pre_norm(self, t, hs, hb, tbank=7):
        P = self.P
        r, r_b = self.rstd_of(hs, [hb], D)
        s = t % 2
        ab, ab_b = self.abf[:, s, :], self.abf_b[s]
        P.op("dve", lambda e: e.scalar_tensor_tensor(out=ab, in0=hs, scalar=r, in1=self.gpre[:], op0=ALU.mult, op1=ALU.mult),
             reads=[hb, r_b, self.gpre_b], writes=[ab_b])
        pst = self.bank[tbank][:].bitcast(BF16).rearrange("p (k n) -> p k n", k=KC)
        for kc in range(KC):
            P.op("pe", lambda e, kc=kc: e.transpose(out=pst[:, kc, :], in_=self.abf[:, s, kc * 128:(kc + 1) * 128], identity=self.ident[:]),
                 reads=[ab_b, self.ident_b], writes=[self.bank_b[tbank]])
        P.op("act", lambda e: e.copy(out=self.aT[:, :, t * 128:(t + 1) * 128], in_=pst),
             reads=[self.bank_b[tbank]], writes=[self.aT_b[t]])

    def boundary(self, t, src_ap, src_b, dst_ap, dst_b, do_pre=True, tbank=7):
        P = self.P
        r, r_b = self.rstd_of(src_ap, src_b, D)
        P.op("dve", lambda e: e.scalar_tensor_tensor(out=dst_ap, in0=src_ap, scalar=r, in1=self.gpost[:], op0=ALU.mult, op1=ALU.mult),
             reads=list(src_b) + [r_b, self.gpost_b], writes=[dst_b])
        hs, hb = self.h_load(t)
        P.op("pool", lambda e: e.tensor_tensor(out=hs, in0=hs, in1=dst_ap, op=ALU.add), reads=[dst_b], writes=[hb])
        tok = self.h_store(t, hs, hb)
        if do_pre:
            self.pre_norm(t, hs, hb, tbank=tbank)
        return tok

    def stage_barrier(self):
        self.P.barrier()
        self.hextra = []

    def first_norm(self):
        for t in range(NT):
            hs, hb = self.h_load(t)
            self.pre_norm(t, hs, hb)

    def wload(self, dst_ap, src_ap, buf):
        return self.P.dma("pool", dst_ap, src_ap, writes=[buf])

    def ffn(self, i):
        P = self.P
        P.barrier()
        w_in = self.dr["ffn_w_in"][i]
        w_out = self.dr["ffn_w_out"][i]
        off = 0
        winA, winA_b = [], []
        for s in range(3):
            v, off = self.st_view(off, [KC, 512], BF16)
            winA.append(v)
            winA_b.append(Buf(f"f{i}winA{s}"))
        woutB, woutB_b = [], []
        for s in range(2):
            v, off = self.st_view(off, [4, D], BF16)
            woutB.append(v)
            woutB_b.append(Buf(f"f{i}woutB{s}"))
        hidT, hidT_b = [], []
        for s in range(2):
            v, off = self.st_view(off, [4, T], BF16)
            hidT.append(v)
            hidT_b.append([[Buf(f"f{i}hid{s}_{jl}_{tg}") for tg in range(4)] for jl in range(4)])
        sil, sil_b = [], []
        for s in range(2):
            v, off = self.st_view(off, [1, 512], F32)
            sil.append(v)
            sil_b.append(Buf(f"f{i}sil{s}"))
        nchunk = FFN_H // 128
        ngrp = (nchunk + 3) // 4
        unit = 0
        for pi in range(nchunk // 2):
            ws, wb = winA[pi % 3], winA_b[pi % 3]
            self.wload(ws[:, :, 0:256], w_in[:, pi * 256:(pi + 1) * 256].rearrange("(kc p) n -> p kc n", p=128), wb)
            self.wload(ws[:, :, 256:512], w_in[:, FFN_H + pi * 256:FFN_H + (pi + 1) * 256].rearrange("(kc p) n -> p kc n", p=128), wb)
            for jj in range(2):
                j = 2 * pi + jj
                grp, jl = j // 4, j % 4
                hs_, hs_b = hidT[grp % 2], hidT_b[grp % 2]
                for tg in range(4):
                    bg, bu = 2 * (unit % 2), 2 * (unit % 2) + 1
                    sl, slb = sil[unit % 2], sil_b[unit % 2]
                    unit += 1
                    for (bk, coff) in ((bg, jj * 128), (bu, 256 + jj * 128)):
                        for kc in range(KC):
                            P.op("pe", lambda e, bk=bk, coff=coff, kc=kc, tg=tg, ws=ws: e.matmul(
                                self.bank[bk][:], lhsT=ws[:, kc, coff:coff + 128], rhs=self.aT[:, kc, tg * 512:(tg + 1) * 512],
                                start=(kc == 0), stop=(kc == KC - 1)),
                                reads=[wb] + self.aT_b[tg * 4:(tg + 1) * 4], writes=[self.bank_b[bk]])
                    P.op("act", lambda e, bg=bg, sl=sl: e.activation(out=sl[:, 0, :], in_=self.bank[bg][:], func=AF.Silu),
                         reads=[self.bank_b[bg]], writes=[slb])
                    P.op("dve", lambda e, bu=bu, sl=sl, hs_=hs_, jl=jl, tg=tg: e.tensor_tensor(
                        out=hs_[:, jl, tg * 512:(tg + 1) * 512], in0=sl[:, 0, :], in1=self.bank[bu][:], op=ALU.mult),
                        reads=[slb, self.bank_b[bu]], writes=[hs_b[jl][tg]])
                if jl == 3 or j == nchunk - 1:
                    n = jl + 1
                    wo, wob = woutB[grp % 2], woutB_b[grp % 2]
                    self.wload(wo[:, 0:n, :], w_out[grp * 512:grp * 512 + n * 128, :].rearrange("(jl p) n -> p jl n", p=128), wob)
                    for t in range(NT):
                        for half in range(2):
                            bk = 4 + 2 * (t % 2) + half
                            for q in range(n):
                                P.op("pe", lambda e, bk=bk, q=q, t=t, half=half, hs_=hs_, wo=wo, n=n: e.matmul(
                                    self.bank[bk][:], lhsT=hs_[:, q, t * 128:(t + 1) * 128], rhs=wo[:, q, half * 512:(half + 1) * 512],
                                    start=(q == 0), stop=(q == n - 1)),
                                    reads=[wob, hs_b[q][t // 4]], writes=[self.bank_b[bk]])
                            dst = self.acc[:, t, half * 512:(half + 1) * 512]
                            if grp == 0:
                                P.op("act", lambda e, bk=bk, dst=dst: e.copy(out=dst, in_=self.bank[bk][:]),
                                     reads=[self.bank_b[bk]], writes=[self.acc_b[t]])
                            else:
                                P.op("dve", lambda e, bk=bk, dst=dst: e.tensor_tensor(out=dst, in0=dst, in1=self.bank[bk][:], op=ALU.add),
                                     reads=[self.bank_b[bk]], writes=[self.acc_b[t]])
        assert grp == ngrp - 1

    def ple(self, i, do_pre=True):
        P = self.P
        P.barrier()
        off = 0
        Wg, off = self.st_view(off, [KC, D], BF16)
        Wg_b = Buf(f"p{i}Wg")
        Wp, off = self.st_view(off, [2, D], BF16)
        Wp_b = Buf(f"p{i}Wp")
        pb, pb_b, pT, pT_b, sg, sg_b = [], [], [], [], [], []
        for s in range(2):
            v, off = self.st_view(off, [1, 256], BF16)
            pb.append(v)
            pb_b.append(Buf(f"p{i}pb{s}"))
            v, off = self.st_view(off, [2, 128], BF16)
            pT.append(v)
            pT_b.append(Buf(f"p{i}pT{s}"))
            v, off = self.st_view(off, [2, 512], F32)
            sg.append(v)
            sg_b.append(Buf(f"p{i}sg{s}"))
        self.wload(Wg, self.dr["ple_w_gate"][i].rearrange("(kc p) n -> p kc n", p=128), Wg_b)
        self.wload(Wp, self.dr["ple_w_proj"][i].rearrange("(kc p) n -> p kc n", p=128), Wp_b)
        toks = []
        for t in range(NT):
            s = t % 2
            self.wload(pb[s][:, 0, :], self.dr["p"][i, t * 128:(t + 1) * 128, :], pb_b[s])
            ptb = 6
            pst = self.bank[ptb][:].bitcast(BF16)[:, 0:256].rearrange("p (k n) -> p k n", k=2)
            for kc in range(2):
                P.op("pe", lambda e, kc=kc, s=s, pst=pst: e.transpose(out=pst[:, kc, :], in_=pb[s][:, 0, kc * 128:(kc + 1) * 128], identity=self.ident[:]),
                     reads=[pb_b[s], self.ident_b], writes=[self.bank_b[ptb]])
            P.op("dve", lambda e, s=s, pst=pst: e.tensor_copy(out=pT[s], in_=pst), reads=[self.bank_b[ptb]], writes=[pT_b[s]])
            for half in range(2):
                u = 2 * t + half
                bG, bE = u % 3, 3 + u % 3
                for kc in range(KC):
                    P.op("pe", lambda e, kc=kc, half=half, bG=bG, t=t: e.matmul(
                        self.bank[bG][:], lhsT=self.aT[:, kc, t * 128:(t + 1) * 128], rhs=Wg[:, kc, half * 512:(half + 1) * 512],
                        start=(kc == 0), stop=(kc == KC - 1)), reads=[Wg_b, self.aT_b[t]], writes=[self.bank_b[bG]])
                for kc in range(2):
                    P.op("pe", lambda e, kc=kc, half=half, bE=bE, s=s: e.matmul(
                        self.bank[bE][:], lhsT=pT[s][:, kc, :], rhs=Wp[:, kc, half * 512:(half + 1) * 512],
                        start=(kc == 0), stop=(kc == 1)), reads=[Wp_b, pT_b[s]], writes=[self.bank_b[bE]])
                P.op("act", lambda e, half=half, bG=bG, s=s: e.activation(out=sg[s][:, half, :], in_=self.bank[bG][:], func=AF.Sigmoid),
                     reads=[self.bank_b[bG]], writes=[sg_b[s]])
                P.op("dve", lambda e, half=half, bE=bE, s=s: e.tensor_tensor(out=sg[s][:, half, :], in0=sg[s][:, half, :], in1=self.bank[bE][:], op=ALU.mult),
                     reads=[self.bank_b[bE]], writes=[sg_b[s]])
            hs, hb = self.h_load(t)
            P.op("pool", lambda e, hs=hs, s=s: e.tensor_tensor(out=hs, in0=hs, in1=sg[s].rearrange("p a b -> p (a b)"), op=ALU.add),
                 reads=[sg_b[s]], writes=[hb])
            toks.append(self.h_store(t, hs, hb))
            if do_pre:
                self.pre_norm(t, hs, hb)
        return toks

    def ffn_stage(self, i):
        self.load_g("post", (i, 3))
        self.load_g("pre", (i, 4))
        self.ffn(i)
        toks = []
        for t in range(NT):
            a = self.acc[:, t, :]
            toks.append(self.boundary(t, a, [self.acc_b[t]], a, self.acc_b[t]))
        return toks

    def ple_stage(self, i, last):
        if not last:
            self.load_g("pre", (i + 1, 0))
        return self.ple(i, do_pre=not last)


def _ret_emit(self, j, pre):
    P = self.P
    P.barrier()
    dr = self.dr
    w_in = dr["ret_w_in"][j]
    w_out = dr["ret_w_out"][j]
    off = 0
    W, off = self.st_view(off, [KC, 1536], BF16)
    W_b = {k: Buf(f"r{j}{pre}W{k}") for k in "qkvg"}
    wo, off = self.st_view(off, [4, D], BF16)
    wo_b = Buf(f"r{j}{pre}wo")
    cs, cs_b = [], []
    for s in range(2):
        v, off = self.st_view(off, [2, 512], F32)
        cs.append(v)
        cs_b.append(Buf(f"r{j}{pre}cs{s}"))
    qT, off = self.st_view(off, [2, 512], BF16)
    kT, off = self.st_view(off, [2, 512], BF16)
    qdT, off = self.st_view(off, [2, 512], BF16)
    qT_b, kT_b, qdT_b = Buf("qT"), Buf("kT"), Buf("qdT")
    rt, off = self.st_view(off, [4, 512], F32)
    rt_b = [Buf(f"rt{x}") for x in range(4)]
    qdec, off = self.st_view(off, [1, 512], F32)
    qdec_b = Buf("qdec")
    v_sb, sg, sm, kd, gy, gyT = [], [], [], [], [], []
    v_b, sg_b, sm_b, kd_b, gy_b, gyT_b = [], [], [], [], [], []
    for s in range(2):
        a, off = self.st_view(off, [1, 512], BF16); v_sb.append(a); v_b.append(Buf(f"v{s}"))
        a, off = self.st_view(off, [1, 512], BF16); sg.append(a); sg_b.append(Buf(f"sg{s}"))
        a, off = self.st_view(off, [1, 128], BF16); sm.append(a); sm_b.append(Buf(f"sm{s}"))
        a, off = self.st_view(off, [1, 256], BF16); kd.append(a); kd_b.append(Buf(f"kd{s}"))
        a, off = self.st_view(off, [1, 512], BF16); gy.append(a); gy_b.append(Buf(f"gy{s}"))
        a, off = self.st_view(off, [4, 128], BF16); gyT.append(a); gyT_b.append(Buf(f"gyT{s}"))
    maskT, off = self.st_view(off, [RET_H, 128], F32)
    maskT_b = Buf("maskT")
    kdec, off = self.st_view(off, [1, RET_H], F32)
    kdec_b = Buf("kdec")
    coef, off = self.st_view(off, [1, NCORES * RET_H], F32)
    coef_b = Buf("coef")
    state, off = self.st_view(off, [2, 512], F32)
    state_b = Buf("state")
    sbf, off = self.st_view(off, [2, 512], BF16)
    sbf_b = Buf("sbf")
    P.dma("sp", maskT, dr["ret_maskT"].rearrange("h m n -> m h n"), writes=[maskT_b])
    P.dma("sp", kdec[:, 0, :], dr["ret_kdec"][:, :], writes=[kdec_b])
    P.dma("sp", coef[:, 0, :], dr["ret_coef"][:, :], writes=[coef_b])
    bk = self.bank
    bb = self.bank_b
    b6 = bk[6][:]
    sT_ps = b6[:, 0:128]
    sT_pb = Buf("b6a")
    kdT_ps = b6[:, 128:256].bitcast(BF16).rearrange("p (a b) -> p a b", a=2)
    kdT_pb = Buf("b6b")
    gyT_ps = b6[:, 256:512].bitcast(BF16).rearrange("p (a b) -> p a b", a=4)
    gyT_pb = Buf("b6c")
    gam = [1.0 - 2.0 ** (-5.0 - h) for h in range(RET_H)]

    for hd in range(RET_H):
        cd = gam[hd] ** 128
        cols = {"q": (hd * 256, 256, 0), "k": (1024 + hd * 256, 256, 256), "v": (2048 + hd * 512, 512, 512), "g": (4096 + hd * 512, 512, 1024)}
        for k in ("k", "v") if pre else ("q", "k", "v", "g"):
            c0, n, o = cols[k]
            self.wload(W[:, :, o:o + n], w_in[:, c0:c0 + n].rearrange("(kc p) n -> p kc n", p=128), W_b[k])
        if not pre:
            self.wload(wo, w_out[hd * 512:(hd + 1) * 512, :].rearrange("(q p) n -> p q n", p=128), wo_b)
            P.dma("sp", qdec[:, 0, :], dr["ret_qdec"][hd], writes=[qdec_b])
        if pre:
            P.op("pool", lambda e: e.memset(state, 0.0), writes=[state_b])
        else:
            tmp = rt[:, 0:2, :]
            for r in range(NCORES):
                P.dma("sp", tmp, dr["L_all"][r, hd].rearrange("(fc p) n -> p fc n", p=128), writes=[rt_b[0], rt_b[1]])
                cf = coef[:, 0, r * RET_H + hd:r * RET_H + hd + 1]
                if r == 0:
                    P.op("dve", lambda e, cf=cf: e.tensor_scalar(out=state, in0=tmp, scalar1=cf, scalar2=None, op0=ALU.mult),
                         reads=[rt_b[0], rt_b[1], coef_b], writes=[state_b])
                else:
                    P.op("dve", lambda e, cf=cf: e.scalar_tensor_tensor(out=state, in0=tmp, scalar=cf, in1=state, op0=ALU.mult, op1=ALU.add),
                         reads=[rt_b[0], rt_b[1], coef_b], writes=[state_b])
        P.op("act", lambda e: e.copy(out=sbf, in_=state), reads=[state_b], writes=[sbf_b])

        for tg in range(4):
            cst, cst_b = cs[tg % 2], cs_b[tg % 2]
            P.dma("sp", cst[:, 0, :], dr["rope_cos"][:, tg * 512:(tg + 1) * 512], writes=[cst_b])
            P.dma("sp", cst[:, 1, :], dr["rope_sin"][:, tg * 512:(tg + 1) * 512], writes=[cst_b])
            for which in ("k",) if pre else ("q", "k"):
                o = cols[which][2]
                b0 = 0 if which == "q" else 2
                for fc in range(2):
                    for kc in range(KC):
                        P.op("pe", lambda e, fc=fc, kc=kc, o=o, b0=b0, tg=tg: e.matmul(
                            bk[b0 + fc][:], lhsT=W[:, kc, o + fc * 128:o + (fc + 1) * 128], rhs=self.aT[:, kc, tg * 512:(tg + 1) * 512],
                            start=(kc == 0), stop=(kc == KC - 1)),
                            reads=[W_b[which]] + self.aT_b[tg * 4:(tg + 1) * 4], writes=[bb[b0 + fc]])
                dstT, dst_b = (qT, qT_b) if which == "q" else (kT, kT_b)
                t1, t2 = bk[b0][:], bk[b0 + 1][:]
                P.op("dve", lambda e, t1=t1, cst=cst: e.tensor_tensor(out=rt[:, 0, :], in0=t1, in1=cst[:, 0, :], op=ALU.mult),
                     reads=[bb[b0], cst_b], writes=[rt_b[0]])
                P.op("dve", lambda e, t2=t2, cst=cst: e.tensor_tensor(out=rt[:, 1, :], in0=t2, in1=cst[:, 1, :], op=ALU.mult),
                     reads=[bb[b0 + 1], cst_b], writes=[rt_b[1]])
                P.op("dve", lambda e, t2=t2, cst=cst: e.tensor_tensor(out=rt[:, 2, :], in0=t2, in1=cst[:, 0, :], op=ALU.mult),
                     reads=[bb[b0 + 1], cst_b], writes=[rt_b[2]])
                P.op("dve", lambda e, t1=t1, cst=cst: e.tensor_tensor(out=rt[:, 3, :], in0=t1, in1=cst[:, 1, :], op=ALU.mult),
                     reads=[bb[b0], cst_b], writes=[rt_b[3]])
                P.op("pool", lambda e, dstT=dstT: e.tensor_tensor(out=dstT[:, 0, :], in0=rt[:, 0, :], in1=rt[:, 1, :], op=ALU.subtract),
                     reads=[rt_b[0], rt_b[1]], writes=[dst_b])
                P.op("pool", lambda e, dstT=dstT: e.tensor_tensor(out=dstT[:, 1, :], in0=rt[:, 2, :], in1=rt[:, 3, :], op=ALU.add),
                     reads=[rt_b[2], rt_b[3]], writes=[dst_b])
                if which == "q":
                    for fc in range(2):
                        P.op("pool", lambda e, fc=fc: e.tensor_tensor(out=qdT[:, fc, :], in0=qT[:, fc, :], in1=qdec[:, 0, :], op=ALU.mult),
                             reads=[qT_b, qdec_b], writes=[qdT_b])
            for cl in range(4):
                c = tg * 4 + cl
                s = c % 2
                tsl = slice(cl * 128, (cl + 1) * 128)
                csl = slice(c * 128, (c + 1) * 128)
                for kc in range(KC):
                    P.op("pe", lambda e, kc=kc, csl=csl: e.matmul(bk[4][:], lhsT=self.aT[:, kc, csl], rhs=W[:, kc, 512:1024],
                                                               start=(kc == 0), stop=(kc == KC - 1)),
                         reads=[W_b["v"], self.aT_b[c]], writes=[bb[4]])
                P.op("act", lambda e, s=s: e.copy(out=v_sb[s][:, 0, :], in_=bk[4][:]), reads=[bb[4]], writes=[v_b[s]])
                if not pre:
                    for kc in range(KC):
                        P.op("pe", lambda e, kc=kc, csl=csl: e.matmul(bk[5][:], lhsT=self.aT[:, kc, csl], rhs=W[:, kc, 1024:1536],
                                                                   start=(kc == 0), stop=(kc == KC - 1)),
                             reads=[W_b["g"], self.aT_b[c]], writes=[bb[5]])
                    P.op("act", lambda e, s=s: e.activation(out=sg[s][:, 0, :], in_=bk[5][:], func=AF.Silu), reads=[bb[5]], writes=[sg_b[s]])
                    for fc in range(2):
                        P.op("pe", lambda e, fc=fc, tsl=tsl: e.matmul(sT_ps, lhsT=kT[:, fc, tsl], rhs=qT[:, fc, tsl], start=(fc == 0), stop=(fc == 1)),
                             reads=[kT_b, qT_b], writes=[sT_pb])
                    P.op("dve", lambda e, s=s, hd=hd: e.tensor_tensor(out=sm[s][:, 0, :], in0=sT_ps, in1=maskT[:, hd, :], op=ALU.mult),
                         reads=[sT_pb, maskT_b], writes=[sm_b[s]])
                for fc in range(2):
                    P.op("pe", lambda e, fc=fc, tsl=tsl: e.transpose(out=kdT_ps[:, fc, :], in_=kT[:, fc, tsl], identity=self.ident[:]),
                         reads=[kT_b, self.ident_b], writes=[kdT_pb])
                P.op("dve", lambda e, s=s, hd=hd: e.tensor_scalar(out=kd[s][:, 0, :], in0=kdT_ps.rearrange("p a b -> p (a b)"),
                                                                 scalar1=kdec[:, 0, hd:hd + 1], scalar2=None, op0=ALU.mult),
                     reads=[kdT_pb, kdec_b], writes=[kd_b[s]])
                if not pre:
                    P.op("pe", lambda e, s=s: e.matmul(bk[7][:], lhsT=sm[s][:, 0, :], rhs=v_sb[s][:, 0, :], start=True, stop=False),
                         reads=[sm_b[s], v_b[s]], writes=[bb[7]])
                    for fc in range(2):
                        P.op("pe", lambda e, fc=fc, tsl=tsl: e.matmul(bk[7][:], lhsT=qdT[:, fc, tsl], rhs=sbf[:, fc, :], start=False, stop=(fc == 1)),
                             reads=[qdT_b, sbf_b], writes=[bb[7]])
                for fc in range(2):
                    P.op("pe", lambda e, fc=fc, s=s: e.matmul(bk[fc][:], lhsT=kd[s][:, 0, fc * 128:(fc + 1) * 128], rhs=v_sb[s][:, 0, :], start=True, stop=True),
                         reads=[kd_b[s], v_b[s]], writes=[bb[fc]])
                for fc in range(2):
                    P.op("dve", lambda e, fc=fc, cd=cd: e.scalar_tensor_tensor(out=state[:, fc, :], in0=state[:, fc, :], scalar=float(cd), in1=bk[fc][:],
                                                                           op0=ALU.mult, op1=ALU.add),
                         reads=[bb[fc]], writes=[state_b])
                P.op("act", lambda e: e.copy(out=sbf, in_=state), reads=[state_b], writes=[sbf_b])
                if not pre:
                    r, r_b = self.rstd_of(bk[7][:], [bb[7]], 512)
                    P.op("dve", lambda e, s=s, r=r: e.scalar_tensor_tensor(out=gy[s][:, 0, :], in0=bk[7][:], scalar=r, in1=sg[s][:, 0, :],
                                                                         op0=ALU.mult, op1=ALU.mult),
                         reads=[bb[7], r_b, sg_b[s]], writes=[gy_b[s]])
                    for q in range(4):
                        P.op("pe", lambda e, q=q, s=s: e.transpose(out=gyT_ps[:, q, :], in_=gy[s][:, 0, q * 128:(q + 1) * 128], identity=self.ident[:]),
                             reads=[gy_b[s], self.ident_b], writes=[gyT_pb])
                    P.op("act", lambda e, s=s: e.copy(out=gyT[s], in_=gyT_ps), reads=[gyT_pb], writes=[gyT_b[s]])
                    for half in range(2):
                        for q in range(4):
                            P.op("pe", lambda e, q=q, s=s, half=half: e.matmul(bk[2 + half][:], lhsT=gyT[s][:, q, :], rhs=wo[:, q, half * 512:(half + 1) * 512],
                                                                              start=(q == 0), stop=(q == 3)),
                                 reads=[gyT_b[s], wo_b], writes=[bb[2 + half]])
                        dst = self.acc[:, c, half * 512:(half + 1) * 512]
                        if hd == 0:
                            P.op("act", lambda e, dst=dst, half=half: e.copy(out=dst, in_=bk[2 + half][:]), reads=[bb[2 + half]], writes=[self.acc_b[c]])
                        else:
                            P.op("dve", lambda e, dst=dst, half=half: e.tensor_tensor(out=dst, in0=dst, in1=bk[2 + half][:], op=ALU.add),
                                 reads=[bb[2 + half]], writes=[self.acc_b[c]])
        if pre:
            P.dma("sp", dr["L_out"][hd].rearrange("(fc p) n -> p fc n", p=128), state, reads=[state_b], chan=state_b)
    if pre:
        return [(state_b.dsem, state_b.dcount)]
    return []


Builder.ret_emit = _ret_emit


def _ret_emit2(self, j, pre, boundary=False):
    P = self.P
    P.barrier()
    dr = self.dr
    w_in = dr["ret_w_in"][j]
    w_out = dr["ret_w_out"][j]
    off = 0
    W, off = self.st_view(off, [KC, 1536], BF16)
    W_b = {k: Buf(f"r{j}{pre}W{k}") for k in "qkvg"}
    wo, off = self.st_view(off, [4, D], BF16)
    wo_b = Buf(f"r{j}{pre}wo")
    cs, cs_b = [], []
    for s in range(2):
        v, off = self.st_view(off, [2, 512], F32)
        cs.append(v)
        cs_b.append(Buf(f"r{j}{pre}cs{s}"))
    qT, off = self.st_view(off, [2, 512], BF16)
    kT, off = self.st_view(off, [2, 512], BF16)
    qdT, off = self.st_view(off, [2, 512], BF16)
    qT_b, kT_b, qdT_b = Buf("qT"), Buf("kT"), Buf("qdT")
    rt, off = self.st_view(off, [4, 512], F32)
    rt_b = [Buf(f"rt{x}") for x in range(4)]
    qdec, off = self.st_view(off, [1, 512], F32)
    qdec_b = Buf("qdec")
    v_sb, sg, sm, kd, gy, gyT = [], [], [], [], [], []
    v_b, sg_b, sm_b, kd_b, gy_b, gyT_b = [], [], [], [], [], []
    for s in range(2):
        a, off = self.st_view(off, [1, 512], BF16); v_sb.append(a); v_b.append(Buf(f"v{s}"))
        a, off = self.st_view(off, [1, 512], BF16); sg.append(a); sg_b.append(Buf(f"sg{s}"))
        a, off = self.st_view(off, [1, 128], BF16); sm.append(a); sm_b.append(Buf(f"sm{s}"))
        a, off = self.st_view(off, [1, 256], BF16); kd.append(a); kd_b.append(Buf(f"kd{s}"))
        a, off = self.st_view(off, [1, 512], BF16); gy.append(a); gy_b.append(Buf(f"gy{s}"))
        a, off = self.st_view(off, [4, 128], BF16); gyT.append(a); gyT_b.append(Buf(f"gyT{s}"))
    maskT, off = self.st_view(off, [RET_H, 128], F32)
    maskT_b = Buf("maskT")
    kdec, off = self.st_view(off, [1, RET_H], F32)
    kdec_b = Buf("kdec")
    coef, off = self.st_view(off, [1, L_RANKS * RET_H], F32)
    coef_b = Buf("coef")
    state, off = self.st_view(off, [2, 512], F32)
    state_b = Buf("state")
    sbf, off = self.st_view(off, [2, 512], BF16)
    sbf_b = Buf("sbf")
    self.set_hextra(off, max(0, (self.ST_BYTES - off) // 4096))
    P.dma("sp", maskT, dr["ret_maskT"].rearrange("h m n -> m h n"), writes=[maskT_b])
    P.dma("sp", kdec[:, 0, :], dr["ret_kdec"][:, :], writes=[kdec_b])
    P.dma("sp", coef[:, 0, :], dr["ret_coef"][:, :], writes=[coef_b])
    bk = self.bank
    bb = self.bank_b
    b6 = bk[6][:]
    sT_ps = b6[:, 0:128]
    sT_pb = bb[6]
    kdT_ps = b6[:, 128:256].bitcast(BF16).rearrange("p (a b) -> p a b", a=2)
    kdT_pb = bb[6]
    gyT_ps = bk[7][:, 256:512].bitcast(BF16).rearrange("p (a b) -> p a b", a=4)
    gyT_pb = bb[7]
    gam = [1.0 - 2.0 ** (-5.0 - h) for h in range(RET_H)]
    btoks = []

    for hd in range(RET_H):
        cd = gam[hd] ** 128
        cols = {"q": (hd * 256, 256, 0), "k": (1024 + hd * 256, 256, 256), "v": (2048 + hd * 512, 512, 512), "g": (4096 + hd * 512, 512, 1024)}
        for k in ("k", "v") if pre else ("q", "k", "v", "g"):
            c0, n, o = cols[k]
            self.wload(W[:, :, o:o + n], w_in[:, c0:c0 + n].rearrange("(kc p) n -> p kc n", p=128), W_b[k])
        if not pre:
            self.wload(wo, w_out[hd * 512:(hd + 1) * 512, :].rearrange("(q p) n -> p q n", p=128), wo_b)
            P.dma("sp", qdec[:, 0, :], dr["ret_qdec"][hd], writes=[qdec_b])
        if pre:
            P.op("pool", lambda e: e.memset(state, 0.0), writes=[state_b])
        else:
            tmp = rt[:, 0:2, :]
            for r in range(L_RANKS):
                P.dma("sp", tmp, dr["L_all"][r, hd].rearrange("(fc p) n -> p fc n", p=128), writes=[rt_b[0], rt_b[1]])
                cf = coef[:, 0, r * RET_H + hd:r * RET_H + hd + 1]
                if r == 0:
                    P.op("dve", lambda e, cf=cf: e.tensor_scalar(out=state, in0=tmp, scalar1=cf, scalar2=None, op0=ALU.mult),
                         reads=[rt_b[0], rt_b[1], coef_b], writes=[state_b])
                else:
                    P.op("dve", lambda e, cf=cf: e.scalar_tensor_tensor(out=state, in0=tmp, scalar=cf, in1=state, op0=ALU.mult, op1=ALU.add),
                         reads=[rt_b[0], rt_b[1], coef_b], writes=[state_b])
        P.op("act", lambda e: e.copy(out=sbf, in_=state), reads=[state_b], writes=[sbf_b])

        def qk_proj(tg, hd=hd, cols=cols):
            cst, cst_b = cs[tg % 2], cs_b[tg % 2]
            P.dma("sp", cst[:, 0, :], dr["rope_cos"][:, tg * 512:(tg + 1) * 512], writes=[cst_b])
            P.dma("sp", cst[:, 1, :], dr["rope_sin"][:, tg * 512:(tg + 1) * 512], writes=[cst_b])
            for which in ("k",) if pre else ("q", "k"):
                o = cols[which][2]
                b0 = 0 if which == "q" else 2
                for fc in range(2):
                    for kc in range(KC):
                        P.op("pe", lambda e, fc=fc, kc=kc, o=o, b0=b0, tg=tg: e.matmul(
                            bk[b0 + fc][:], lhsT=W[:, kc, o + fc * 128:o + (fc + 1) * 128], rhs=self.aT[:, kc, tg * 512:(tg + 1) * 512],
                            start=(kc == 0), stop=(kc == KC - 1)),
                            reads=[W_b[which]] + self.aT_b[tg * 4:(tg + 1) * 4], writes=[bb[b0 + fc]])
                dstT, dst_b = (qT, qT_b) if which == "q" else (kT, kT_b)
                t1, t2 = bk[b0][:], bk[b0 + 1][:]
                P.op("dve", lambda e, t1=t1, cst=cst: e.tensor_tensor(out=rt[:, 0, :], in0=t1, in1=cst[:, 0, :], op=ALU.mult),
                     reads=[bb[b0], cst_b], writes=[rt_b[0]])
                P.op("dve", lambda e, t2=t2, cst=cst: e.tensor_tensor(out=rt[:, 1, :], in0=t2, in1=cst[:, 1, :], op=ALU.mult),
                     reads=[bb[b0 + 1], cst_b], writes=[rt_b[1]])
                P.op("dve", lambda e, t2=t2, cst=cst: e.tensor_tensor(out=rt[:, 2, :], in0=t2, in1=cst[:, 0, :], op=ALU.mult),
                     reads=[bb[b0 + 1], cst_b], writes=[rt_b[2]])
                P.op("dve", lambda e, t1=t1, cst=cst: e.tensor_tensor(out=rt[:, 3, :], in0=t1, in1=cst[:, 1, :], op=ALU.mult),
                     reads=[bb[b0], cst_b], writes=[rt_b[3]])
                P.op("pool", lambda e, dstT=dstT: e.tensor_tensor(out=dstT[:, 0, :], in0=rt[:, 0, :], in1=rt[:, 1, :], op=ALU.subtract),
                     reads=[rt_b[0], rt_b[1]], writes=[dst_b])
                P.op("pool", lambda e, dstT=dstT: e.tensor_tensor(out=dstT[:, 1, :], in0=rt[:, 2, :], in1=rt[:, 3, :], op=ALU.add),
                     reads=[rt_b[2], rt_b[3]], writes=[dst_b])
                if which == "q":
                    for fc in range(2):
                        P.op("pool", lambda e, fc=fc: e.tensor_tensor(out=qdT[:, fc, :], in0=qT[:, fc, :], in1=qdec[:, 0, :], op=ALU.mult),
                             reads=[qT_b, qdec_b], writes=[qdT_b])
        def ph0(c, hd=hd, cd=cd, qk_proj=qk_proj):
            if True:
                tg, cl = c // 4, c % 4
                if cl == 0:
                    qk_proj(tg)
                s = c % 2
                tsl = slice(cl * 128, (cl + 1) * 128)
                csl = slice(c * 128, (c + 1) * 128)
                for kc in range(KC):
                    P.op("pe", lambda e, kc=kc, csl=csl: e.matmul(bk[4][:], lhsT=self.aT[:, kc, csl], rhs=W[:, kc, 512:1024],
                                                               start=(kc == 0), stop=(kc == KC - 1)),
                         reads=[W_b["v"], self.aT_b[c]], writes=[bb[4]])
                P.op("act", lambda e, s=s: e.copy(out=v_sb[s][:, 0, :], in_=bk[4][:]), reads=[bb[4]], writes=[v_b[s]])
                if not pre:
                    for kc in range(KC):
                        P.op("pe", lambda e, kc=kc, csl=csl: e.matmul(bk[5][:], lhsT=self.aT[:, kc, csl], rhs=W[:, kc, 1024:1536],
                                                                   start=(kc == 0), stop=(kc == KC - 1)),
                             reads=[W_b["g"], self.aT_b[c]], writes=[bb[5]])
                    P.op("act", lambda e, s=s: e.activation(out=sg[s][:, 0, :], in_=bk[5][:], func=AF.Silu), reads=[bb[5]], writes=[sg_b[s]])
                    for fc in range(2):
                        P.op("pe", lambda e, fc=fc, tsl=tsl: e.matmul(sT_ps, lhsT=kT[:, fc, tsl], rhs=qT[:, fc, tsl], start=(fc == 0), stop=(fc == 1)),
                             reads=[kT_b, qT_b], writes=[sT_pb])
                    P.op("dve", lambda e, s=s, hd=hd: e.tensor_tensor(out=sm[s][:, 0, :], in0=sT_ps, in1=maskT[:, hd, :], op=ALU.mult),
                         reads=[sT_pb, maskT_b], writes=[sm_b[s]])
                for fc in range(2):
                    P.op("pe", lambda e, fc=fc, tsl=tsl: e.transpose(out=kdT_ps[:, fc, :], in_=kT[:, fc, tsl], identity=self.ident[:]),
                         reads=[kT_b, self.ident_b], writes=[kdT_pb])
                P.op("dve", lambda e, s=s, hd=hd: e.tensor_scalar(out=kd[s][:, 0, :], in0=kdT_ps.rearrange("p a b -> p (a b)"),
                                                                 scalar1=kdec[:, 0, hd:hd + 1], scalar2=None, op0=ALU.mult),
                     reads=[kdT_pb, kdec_b], writes=[kd_b[s]])
        def ph1(c, hd=hd, cd=cd):
            if True:
                tg, cl = c // 4, c % 4
                s = c % 2
                tsl = slice(cl * 128, (cl + 1) * 128)
                csl = slice(c * 128, (c + 1) * 128)
                if not pre:
                    P.op("pe", lambda e, s=s: e.matmul(bk[7][:], lhsT=sm[s][:, 0, :], rhs=v_sb[s][:, 0, :], start=True, stop=False),
                         reads=[sm_b[s], v_b[s]], writes=[bb[7]])
                    for fc in range(2):
                        P.op("pe", lambda e, fc=fc, tsl=tsl: e.matmul(bk[7][:], lhsT=qdT[:, fc, tsl], rhs=sbf[:, fc, :], start=False, stop=(fc == 1)),
                             reads=[qdT_b, sbf_b], writes=[bb[7]])
                for fc in range(2):
                    P.op("pe", lambda e, fc=fc, s=s: e.matmul(bk[fc][:], lhsT=kd[s][:, 0, fc * 128:(fc + 1) * 128], rhs=v_sb[s][:, 0, :], start=True, stop=True),
                         reads=[kd_b[s], v_b[s]], writes=[bb[fc]])
                for fc in range(2):
                    P.op("dve", lambda e, fc=fc, cd=cd: e.scalar_tensor_tensor(out=state[:, fc, :], in0=state[:, fc, :], scalar=float(cd), in1=bk[fc][:],
                                                                           op0=ALU.mult, op1=ALU.add),
                         reads=[bb[fc]], writes=[state_b])
                if not pre:
                    P.op("act", lambda e: e.copy(out=sbf, in_=state), reads=[state_b], writes=[sbf_b])
                if not pre:
                    r, r_b = self.rstd_of(bk[7][:], [bb[7]], 512)
                    P.op("dve", lambda e, s=s, r=r: e.scalar_tensor_tensor(out=gy[s][:, 0, :], in0=bk[7][:], scalar=r, in1=sg[s][:, 0, :],
                                                                         op0=ALU.mult, op1=ALU.mult),
                         reads=[bb[7], r_b, sg_b[s]], writes=[gy_b[s]])
        def ph2(c, hd=hd, cd=cd):
            if True:
                tg, cl = c // 4, c % 4
                s = c % 2
                if not pre:
                    for q in range(4):
                        P.op("pe", lambda e, q=q, s=s: e.transpose(out=gyT_ps[:, q, :], in_=gy[s][:, 0, q * 128:(q + 1) * 128], identity=self.ident[:]),
                             reads=[gy_b[s], self.ident_b], writes=[gyT_pb])
                    P.op("act", lambda e, s=s: e.copy(out=gyT[s], in_=gyT_ps), reads=[gyT_pb], writes=[gyT_b[s]])
                    for half in range(2):
                        for q in range(4):
                            P.op("pe", lambda e, q=q, s=s, half=half: e.matmul(bk[2 + half][:], lhsT=gyT[s][:, q, :], rhs=wo[:, q, half * 512:(half + 1) * 512],
                                                                              start=(q == 0), stop=(q == 3)),
                                 reads=[gyT_b[s], wo_b], writes=[bb[2 + half]])
                        dst = self.acc[:, c, half * 512:(half + 1) * 512]
                        if hd == 0:
                            P.op("act", lambda e, dst=dst, half=half: e.copy(out=dst, in_=bk[2 + half][:]), reads=[bb[2 + half]], writes=[self.acc_b[c]])
                        else:
                            P.op("dve", lambda e, dst=dst, half=half: e.tensor_tensor(out=dst, in0=dst, in1=bk[2 + half][:], op=ALU.add),
                                 reads=[bb[2 + half]], writes=[self.acc_b[c]])
        phases = [ph0, ph1] if pre else [ph0, ph1, ph2]
        if boundary and (not pre) and hd == RET_H - 1:
            phases = phases + self.boundary_phases_f(lambda t: (self.acc[:, t, :], [self.acc_b[t]]), lambda t: (self.acc[:, t, :], self.acc_b[t]),
                                                   toks=btoks, tbanks=(5,), merge_b=True)
        pipeline(NT, phases)
        if pre:
            P.dma("sp", dr["L_out"][hd].rearrange("(fc p) n -> p fc n", p=128), state, reads=[state_b], chan=state_b)
    if pre:
        return [(state_b.dsem, state_b.dcount)]
    return btoks


Builder.ret_emit = _ret_emit2


def _mixer_boundaries(self, i):
    toks = []
    for t in range(NT):
        a = self.acc[:, t, :]
        toks.append(self.boundary(t, a, [self.acc_b[t]], a, self.acc_b[t]))
    return toks


Builder.mixer_boundaries = _mixer_boundaries


def ret_tables():
    gam = np.array([1.0 - 2.0 ** (-5.0 - h) for h in range(RET_H)], np.float64)
    idx = np.arange(128, dtype=np.float64)
    diff = idx[None, :] - idx[:, None]
    maskT = np.where(diff >= 0, gam[:, None, None] ** np.maximum(diff, 0.0), 0.0) / 16.0
    qdec = gam[:, None] ** (np.tile(idx, 4) + 1.0)[None, :]
    qdec = np.broadcast_to(qdec[:, None, :], (RET_H, 128, 512))
    kdec = (gam[None, :] ** (127.0 - idx[:, None])) / 16.0
    return (np.ascontiguousarray(maskT, dtype=np.float32), np.ascontiguousarray(qdec, dtype=np.float32),
            np.ascontiguousarray(kdec, dtype=np.float32))


def ret_coef(core):
    gam = np.array([1.0 - 2.0 ** (-5.0 - h) for h in range(RET_H)], np.float64)
    b, s = core // 4, core % 4
    c = np.zeros((L_RANKS, RET_H), np.float64)
    for rs in range(L_RANKS):
        if rs < s:
            c[rs] = gam ** (float(T) * (s - 1 - rs))
    return np.ascontiguousarray(np.broadcast_to(c.reshape(1, -1), (128, L_RANKS * RET_H)), dtype=np.float32)


def rope_tables(core):
    s = core % 4
    inv_freq = (np.float32(10000.0) ** (-(np.arange(0, RET_DK, 2, dtype=np.float32) / np.float32(RET_DK)))).astype(np.float32)
    pos = (np.arange(T, dtype=np.float32) + np.float32(s * T)).astype(np.float32)
    ang = (inv_freq[:, None] * pos[None, :]).astype(np.float32)
    return np.cos(ang).astype(np.float32), np.sin(ang).astype(np.float32)


def _gmlp_stage(self, i, j):
    P = self.P
    dr = self.dr
    P.barrier()
    bigb = self.big[:].bitcast(BF16)
    Wv = bigb[:, 0:16384].rearrange("p (k n) -> p k n", k=KC)
    Wu = bigb[:, 16384:32768].rearrange("p (k n) -> p k n", k=KC)
    Wv_b, Wu_b = Buf("gWv"), Buf("gWu")
    off = 0
    Wo, off = self.st_view(off, [16, D], BF16)
    Wo_b = Buf("gWo")
    wsT, off = self.st_view(off, [GM_G, 128], BF16)
    wsT_b = Buf("gwsT")
    gtab, off = self.st_view(off, [1, GM_W], F32)
    btab, off = self.st_view(off, [1, GM_W], F32)
    gtab_b, btab_b = Buf("ggtab"), Buf("gbtab")
    bsT, off = self.st_view(off, [1, GM_G], F32)
    bsT_b = Buf("gbsT")
    cmask, off = self.st_view(off, [1, 128], F32)
    cmask_b = Buf("gcmask")
    off_vf = off
    vf, off = self.st_view(off, [1, GM_W], F32)
    vf_b = Buf("gvf")
    vn, off = self.st_view(off, [1, GM_W], BF16)
    vn_b = Buf("gvn")
    ssb, off = self.st_view(off, [1, GM_W], F32)
    ssb_b = Buf("gssb")
    us, off = self.st_view(off, [1, GM_W], BF16)
    us_b = Buf("gus")
    usT, off = self.st_view(off, [16, 128], BF16)
    usT_b = Buf("gusT")
    gu, gu_b = [], []
    for s in range(2):
        a, off = self.st_view(off, [1, 512], F32)
        gu.append(a)
        gu_b.append(Buf(f"ggu{s}"))
    wsl, off2 = self.st_view(off_vf, [GM_G, 128], BF16)
    wsl_b = vf_b
    w_in = dr["gm_w_in"][j]
    for cb in range(4):
        self.P.dma("pool", Wv[:, :, cb * 512:(cb + 1) * 512], w_in[:, GM_W + cb * 512:GM_W + (cb + 1) * 512].rearrange("(kc p) n -> p kc n", p=128),
                   writes=[Wv_b] + self.acc_b[0:8])
    for cb in range(4):
        self.P.dma("pool", Wu[:, :, cb * 512:(cb + 1) * 512], w_in[:, cb * 512:(cb + 1) * 512].rearrange("(kc p) n -> p kc n", p=128),
                   writes=[Wu_b] + self.acc_b[8:16])
    for q in range(4):
        self.wload(Wo[:, q * 4:(q + 1) * 4, :], dr["gm_w_out"][j][q * 512:(q + 1) * 512, :].rearrange("(kc p) n -> p kc n", p=128), Wo_b)
    self.wload(wsl, dr["gm_w_s"][j].rearrange("g n m -> n g m"), wsl_b)
    P.dma("sp", gtab[:, 0, :], dr["gm_ln_g"][j, :].partition_broadcast(128), writes=[gtab_b])
    P.dma("sp", btab[:, 0, :], dr["gm_ln_b"][j, :].partition_broadcast(128), writes=[btab_b])
    P.dma("sp", bsT[:, 0, :], dr["gm_b_s"][j].rearrange("g n -> n g"), writes=[bsT_b], allow_slow_non_contiguous=True)
    P.dma("sp", cmask[:, 0, :], dr["causalT"][:, :], writes=[cmask_b])
    bk, bb = self.bank, self.bank_b
    tps = bk[0][:].bitcast(BF16)[:, 0:1024].rearrange("p (g n) -> p g n", g=GM_G)
    for g in range(GM_G):
        P.op("pe", lambda e, g=g: e.transpose(out=tps[:, g, :], in_=wsl[:, g, :], identity=self.ident[:]),
             reads=[wsl_b, self.ident_b], writes=[bb[0]])
    for g in range(GM_G):
        P.op("dve", lambda e, g=g: e.tensor_tensor(out=wsT[:, g, :], in0=tps[:, g, :], in1=cmask[:, 0, :], op=ALU.mult),
             reads=[bb[0], cmask_b], writes=[wsT_b])
    toks = []
    for t in range(NT):
        tsl = slice(t * 128, (t + 1) * 128)
        sv, sv_b = [], []
        for cb in range(4):
            for kc in range(KC):
                P.op("pe", lambda e, cb=cb, kc=kc, tsl=tsl: e.matmul(bk[cb][:], lhsT=self.aT[:, kc, tsl], rhs=Wv[:, kc, cb * 512:(cb + 1) * 512],
                                                                     start=(kc == 0), stop=(kc == KC - 1)),
                     reads=[Wv_b, self.aT_b[t]], writes=[bb[cb]])
            a, a_b = self.new_stat()
            sv.append(a)
            sv_b.append(a_b)
            P.op("act", lambda e, cb=cb, a=a: e.activation(out=vf[:, 0, cb * 512:(cb + 1) * 512], in_=bk[cb][:], func=AF.Gelu, accum_out=a),
                 reads=[bb[cb]], writes=[vf_b, a_b])
        sq, sq_b = [], []
        for hh in range(2):
            a, a_b = self.new_stat()
            sq.append(a)
            sq_b.append(a_b)
            P.op("act", lambda e, hh=hh, a=a: e.activation(out=self.junk[:, :], in_=vf[:, 0, hh * 1024:(hh + 1) * 1024], func=AF.Square, accum_out=a),
                 reads=[vf_b], writes=[self.junk_b, a_b])
        m1, m1_b = self.new_stat()
        P.op("pool", lambda e, sv=sv, m1=m1: e.tensor_tensor(out=m1, in0=sv[0], in1=sv[1], op=ALU.add), reads=[sv_b[0], sv_b[1]], writes=[m1_b])
        m2, m2_b = self.new_stat()
        P.op("pool", lambda e, sv=sv, m2=m2: e.tensor_tensor(out=m2, in0=sv[2], in1=sv[3], op=ALU.add), reads=[sv_b[2], sv_b[3]], writes=[m2_b])
        mu, mu_b = self.new_stat()
        P.op("pool", lambda e, m1=m1, m2=m2, mu=mu: e.tensor_tensor(out=mu, in0=m1, in1=m2, op=ALU.add), reads=[m1_b, m2_b], writes=[mu_b])
        P.op("pool", lambda e, mu=mu: e.tensor_scalar(out=mu, in0=mu, scalar1=1.0 / GM_W, scalar2=None, op0=ALU.mult), reads=[], writes=[mu_b])
        e2, e2_b = self.new_stat()
        P.op("pool", lambda e, sq=sq, e2=e2: e.tensor_tensor(out=e2, in0=sq[0], in1=sq[1], op=ALU.add), reads=[sq_b[0], sq_b[1]], writes=[e2_b])
        mm, mm_b = self.new_stat()
        P.op("pool", lambda e, mu=mu, mm=mm: e.tensor_tensor(out=mm, in0=mu, in1=mu, op=ALU.mult), reads=[mu_b], writes=[mm_b])
        var, var_b = self.new_stat()
        P.op("pool", lambda e, e2=e2, var=var: e.tensor_scalar(out=var, in0=e2, scalar1=1.0 / GM_W, scalar2=EPS, op0=ALU.mult, op1=ALU.add),
             reads=[e2_b], writes=[var_b])
        P.op("pool", lambda e, var=var, mm=mm: e.tensor_tensor(out=var, in0=var, in1=mm, op=ALU.subtract), reads=[mm_b], writes=[var_b])
        rs, rs_b = self.new_stat()
        P.op("pool", lambda e, var=var, rs=rs: e.tensor_tensor(out=rs, in0=var, in1=self.cm05[:], op=ALU.pow), reads=[var_b, self.cm05_b], writes=[rs_b])
        P.op("dve", lambda e, mu=mu, rs=rs: e.tensor_scalar(out=vf[:, 0, :], in0=vf[:, 0, :], scalar1=mu, scalar2=rs, op0=ALU.subtract, op1=ALU.mult),
             reads=[mu_b, rs_b], writes=[vf_b])
        P.op("pool", lambda e: e.tensor_tensor(out=vf[:, 0, :], in0=vf[:, 0, :], in1=gtab[:, 0, :], op=ALU.mult), reads=[gtab_b], writes=[vf_b])
        P.op("dve", lambda e: e.tensor_tensor(out=vn[:, 0, :], in0=vf[:, 0, :], in1=btab[:, 0, :], op=ALU.add), reads=[vf_b, btab_b], writes=[vn_b])
        for g in range(GM_G):
            bkx = 4 + g // 2
            P.op("pe", lambda e, g=g, bkx=bkx: e.matmul(bk[bkx][:, (g % 2) * 256:(g % 2 + 1) * 256], lhsT=wsT[:, g, :], rhs=vn[:, 0, g * 256:(g + 1) * 256],
                                                        start=True, stop=True),
                 reads=[wsT_b, vn_b], writes=[bb[bkx]])
        for g in range(GM_G):
            bkx = 4 + g // 2
            P.op("dve", lambda e, g=g, bkx=bkx: e.tensor_scalar(out=ssb[:, 0, g * 256:(g + 1) * 256], in0=bk[bkx][:, (g % 2) * 256:(g % 2 + 1) * 256],
                                                                scalar1=bsT[:, 0, g:g + 1], scalar2=None, op0=ALU.add),
                 reads=[bb[bkx], bsT_b], writes=[ssb_b])
        for cb in range(4):
            for kc in range(KC):
                P.op("pe", lambda e, cb=cb, kc=kc, tsl=tsl: e.matmul(bk[cb][:], lhsT=self.aT[:, kc, tsl], rhs=Wu[:, kc, cb * 512:(cb + 1) * 512],
                                                                     start=(kc == 0), stop=(kc == KC - 1)),
                     reads=[Wu_b, self.aT_b[t]], writes=[bb[cb]])
            s = cb % 2
            P.op("act", lambda e, cb=cb, s=s: e.activation(out=gu[s][:, 0, :], in_=bk[cb][:], func=AF.Gelu), reads=[bb[cb]], writes=[gu_b[s]])
            P.op("dve", lambda e, cb=cb, s=s: e.tensor_tensor(out=us[:, 0, cb * 512:(cb + 1) * 512], in0=gu[s][:, 0, :], in1=ssb[:, 0, cb * 512:(cb + 1) * 512], op=ALU.mult),
                 reads=[gu_b[s], ssb_b], writes=[us_b])
        for half in range(2):
            tp = bk[4 + half][:].bitcast(BF16).rearrange("p (k n) -> p k n", k=8)
            for q in range(8):
                kc = half * 8 + q
                P.op("pe", lambda e, tp=tp, q=q, kc=kc: e.transpose(out=tp[:, q, :], in_=us[:, 0, kc * 128:(kc + 1) * 128], identity=self.ident[:]),
                     reads=[us_b, self.ident_b], writes=[bb[4 + half]])
            P.op("act", lambda e, tp=tp, half=half: e.copy(out=usT[:, half * 8:(half + 1) * 8, :], in_=tp), reads=[bb[4 + half]], writes=[usT_b])
        for half in range(2):
            for kc in range(16):
                P.op("pe", lambda e, half=half, kc=kc: e.matmul(bk[6 + half][:], lhsT=usT[:, kc, :], rhs=Wo[:, kc, half * 512:(half + 1) * 512],
                                                                start=(kc == 0), stop=(kc == 15)),
                     reads=[usT_b, Wo_b], writes=[bb[6 + half]])
        src = self.psum_all[:, 6:8, :].rearrange("p a b -> p (a b)")
        toks.append(self.boundary(t, src, [bb[6], bb[7]], vf[:, 0, 0:D], vf_b, tbank=5))
    for t in range(NT):
        src_b = Wv_b if t < 8 else Wu_b
        self.acc_b[t].r = list(self.acc_b[t].r) + list(src_b.r) + ([src_b.w] if src_b.w else [])
    return toks


Builder.gmlp_stage = _gmlp_stage


def att_tables(rel_bias):
    p = np.arange(128)[:, None, None]
    kb = np.arange(2)[None, :, None]
    q = np.arange(128)[None, None, :]
    dist = 128 * (1 - kb) + q - p
    mask01 = ((dist >= 0) & (dist <= 128)).astype(np.float32).reshape(128, 256)
    dcl = np.maximum(dist, 0)
    out = np.zeros((3, ATT_H, 128, 256), np.float32)
    for g, (w, r) in enumerate(DIL):
        dd = (dcl * r).astype(np.int32)
        df = np.maximum(dd, 1).astype(np.float32)
        large = 16 + (np.log(df / np.float32(16)) / np.float32(math.log(2048 / 16)) * np.float32(16)).astype(np.int32)
        large = np.minimum(large, 31)
        bucket = np.where(dd < 16, dd, large).reshape(128, 256)
        tab = rel_bias[:, g * ATT_H:(g + 1) * ATT_H]
        out[g] = np.transpose(tab[bucket], (2, 0, 1))
    return out, mask01


def att_prevmask(core, mask01):
    m = mask01.copy()
    if core % 4 == 0:
        m[:, 0:128] = 0.0
    return m


def _att_kv_pass(self, j):
    P = self.P
    dr = self.dr
    P.barrier()
    w_in = dr["attn_w_in"][j]
    off = 0
    ring, ring_b = [], []
    for s in range(3):
        v, off = self.st_view(off, [KC, 512], BF16)
        ring.append(v)
        ring_b.append(Buf(f"akv_w{s}"))
    kst, kst_b = [], []
    for s in range(2):
        v, off = self.st_view(off, [1, T], BF16)
        kst.append(v)
        kst_b.append(Buf(f"akv_k{s}"))
    vst, vst_b = [], []
    for s in range(2):
        v, off = self.st_view(off, [1, D], BF16)
        vst.append(v)
        vst_b.append(Buf(f"akv_v{s}"))
    bk, bb = self.bank, self.bank_b
    self.KT_b = [[Buf(f"KT{g}_{fb}") for fb in range(8)] for g in range(3)]
    self.V_b = [[Buf(f"V{g}_{blk}") for blk in range(NT)] for g in range(3)]
    ri = 0
    ui = 0
    for g, (w, r) in enumerate(DIL):
        nbc = 16 // r
        for hf in range(2):
            ws, wb = ring[ri % 3], ring_b[ri % 3]
            ri += 1
            c0 = g * 3072 + 1024 + hf * 512
            self.wload(ws, w_in[:, c0:c0 + 512].rearrange("(kc p) n -> p kc n", p=128), wb)
            for fq in range(4):
                fb = hf * 4 + fq
                ks, ksb = kst[fb % 2], kst_b[fb % 2]
                for tg in range(4):
                    b = ui % 4
                    ui += 1
                    for kc in range(KC):
                        P.op("pe", lambda e, b=b, kc=kc, fq=fq, tg=tg, ws=ws: e.matmul(
                            bk[b][:], lhsT=ws[:, kc, fq * 128:(fq + 1) * 128], rhs=self.aT[:, kc, tg * 512:(tg + 1) * 512],
                            start=(kc == 0), stop=(kc == KC - 1)), reads=[wb] + self.aT_b[tg * 4:(tg + 1) * 4], writes=[bb[b]])
                    jn = 512 // r
                    dst = ks[:, 0, :].rearrange("p (c j) -> p c j", c=r)[:, :, tg * jn:(tg + 1) * jn]
                    src = bk[b][:].rearrange("p (j c) -> p c j", c=r)
                    eng = "act" if (ui % 2 == 0) else "dve"
                    if eng == "act":
                        P.op("act", lambda e, dst=dst, src=src: e.copy(out=dst, in_=src), reads=[bb[b]], writes=[ksb])
                    else:
                        P.op("dve", lambda e, dst=dst, src=src: e.tensor_copy(out=dst, in_=src), reads=[bb[b]], writes=[ksb])
                P.dma("sp", dr["KT_own"][g, fb * 128:(fb + 1) * 128, :], ks[:, 0, :], reads=[ksb], writes=[self.KT_b[g][fb]], chan=ksb)
        wsl = []
        for hf in range(2):
            ws, wb = ring[ri % 3], ring_b[ri % 3]
            ri += 1
            c0 = g * 3072 + 2048 + hf * 512
            self.wload(ws, w_in[:, c0:c0 + 512].rearrange("(kc p) n -> p kc n", p=128), wb)
            wsl.append((ws, wb))
        for blk in range(NT):
            c, n = blk // nbc, blk % nbc
            start = n * 128 * r + c
            vs, vsb = vst[blk % 2], vst_b[blk % 2]
            toks_t = sorted(set((start + i * r) // 128 for i in range(128)))
            for hf in range(2):
                ws, wb = wsl[hf]
                b = ui % 4
                ui += 1
                for kc in range(KC):
                    P.op("pe", lambda e, b=b, kc=kc, ws=ws, start=start, r=r: e.matmul(
                        bk[b][:], lhsT=self.aT[:, kc, start:start + 127 * r + 1:r], rhs=ws[:, kc, :],
                        start=(kc == 0), stop=(kc == KC - 1)), reads=[wb] + [self.aT_b[t] for t in toks_t], writes=[bb[b]])
                if hf == 0:
                    P.op("act", lambda e, b=b, vs=vs: e.copy(out=vs[:, 0, 0:512], in_=bk[b][:]), reads=[bb[b]], writes=[vsb])
                else:
                    P.op("dve", lambda e, b=b, vs=vs: e.tensor_copy(out=vs[:, 0, 512:1024], in_=bk[b][:]), reads=[bb[b]], writes=[vsb])
            P.dma("sp", dr["V_own"][g, blk * 128:(blk + 1) * 128, :], vs[:, 0, :], reads=[vsb], writes=[self.V_b[g][blk]], chan=vsb)


Builder.att_kv_pass = _att_kv_pass


def _att_main(self, i, j):
    P = self.P
    dr = self.dr
    P.barrier()
    w_in = dr["attn_w_in"][j]
    off = 0
    Wq, off = self.st_view(off, [KC, 384], BF16)
    Wq_b = Buf("aWq")
    QT, off = self.st_view(off, [1, T], BF16)
    QT_b = Buf("aQT")
    KT, off = self.st_view(off, [1, T], BF16)
    KT_sb = Buf("aKT")
    KH, off = self.st_view(off, [1, T], BF16)
    KH_b = Buf("aKH")
    Vt, off = self.st_view(off, [NT, 128], BF16)
    Vt_b = Buf("aVt")
    VH, off = self.st_view(off, [NT, 128], BF16)
    VH_b = Buf("aVH")
    Nacc, off = self.st_view(off, [2, T], F32)
    Dacc, off = self.st_view(off, [2, T], F32)
    ND_b = [Buf("aND0"), Buf("aND1")]
    bst, off = self.st_view(off, [2, 256], F32)
    bst_b = Buf("abst")
    Et, off = self.st_view(off, [2, 256], F32)
    Et0, off = self.st_view(off, [2, 256], F32)
    Et_b = Buf("aEt")
    m01, off = self.st_view(off, [1, 256], F32)
    m01p, off = self.st_view(off, [1, 256], F32)
    m_b = Buf("am01")
    ones, off = self.st_view(off, [1, 64], BF16)
    ones_b = Buf("aones")
    ex, ex_b, pt, pt_b = [], [], [], []
    for s in range(2):
        a, off = self.st_view(off, [1, 256], F32); ex.append(a); ex_b.append(Buf(f"aex{s}"))
        a, off = self.st_view(off, [2, 128], BF16); pt.append(a); pt_b.append(Buf(f"apt{s}"))
    rc, off = self.st_view(off, [1, T], F32)
    rc_b = Buf("arc")
    bk, bb = self.bank, self.bank_b
    if getattr(self, "KT_b", None) is None:
        self.KT_b = [[Buf(f"KT{g}_{fb}") for fb in range(8)] for g in range(3)]
        self.V_b = [[Buf(f"V{g}_{blk}") for blk in range(NT)] for g in range(3)]
    oT = self.big[:].bitcast(BF16).rearrange("p (h t) -> p h t", h=ATT_H)
    oT_b = [Buf(f"aoT{h}") for h in range(ATT_H)]
    P.dma("sp", m01[:, 0, :], dr["att_mask01"][:, :], writes=[m_b])
    P.dma("sp", m01p[:, 0, :], dr["att_prevmask"][:, :], writes=[m_b])
    P.op("pool", lambda e: e.memset(ones, 1.0), writes=[ones_b])
    sT_ps = [self.psum_all[:, 2 + s // 2, (s % 2) * 256:(s % 2 + 1) * 256].rearrange("p (a b) -> p a b", a=2) for s in range(4)]
    sT_pb = [Buf(f"asT{s}") for s in range(4)]
    nd_ps = [self.psum_all[0:64, 4, s * 256:(s + 1) * 256].rearrange("p (a b) -> p a b", a=2) for s in range(2)]
    nd_pb = [Buf(f"and{s}") for s in range(2)]
    ui = 0
    si = 0
    for hp in range(8):
        for g in range(3):
            c0 = g * 3072 + hp * 128
            self.wload(Wq[:, :, g * 128:(g + 1) * 128], w_in[:, c0:c0 + 128].rearrange("(kc p) n -> p kc n", p=128), Wq_b)
        for g, (w, r) in enumerate(DIL):
            nbc = 16 // r
            P.dma("sp", KT[:, 0, :], dr["KT_own"][g, hp * 128:(hp + 1) * 128, :], reads=self.KT_b[g], writes=[KT_sb])
            for c in range(r):
                p0 = c * (T // r) + (nbc - 1) * 128
                P.dma("sp", KH[:, 0, c * 128:(c + 1) * 128], dr["KT_prev"][g, hp * 128:(hp + 1) * 128, p0:p0 + 128], writes=[KH_b])
            P.dma("sp", Vt, dr["V_own"][g, :, hp * 128:(hp + 1) * 128].rearrange("(b p) f -> p b f", p=128), reads=self.V_b[g], writes=[Vt_b])
            for c in range(r):
                p0 = c * (T // r) + (nbc - 1) * 128
                P.dma("sp", VH[:, c, :], dr["V_prev"][g, p0:p0 + 128, hp * 128:(hp + 1) * 128], writes=[VH_b])
            P.dma("sp", bst, dr["att_bias"][g, 2 * hp:2 * hp + 2].rearrange("h p n -> p h n"), writes=[bst_b])
            P.op("act", lambda e: e.activation(out=bst, in_=bst, func=AF.Exp), reads=[], writes=[bst_b])
            for hh in range(2):
                P.op("dve", lambda e, hh=hh: e.tensor_tensor(out=Et[:, hh, :], in0=bst[:, hh, :], in1=m01[:, 0, :], op=ALU.mult),
                     reads=[bst_b, m_b], writes=[Et_b])
                P.op("dve", lambda e, hh=hh: e.tensor_tensor(out=Et0[:, hh, :], in0=bst[:, hh, :], in1=m01p[:, 0, :], op=ALU.mult),
                     reads=[bst_b, m_b], writes=[Et_b])
            for tg in range(4):
                b = ui % 2
                ui += 1
                for kc in range(KC):
                    P.op("pe", lambda e, b=b, kc=kc, g=g, tg=tg: e.matmul(
                        bk[b][:], lhsT=Wq[:, kc, g * 128:(g + 1) * 128], rhs=self.aT[:, kc, tg * 512:(tg + 1) * 512],
                        start=(kc == 0), stop=(kc == KC - 1)), reads=[Wq_b] + self.aT_b[tg * 4:(tg + 1) * 4], writes=[bb[b]])
                jn = 512 // r
                dst = QT[:, 0, :].rearrange("p (c j) -> p c j", c=r)[:, :, tg * jn:(tg + 1) * jn]
                src = bk[b][:].rearrange("p (j c) -> p c j", c=r)
                P.op("act", lambda e, dst=dst, src=src: e.copy(out=dst, in_=src), reads=[bb[b]], writes=[QT_b])
            for blk in range(NT):
                c, n = blk // nbc, blk % nbc
                qs = slice(blk * 128, (blk + 1) * 128)
                first = (n == 0)
                for hh in range(2):
                    ps = slice(hh * 64, (hh + 1) * 64)
                    s4 = si % 4
                    s2 = si % 2
                    si += 1
                    kprev = KH[ps, 0, c * 128:(c + 1) * 128] if first else KT[ps, 0, (blk - 1) * 128:blk * 128]
                    P.op("pe", lambda e, s4=s4, kprev=kprev, ps=ps, qs=qs: e.matmul(sT_ps[s4][:, 0, :], lhsT=kprev, rhs=QT[ps, 0, qs], start=True, stop=True),
                         reads=[KH_b if first else KT_sb, QT_b], writes=[sT_pb[s4]])
                    P.op("pe", lambda e, s4=s4, ps=ps, qs=qs: e.matmul(sT_ps[s4][:, 1, :], lhsT=KT[ps, 0, qs], rhs=QT[ps, 0, qs], start=True, stop=True),
                         reads=[KT_sb, QT_b], writes=[sT_pb[s4]])
                    P.op("act", lambda e, s4=s4, s2=s2: e.activation(out=ex[s2][:, 0, :], in_=sT_ps[s4].rearrange("p a b -> p (a b)"), func=AF.Exp, scale=0.125),
                         reads=[sT_pb[s4]], writes=[ex_b[s2]])
                    tab = Et0 if first else Et
                    P.op("dve", lambda e, s2=s2, tab=tab, hh=hh: e.tensor_tensor(out=pt[s2].rearrange("p a b -> p (a b)"), in0=ex[s2][:, 0, :], in1=tab[:, hh, :], op=ALU.mult),
                         reads=[ex_b[s2], Et_b], writes=[pt_b[s2]])
                    vprev = VH[:, c, ps] if first else Vt[:, blk - 1, ps]
                    P.op("pe", lambda e, s2=s2, vprev=vprev: e.matmul(nd_ps[s2][:, 0, :], lhsT=vprev, rhs=pt[s2][:, 0, :], start=True, stop=False),
                         reads=[VH_b if first else Vt_b, pt_b[s2]], writes=[nd_pb[s2]])
                    P.op("pe", lambda e, s2=s2, blk=blk, ps=ps: e.matmul(nd_ps[s2][:, 0, :], lhsT=Vt[:, blk, ps], rhs=pt[s2][:, 1, :], start=False, stop=True),
                         reads=[Vt_b, pt_b[s2]], writes=[nd_pb[s2]])
                    P.op("pe", lambda e, s2=s2: e.matmul(nd_ps[s2][:, 1, :], lhsT=ones[:, 0, :], rhs=pt[s2][:, 0, :], start=True, stop=False),
                         reads=[ones_b, pt_b[s2]], writes=[nd_pb[s2]])
                    P.op("pe", lambda e, s2=s2: e.matmul(nd_ps[s2][:, 1, :], lhsT=ones[:, 0, :], rhs=pt[s2][:, 1, :], start=False, stop=True),
                         reads=[ones_b, pt_b[s2]], writes=[nd_pb[s2]])
                    t0 = n * 128 * r + c
                    tsl = slice(t0, t0 + 127 * r + 1, r)
                    if g == 0:
                        P.op("act", lambda e, s2=s2, hh=hh, tsl=tsl: e.copy(out=Nacc[0:64, hh, tsl], in_=nd_ps[s2][:, 0, :]), reads=[nd_pb[s2]], writes=[ND_b[hh]])
                        P.op("dve", lambda e, s2=s2, hh=hh, tsl=tsl: e.tensor_copy(out=Dacc[0:64, hh, tsl], in_=nd_ps[s2][:, 1, :]), reads=[nd_pb[s2]], writes=[ND_b[hh]])
                    else:
                        P.op("dve", lambda e, s2=s2, hh=hh, tsl=tsl: e.tensor_tensor(out=Nacc[0:64, hh, tsl], in0=Nacc[0:64, hh, tsl], in1=nd_ps[s2][:, 0, :], op=ALU.add),
                             reads=[nd_pb[s2]], writes=[ND_b[hh]])
                        P.op("dve", lambda e, s2=s2, hh=hh, tsl=tsl: e.tensor_tensor(out=Dacc[0:64, hh, tsl], in0=Dacc[0:64, hh, tsl], in1=nd_ps[s2][:, 1, :], op=ALU.add),
                             reads=[nd_pb[s2]], writes=[ND_b[hh]])
        for hh in range(2):
            h = 2 * hp + hh
            P.op("dve", lambda e, hh=hh: e.reciprocal(out=rc[0:64, 0, :], in_=Dacc[0:64, hh, :]), reads=[ND_b[hh]], writes=[rc_b])
            P.op("pool", lambda e, hh=hh, h=h: e.tensor_tensor(out=oT[0:64, h, :], in0=Nacc[0:64, hh, :], in1=rc[0:64, 0, :], op=ALU.mult),
                 reads=[ND_b[hh], rc_b], writes=[oT_b[h]] + self.acc_b)
    P.barrier()
    off = 0
    Wo, off = self.st_view(off, [ATT_H, D], BF16)
    Wo_b = Buf("aWo")
    nrm, off = self.st_view(off, [2, D], F32)
    nrm_b = [Buf("anrm0"), Buf("anrm1")]
    self.wload(Wo[0:64], dr["attn_w_out"][j].rearrange("(h p) n -> p h n", p=64), Wo_b)
    toks = []
    for t in range(NT):
        tsl = slice(t * 128, (t + 1) * 128)
        for half in range(2):
            for h in range(ATT_H):
                P.op("pe", lambda e, h=h, half=half, tsl=tsl: e.matmul(bk[6 + half][:], lhsT=oT[0:64, h, tsl], rhs=Wo[0:64, h, half * 512:(half + 1) * 512],
                                                                        start=(h == 0), stop=(h == ATT_H - 1)),
                     reads=[oT_b[h], Wo_b], writes=[bb[6 + half]])
        src = self.psum_all[:, 6:8, :].rearrange("p a b -> p (a b)")
        toks.append(self.boundary(t, src, [bb[6], bb[7]], nrm[:, t % 2, :], nrm_b[t % 2], tbank=5))
    for t in range(NT):
        for h in range(ATT_H):
            self.acc_b[t].r = list(self.acc_b[t].r) + list(oT_b[h].r)
    return toks


Builder.att_main = _att_main


W_SHAPES = {
    "norm_g": [DEPTH, 5, D], "ret_w_in": [2, D, 6144], "ret_w_out": [2, 2048, D],
    "attn_w_in": [1, D, 9216], "attn_w_out": [1, D, D],
    "gm_w_in": [1, D, 4096], "gm_ln_g": [1, 2048], "gm_ln_b": [1, 2048], "gm_w_s": [1, 8, 128, 128], "gm_b_s": [1, 8, 128],
    "gm_w_out": [1, 2048, D], "ffn_w_in": [DEPTH, D, 2 * FFN_H], "ffn_w_out": [DEPTH, FFN_H, D],
    "ple_w_proj": [DEPTH, PLE_DIM, D], "ple_w_gate": [DEPTH, D, D],
}
T_SHAPES = {
    "ident": [128, 128], "causalT": [128, 128],
    "ret_maskT": [RET_H, 128, 128], "ret_qdec": [RET_H, 128, 512], "ret_kdec": [128, RET_H], "ret_coef": [128, L_RANKS * RET_H],
    "rope_cos": [128, T], "rope_sin": [128, T],
    "att_bias": [3, ATT_H, 128, 256], "att_mask01": [128, 256], "att_prevmask": [128, 256],
}
RET_IN = ["ret_w_in", "ret_w_out", "ret_maskT", "ret_qdec", "ret_kdec", "ret_coef", "rope_cos", "rope_sin"]
FFN_IN = ["ffn_w_in", "ffn_w_out", "ple_w_proj", "ple_w_gate", "p"]
ATT_IN = ["attn_w_in", "attn_w_out", "att_bias", "att_mask01", "att_prevmask"]
GM_IN = ["gm_w_in", "gm_ln_g", "gm_ln_b", "gm_w_s", "gm_b_s", "gm_w_out", "causalT"]
SEG_IN = {
    1: ["hin", "norm_g", "ident"] + RET_IN,
    2: ["hin", "norm_g", "ident", "L_all"] + RET_IN + FFN_IN + ["attn_w_in"],
    3: ["hin", "norm_g", "ident", "KT_own", "V_own", "KT_prev", "V_prev"] + ATT_IN + FFN_IN + GM_IN + RET_IN,
    4: ["hin", "norm_g", "ident", "L_all"] + RET_IN + FFN_IN,
}
SEG_OUT = {1: ["L_out"], 2: ["hout", "KT_own", "V_own"], 3: ["hout", "L_out"], 4: ["hout"]}
IO_SHAPES = {
    "hin": ([T, D], F32), "hout": ([T, D], F32), "p": ([DEPTH, T, PLE_DIM], F32),
    "L_all": ([L_RANKS, RET_H, RET_DK, RET_DV], F32), "L_out": ([RET_H, RET_DK, RET_DV], F32),
    "KT_own": ([3, D, T], BF16), "V_own": ([3, T, D], BF16), "KT_prev": ([3, D, T], BF16), "V_prev": ([3, T, D], BF16),
}


def build_segment(seg):
    nc = bass.Bass("TRN2", target_bir_lowering=False)
    dr = {}
    for name in dict.fromkeys(SEG_IN[seg]):
        if name in SEG_OUT[seg]:
            continue
        if name in W_SHAPES:
            shape, dt = W_SHAPES[name], F32
        elif name in T_SHAPES:
            shape, dt = T_SHAPES[name], F32
        else:
            shape, dt = IO_SHAPES[name]
        dr[name] = nc.dram_tensor(name, list(shape), dt, kind="ExternalInput").ap()
    for name in SEG_OUT[seg]:
        shape, dt = IO_SHAPES[name]
        dr[name] = nc.dram_tensor(name, list(shape), dt, kind="ExternalOutput").ap()
    hD = nc.dram_tensor("hD", [T, D], F32, kind="Internal").ap()
    B = Builder(nc, dr)
    P = B.P
    fin = []
    if seg == 1:
        B.h_rd = dr["hin"]
        B.load_g("pre", (0, 0))
        B.first_norm()
        fin += B.ret_emit(0, True)
    elif seg == 2:
        B.h_rd, B.h_wr = dr["hin"], hD
        B.load_g("pre", (0, 0))
        B.first_norm()
        B.load_g("post", (0, 1))
        B.load_g("pre", (0, 2))
        B.ret_emit(0, False, boundary=True)
        B.h_rd = hD
        B.ffn_stage(0)
        B.h_wr = dr["hout"]
        fin += B.ple_stage(0, last=False)
        B.att_kv_pass(0)
        for g in range(3):
            for b in B.KT_b[g] + B.V_b[g]:
                fin.append(b.w)
    elif seg == 3:
        B.h_rd, B.h_wr = dr["hin"], hD
        B.load_g("pre", (1, 0))
        B.first_norm()
        B.load_g("post", (1, 1))
        B.load_g("pre", (1, 2))
        B.att_main(1, 0)
        B.h_rd = hD
        B.ffn_stage(1)
        B.ple_stage(1, last=False)
        B.load_g("post", (2, 1))
        B.load_g("pre", (2, 2))
        B.gmlp_stage(2, 0)
        B.ffn_stage(2)
        B.h_wr = dr["hout"]
        fin += B.ple_stage(2, last=False)
        fin += B.ret_emit(1, True)
    elif seg == 4:
        B.h_rd, B.h_wr = dr["hin"], hD
        B.load_g("pre", (3, 0))
        B.first_norm()
        B.load_g("post", (3, 1))
        B.load_g("pre", (3, 2))
        B.ret_emit(1, False, boundary=True)
        B.h_rd = hD
        B.ffn_stage(3)
        B.h_wr = dr["hout"]
        fin += B.ple_stage(3, last=True)
    P.finish_on("sp", [t for t in fin if t is not None])
    P.emit()
    return nc


_PROG_CACHE = {}


def _prog(seg):
    if seg not in _PROG_CACHE:
        _PROG_CACHE[seg] = build_segment(seg)
    return _PROG_CACHE[seg]


def _run(seg, maps):
    nc = build_segment(seg)
    res = run_bass_kernel_spmd(nc, maps, core_ids=list(range(NCORES)))
    return res.results


def kernel(x, p, norm_g, ret_w_in, ret_w_out, attn_w_in, attn_w_out, rel_bias,
           gm_w_in, gm_ln_g, gm_ln_b, gm_w_s, gm_b_s, gm_w_out,
           ffn_w_in, ffn_w_out, ple_w_proj, ple_w_gate):
    import ml_dtypes
    f32 = lambda a: np.ascontiguousarray(np.asarray(a), dtype=np.float32)
    wts = {"norm_g": f32(norm_g), "ret_w_in": f32(ret_w_in), "ret_w_out": f32(ret_w_out), "attn_w_in": f32(attn_w_in),
           "attn_w_out": f32(attn_w_out), "gm_w_in": f32(gm_w_in), "gm_ln_g": f32(gm_ln_g), "gm_ln_b": f32(gm_ln_b),
           "gm_w_s": f32(gm_w_s), "gm_b_s": f32(gm_b_s), "gm_w_out": f32(gm_w_out), "ffn_w_in": f32(ffn_w_in),
           "ffn_w_out": f32(ffn_w_out), "ple_w_proj": f32(ple_w_proj), "ple_w_gate": f32(ple_w_gate)}
    x = f32(x)
    p = f32(p)
    maskT, qdec, kdec = ret_tables()
    bias_full, mask01 = att_tables(f32(rel_bias))
    idx = np.arange(128)
    consts = {"ident": np.eye(128, dtype=np.float32), "causalT": (idx[None, :] >= idx[:, None]).astype(np.float32),
              "ret_maskT": maskT, "ret_qdec": qdec, "ret_kdec": kdec, "att_bias": bias_full, "att_mask01": mask01}
    per_core = []
    for c in range(NCORES):
        b, s = c // 4, c % 4
        cos, sin = rope_tables(c)
        per_core.append({"ret_coef": ret_coef(c), "rope_cos": cos, "rope_sin": sin, "att_prevmask": att_prevmask(c, mask01),
                         "p": np.ascontiguousarray(p[:, b, s * T:(s + 1) * T, :])})

    def maps(seg, extra):
        out = []
        for c in range(NCORES):
            m = {}
            for name in dict.fromkeys(SEG_IN[seg]):
                if name in SEG_OUT[seg]:
                    continue
                if name in wts:
                    m[name] = wts[name]
                elif name in consts:
                    m[name] = consts[name]
                elif name in per_core[c]:
                    m[name] = per_core[c][name]
                else:
                    m[name] = extra[c][name]
            out.append(m)
        return out

    h = [np.ascontiguousarray(x[c // 4, (c % 4) * T:(c % 4 + 1) * T, :]) for c in range(NCORES)]
    r1 = _run(1, maps(1, [{"hin": h[c]} for c in range(NCORES)]))
    L_all = [np.ascontiguousarray(np.stack([r1[(c // 4) * 4 + rs]["L_out"] for rs in range(L_RANKS)], 0)) for c in range(NCORES)]
    r2 = _run(2, maps(2, [{"hin": h[c], "L_all": L_all[c]} for c in range(NCORES)]))
    h = [r2[c]["hout"] for c in range(NCORES)]
    zk = np.zeros((3, D, T), ml_dtypes.bfloat16)
    zv = np.zeros((3, T, D), ml_dtypes.bfloat16)
    ex3 = []
    for c in range(NCORES):
        prev = c - 1 if c % 4 > 0 else None
        ex3.append({"hin": h[c], "KT_own": r2[c]["KT_own"], "V_own": r2[c]["V_own"],
                    "KT_prev": r2[prev]["KT_own"] if prev is not None else zk,
                    "V_prev": r2[prev]["V_own"] if prev is not None else zv})
    r3 = _run(3, maps(3, ex3))
    h = [r3[c]["hout"] for c in range(NCORES)]
    L_all = [np.ascontiguousarray(np.stack([r3[(c // 4) * 4 + rs]["L_out"] for rs in range(L_RANKS)], 0)) for c in range(NCORES)]
    r4 = _run(4, maps(4, [{"hin": h[c], "L_all": L_all[c]} for c in range(NCORES)]))
    out = np.empty((BATCH, SEQ, D), np.float32)
    for c in range(NCORES):
        out[c // 4, (c % 4) * T:(c % 4 + 1) * T, :] = r4[c]["hout"]
    return out


def pipeline(n, phases):
    nph = len(phases)
    if _SEQ_DEBUG:
        for t in range(n):
            for ph in phases:
                ph(t)
        return
    for step in range(n + nph - 1):
        for ph in reversed(range(nph)):
            t = step - ph
            if 0 <= t < n:
                phases[ph](t)


def _pre_norm_phases(self, get_h, tbanks=(7,), gain=None):
    P = self.P
    stt = {}
    gt, gt_b = gain if gain is not None else (self.gpre, self.gpre_b)
    slots = (0, 1) if gain is None else (2, 3)

    def pa(t):
        hs, hb = get_h(t)
        r, r_b = self.rstd_of(hs, [hb], D)
        s = slots[t % 2] if self.abf_n == 4 else t % 2
        ab, ab_b = self.abf[:, s, :], self.abf_b[s]
        P.op("dve", lambda e: e.scalar_tensor_tensor(out=ab, in0=hs, scalar=r, in1=gt[:], op0=ALU.mult, op1=ALU.mult),
             reads=[hb, r_b, gt_b], writes=[ab_b])
        stt[t] = (s, ab_b)

    def pb(t):
        s, ab_b = stt.pop(t)
        tbank = tbanks[t % len(tbanks)]
        pst = self.bank[tbank][:].bitcast(BF16).rearrange("p (k n) -> p k n", k=KC)
        for kc in range(KC):
            P.op("pe", lambda e, kc=kc: e.transpose(out=pst[:, kc, :], in_=self.abf[:, s, kc * 128:(kc + 1) * 128], identity=self.ident[:]),
                 reads=[ab_b, self.ident_b], writes=[self.bank_b[tbank]])
        P.op("act", lambda e: e.copy(out=self.aT[:, :, t * 128:(t + 1) * 128], in_=pst),
             reads=[self.bank_b[tbank]], writes=[self.aT_b[t]])

    return [pa, pb]


def _boundary_phases(self, src_fn, dst_fn, do_pre=True, tbanks=(7,), toks=None):
    P = self.P
    st1 = {}
    st2 = {}

    def p0(t):
        src_ap, src_b = src_fn(t)
        dst_ap, dst_b = dst_fn(t)
        r, r_b = self.rstd_of(src_ap, src_b, D)
        P.op("dve", lambda e: e.scalar_tensor_tensor(out=dst_ap, in0=src_ap, scalar=r, in1=self.gpost[:], op0=ALU.mult, op1=ALU.mult),
             reads=list(src_b) + [r_b, self.gpost_b], writes=[dst_b])
        hs, hb = self.h_load(t)
        st1[t] = (dst_ap, dst_b, hs, hb)

    def p1(t):
        dst_ap, dst_b, hs, hb = st1.pop(t)
        P.op("pool", lambda e: e.tensor_tensor(out=hs, in0=hs, in1=dst_ap, op=ALU.add), reads=[dst_b], writes=[hb])
        tok = self.h_store(t, hs, hb)
        if toks is not None:
            toks.append(tok)
        st2[t] = (hs, hb)

    phases = [p0, p1]
    if do_pre:
        pn = self.pre_norm_phases(lambda t: st2.pop(t), tbanks=tbanks)
        def p1b(t, p1=p1, pa=pn[0]):
            p1(t)
            pa(t)
        phases = [p0, p1b, pn[1]]
    return phases


def _first_norm(self):
    self.set_hextra(0, 5)
    pipeline(NT, self.pre_norm_phases(lambda t: self.h_load(t), tbanks=(6, 7)))


def _mixer_boundaries2(self, i):
    toks = []
    ph = self.boundary_phases(lambda t: (self.acc[:, t, :], [self.acc_b[t]]), lambda t: (self.acc[:, t, :], self.acc_b[t]), toks=toks, tbanks=(6, 7))
    pipeline(NT, ph)
    return toks


Builder.pre_norm_phases = _pre_norm_phases
Builder.boundary_phases = _boundary_phases
Builder.first_norm = _first_norm
Builder.mixer_boundaries = _mixer_boundaries2


def _ffn_stage2(self, i):
    P = self.P
    self.load_g("post", (i, 3))
    self.load_g("pre", (i, 4))
    P.barrier()
    w_in = self.dr["ffn_w_in"][i]
    w_out = self.dr["ffn_w_out"][i]
    off = 0
    winA, winA_b = [], []
    for s in range(3):
        v, off = self.st_view(off, [KC, 512], BF16)
        winA.append(v)
        winA_b.append(Buf(f"f{i}winA{s}"))
    woutB, woutB_b = [], []
    for s in range(2):
        v, off = self.st_view(off, [4, D], BF16)
        woutB.append(v)
        woutB_b.append(Buf(f"f{i}woutB{s}"))
    hidT, hidT_b = [], []
    for s in range(2):
        v, off = self.st_view(off, [4, T], BF16)
        hidT.append(v)
        hidT_b.append([[Buf(f"f{i}hid{s}_{jl}_{tg}") for tg in range(4)] for jl in range(4)])
    sil, sil_b = [], []
    for s in range(2):
        v, off = self.st_view(off, [1, 512], F32)
        sil.append(v)
        sil_b.append(Buf(f"f{i}sil{s}"))
    nchunk = FFN_H // 128
    ngrp = (nchunk + 3) // 4
    unit = [0]
    self.set_hextra(off, (self.ST_BYTES - off) // 4096)

    def in_phase(pi):
        ws, wb = winA[pi % 3], winA_b[pi % 3]
        self.wload(ws[:, :, 0:256], w_in[:, pi * 256:(pi + 1) * 256].rearrange("(kc p) n -> p kc n", p=128), wb)
        self.wload(ws[:, :, 256:512], w_in[:, FFN_H + pi * 256:FFN_H + (pi + 1) * 256].rearrange("(kc p) n -> p kc n", p=128), wb)
        for jj in range(2):
            j = 2 * pi + jj
            grp, jl = j // 4, j % 4
            hs_, hs_b = hidT[grp % 2], hidT_b[grp % 2]
            for tg in range(4):
                u = unit[0]
                unit[0] += 1
                bg, bu = 2 * (u % 2), 2 * (u % 2) + 1
                sl, slb = sil[u % 2], sil_b[u % 2]
                for (bk, coff) in ((bg, jj * 128), (bu, 256 + jj * 128)):
                    for kc in range(KC):
                        P.op("pe", lambda e, bk=bk, coff=coff, kc=kc, tg=tg, ws=ws: e.matmul(
                            self.bank[bk][:], lhsT=ws[:, kc, coff:coff + 128], rhs=self.aT[:, kc, tg * 512:(tg + 1) * 512],
                            start=(kc == 0), stop=(kc == KC - 1)),
                            reads=[wb] + self.aT_b[tg * 4:(tg + 1) * 4], writes=[self.bank_b[bk]])
                P.op("act", lambda e, bg=bg, sl=sl: e.activation(out=sl[:, 0, :], in_=self.bank[bg][:], func=AF.Silu),
                     reads=[self.bank_b[bg]], writes=[slb])
                P.op("dve", lambda e, bu=bu, sl=sl, hs_=hs_, jl=jl, tg=tg: e.tensor_tensor(
                    out=hs_[:, jl, tg * 512:(tg + 1) * 512], in0=sl[:, 0, :], in1=self.bank[bu][:], op=ALU.mult),
                    reads=[slb, self.bank_b[bu]], writes=[hs_b[jl][tg]])

    def out_load(grp):
        n = min(4, nchunk - grp * 4)
        wo, wob = woutB[grp % 2], woutB_b[grp % 2]
        self.wload(wo[:, 0:n, :], w_out[grp * 512:grp * 512 + n * 128, :].rearrange("(jl p) n -> p jl n", p=128), wob)

    def out_tile(grp, t):
        n = min(4, nchunk - grp * 4)
        wo, wob = woutB[grp % 2], woutB_b[grp % 2]
        hs_, hs_b = hidT[grp % 2], hidT_b[grp % 2]
        for half in range(2):
            bk = 4 + 2 * (t % 2) + half
            for q in range(n):
                P.op("pe", lambda e, bk=bk, q=q, half=half: e.matmul(
                    self.bank[bk][:], lhsT=hs_[:, q, t * 128:(t + 1) * 128], rhs=wo[:, q, half * 512:(half + 1) * 512],
                    start=(q == 0), stop=(q == n - 1)),
                    reads=[wob, hs_b[q][t // 4]], writes=[self.bank_b[bk]])
            dst = self.acc[:, t, half * 512:(half + 1) * 512]
            if grp == 0:
                P.op("act", lambda e, bk=bk, dst=dst: e.copy(out=dst, in_=self.bank[bk][:]),
                     reads=[self.bank_b[bk]], writes=[self.acc_b[t]])
            else:
                P.op("dve", lambda e, bk=bk, dst=dst: e.tensor_tensor(out=dst, in0=dst, in1=self.bank[bk][:], op=ALU.add),
                     reads=[self.bank_b[bk]], writes=[self.acc_b[t]])

    toks = []
    pairs_of = lambda grp: range(grp * 2, min(grp * 2 + 2, nchunk // 2))
    for pi in pairs_of(0):
        in_phase(pi)
    for grp in range(ngrp):
        out_load(grp)
        if grp + 1 < ngrp:
            for pi in pairs_of(grp + 1):
                in_phase(pi)
            for t in range(NT):
                out_tile(grp, t)
        else:
            ph = self.boundary_phases_f(lambda t: (self.acc[:, t, :], [self.acc_b[t]]), lambda t: (self.acc[:, t, :], self.acc_b[t]),
                                        toks=toks, tbanks=(0, 1))
            pipeline(NT, [lambda t, grp=grp: out_tile(grp, t)] + ph)
    return toks


Builder.ffn_stage = _ffn_stage2


def _ple2(self, i, do_pre=True):
    P = self.P
    P.barrier()
    off = 0
    Wg, off = self.st_view(off, [KC, D], BF16)
    Wg_b = Buf(f"p{i}Wg")
    Wp, off = self.st_view(off, [2, D], BF16)
    Wp_b = Buf(f"p{i}Wp")
    pb, pb_b, pT, pT_b, sg, sg_b = [], [], [], [], [], []
    for s in range(3):
        v, off = self.st_view(off, [1, 256], BF16)
        pb.append(v)
        pb_b.append(Buf(f"p{i}pb{s}"))
        v, off = self.st_view(off, [2, 128], BF16)
        pT.append(v)
        pT_b.append(Buf(f"p{i}pT{s}"))
        v, off = self.st_view(off, [2, 512], F32)
        sg.append(v)
        sg_b.append(Buf(f"p{i}sg{s}"))
    self.set_hextra(off, 6)
    self.wload(Wg[:, :, 0:512], self.dr["ple_w_gate"][i][:, 0:512].rearrange("(kc p) n -> p kc n", p=128), Wg_b)
    self.wload(Wp, self.dr["ple_w_proj"][i].rearrange("(kc p) n -> p kc n", p=128), Wp_b)
    Wg2_b = Buf(f"p{i}Wg2")
    self.wload(Wg[:, :, 512:1024], self.dr["ple_w_gate"][i][:, 512:1024].rearrange("(kc p) n -> p kc n", p=128), Wg2_b)
    toks = []
    hst = {}

    def mm(t):
        s = t % 3
        self.wload(pb[s][:, 0, :], self.dr["p"][i, t * 128:(t + 1) * 128, :], pb_b[s])
        ptb = 6
        pst = self.bank[ptb][:].bitcast(BF16)[:, (t % 2) * 256:(t % 2) * 256 + 256].rearrange("p (k n) -> p k n", k=2)
        for kc in range(2):
            P.op("pe", lambda e, kc=kc: e.transpose(out=pst[:, kc, :], in_=pb[s][:, 0, kc * 128:(kc + 1) * 128], identity=self.ident[:]),
                 reads=[pb_b[s], self.ident_b], writes=[self.bank_b[ptb]])
        P.op("dve", lambda e: e.tensor_copy(out=pT[s], in_=pst), reads=[self.bank_b[ptb]], writes=[pT_b[s]])
        for half in range(2):
            u = 2 * t + half
            bG, bE = u % 3, 3 + u % 3
            wgb = Wg_b if half == 0 else Wg2_b
            for kc in range(KC):
                P.op("pe", lambda e, kc=kc, half=half, bG=bG: e.matmul(
                    self.bank[bG][:], lhsT=self.aT[:, kc, t * 128:(t + 1) * 128], rhs=Wg[:, kc, half * 512:(half + 1) * 512],
                    start=(kc == 0), stop=(kc == KC - 1)), reads=[wgb, self.aT_b[t]], writes=[self.bank_b[bG]])
            for kc in range(2):
                P.op("pe", lambda e, kc=kc, half=half, bE=bE: e.matmul(
                    self.bank[bE][:], lhsT=pT[s][:, kc, :], rhs=Wp[:, kc, half * 512:(half + 1) * 512],
                    start=(kc == 0), stop=(kc == 1)), reads=[Wp_b, pT_b[s]], writes=[self.bank_b[bE]])

    def post(t):
        s = t % 3
        for half in range(2):
            u = 2 * t + half
            bG, bE = u % 3, 3 + u % 3
            P.op("act", lambda e, half=half, bG=bG: e.activation(out=sg[s][:, half, :], in_=self.bank[bG][:], func=AF.Sigmoid),
                 reads=[self.bank_b[bG]], writes=[sg_b[s]])
            P.op("dve", lambda e, half=half, bE=bE: e.tensor_tensor(out=sg[s][:, half, :], in0=sg[s][:, half, :], in1=self.bank[bE][:], op=ALU.mult),
                 reads=[self.bank_b[bE]], writes=[sg_b[s]])
        hs, hb = self.h_load(t)
        hst[t] = (hs, hb)

    def resid(t):
        s = t % 3
        hs, hb = hst[t]
        P.op("dve", lambda e: e.tensor_tensor(out=hs, in0=hs, in1=sg[s].rearrange("p a b -> p (a b)"), op=ALU.add),
             reads=[sg_b[s]], writes=[hb])
        toks.append(self.h_store(t, hs, hb))

    phases = [mm, post]
    if do_pre:
        pn = self.pre_norm_phases_f(lambda t: hst.pop(t), tbanks=(7,))

        def resid_pa(t):
            resid(t)
            pn[0](t)
        phases += [resid_pa] + pn[1:]
    else:
        phases += [resid]
    pipeline(NT, phases)
    return toks


Builder.ple = _ple2


def _att_main2(self, i, j):
    P = self.P
    dr = self.dr
    P.barrier()
    w_in = dr["attn_w_in"][j]
    if getattr(self, "KT_b", None) is None:
        self.KT_b = [[Buf(f"KT{g}_{fb}") for fb in range(8)] for g in range(3)]
        self.V_b = [[Buf(f"V{g}_{blk}") for blk in range(NT)] for g in range(3)]
    off = 0
    Wq, off = self.st_view(off, [KC, 384], BF16)
    Wq_b = [Buf(f"aWq{g}") for g in range(3)]
    QT, KT, Vt, KH, VH, Et, Et0 = [], [], [], [], [], [], []
    QT_b, KT_sb, Vt_b, KH_b, VH_b, Et_b = [], [], [], [], [], []
    for g, (w, r) in enumerate(DIL):
        a, off = self.st_view(off, [1, T], BF16); QT.append(a); QT_b.append(Buf(f"aQT{g}"))
        a, off = self.st_view(off, [1, T], BF16); KT.append(a); KT_sb.append(Buf(f"aKT{g}"))
        a, off = self.st_view(off, [NT, 128], BF16); Vt.append(a); Vt_b.append(Buf(f"aVt{g}"))
        a, off = self.st_view(off, [1, r * 128], BF16); KH.append(a); KH_b.append(Buf(f"aKH{g}"))
        a, off = self.st_view(off, [r, 128], BF16); VH.append(a); VH_b.append(Buf(f"aVH{g}"))
        a, off = self.st_view(off, [2, 256], F32); Et.append(a)
        a, off = self.st_view(off, [2, 256], F32); Et0.append(a); Et_b.append(Buf(f"aEt{g}"))
    bst, off = self.st_view(off, [2, 256], F32)
    bst_b = Buf("abst")
    m01, off = self.st_view(off, [1, 256], F32)
    m01p, off = self.st_view(off, [1, 256], F32)
    m_b = Buf("am01")
    ones, off = self.st_view(off, [1, 64], BF16)
    ones_b = Buf("aones")
    ex, ex_b, pt, pt_b = [], [], [], []
    for s in range(2):
        a, off = self.st_view(off, [1, 1024], F32); ex.append(a); ex_b.append(Buf(f"aex{s}"))
        a, off = self.st_view(off, [8, 128], BF16); pt.append(a); pt_b.append(Buf(f"apt{s}"))
    bk, bb = self.bank, self.bank_b
    bigb = self.big[:].bitcast(BF16)
    oT = bigb[:, 0:16384].rearrange("p (h t) -> p h t", h=8)
    oT_b = [Buf(f"aoT{h}") for h in range(8)]
    NDacc = self.big[:, 8192:12288].rearrange("p (a t) -> p a t", a=2)
    ND_b = Buf("aND")
    P.dma("sp", m01[:, 0, :], dr["att_mask01"][:, :], writes=[m_b])
    P.dma("sp", m01p[:, 0, :], dr["att_prevmask"][:, :], writes=[m_b])
    P.op("pool", lambda e: e.memset(ones, 1.0), writes=[ones_b])
    uq = [0]
    for hp in range(8):
        for g, (w, r) in enumerate(DIL):
            nbc = 16 // r
            c0 = g * 3072 + hp * 128
            self.wload(Wq[:, :, g * 128:(g + 1) * 128], w_in[:, c0:c0 + 128].rearrange("(kc p) n -> p kc n", p=128), Wq_b[g])
            P.dma("sp", KT[g][:, 0, :], dr["KT_own"][g, hp * 128:(hp + 1) * 128, :], reads=[self.KT_b[g][hp]], writes=[KT_sb[g]])
            P.dma("sp", Vt[g], dr["V_own"][g, :, hp * 128:(hp + 1) * 128].rearrange("(b p) f -> p b f", p=128), reads=self.V_b[g], writes=[Vt_b[g]])
            P.dma("sp", KH[g][:, 0, :].rearrange("p (c j) -> p c j", c=r),
                  dr["KT_prev"][g, hp * 128:(hp + 1) * 128, :].rearrange("f (c n j) -> f c n j", c=r, n=nbc)[:, :, nbc - 1, :], writes=[KH_b[g]])
            P.dma("sp", VH[g],
                  dr["V_prev"][g, :, hp * 128:(hp + 1) * 128].rearrange("(c n p) f -> p c n f", c=r, n=nbc)[:, :, nbc - 1, :], writes=[VH_b[g]])
            P.dma("sp", bst, dr["att_bias"][g, 2 * hp:2 * hp + 2].rearrange("h p n -> p h n"), writes=[bst_b])
            P.op("act", lambda e: e.activation(out=bst, in_=bst, func=AF.Exp), reads=[], writes=[bst_b])
            for hh in range(2):
                P.op("pool", lambda e, hh=hh, g=g: e.tensor_tensor(out=Et[g][:, hh, :], in0=bst[:, hh, :], in1=m01[:, 0, :], op=ALU.mult),
                     reads=[bst_b, m_b], writes=[Et_b[g]])
                P.op("pool", lambda e, hh=hh, g=g: e.tensor_tensor(out=Et0[g][:, hh, :], in0=bst[:, hh, :], in1=m01p[:, 0, :], op=ALU.mult),
                     reads=[bst_b, m_b], writes=[Et_b[g]])

        def q_proj(g):
            r = DIL[g][1]
            for tg in range(4):
                b = uq[0] % 2
                uq[0] += 1
                for kc in range(KC):
                    P.op("pe", lambda e, b=b, kc=kc, tg=tg: e.matmul(
                        bk[b][:], lhsT=Wq[:, kc, g * 128:(g + 1) * 128], rhs=self.aT[:, kc, tg * 512:(tg + 1) * 512],
                        start=(kc == 0), stop=(kc == KC - 1)), reads=[Wq_b[g]] + self.aT_b[tg * 4:(tg + 1) * 4], writes=[bb[b]])
                jn = 512 // r
                dst = QT[g][:, 0, :].rearrange("p (c j) -> p c j", c=r)[:, :, tg * jn:(tg + 1) * jn]
                src = bk[b][:].rearrange("p (j c) -> p c j", c=r)
                P.op("act", lambda e, dst=dst, src=src: e.copy(out=dst, in_=src), reads=[bb[b]], writes=[QT_b[g]])

        def geom(it, d):
            g, bp = it // 8, it % 8
            blk = 2 * bp + d
            r = DIL[g][1]
            nbc = 16 // r
            c, n = blk // nbc, blk % nbc
            return g, blk, r, c, n

        def ph0(it):
            if it % 8 == 0:
                q_proj(it // 8)
            sb0 = 2 if it % 2 == 0 else 6
            for d in range(2):
                g, blk, r, c, n = geom(it, d)
                qs = slice(blk * 128, (blk + 1) * 128)
                first = (n == 0)
                for hh in range(2):
                    ps = slice(hh * 64, (hh + 1) * 64)
                    sT = bk[sb0 + hh][:].rearrange("p (d k q) -> p d k q", d=2, k=2)
                    kprev = KH[g][ps, 0, c * 128:(c + 1) * 128] if first else KT[g][ps, 0, (blk - 1) * 128:blk * 128]
                    P.op("pe", lambda e, sT=sT, kprev=kprev, ps=ps, d=d, qs=qs, g=g: e.matmul(sT[:, d, 0, :], lhsT=kprev, rhs=QT[g][ps, 0, qs], start=True, stop=True),
                         reads=[KH_b[g] if first else KT_sb[g], QT_b[g]], writes=[bb[sb0 + hh]])
                    P.op("pe", lambda e, sT=sT, ps=ps, d=d, qs=qs, g=g: e.matmul(sT[:, d, 1, :], lhsT=KT[g][ps, 0, qs], rhs=QT[g][ps, 0, qs], start=True, stop=True),
                         reads=[KT_sb[g], QT_b[g]], writes=[bb[sb0 + hh]])

        def ph1(it):
            sb0 = 2 if it % 2 == 0 else 6
            s2 = it % 2
            P.op("act", lambda e: e.activation(out=ex[s2][:, 0, :].rearrange("p (h x) -> p h x", h=2), in_=self.psum_all[:, sb0:sb0 + 2, :], func=AF.Exp, scale=0.125),
                 reads=[bb[sb0], bb[sb0 + 1]], writes=[ex_b[s2]])
            exv = ex[s2][:, 0, :].rearrange("p (h d x) -> p h d x", h=2, d=2)
            ptv = pt[s2].rearrange("p (h d k) q -> p h d (k q)", h=2, d=2)
            for d in range(2):
                g, blk, r, c, n = geom(it, d)
                tab = Et0[g] if n == 0 else Et[g]
                P.op("dve", lambda e, d=d, tab=tab: e.tensor_tensor(out=ptv[:, :, d, :], in0=exv[:, :, d, :], in1=tab, op=ALU.mult),
                     reads=[ex_b[s2], Et_b[g]], writes=[pt_b[s2]])

        def ph2(it):
            s2 = it % 2
            nb_ = 4 + it % 2
            ndb = bk[nb_][:].rearrange("p (d a q) -> p d a q", d=2, a=2)
            for d in range(2):
                g, blk, r, c, n = geom(it, d)
                first = (n == 0)
                for hh in range(2):
                    ps = slice(hh * 64, (hh + 1) * 64)
                    i0 = hh * 4 + d * 2
                    vprev = VH[g][:, c, ps] if first else Vt[g][:, blk - 1, ps]
                    P.op("pe", lambda e, d=d, vprev=vprev, ps=ps, i0=i0: e.matmul(ndb[ps, d, 0, :], lhsT=vprev, rhs=pt[s2][:, i0, :], start=True, stop=False),
                         reads=[VH_b[g] if first else Vt_b[g], pt_b[s2]], writes=[bb[nb_]])
                    P.op("pe", lambda e, d=d, ps=ps, i0=i0, g=g, blk=blk: e.matmul(ndb[ps, d, 0, :], lhsT=Vt[g][:, blk, ps], rhs=pt[s2][:, i0 + 1, :], start=False, stop=True),
                         reads=[Vt_b[g], pt_b[s2]], writes=[bb[nb_]])
                    P.op("pe", lambda e, d=d, ps=ps, i0=i0: e.matmul(ndb[ps, d, 1, :], lhsT=ones[:, 0, :], rhs=pt[s2][:, i0, :], start=True, stop=False),
                         reads=[ones_b, pt_b[s2]], writes=[bb[nb_]])
                    P.op("pe", lambda e, d=d, ps=ps, i0=i0: e.matmul(ndb[ps, d, 1, :], lhsT=ones[:, 0, :], rhs=pt[s2][:, i0 + 1, :], start=False, stop=True),
                         reads=[ones_b, pt_b[s2]], writes=[bb[nb_]])

        def ph3(it):
            nb_ = 4 + it % 2
            ndb = bk[nb_][:].rearrange("p (d a q) -> p d a q", d=2, a=2)
            for d in range(2):
                g, blk, r, c, n = geom(it, d)
                t0 = n * 128 * r + c
                tsl = slice(t0, t0 + 127 * r + 1, r)
                if g == 0:
                    P.op("dve", lambda e, d=d, tsl=tsl: e.tensor_copy(out=NDacc[:, :, tsl], in_=ndb[:, d, :, :]), reads=[bb[nb_]], writes=[ND_b] + self.acc_b[8:12])
                else:
                    P.op("dve", lambda e, d=d, tsl=tsl: e.tensor_tensor(out=NDacc[:, :, tsl], in0=NDacc[:, :, tsl], in1=ndb[:, d, :, :], op=ALU.add), reads=[bb[nb_]], writes=[ND_b])

        _lvl = int(_os.environ.get('ADBG', '9'))
        pipeline(3 * 8, [ph0, ph1, ph2, ph3][:min(_lvl, 4)])
        if _lvl < 5:
            continue
        P.op("dve", lambda e: e.reciprocal(out=NDacc[:, 1, :], in_=NDacc[:, 1, :]), reads=[], writes=[ND_b])
        P.op("pool", lambda e, hp=hp: e.tensor_tensor(out=oT[:, hp, :], in0=NDacc[:, 0, :], in1=NDacc[:, 1, :], op=ALU.mult),
             reads=[ND_b], writes=[oT_b[hp]] + self.acc_b[hp:hp + 1])
    P.barrier()
    off = 0
    Wo, off = self.st_view(off, [KC, D], BF16)
    Wo_b = Buf("aWo")
    nrm, off = self.st_view(off, [2, D], F32)
    nrm_b = [Buf("anrm0"), Buf("anrm1")]
    self.wload(Wo, dr["attn_w_out"][j].rearrange("(kc p) n -> p kc n", p=128), Wo_b)
    self.set_hextra(off, 6)
    toks = []

    def outp(t):
        tsl = slice(t * 128, (t + 1) * 128)
        for half in range(2):
            for h in range(8):
                P.op("pe", lambda e, h=h, half=half: e.matmul(bk[6 + half][:], lhsT=oT[:, h, tsl], rhs=Wo[:, h, half * 512:(half + 1) * 512],
                                                             start=(h == 0), stop=(h == 7)),
                     reads=[oT_b[h], Wo_b], writes=[bb[6 + half]])

    src = self.psum_all[:, 6:8, :].rearrange("p a b -> p (a b)")
    ph = self.boundary_phases(lambda t: (src, [bb[6], bb[7]]), lambda t: (nrm[:, t % 2, :], nrm_b[t % 2]), toks=toks, tbanks=(0, 1))
    pipeline(NT, [outp] + ph)
    for t in range(NT):
        for h in range(8):
            self.acc_b[t].r = list(self.acc_b[t].r) + list(oT_b[h].r)
        self.acc_b[t].r = list(self.acc_b[t].r) + list(ND_b.r) + ([ND_b.w] if ND_b.w else [])
    return toks


Builder.att_main = _att_main2


def _gmlp_stage2(self, i, j):
    P = self.P
    dr = self.dr
    P.barrier()
    bigb = self.big[:].bitcast(BF16)
    Wv = bigb[:, 0:16384].rearrange("p (k n) -> p k n", k=KC)
    Wu = bigb[:, 16384:32768].rearrange("p (k n) -> p k n", k=KC)
    Wv_b = [Buf(f"gWv{c}") for c in range(4)]
    Wu_b = [Buf(f"gWu{c}") for c in range(4)]
    off = 0
    Wo, off = self.st_view(off, [16, D], BF16)
    Wo_b = [Buf(f"gWo{q}") for q in range(4)]
    wsT, off = self.st_view(off, [GM_G, 128], BF16)
    wsT_b = Buf("gwsT")
    gtab, off = self.st_view(off, [1, GM_W], F32)
    btab, off = self.st_view(off, [1, GM_W], BF16)
    gtab_b, btab_b = Buf("ggtab"), Buf("gbtab")
    bsT, off = self.st_view(off, [1, GM_G], F32)
    bsT_b = Buf("gbsT")
    cmask, off = self.st_view(off, [1, 128], F32)
    cmask_b = Buf("gcmask")
    off_vf = off
    vf, off = self.st_view(off, [1, GM_W], F32)
    vf_b = Buf("gvf")
    vn, off = self.st_view(off, [1, GM_W], BF16)
    vn_b = Buf("gvn")
    ssb, off = self.st_view(off, [1, GM_W], F32)
    ssb_b = Buf("gssb")
    us, off = self.st_view(off, [1, GM_W], BF16)
    us_b = Buf("gus")
    usT, off = self.st_view(off, [16, 128], BF16)
    usT_b = Buf("gusT")
    gu, gu_b = [], []
    for s in range(2):
        a, off = self.st_view(off, [1, 512], F32)
        gu.append(a)
        gu_b.append(Buf(f"ggu{s}"))
    nrm, off = self.st_view(off, [1, D], F32)
    nrm_b = Buf("gnrm")
    wsl, off2 = self.st_view(off_vf, [GM_G, 128], BF16)
    wsl_b = vf_b
    w_in = dr["gm_w_in"][j]
    self.wload(wsl, dr["gm_w_s"][j].rearrange("g n m -> n g m"), wsl_b)
    self.wload(btab[:, 0, :], dr["gm_ln_b"][j, :].partition_broadcast(128), btab_b)
    P.dma("sp", gtab[:, 0, :], dr["gm_ln_g"][j, :].partition_broadcast(128), writes=[gtab_b])
    for cb in range(4):
        self.P.dma("pool", Wv[:, :, cb * 512:(cb + 1) * 512], w_in[:, GM_W + cb * 512:GM_W + (cb + 1) * 512].rearrange("(kc p) n -> p kc n", p=128),
                   writes=[Wv_b[cb]])
    for cb in range(4):
        self.P.dma("pool", Wu[:, :, cb * 512:(cb + 1) * 512], w_in[:, cb * 512:(cb + 1) * 512].rearrange("(kc p) n -> p kc n", p=128),
                   writes=[Wu_b[cb]])
    for q in range(4):
        self.wload(Wo[:, q * 4:(q + 1) * 4, :], dr["gm_w_out"][j][q * 512:(q + 1) * 512, :].rearrange("(kc p) n -> p kc n", p=128), Wo_b[q])
    P.dma("sp", bsT[:, 0, :], dr["gm_b_s"][j].rearrange("g n -> n g"), writes=[bsT_b], allow_slow_non_contiguous=True)
    P.dma("sp", cmask[:, 0, :], dr["causalT"][:, :], writes=[cmask_b])
    bk, bb = self.bank, self.bank_b
    tps = bk[0][:].bitcast(BF16)[:, 0:1024].rearrange("p (g n) -> p g n", g=GM_G)
    for g in range(GM_G):
        P.op("pe", lambda e, g=g: e.transpose(out=tps[:, g, :], in_=wsl[:, g, :], identity=self.ident[:]),
             reads=[wsl_b, self.ident_b], writes=[bb[0]])
    for g in range(GM_G):
        P.op("dve", lambda e, g=g: e.tensor_tensor(out=wsT[:, g, :], in0=tps[:, g, :], in1=cmask[:, 0, :], op=ALU.mult),
             reads=[bb[0], cmask_b], writes=[wsT_b])
    toks = []
    stt = {}
    uu = [0]

    def ph0(t):
        tsl = slice(t * 128, (t + 1) * 128)
        sv, sv_b = [], []
        for cb in range(4):
            b = uu[0] % 2
            uu[0] += 1
            for kc in range(KC):
                P.op("pe", lambda e, cb=cb, kc=kc, b=b: e.matmul(bk[b][:], lhsT=self.aT[:, kc, tsl], rhs=Wv[:, kc, cb * 512:(cb + 1) * 512],
                                                               start=(kc == 0), stop=(kc == KC - 1)),
                     reads=[Wv_b[cb], self.aT_b[t]], writes=[bb[b]])
            a, a_b = self.new_stat()
            sv.append(a)
            sv_b.append(a_b)
            P.op("act", lambda e, cb=cb, a=a, b=b: e.activation(out=vf[:, 0, cb * 512:(cb + 1) * 512], in_=bk[b][:], func=AF.Gelu, accum_out=a),
                 reads=[bb[b]], writes=[vf_b, a_b])
        stt[t] = (sv, sv_b)

    def ph1(t):
        sv, sv_b = stt.pop(t)
        sq, sq_b = [], []
        for hh in range(2):
            a, a_b = self.new_stat()
            sq.append(a)
            sq_b.append(a_b)
            P.op("act", lambda e, hh=hh, a=a: e.activation(out=self.junk[:, :], in_=vf[:, 0, hh * 1024:(hh + 1) * 1024], func=AF.Square, accum_out=a),
                 reads=[vf_b], writes=[self.junk_b, a_b])
        m1, m1_b = self.new_stat()
        P.op("pool", lambda e: e.tensor_tensor(out=m1, in0=sv[0], in1=sv[1], op=ALU.add), reads=[sv_b[0], sv_b[1]], writes=[m1_b])
        m2, m2_b = self.new_stat()
        P.op("pool", lambda e: e.tensor_tensor(out=m2, in0=sv[2], in1=sv[3], op=ALU.add), reads=[sv_b[2], sv_b[3]], writes=[m2_b])
        mu, mu_b = self.new_stat()
        P.op("pool", lambda e: e.tensor_tensor(out=mu, in0=m1, in1=m2, op=ALU.add), reads=[m1_b, m2_b], writes=[mu_b])
        P.op("pool", lambda e: e.tensor_scalar(out=mu, in0=mu, scalar1=1.0 / GM_W, scalar2=None, op0=ALU.mult), reads=[], writes=[mu_b])
        e2, e2_b = self.new_stat()
        P.op("pool", lambda e: e.tensor_tensor(out=e2, in0=sq[0], in1=sq[1], op=ALU.add), reads=[sq_b[0], sq_b[1]], writes=[e2_b])
        mm, mm_b = self.new_stat()
        P.op("pool", lambda e: e.tensor_tensor(out=mm, in0=mu, in1=mu, op=ALU.mult), reads=[mu_b], writes=[mm_b])
        var, var_b = self.new_stat()
        P.op("pool", lambda e: e.tensor_scalar(out=var, in0=e2, scalar1=1.0 / GM_W, scalar2=EPS, op0=ALU.mult, op1=ALU.add),
             reads=[e2_b], writes=[var_b])
        P.op("pool", lambda e: e.tensor_tensor(out=var, in0=var, in1=mm, op=ALU.subtract), reads=[mm_b], writes=[var_b])
        rs, rs_b = self.new_stat()
        P.op("pool", lambda e: e.tensor_tensor(out=rs, in0=var, in1=self.cm05[:], op=ALU.pow), reads=[var_b, self.cm05_b], writes=[rs_b])
        P.op("dve", lambda e: e.scalar_tensor_tensor(out=vf[:, 0, :], in0=vf[:, 0, :], scalar=mu, in1=gtab[:, 0, :], op0=ALU.subtract, op1=ALU.mult),
             reads=[mu_b, gtab_b], writes=[vf_b])
        P.op("dve", lambda e: e.scalar_tensor_tensor(out=vn[:, 0, :], in0=vf[:, 0, :], scalar=rs, in1=btab[:, 0, :], op0=ALU.mult, op1=ALU.add),
             reads=[vf_b, rs_b, btab_b], writes=[vn_b])

    def ph2(t):
        tsl = slice(t * 128, (t + 1) * 128)
        for un in range(4):
            bkx = 2 + un % 2
            for gg in range(2):
                g = un * 2 + gg
                P.op("pe", lambda e, g=g, gg=gg, bkx=bkx: e.matmul(bk[bkx][:, gg * 256:(gg + 1) * 256], lhsT=wsT[:, g, :], rhs=vn[:, 0, g * 256:(g + 1) * 256],
                                                                   start=True, stop=True),
                     reads=[wsT_b, vn_b], writes=[bb[bkx]])
            for gg in range(2):
                g = un * 2 + gg
                P.op("dve", lambda e, g=g, gg=gg, bkx=bkx: e.tensor_scalar(out=ssb[:, 0, g * 256:(g + 1) * 256], in0=bk[bkx][:, gg * 256:(gg + 1) * 256],
                                                                            scalar1=bsT[:, 0, g:g + 1], scalar2=None, op0=ALU.add),
                     reads=[bb[bkx], bsT_b], writes=[ssb_b])
        for cb in range(4):
            b = uu[0] % 2
            uu[0] += 1
            for kc in range(KC):
                P.op("pe", lambda e, cb=cb, kc=kc, b=b: e.matmul(bk[b][:], lhsT=self.aT[:, kc, tsl], rhs=Wu[:, kc, cb * 512:(cb + 1) * 512],
                                                               start=(kc == 0), stop=(kc == KC - 1)),
                     reads=[Wu_b[cb], self.aT_b[t]], writes=[bb[b]])
            s = cb % 2
            P.op("act", lambda e, b=b, s=s: e.activation(out=gu[s][:, 0, :], in_=bk[b][:], func=AF.Gelu), reads=[bb[b]], writes=[gu_b[s]])
            P.op("dve", lambda e, cb=cb, s=s: e.tensor_tensor(out=us[:, 0, cb * 512:(cb + 1) * 512], in0=gu[s][:, 0, :], in1=ssb[:, 0, cb * 512:(cb + 1) * 512], op=ALU.mult),
                 reads=[gu_b[s], ssb_b], writes=[us_b])

    def ph3(t):
        for half in range(2):
            tp = bk[4 + half][:].bitcast(BF16).rearrange("p (k n) -> p k n", k=8)
            for q in range(8):
                kc = half * 8 + q
                P.op("pe", lambda e, tp=tp, q=q, kc=kc: e.transpose(out=tp[:, q, :], in_=us[:, 0, kc * 128:(kc + 1) * 128], identity=self.ident[:]),
                     reads=[us_b, self.ident_b], writes=[bb[4 + half]])
            P.op("act", lambda e, tp=tp, half=half: e.copy(out=usT[:, half * 8:(half + 1) * 8, :], in_=tp), reads=[bb[4 + half]], writes=[usT_b])
        for half in range(2):
            for kc in range(16):
                P.op("pe", lambda e, half=half, kc=kc: e.matmul(bk[6 + half][:], lhsT=usT[:, kc, :], rhs=Wo[:, kc, half * 512:(half + 1) * 512],
                                                                start=(kc == 0), stop=(kc == 15)),
                     reads=[usT_b, Wo_b[kc // 4]], writes=[bb[6 + half]])

    src = self.psum_all[:, 6:8, :].rearrange("p a b -> p (a b)")
    bph = self.boundary_phases(lambda t: (src, [bb[6], bb[7]]), lambda t: (nrm[:, 0, :], nrm_b), toks=toks, tbanks=(4, 5))
    def ph3b(t):
        ph3(t)
        bph[0](t)
    pipeline(NT, [ph0, ph1, ph2, ph3b] + bph[1:])
    for t in range(NT):
        for src_b in (Wv_b if t < 8 else Wu_b):
            self.acc_b[t].r = list(self.acc_b[t].r) + list(src_b.r) + ([src_b.w] if src_b.w else [])
    return toks


Builder.gmlp_stage = _gmlp_stage2


def _ffn_ple_stage(self, i, last):
    P = self.P
    self.load_g("post", (i, 3))
    self.load_g("pre", (i, 4))
    P.barrier()
    w_in = self.dr["ffn_w_in"][i]
    w_out = self.dr["ffn_w_out"][i]
    off = 0
    winA, winA_b = [], []
    for s in range(3):
        v, off = self.st_view(off, [KC, 512], BF16)
        winA.append(v)
        winA_b.append(Buf(f"f{i}winA{s}"))
    woutB, woutB_b, woutB_off = [], [], []
    for s in range(2):
        woutB_off.append(off)
        v, off = self.st_view(off, [4, D], BF16)
        woutB.append(v)
        woutB_b.append(Buf(f"f{i}woutB{s}"))
    hidT, hidT_b, hidT_off = [], [], []
    for s in range(2):
        hidT_off.append(off)
        v, off = self.st_view(off, [4, T], BF16)
        hidT.append(v)
        hidT_b.append([[Buf(f"f{i}hid{s}_{jl}_{tg}") for tg in range(4)] for jl in range(4)])
    sil, sil_b = [], []
    for s in range(2):
        v, off = self.st_view(off, [1, 512], F32)
        sil.append(v)
        sil_b.append(Buf(f"f{i}sil{s}"))
    nchunk = FFN_H // 128
    ngrp = (nchunk + 3) // 4
    unit = [0]

    def in_phase(pi):
        ws, wb = winA[pi % 3], winA_b[pi % 3]
        self.wload(ws[:, :, 0:256], w_in[:, pi * 256:(pi + 1) * 256].rearrange("(kc p) n -> p kc n", p=128), wb)
        self.wload(ws[:, :, 256:512], w_in[:, FFN_H + pi * 256:FFN_H + (pi + 1) * 256].rearrange("(kc p) n -> p kc n", p=128), wb)
        for jj in range(2):
            j = 2 * pi + jj
            grp, jl = j // 4, j % 4
            hs_, hs_b = hidT[grp % 2], hidT_b[grp % 2]
            for tg in range(4):
                u = unit[0]
                unit[0] += 1
                bg, bu = 2 * (u % 2), 2 * (u % 2) + 1
                sl, slb = sil[u % 2], sil_b[u % 2]
                for (bk, coff) in ((bg, jj * 128), (bu, 256 + jj * 128)):
                    for kc in range(KC):
                        P.op("pe", lambda e, bk=bk, coff=coff, kc=kc, tg=tg, ws=ws: e.matmul(
                            self.bank[bk][:], lhsT=ws[:, kc, coff:coff + 128], rhs=self.aT[:, kc, tg * 512:(tg + 1) * 512],
                            start=(kc == 0), stop=(kc == KC - 1)),
                            reads=[wb] + self.aT_b[tg * 4:(tg + 1) * 4], writes=[self.bank_b[bk]])
                P.op("act", lambda e, bg=bg, sl=sl: e.activation(out=sl[:, 0, :], in_=self.bank[bg][:], func=AF.Silu),
                     reads=[self.bank_b[bg]], writes=[slb])
                P.op("dve", lambda e, bu=bu, sl=sl, hs_=hs_, jl=jl, tg=tg: e.tensor_tensor(
                    out=hs_[:, jl, tg * 512:(tg + 1) * 512], in0=sl[:, 0, :], in1=self.bank[bu][:], op=ALU.mult),
                    reads=[slb, self.bank_b[bu]], writes=[hs_b[jl][tg]])

    def out_load(grp):
        n = min(4, nchunk - grp * 4)
        wo, wob = woutB[grp % 2], woutB_b[grp % 2]
        self.wload(wo[:, 0:n, :], w_out[grp * 512:grp * 512 + n * 128, :].rearrange("(jl p) n -> p jl n", p=128), wob)

    def out_tile(grp, t, banks):
        n = min(4, nchunk - grp * 4)
        wo, wob = woutB[grp % 2], woutB_b[grp % 2]
        hs_, hs_b = hidT[grp % 2], hidT_b[grp % 2]
        for half in range(2):
            bk = banks[half]
            for q in range(n):
                P.op("pe", lambda e, bk=bk, q=q, half=half: e.matmul(
                    self.bank[bk][:], lhsT=hs_[:, q, t * 128:(t + 1) * 128], rhs=wo[:, q, half * 512:(half + 1) * 512],
                    start=(q == 0), stop=(q == n - 1)),
                    reads=[wob, hs_b[q][t // 4]], writes=[self.bank_b[bk]])
            dst = self.acc[:, t, half * 512:(half + 1) * 512]
            if grp == 0:
                P.op("act", lambda e, bk=bk, dst=dst: e.copy(out=dst, in_=self.bank[bk][:]),
                     reads=[self.bank_b[bk]], writes=[self.acc_b[t]])
            else:
                P.op("dve", lambda e, bk=bk, dst=dst: e.tensor_tensor(out=dst, in0=dst, in1=self.bank[bk][:], op=ALU.add),
                     reads=[self.bank_b[bk]], writes=[self.acc_b[t]])

    pairs_of = lambda grp: range(grp * 2, min(grp * 2 + 2, nchunk // 2))
    for pi in pairs_of(0):
        in_phase(pi)
    for grp in range(ngrp - 1):
        out_load(grp)
        for pi in pairs_of(grp + 1):
            in_phase(pi)
        for t in range(NT):
            out_tile(grp, t, (4 + 2 * (t % 2), 5 + 2 * (t % 2)))
    lg = ngrp - 1
    out_load(lg)
    oth = (lg + 1) % 2
    dead = list(winA_b) + [woutB_b[oth]] + [b for row in hidT_b[oth] for b in row] + list(sil_b)
    assert woutB_off[oth] == 24 * 1024 or woutB_off[oth] == 32 * 1024
    po = 0
    Wg, po = self.st_view(po, [KC, D], BF16)
    Wp, po = self.st_view(po, [2, D], BF16)
    Wg_b, Wg2_b, Wp_b = Buf(f"p{i}Wg"), Buf(f"p{i}Wg2"), Buf(f"p{i}Wp")
    pb, pb_b, pT, pT_b = [], [], [], []
    for s in range(3):
        v, po = self.st_view(po, [1, 256], BF16)
        pb.append(v)
        pb_b.append(Buf(f"p{i}pb{s}"))
        v, po = self.st_view(po, [2, 128], BF16)
        pT.append(v)
        pT_b.append(Buf(f"p{i}pT{s}"))
    lim = 24 * 1024 if woutB_off[oth] != 24 * 1024 else 32 * 1024
    assert po <= lim, (po, lim)
    so = hidT_off[oth]
    sg, sg_b = [], []
    for s in range(3):
        v, so = self.st_view(so, [2, 512], F32)
        sg.append(v)
        sg_b.append(Buf(f"p{i}sg{s}"))
    assert so <= hidT_off[oth] + 16 * 1024
    if lim == 24 * 1024:
        dead = list(winA_b) + [b for row in hidT_b[oth] for b in row]
    P.dma("pool", Wg[:, :, 0:512], self.dr["ple_w_gate"][i][:, 0:512].rearrange("(kc p) n -> p kc n", p=128), writes=[Wg_b] + dead)
    P.dma("pool", Wp, self.dr["ple_w_proj"][i].rearrange("(kc p) n -> p kc n", p=128), writes=[Wp_b])
    P.dma("pool", Wg[:, :, 512:1024], self.dr["ple_w_gate"][i][:, 512:1024].rearrange("(kc p) n -> p kc n", p=128), writes=[Wg2_b])
    if not last:
        self.load_g("pre2", (i + 1, 0))
    toks_b, toks = [], []
    hst = {}

    def mm(t):
        s = t % 3
        P.dma("pool", pb[s][:, 0, :], self.dr["p"][i, t * 128:(t + 1) * 128, :], writes=[pb_b[s]])
        pst = self.bank[7][:].bitcast(BF16)[:, 0:256].rearrange("p (k n) -> p k n", k=2)
        for kc in range(2):
            P.op("pe", lambda e, kc=kc: e.transpose(out=pst[:, kc, :], in_=pb[s][:, 0, kc * 128:(kc + 1) * 128], identity=self.ident[:]),
                 reads=[pb_b[s], self.ident_b], writes=[self.bank_b[7]])
        P.op("dve", lambda e: e.tensor_copy(out=pT[s], in_=pst), reads=[self.bank_b[7]], writes=[pT_b[s]])
        for half in range(2):
            u = 2 * t + half
            bG, bE = u % 2, 2 + u % 2
            wgb = Wg_b if half == 0 else Wg2_b
            for kc in range(KC):
                P.op("pe", lambda e, kc=kc, half=half, bG=bG: e.matmul(
                    self.bank[bG][:], lhsT=self.aT[:, kc, t * 128:(t + 1) * 128], rhs=Wg[:, kc, half * 512:(half + 1) * 512],
                    start=(kc == 0), stop=(kc == KC - 1)), reads=[wgb, self.aT_b[t]], writes=[self.bank_b[bG]])
            for kc in range(2):
                P.op("pe", lambda e, kc=kc, half=half, bE=bE: e.matmul(
                    self.bank[bE][:], lhsT=pT[s][:, kc, :], rhs=Wp[:, kc, half * 512:(half + 1) * 512],
                    start=(kc == 0), stop=(kc == 1)), reads=[Wp_b, pT_b[s]], writes=[self.bank_b[bE]])
            P.op("act", lambda e, half=half, bG=bG: e.activation(out=sg[s][:, half, :], in_=self.bank[bG][:], func=AF.Sigmoid),
                 reads=[self.bank_b[bG]], writes=[sg_b[s]])
            P.op("dve", lambda e, half=half, bE=bE: e.tensor_tensor(out=sg[s][:, half, :], in0=sg[s][:, half, :], in1=self.bank[bE][:], op=ALU.mult),
                 reads=[self.bank_b[bE]], writes=[sg_b[s]])

    def post(t):
        hs, hb = self.h_load(t, src=getattr(self, "h_rd_ple", None))
        hst[t] = (hs, hb)

    def resid(t):
        s = t % 3
        hs, hb = hst[t]
        P.op("pool", lambda e: e.tensor_tensor(out=hs, in0=hs, in1=sg[s].rearrange("p a b -> p (a b)"), op=ALU.add),
             reads=[sg_b[s]], writes=[hb])
        toks.append(self.h_store(t, hs, hb, dst=getattr(self, "h_wr_ple", None)))

    bph = self.boundary_phases(lambda t: (self.acc[:, t, :], [self.acc_b[t]]), lambda t: (self.acc[:, t, :], self.acc_b[t]),
                               toks=toks_b, tbanks=(6,))
    phases = [lambda t: out_tile(lg, t, (4, 5))] + bph + [mm, post]
    if not last:
        pn = self.pre_norm_phases(lambda t: hst.pop(t), tbanks=(7,), gain=(self.gpre2, self.gpre2_b))

        def resid_pa(t):
            resid(t)
            pn[0](t)
        phases += [resid_pa, pn[1]]
    else:
        phases += [resid]
    pipeline(NT, phases)
    return toks


Builder.ffn_ple_stage = _ffn_ple_stage


def _rstd_split(self, src_ap, src_bufs, n):
    P = self.P
    st = {}

    def sq():
        ss, ss_b = self.new_stat()
        P.op("act", lambda e: e.activation(out=self.junk[:, 0:n], in_=src_ap, func=AF.Square, scale=float(n) ** -0.5, accum_out=ss),
             reads=list(src_bufs), writes=[self.junk_b, ss_b])
        st["ss"] = (ss, ss_b)

    def pw():
        ss, ss_b = st["ss"]
        r, r_b = self.new_stat()
        P.op("pool", lambda e: e.tensor_scalar(out=r, in0=ss, scalar1=EPS, scalar2=None, op0=ALU.add), reads=[ss_b], writes=[r_b])
        r2, r2_b = self.new_stat()
        P.op("pool", lambda e: e.tensor_tensor(out=r2, in0=r, in1=self.cm05[:], op=ALU.pow), reads=[r_b, self.cm05_b], writes=[r2_b])
        st["r"] = (r2, r2_b)

    return sq, pw, (lambda: st["r"])


def _pre_norm_phases_f(self, get_h, tbanks=(6, 7), merge_b=False):
    P = self.P
    S = {}

    def a1(t):
        hs, hb = get_h(t)
        sq, pw, get = self.rstd_split(hs, [hb], D)
        sq()
        S[t] = [hs, hb, pw, get]

    def a2(t):
        S[t][2]()

    def a3(t):
        hs, hb, pw, get = S[t]
        r, r_b = get()
        s = t % 2
        ab, ab_b = self.abf[:, s, :], self.abf_b[s]
        P.op("dve", lambda e: e.scalar_tensor_tensor(out=ab, in0=hs, scalar=r, in1=self.gpre[:], op0=ALU.mult, op1=ALU.mult),
             reads=[hb, r_b, self.gpre_b], writes=[ab_b])
        S[t] = (s, ab_b)

    def b1(t):
        s, ab_b = S[t]
        tbank = tbanks[t % len(tbanks)]
        pst = self.bank[tbank][:].bitcast(BF16).rearrange("p (k n) -> p k n", k=KC)
        for kc in range(KC):
            P.op("pe", lambda e, kc=kc: e.transpose(out=pst[:, kc, :], in_=self.abf[:, s, kc * 128:(kc + 1) * 128], identity=self.ident[:]),
                 reads=[ab_b, self.ident_b], writes=[self.bank_b[tbank]])
        S[t] = (tbank, pst)

    def b2(t):
        tbank, pst = S.pop(t)
        P.op("act", lambda e: e.copy(out=self.aT[:, :, t * 128:(t + 1) * 128], in_=pst),
             reads=[self.bank_b[tbank]], writes=[self.aT_b[t]])

    if merge_b:
        def b12(t):
            b1(t)
            b2(t)
        return [a1, a2, a3, b12]
    return [a1, a2, a3, b1, b2]


def _boundary_phases_f(self, src_fn, dst_fn, do_pre=True, tbanks=(6, 7), toks=None, merge_b=False):
    P = self.P
    S = {}
    H = {}

    def p0a(t):
        src_ap, src_b = src_fn(t)
        sq, pw, get = self.rstd_split(src_ap, src_b, D)
        sq()
        S[t] = [src_ap, src_b, pw, get]

    def p0b(t):
        S[t][2]()

    def p0c(t):
        src_ap, src_b, pw, get = S[t]
        dst_ap, dst_b = dst_fn(t)
        r, r_b = get()
        P.op("dve", lambda e: e.scalar_tensor_tensor(out=dst_ap, in0=src_ap, scalar=r, in1=self.gpost[:], op0=ALU.mult, op1=ALU.mult),
             reads=list(src_b) + [r_b, self.gpost_b], writes=[dst_b])
        hs, hb = self.h_load(t)
        S[t] = (dst_ap, dst_b, hs, hb)

    def p1(t):
        dst_ap, dst_b, hs, hb = S.pop(t)
        P.op("dve", lambda e: e.tensor_tensor(out=hs, in0=hs, in1=dst_ap, op=ALU.add), reads=[dst_b], writes=[hb])
        tok = self.h_store(t, hs, hb)
        if toks is not None:
            toks.append(tok)
        H[t] = (hs, hb)

    phases = [p0a, p0b, p0c, p1]
    if do_pre:
        pn = self.pre_norm_phases_f(lambda t: H.pop(t), tbanks=tbanks, merge_b=merge_b)

        def p1a(t):
            p1(t)
            pn[0](t)
        phases = [p0a, p0b, p0c, p1a] + pn[1:]
    return phases


Builder.rstd_split = _rstd_split
Builder.pre_norm_phases_f = _pre_norm_phases_f
Builder.boundary_phases_f = _boundary_phases_f


def _first_norm_f(self):
    self.set_hextra(0, 5)
    pipeline(NT, self.pre_norm_phases_f(lambda t: self.h_load(t), tbanks=(6, 7)))


Builder.first_norm = _first_norm_f
```

```python
import contextlib
import math
import numpy as np
import concourse.bass as bass
import concourse.mybir as mybir
from concourse.bass_utils import run_bass_kernel_spmd

F32 = mybir.dt.float32
BF16 = mybir.dt.bfloat16
I32 = mybir.dt.int32
AF = mybir.ActivationFunctionType
ALU = mybir.AluOpType
AX = mybir.AxisListType

SEM_ROLL = 30000
import os as _os
_SEQ_DEBUG = bool(_os.environ.get('KSEQ'))

D = 1024
KC = 8
BATCH = 2
SEQ = 8192
DEPTH = 4
NCORES = 8
T = 2048
NT = 16
EPS = 1e-6
PLE_DIM = 256
FFN_H = 2816
RET_H, RET_DK, RET_DV = 4, 256, 512
L_RANKS = 3
ATT_H, ATT_DH = 16, 64
DIL = ((128, 1), (512, 4), (2048, 16))
GM_W = 2048
GM_G = 8


class Buf:
    __slots__ = ("name", "w", "r", "dsem", "dcount")

    def __init__(self, name):
        self.name = name
        self.w = None
        self.r = []
        self.dsem = None
        self.dcount = 0


class Prog:
    ENGS = ("pe", "act", "dve", "pool", "sp")

    def __init__(self, nc):
        self.nc = nc
        self.stack = contextlib.ExitStack()
        self.ops = {e: [] for e in self.ENGS}
        self.sems = {}
        self.esem = {}
        self.ecount = {e: 0 for e in self.ENGS}
        self.eroll = {e: 0 for e in self.ENGS}
        self.known = {e: {} for e in self.ENGS}
        self.nsem = 0
        self.final_tokens = []
        self.chans = []
        self.free_sems = []
        for e in self.ENGS:
            self._new_esem(e)

    def sem(self, name):
        h = self.stack.enter_context(self.nc.semaphore(name))
        self.sems[name] = h
        self.nsem += 1
        return name

    def _new_esem(self, e):
        k = self.sem(f"s_{e}_{self.eroll[e]}")
        self.eroll[e] += 1
        self.esem[e] = k
        self.ecount[e] = 0

    def sbuf(self, name, shape, dtype):
        return self.stack.enter_context(self.nc.sbuf_tensor("sb_" + name, list(shape), dtype))

    def psum(self, name, shape, dtype):
        return self.stack.enter_context(self.nc.psum_tensor("ps_" + name, list(shape), dtype))

    def _deps(self, eng, reads, writes, skip_same_pe=False):
        waits = {}

        def need(tok):
            if tok is None:
                return
            k, v = tok
            if skip_same_pe and eng == "pe" and k == self.esem["pe"]:
                return
            if self.known[eng].get(k, 0) >= v:
                return
            if waits.get(k, 0) < v:
                waits[k] = v

        for b in reads:
            need(b.w)
        for b in writes:
            need(b.w)
            for t in b.r:
                need(t)
        for k, v in waits.items():
            self.known[eng][k] = v
        return list(waits.items())

    def _commit(self, tok, reads, writes):
        for b in writes:
            b.w = tok
            b.r = []
        for b in reads:
            b.r.append(tok)
            if len(b.r) > 48:
                best = {}
                for k, v in b.r:
                    if best.get(k, 0) < v:
                        best[k] = v
                b.r = list(best.items())

    def op(self, eng, fn, reads=(), writes=()):
        if self.ecount[eng] >= SEM_ROLL:
            self._new_esem(eng)
        waits = self._deps(eng, reads, writes, skip_same_pe=True)
        self.ecount[eng] += 1
        tok = (self.esem[eng], self.ecount[eng])
        self.ops[eng].append((fn, waits, (self.esem[eng], 1)))
        self._commit(tok, reads, writes)
        return tok

    def dma(self, q, out_ap, in_ap, reads=(), writes=(), chan=None, **kw):
        if chan is None:
            chan = writes[0] if writes else reads[0]
        if chan.dsem is None:
            self.free_sems.sort(key=lambda x: x[1])
            if self.free_sems and self.free_sems[0][1] < 16000:
                chan.dsem, chan.dcount = self.free_sems.pop(0)
            else:
                chan.dsem = self.sem(f"d_{chan.name}")
                chan.dcount = 0
            self.chans.append(chan)
        waits = self._deps(q, reads, writes)
        if chan.dcount > 0 and self.known[q].get(chan.dsem, 0) < chan.dcount:
            waits = [w for w in waits if w[0] != chan.dsem] + [(chan.dsem, chan.dcount)]
            self.known[q][chan.dsem] = chan.dcount
        chan.dcount += 16
        tok = (chan.dsem, chan.dcount)

        def fn(e, out_ap=out_ap, in_ap=in_ap, kw=kw):
            return e.dma_start(out=out_ap, in_=in_ap, **kw)

        self.ops[q].append((fn, waits, (chan.dsem, 16)))
        self._commit(tok, reads, writes)
        return tok

    def barrier(self):
        toks = {}
        for e in self.ENGS:
            if self.ecount[e] > 0:
                toks[self.esem[e]] = self.ecount[e]
        for c in self.chans:
            if c.dcount > 0:
                toks[c.dsem] = c.dcount
        for e in self.ENGS:
            waits = []
            for k, v in toks.items():
                if self.known[e].get(k, 0) < v:
                    waits.append((k, v))
                    self.known[e][k] = v
            if waits:
                self.ops[e].append((None, waits, None))

    def retire(self, bufs):
        for c in bufs:
            if c.dsem is not None:
                self.free_sems.append((c.dsem, c.dcount))
                self.chans.remove(c)
                c.dsem = None
                c.dcount = 0

    def finish_on(self, eng, toks):
        self.final_tokens.append((eng, list(toks)))

    def emit(self):
        nc = self.nc
        for eng, toks in self.final_tokens:
            best = {}
            for k, v in toks:
                if best.get(k, 0) < v:
                    best[k] = v
            self.ops[eng].append((None, list(best.items()), None))
        handles = {"pe": "tensor", "act": "scalar", "dve": "vector", "pool": "gpsimd", "sp": "sync"}
        with nc.Block() as block:
            for eng in self.ENGS:
                ops = self.ops[eng]
                if not ops:
                    continue

                def body(e, ops=ops):
                    for fn, waits, inc in ops:
                        for k, v in waits:
                            e.wait_ge(self.sems[k], v)
                        if fn is not None:
                            ins = fn(e)
                            ins.then_inc(self.sems[inc[0]], inc[1])

                getattr(block, handles[eng])(body)
        self.stack.close()

    def stats(self):
        d = {e: len(v) for e, v in self.ops.items()}
        d["nsem"] = self.nsem
        return d


class Builder:
    ST_BYTES = 84 * 1024

    def __init__(self, nc, dr):
        self.nc = nc
        self.dr = dr
        P = self.P = Prog(nc)
        self.aT = P.sbuf("aT", [128, KC, T], BF16)
        self.aT_b = [Buf(f"aT{t}") for t in range(NT)]
        self.big = P.sbuf("big", [128, NT * D], F32)
        self.acc = self.big[:].rearrange("p (t n) -> p t n", t=NT)
        self.acc_b = [Buf(f"acc{t}") for t in range(NT)]
        self.st = P.sbuf("st", [128, self.ST_BYTES // 2], BF16)
        self.ident = P.sbuf("ident", [128, 128], BF16)
        self.ident_b = Buf("ident")
        self.hring = P.sbuf("hring", [128, 3, D], F32)
        self.hr_b = [Buf("hr0"), Buf("hr1"), Buf("hr2")]
        self.hextra = []
        self.gpost = P.sbuf("gpost", [128, D], F32)
        self.gpost_b = Buf("gpost")
        self.gpre = P.sbuf("gpre", [128, D], F32)
        self.gpre_b = Buf("gpre")
        self.gpre2 = None
        self.gpre2_b = None
        self.abf_n = 2
        self.abf = P.sbuf("abf", [128, 2, D], BF16)
        self.abf_b = [Buf("abf0"), Buf("abf1")]
        self.junk = P.sbuf("junk", [128, D], BF16)
        self.junk_b = Buf("junk")
        self.stat = P.sbuf("stat", [128, 64], F32)
        self.stat_b = [Buf(f"stat{i}") for i in range(64)]
        self.stat_i = 0
        self.psum_all = P.psum("all", [128, 8, 512], F32)
        self.bank = [self.psum_all[:, i, :] for i in range(8)]
        self.bank_b = [Buf(f"bank{i}") for i in range(8)]
        self.hD_b = [Buf(f"hD{t}") for t in range(NT)]
        self.h_rd = None
        self.h_wr = None
        self.hcnt = 0
        P.dma("pool", self.ident[:], dr["ident"][:, :], writes=[self.ident_b])
        _orig_barrier = P.barrier

        def _barrier():
            _orig_barrier()
            P.retire([b for _, b in self.hextra])
            self.hextra = []
        P.barrier = _barrier
        self.cm05 = P.sbuf("cm05", [128, 1], F32)
        self.cm05_b = Buf("cm05")
        P.op("pool", lambda e: e.memset(self.cm05[:], -0.5), writes=[self.cm05_b])

    def st_view(self, off_bytes, shape, dtype):
        n = int(np.prod(shape))
        if dtype == BF16:
            a = self.st[:, off_bytes // 2: off_bytes // 2 + n]
            nb = n * 2
        else:
            a = self.st[:, off_bytes // 2: off_bytes // 2 + 2 * n].bitcast(F32)
            nb = n * 4
        assert off_bytes + nb <= self.ST_BYTES, (off_bytes, nb)
        if len(shape) == 2:
            v = a.rearrange("p (a b) -> p a b", a=shape[0])
        elif len(shape) == 3:
            v = a.rearrange("p (a b c) -> p a b c", a=shape[0], b=shape[1])
        else:
            v = a
        return v, off_bytes + nb

    def set_hextra(self, off_bytes, n):
        self.hextra = []
        for k in range(n):
            v, off_bytes = self.st_view(off_bytes, [1, D], F32)
            self.hextra.append((v[:, 0, :], Buf(f"hx{self.hcnt}_{k}")))

    def new_stat(self):
        i = self.stat_i % 64
        self.stat_i += 1
        return self.stat[:, i:i + 1], self.stat_b[i]

    def load_g(self, which, idx):
        tile, b = {"post": (self.gpost, self.gpost_b), "pre": (self.gpre, self.gpre_b), "pre2": (self.gpre2, self.gpre2_b)}[which]
        i, j = idx
        self.P.dma("sp", tile[:], self.dr["norm_g"][i, j, :].partition_broadcast(128), writes=[b])

    def rstd_of(self, src_ap, src_bufs, n):
        P = self.P
        ss, ss_b = self.new_stat()
        P.op("act", lambda e: e.activation(out=self.junk[:, 0:n], in_=src_ap, func=AF.Square, scale=float(n) ** -0.5, accum_out=ss),
             reads=list(src_bufs), writes=[self.junk_b, ss_b])
        r, r_b = self.new_stat()
        P.op("pool", lambda e: e.tensor_scalar(out=r, in0=ss, scalar1=EPS, scalar2=None, op0=ALU.add),
             reads=[ss_b], writes=[r_b])
        r2, r2_b = self.new_stat()
        P.op("pool", lambda e: e.tensor_tensor(out=r2, in0=r, in1=self.cm05[:], op=ALU.pow),
             reads=[r_b, self.cm05_b], writes=[r2_b])
        return r2, r2_b

    def h_load(self, t, src=None):
        P = self.P
        slots = [(self.hring[:, k, :], self.hr_b[k]) for k in range(3)] + list(self.hextra)
        hs, hb = slots[self.hcnt % len(slots)]
        self.hcnt += 1
        src = self.h_rd if src is None else src
        P.dma("sp", hs, src[t * 128:(t + 1) * 128, :], reads=[self.hD_b[t]], writes=[hb])
        return hs, hb

    def h_store(self, t, hs, hb, dst=None):
        dst = self.h_wr if dst is None else dst
        return self.P.dma("sp", dst[t * 128:(t + 1) * 128, :], hs, reads=[hb], writes=[self.hD_b[t]], chan=hb)

    def pre_norm(self, t, hs, hb, tbank=7):
        P = self.P
        r, r_b = self.rstd_of(hs, [hb], D)
        s = t % 2
        ab, ab_b = self.abf[:, s, :], self.abf_b[s]
        P.op("dve", lambda e: e.scalar_tensor_tensor(out=ab, in0=hs, scalar=r, in1=self.gpre[:], op0=ALU.mult, op1=ALU.mult),
             reads=[hb, r_b, self.gpre_b], writes=[ab_b])
        pst = self.bank[tbank][:].bitcast(BF16).rearrange("p (k n) -> p k n", k=KC)
        for kc in range(KC):
            P.op("pe", lambda e, kc=kc: e.transpose(out=pst[:, kc, :], in_=self.abf[:, s, kc * 128:(kc + 1) * 128], identity=self.ident[:]),
                 reads=[ab_b, self.ident_b], writes=[self.bank_b[tbank]])
        P.op("act", lambda e: e.copy(out=self.aT[:, :, t * 128:(t + 1) * 128], in_=pst),
             reads=[self.bank_b[tbank]], writes=[self.aT_b[t]])

    def boundary(self, t, src_ap, src_b, dst_ap, dst_b, do_pre=True, tbank=7):
        P = self.P
        r, r_b = self.rstd_of(src_ap, src_b, D)
        P.op("dve", lambda e: e.scalar_tensor_tensor(out=dst_ap, in0=src_ap, scalar=r, in1=self.gpost[:], op0=ALU.mult, op1=ALU.mult),
             reads=list(src_b) + [r_b, self.gpost_b], writes=[dst_b])
        hs, hb = self.h_load(t)
        P.op("pool", lambda e: e.tensor_tensor(out=hs, in0=hs, in1=dst_ap, op=ALU.add), reads=[dst_b], writes=[hb])
        tok = self.h_store(t, hs, hb)
        if do_pre:
            self.pre_norm(t, hs, hb, tbank=tbank)
        return tok

    def stage_barrier(self):
        self.P.barrier()
        self.hextra = []

    def first_norm(self):
        for t in range(NT):
            hs, hb = self.h_load(t)
            self.pre_norm(t, hs, hb)

    def wload(self, dst_ap, src_ap, buf):
        return self.P.dma("pool", dst_ap, src_ap, writes=[buf])

    def ffn(self, i):
        P = self.P
        P.barrier()
        w_in = self.dr["ffn_w_in"][i]
        w_out = self.dr["ffn_w_out"][i]
        off = 0
        winA, winA_b = [], []
        for s in range(3):
            v, off = self.st_view(off, [KC, 512], BF16)
            winA.append(v)
            winA_b.append(Buf(f"f{i}winA{s}"))
        woutB, woutB_b = [], []
        for s in range(2):
            v, off = self.st_view(off, [4, D], BF16)
            woutB.append(v)
            woutB_b.append(Buf(f"f{i}woutB{s}"))
        hidT, hidT_b = [], []
        for s in range(2):
            v, off = self.st_view(off, [4, T], BF16)
            hidT.append(v)
            hidT_b.append([[Buf(f"f{i}hid{s}_{jl}_{tg}") for tg in range(4)] for jl in range(4)])
        sil, sil_b = [], []
        for s in range(2):
            v, off = self.st_view(off, [1, 512], F32)
            sil.append(v)
            sil_b.append(Buf(f"f{i}sil{s}"))
        nchunk = FFN_H // 128
        ngrp = (nchunk + 3) // 4
        unit = 0
        for pi in range(nchunk // 2):
            ws, wb = winA[pi % 3], winA_b[pi % 3]
            self.wload(ws[:, :, 0:256], w_in[:, pi * 256:(pi + 1) * 256].rearrange("(kc p) n -> p kc n", p=128), wb)
            self.wload(ws[:, :, 256:512], w_in[:, FFN_H + pi * 256:FFN_H + (pi + 1) * 256].rearrange("(kc p) n -> p kc n", p=128), wb)
            for jj in range(2):
                j = 2 * pi + jj
                grp, jl = j // 4, j % 4
                hs_, hs_b = hidT[grp % 2], hidT_b[grp % 2]
                for tg in range(4):
                    bg, bu = 2 * (unit % 2), 2 * (unit % 2) + 1
                    sl, slb = sil[unit % 2], sil_b[unit % 2]
                    unit += 1
                    for (bk, coff) in ((bg, jj * 128), (bu, 256 + jj * 128)):
                        for kc in range(KC):
                            P.op("pe", lambda e, bk=bk, coff=coff, kc=kc, tg=tg, ws=ws: e.matmul(
                                self.bank[bk][:], lhsT=ws[:, kc, coff:coff + 128], rhs=self.aT[:, kc, tg * 512:(tg + 1) * 512],
                                start=(kc == 0), stop=(kc == KC - 1)),
                                reads=[wb] + self.aT_b[tg * 4:(tg + 1) * 4], writes=[self.bank_b[bk]])
                    P.op("act", lambda e, bg=bg, sl=sl: e.activation(out=sl[:, 0, :], in_=self.bank[bg][:], func=AF.Silu),
                         reads=[self.bank_b[bg]], writes=[slb])
                    P.op("dve", lambda e, bu=bu, sl=sl, hs_=hs_, jl=jl, tg=tg: e.tensor_tensor(
                        out=hs_[:, jl, tg * 512:(tg + 1) * 512], in0=sl[:, 0, :], in1=self.bank[bu][:], op=ALU.mult),
                        reads=[slb, self.bank_b[bu]], writes=[hs_b[jl][tg]])
                if jl == 3 or j == nchunk - 1:
                    n = jl + 1
                    wo, wob = woutB[grp % 2], woutB_b[grp % 2]
                    self.wload(wo[:, 0:n, :], w_out[grp * 512:grp * 512 + n * 128, :].rearrange("(jl p) n -> p jl n", p=128), wob)
                    for t in range(NT):
                        for half in range(2):
                            bk = 4 + 2 * (t % 2) + half
                            for q in range(n):
                                P.op("pe", lambda e, bk=bk, q=q, t=t, half=half, hs_=hs_, wo=wo, n=n: e.matmul(
                                    self.bank[bk][:], lhsT=hs_[:, q, t * 128:(t + 1) * 128], rhs=wo[:, q, half * 512:(half + 1) * 512],
                                    start=(q == 0), stop=(q == n - 1)),
                                    reads=[wob, hs_b[q][t // 4]], writes=[self.bank_b[bk]])
                            dst = self.acc[:, t, half * 512:(half + 1) * 512]
                            if grp == 0:
                                P.op("act", lambda e, bk=bk, dst=dst: e.copy(out=dst, in_=self.bank[bk][:]),
                                     reads=[self.bank_b[bk]], writes=[self.acc_b[t]])
                            else:
                                P.op("dve", lambda e, bk=bk, dst=dst: e.tensor_tensor(out=dst, in0=dst, in1=self.bank[bk][:], op=ALU.add),
                                     reads=[self.bank_b[bk]], writes=[self.acc_b[t]])
        assert grp == ngrp - 1

    def ple(self, i, do_pre=True):
        P = self.P
        P.barrier()
        off = 0
        Wg, off = self.st_view(off, [KC, D], BF16)
        Wg_b = Buf(f"p{i}Wg")
        Wp, off = self.st_view(off, [2, D], BF16)
        Wp_b = Buf(f"p{i}Wp")
        pb, pb_b, pT, pT_b, sg, sg_b = [], [], [], [], [], []
        for s in range(2):
            v, off = self.st_view(off, [1, 256], BF16)
            pb.append(v)
            pb_b.append(Buf(f"p{i}pb{s}"))
            v, off = self.st_view(off, [2, 128], BF16)
            pT.append(v)
            pT_b.append(Buf(f"p{i}pT{s}"))
            v, off = self.st_view(off, [2, 512], F32)
            sg.append(v)
            sg_b.append(Buf(f"p{i}sg{s}"))
        self.wload(Wg, self.dr["ple_w_gate"][i].rearrange("(kc p) n -> p kc n", p=128), Wg_b)
        self.wload(Wp, self.dr["ple_w_proj"][i].rearrange("(kc p) n -> p kc n", p=128), Wp_b)
        toks = []
        for t in range(NT):
            s = t % 2
            self.wload(pb[s][:, 0, :], self.dr["p"][i, t * 128:(t + 1) * 128, :], pb_b[s])
            ptb = 6
            pst = self.bank[ptb][:].bitcast(BF16)[:, 0:256].rearrange("p (k n) -> p k n", k=2)
            for kc in range(2):
                P.op("pe", lambda e, kc=kc, s=s, pst=pst: e.transpose(out=pst[:, kc, :], in_=pb[s][:, 0, kc * 128:(kc + 1) * 128], identity=self.ident[:]),
                     reads=[pb_b[s], self.ident_b], writes=[self.bank_b[ptb]])
            P.op("dve", lambda e, s=s, pst=pst: e.tensor_copy(out=pT[s], in_=pst), reads=[self.bank_b[ptb]], writes=[pT_b[s]])
            for half in range(2):
                u = 2 * t + half
                bG, bE = u % 3, 3 + u % 3
                for kc in range(KC):
                    P.op("pe", lambda e, kc=kc, half=half, bG=bG, t=t: e.matmul(
                        self.bank[bG][:], lhsT=self.aT[:, kc, t * 128:(t + 1) * 128], rhs=Wg[:, kc, half * 512:(half + 1) * 512],
                        start=(kc == 0), stop=(kc == KC - 1)), reads=[Wg_b, self.aT_b[t]], writes=[self.bank_b[bG]])
                for kc in range(2):
                    P.op("pe", lambda e, kc=kc, half=half, bE=bE, s=s: e.matmul(
                        self.bank[bE][:], lhsT=pT[s][:, kc, :], rhs=Wp[:, kc, half * 512:(half + 1) * 512],
                        start=(kc == 0), stop=(kc == 1)), reads=[Wp_b, pT_b[s]], writes=[self.bank_b[bE]])
                P.op("act", lambda e, half=half, bG=bG, s=s: e.activation(out=sg[s][:, half, :], in_=self.bank[bG][:], func=AF.Sigmoid),
                     reads=[self.bank_b[bG]], writes=[sg_b[s]])
                P.op("dve", lambda e, half=half, bE=bE, s=s: e.tensor_tensor(out=sg[s][:, half, :], in0=sg[s][:, half, :], in1=self.bank[bE][:], op=ALU.mult),
                     reads=[self.bank_b[bE]], writes=[sg_b[s]])
            hs, hb = self.h_load(t)
            P.op("pool", lambda e, hs=hs, s=s: e.tensor_tensor(out=hs, in0=hs, in1=sg[s].rearrange("p a b -> p (a b)"), op=ALU.add),
                 reads=[sg_b[s]], writes=[hb])
            toks.append(self.h_store(t, hs, hb))
            if do_pre:
                self.pre_norm(t, hs, hb)
        return toks

    def ffn_stage(self, i):
        self.load_g("post", (i, 3))
        self.load_g("pre", (i, 4))
        self.ffn(i)
        toks = []
        for t in range(NT):
            a = self.acc[:, t, :]
            toks.append(self.boundary(t, a, [self.acc_b[t]], a, self.acc_b[t]))
        return toks

    def ple_stage(self, i, last):
        if not last:
            self.load_g("pre", (i + 1, 0))
        return self.ple(i, do_pre=not last)


def _ret_emit(self, j, pre):
    P = self.P
    P.barrier()
    dr = self.dr
    w_in = dr["ret_w_in"][j]
    w_out = dr["ret_w_out"][j]
    off = 0
    W, off = self.st_view(off, [KC, 1536], BF16)
    W_b = {k: Buf(f"r{j}{pre}W{k}") for k in "qkvg"}
    wo, off = self.st_view(off, [4, D], BF16)
    wo_b = Buf(f"r{j}{pre}wo")
    cs, cs_b = [], []
    for s in range(2):
        v, off = self.st_view(off, [2, 512], F32)
        cs.append(v)
        cs_b.append(Buf(f"r{j}{pre}cs{s}"))
    qT, off = self.st_view(off, [2, 512], BF16)
    kT, off = self.st_view(off, [2, 512], BF16)
    qdT, off = self.st_view(off, [2, 512], BF16)
    qT_b, kT_b, qdT_b = Buf("qT"), Buf("kT"), Buf("qdT")
    rt, off = self.st_view(off, [4, 512], F32)
    rt_b = [Buf(f"rt{x}") for x in range(4)]
    qdec, off = self.st_view(off, [1, 512], F32)
    qdec_b = Buf("qdec")
    v_sb, sg, sm, kd, gy, gyT = [], [], [], [], [], []
    v_b, sg_b, sm_b, kd_b, gy_b, gyT_b = [], [], [], [], [], []
    for s in range(2):
        a, off = self.st_view(off, [1, 512], BF16); v_sb.append(a); v_b.append(Buf(f"v{s}"))
        a, off = self.st_view(off, [1, 512], BF16); sg.append(a); sg_b.append(Buf(f"sg{s}"))
        a, off = self.st_view(off, [1, 128], BF16); sm.append(a); sm_b.append(Buf(f"sm{s}"))
        a, off = self.st_view(off, [1, 256], BF16); kd.append(a); kd_b.append(Buf(f"kd{s}"))
        a, off = self.st_view(off, [1, 512], BF16); gy.append(a); gy_b.append(Buf(f"gy{s}"))
        a, off = self.st_view(off, [4, 128], BF16); gyT.append(a); gyT_b.append(Buf(f"gyT{s}"))
    maskT, off = self.st_view(off, [RET_H, 128], F32)
    maskT_b = Buf("maskT")
    kdec, off = self.st_view(off, [1, RET_H], F32)
    kdec_b = Buf("kdec")
    coef, off = self.st_view(off, [1, NCORES * RET_H], F32)
    coef_b = Buf("coef")
    state, off = self.st_view(off, [2, 512], F32)
    state_b = Buf("state")
    sbf, off = self.st_view(off, [2, 512], BF16)
    sbf_b = Buf("sbf")
    P.dma("sp", maskT, dr["ret_maskT"].rearrange("h m n -> m h n"), writes=[maskT_b])
    P.dma("sp", kdec[:, 0, :], dr["ret_kdec"][:, :], writes=[kdec_b])
    P.dma("sp", coef[:, 0, :], dr["ret_coef"][:, :], writes=[coef_b])
    bk = self.bank
    bb = self.bank_b
    b6 = bk[6][:]
    sT_ps = b6[:, 0:128]
    sT_pb = Buf("b6a")
    kdT_ps = b6[:, 128:256].bitcast(BF16).rearrange("p (a b) -> p a b", a=2)
    kdT_pb = Buf("b6b")
    gyT_ps = b6[:, 256:512].bitcast(BF16).rearrange("p (a b) -> p a b", a=4)
    gyT_pb = Buf("b6c")
    gam = [1.0 - 2.0 ** (-5.0 - h) for h in range(RET_H)]

    for hd in range(RET_H):
        cd = gam[hd] ** 128
        cols = {"q": (hd * 256, 256, 0), "k": (1024 + hd * 256, 256, 256), "v": (2048 + hd * 512, 512, 512), "g": (4096 + hd * 512, 512, 1024)}
        for k in ("k", "v") if pre else ("q", "k", "v", "g"):
            c0, n, o = cols[k]
            self.wload(W[:, :, o:o + n], w_in[:, c0:c0 + n].rearrange("(kc p) n -> p kc n", p=128), W_b[k])
        if not pre:
            self.wload(wo, w_out[hd * 512:(hd + 1) * 512, :].rearrange("(q p) n -> p q n", p=128), wo_b)
            P.dma("sp", qdec[:, 0, :], dr["ret_qdec"][hd], writes=[qdec_b])
        if pre:
            P.op("pool", lambda e: e.memset(state, 0.0), writes=[state_b])
        else:
            tmp = rt[:, 0:2, :]
            for r in range(NCORES):
                P.dma("sp", tmp, dr["L_all"][r, hd].rearrange("(fc p) n -> p fc n", p=128), writes=[rt_b[0], rt_b[1]])
                cf = coef[:, 0, r * RET_H + hd:r * RET_H + hd + 1]
                if r == 0:
                    P.op("dve", lambda e, cf=cf: e.tensor_scalar(out=state, in0=tmp, scalar1=cf, scalar2=None, op0=ALU.mult),
                         reads=[rt_b[0], rt_b[1], coef_b], writes=[state_b])
                else:
                    P.op("dve", lambda e, cf=cf: e.scalar_tensor_tensor(out=state, in0=tmp, scalar=cf, in1=state, op0=ALU.mult, op1=ALU.add),
                         reads=[rt_b[0], rt_b[1], coef_b], writes=[state_b])
        P.op("act", lambda e: e.copy(out=sbf, in_=state), reads=[state_b], writes=[sbf_b])

        for tg in range(4):
            cst, cst_b = cs[tg % 2], cs_b[tg % 2]
            P.dma("sp", cst[:, 0, :], dr["rope_cos"][:, tg * 512:(tg + 1) * 512], writes=[cst_b])
            P.dma("sp", cst[:, 1, :], dr["rope_sin"][:, tg * 512:(tg + 1) * 512], writes=[cst_b])
            for which in ("k",) if pre else ("q", "k"):
                o = cols[which][2]
                b0 = 0 if which == "q" else 2
                for fc in range(2):
                    for kc in range(KC):
                        P.op("pe", lambda e, fc=fc, kc=kc, o=o, b0=b0, tg=tg: e.matmul(
                            bk[b0 + fc][:], lhsT=W[:, kc, o + fc * 128:o + (fc + 1) * 128], rhs=self.aT[:, kc, tg * 512:(tg + 1) * 512],
                            start=(kc == 0), stop=(kc == KC - 1)),
                            reads=[W_b[which]] + self.aT_b[tg * 4:(tg + 1) * 4], writes=[bb[b0 + fc]])
                dstT, dst_b = (qT, qT_b) if which == "q" else (kT, kT_b)
                t1, t2 = bk[b0][:], bk[b0 + 1][:]
                P.op("dve", lambda e, t1=t1, cst=cst: e.tensor_tensor(out=rt[:, 0, :], in0=t1, in1=cst[:, 0, :], op=ALU.mult),
                     reads=[bb[b0], cst_b], writes=[rt_b[0]])
                P.op("dve", lambda e, t2=t2, cst=cst: e.tensor_tensor(out=rt[:, 1, :], in0=t2, in1=cst[:, 1, :], op=ALU.mult),
                     reads=[bb[b0 + 1], cst_b], writes=[rt_b[1]])
                P.op("dve", lambda e, t2=t2, cst=cst: e.tensor_tensor(out=rt[:, 2, :], in0=t2, in1=cst[:, 0, :], op=ALU.mult),
                     reads=[bb[b0 + 1], cst_b], writes=[rt_b[2]])
                P.op("dve", lambda e, t1=t1, cst=cst: e.tensor_tensor(out=rt[:, 3, :], in0=t1, in1=cst[:, 1, :], op=ALU.mult),
                     reads=[bb[b0], cst_b], writes=[rt_b[3]])
                P.op("pool", lambda e, dstT=dstT: e.tensor_tensor(out=dstT[:, 0, :], in0=rt[:, 0, :], in1=rt[:, 1, :], op=ALU.subtract),
                     reads=[rt_b[0], rt_b[1]], writes=[dst_b])
                P.op("pool", lambda e, dstT=dstT: e.tensor_tensor(out=dstT[:, 1, :], in0=rt[:, 2, :], in1=rt[:, 3, :], op=ALU.add),
                     reads=[rt_b[2], rt_b[3]], writes=[dst_b])
                if which == "q":
                    for fc in range(2):
                        P.op("pool", lambda e, fc=fc: e.tensor_tensor(out=qdT[:, fc, :], in0=qT[:, fc, :], in1=qdec[:, 0, :], op=ALU.mult),
                             reads=[qT_b, qdec_b], writes=[qdT_b])
            for cl in range(4):
                c = tg * 4 + cl
                s = c % 2
                tsl = slice(cl * 128, (cl + 1) * 128)
                csl = slice(c * 128, (c + 1) * 128)
                for kc in range(KC):
                    P.op("pe", lambda e, kc=kc, csl=csl: e.matmul(bk[4][:], lhsT=self.aT[:, kc, csl], rhs=W[:, kc, 512:1024],
                                                               start=(kc == 0), stop=(kc == KC - 1)),
                         reads=[W_b["v"], self.aT_b[c]], writes=[bb[4]])
                P.op("act", lambda e, s=s: e.copy(out=v_sb[s][:, 0, :], in_=bk[4][:]), reads=[bb[4]], writes=[v_b[s]])
                if not pre:
                    for kc in range(KC):
                        P.op("pe", lambda e, kc=kc, csl=csl: e.matmul(bk[5][:], lhsT=self.aT[:, kc, csl], rhs=W[:, kc, 1024:1536],
                                                                   start=(kc == 0), stop=(kc == KC - 1)),
                             reads=[W_b["g"], self.aT_b[c]], writes=[bb[5]])
                    P.op("act", lambda e, s=s: e.activation(out=sg[s][:, 0, :], in_=bk[5][:], func=AF.Silu), reads=[bb[5]], writes=[sg_b[s]])
                    for fc in range(2):
                        P.op("pe", lambda e, fc=fc, tsl=tsl: e.matmul(sT_ps, lhsT=kT[:, fc, tsl], rhs=qT[:, fc, tsl], start=(fc == 0), stop=(fc == 1)),
                             reads=[kT_b, qT_b], writes=[sT_pb])
                    P.op("dve", lambda e, s=s, hd=hd: e.tensor_tensor(out=sm[s][:, 0, :], in0=sT_ps, in1=maskT[:, hd, :], op=ALU.mult),
                         reads=[sT_pb, maskT_b], writes=[sm_b[s]])
                for fc in range(2):
                    P.op("pe", lambda e, fc=fc, tsl=tsl: e.transpose(out=kdT_ps[:, fc, :], in_=kT[:, fc, tsl], identity=self.ident[:]),
                         reads=[kT_b, self.ident_b], writes=[kdT_pb])
                P.op("dve", lambda e, s=s, hd=hd: e.tensor_scalar(out=kd[s][:, 0, :], in0=kdT_ps.rearrange("p a b -> p (a b)"),
                                                                 scalar1=kdec[:, 0, hd:hd + 1], scalar2=None, op0=ALU.mult),
                     reads=[kdT_pb, kdec_b], writes=[kd_b[s]])
                if not pre:
                    P.op("pe", lambda e, s=s: e.matmul(bk[7][:], lhsT=sm[s][:, 0, :], rhs=v_sb[s][:, 0, :], start=True, stop=False),
                         reads=[sm_b[s], v_b[s]], writes=[bb[7]])
                    for fc in range(2):
                        P.op("pe", lambda e, fc=fc, tsl=tsl: e.matmul(bk[7][:], lhsT=qdT[:, fc, tsl], rhs=sbf[:, fc, :], start=False, stop=(fc == 1)),
                             reads=[qdT_b, sbf_b], writes=[bb[7]])
                for fc in range(2):
                    P.op("pe", lambda e, fc=fc, s=s: e.matmul(bk[fc][:], lhsT=kd[s][:, 0, fc * 128:(fc + 1) * 128], rhs=v_sb[s][:, 0, :], start=True, stop=True),
                         reads=[kd_b[s], v_b[s]], writes=[bb[fc]])
                for fc in range(2):
                    P.op("dve", lambda e, fc=fc, cd=cd: e.scalar_tensor_tensor(out=state[:, fc, :], in0=state[:, fc, :], scalar=float(cd), in1=bk[fc][:],
                                                                           op0=ALU.mult, op1=ALU.add),
                         reads=[bb[fc]], writes=[state_b])
                P.op("act", lambda e: e.copy(out=sbf, in_=state), reads=[state_b], writes=[sbf_b])
                if not pre:
                    r, r_b = self.rstd_of(bk[7][:], [bb[7]], 512)
                    P.op("dve", lambda e, s=s, r=r: e.scalar_tensor_tensor(out=gy[s][:, 0, :], in0=bk[7][:], scalar=r, in1=sg[s][:, 0, :],
                                                                         op0=ALU.mult, op1=ALU.mult),
                         reads=[bb[7], r_b, sg_b[s]], writes=[gy_b[s]])
                    for q in range(4):
                        P.op("pe", lambda e, q=q, s=s: e.transpose(out=gyT_ps[:, q, :], in_=gy[s][:, 0, q * 128:(q + 1) * 128], identity=self.ident[:]),
                             reads=[gy_b[s], self.ident_b], writes=[gyT_pb])
                    P.op("act", lambda e, s=s: e.copy(out=gyT[s], in_=gyT_ps), reads=[gyT_pb], writes=[gyT_b[s]])
                    for half in range(2):
                        for q in range(4):
                            P.op("pe", lambda e, q=q, s=s, half=half: e.matmul(bk[2 + half][:], lhsT=gyT[s][:, q, :], rhs=wo[:, q, half * 512:(half + 1) * 512],
                                                                              start=(q == 0), stop=(q == 3)),
                                 reads=[gyT_b[s], wo_b], writes=[bb[2 + half]])
                        dst = self.acc[:, c, half * 512:(half + 1) * 512]
                        if hd == 0:
                            P.op("act", lambda e, dst=dst, half=half: e.copy(out=dst, in_=bk[2 + half][:]), reads=[bb[2 + half]], writes=[self.acc_b[c]])
                        else:
                            P.op("dve", lambda e, dst=dst, half=half: e.tensor_tensor(out=dst, in0=dst, in1=bk[2 + half][:], op=ALU.add),
                                 reads=[bb[2 + half]], writes=[self.acc_b[c]])
        if pre:
            P.dma("sp", dr["L_out"][hd].rearrange("(fc p) n -> p fc n", p=128), state, reads=[state_b], chan=state_b)
    if pre:
        return [(state_b.dsem, state_b.dcount)]
    return []


Builder.ret_emit = _ret_emit


def _ret_emit2(self, j, pre, boundary=False):
    P = self.P
    P.barrier()
    dr = self.dr
    w_in = dr["ret_w_in"][j]
    w_out = dr["ret_w_out"][j]
    off = 0
    W, off = self.st_view(off, [KC, 1536], BF16)
    W_b = {k: Buf(f"r{j}{pre}W{k}") for k in "qkvg"}
    wo, off = self.st_view(off, [4, D], BF16)
    wo_b = Buf(f"r{j}{pre}wo")
    cs, cs_b = [], []
    for s in range(2):
        v, off = self.st_view(off, [2, 512], F32)
        cs.append(v)
        cs_b.append(Buf(f"r{j}{pre}cs{s}"))
    qT, off = self.st_view(off, [2, 512], BF16)
    kT, off = self.st_view(off, [2, 512], BF16)
    qdT, off = self.st_view(off, [2, 512], BF16)
    qT_b, kT_b, qdT_b = Buf("qT"), Buf("kT"), Buf("qdT")
    rt, off = self.st_view(off, [4, 512], F32)
    rt_b = [Buf(f"rt{x}") for x in range(4)]
    qdec, off = self.st_view(off, [1, 512], F32)
    qdec_b = Buf("qdec")
    v_sb, sg, sm, kd, gy, gyT = [], [], [], [], [], []
    v_b, sg_b, sm_b, kd_b, gy_b, gyT_b = [], [], [], [], [], []
    for s in range(2):
        a, off = self.st_view(off, [1, 512], BF16); v_sb.append(a); v_b.append(Buf(f"v{s}"))
        a, off = self.st_view(off, [1, 512], BF16); sg.append(a); sg_b.append(Buf(f"sg{s}"))
        a, off = self.st_view(off, [1, 128], BF16); sm.append(a); sm_b.append(Buf(f"sm{s}"))
        a, off = self.st_view(off, [1, 256], BF16); kd.append(a); kd_b.append(Buf(f"kd{s}"))
        a, off = self.st_view(off, [1, 512], BF16); gy.append(a); gy_b.append(Buf(f"gy{s}"))
        a, off = self.st_view(off, [4, 128], BF16); gyT.append(a); gyT_b.append(Buf(f"gyT{s}"))
    maskT, off = self.st_view(off, [RET_H, 128], F32)
    maskT_b = Buf("maskT")
    kdec, off = self.st_view(off, [1, RET_H], F32)
    kdec_b = Buf("kdec")
    coef, off = self.st_view(off, [1, L_RANKS * RET_H], F32)
    coef_b = Buf("coef")
    state, off = self.st_view(off, [2, 512], F32)
    state_b = Buf("state")
    sbf, off = self.st_view(off, [2, 512], BF16)
    sbf_b = Buf("sbf")
    self.set_hextra(off, max(0, (self.ST_BYTES - off) // 4096))
    P.dma("sp", maskT, dr["ret_maskT"].rearrange("h m n -> m h n"), writes=[maskT_b])
    P.dma("sp", kdec[:, 0, :], dr["ret_kdec"][:, :], writes=[kdec_b])
    P.dma("sp", coef[:, 0, :], dr["ret_coef"][:, :], writes=[coef_b])
    bk = self.bank
    bb = self.bank_b
    b6 = bk[6][:]
    sT_ps = b6[:, 0:128]
    sT_pb = bb[6]
    kdT_ps = b6[:, 128:256].bitcast(BF16).rearrange("p (a b) -> p a b", a=2)
    kdT_pb = bb[6]
    gyT_ps = bk[7][:, 256:512].bitcast(BF16).rearrange("p (a b) -> p a b", a=4)
    gyT_pb = bb[7]
    gam = [1.0 - 2.0 ** (-5.0 - h) for h in range(RET_H)]
    btoks = []

    for hd in range(RET_H):
        cd = gam[hd] ** 128
        cols = {"q": (hd * 256, 256, 0), "k": (1024 + hd * 256, 256, 256), "v": (2048 + hd * 512, 512, 512), "g": (4096 + hd * 512, 512, 1024)}
        for k in ("k", "v") if pre else ("q", "k", "v", "g"):
            c0, n, o = cols[k]
            self.wload(W[:, :, o:o + n], w_in[:, c0:c0 + n].rearrange("(kc p) n -> p kc n", p=128), W_b[k])
        if not pre:
            self.wload(wo, w_out[hd * 512:(hd + 1) * 512, :].rearrange("(q p) n -> p q n", p=128), wo_b)
            P.dma("sp", qdec[:, 0, :], dr["ret_qdec"][hd], writes=[qdec_b])
        if pre:
            P.op("pool", lambda e: e.memset(state, 0.0), writes=[state_b])
        else:
            tmp = rt[:, 0:2, :]
            for r in range(L_RANKS):
                P.dma("sp", tmp, dr["L_all"][r, hd].rearrange("(fc p) n -> p fc n", p=128), writes=[rt_b[0], rt_b[1]])
                cf = coef[:, 0, r * RET_H + hd:r * RET_H + hd + 1]
                if r == 0:
                    P.op("dve", lambda e, cf=cf: e.tensor_scalar(out=state, in0=tmp, scalar1=cf, scalar2=None, op0=ALU.mult),
                         reads=[rt_b[0], rt_b[1], coef_b], writes=[state_b])
                else:
                    P.op("dve", lambda e, cf=cf: e.scalar_tensor_tensor(out=state, in0=tmp, scalar=cf, in1=state, op0=ALU.mult, op1=ALU.add),
                         reads=[rt_b[0], rt_b[1], coef_b], writes=[state_b])
        P.op("act", lambda e: e.copy(out=sbf, in_=state), reads=[state_b], writes=[sbf_b])

        def qk_proj(tg, hd=hd, cols=cols):
            cst, cst_b = cs[tg % 2], cs_b[tg % 2]
            P.dma("sp", cst[:, 0, :], dr["rope_cos"][:, tg * 512:(tg + 1) * 512], writes=[cst_b])
            P.dma("sp", cst[:, 1, :], dr["rope_sin"][:, tg * 512:(tg + 1) * 512], writes=[cst_b])
            for which in ("k",) if pre else ("q", "k"):
                o = cols[which][2]
                b0 = 0 if which == "q" else 2
                for fc in range(2):
                    for kc in range(KC):
                        P.op("pe", lambda e, fc=fc, kc=kc, o=o, b0=b0, tg=tg: e.matmul(
                            bk[b0 + fc][:], lhsT=W[:, kc, o + fc * 128:o + (fc + 1) * 128], rhs=self.aT[:, kc, tg * 512:(tg + 1) * 512],
                            start=(kc == 0), stop=(kc == KC - 1)),
                            reads=[W_b[which]] + self.aT_b[tg * 4:(tg + 1) * 4], writes=[bb[b0 + fc]])
                dstT, dst_b = (qT, qT_b) if which == "q" else (kT, kT_b)
                t1, t2 = bk[b0][:], bk[b0 + 1][:]
                P.op("dve", lambda e, t1=t1, cst=cst: e.tensor_tensor(out=rt[:, 0, :], in0=t1, in1=cst[:, 0, :], op=ALU.mult),
                     reads=[bb[b0], cst_b], writes=[rt_b[0]])
                P.op("dve", lambda e, t2=t2, cst=cst: e.tensor_tensor(out=rt[:, 1, :], in0=t2, in1=cst[:, 1, :], op=ALU.mult),
                     reads=[bb[b0 + 1], cst_b], writes=[rt_b[1]])
                P.op("dve", lambda e, t2=t2, cst=cst: e.tensor_tensor(out=rt[:, 2, :], in0=t2, in1=cst[:, 0, :], op=ALU.mult),
                     reads=[bb[b0 + 1], cst_b], writes=[rt_b[2]])
                P.op("dve", lambda e, t1=t1, cst=cst: e.tensor_tensor(out=rt[:, 3, :], in0=t1, in1=cst[:, 1, :], op=ALU.mult),
                     reads=[bb[b0], cst_b], writes=[rt_b[3]])
                P.op("pool", lambda e, dstT=dstT: e.tensor_tensor(out=dstT[:, 0, :], in0=rt[:, 0, :], in1=rt[:, 1, :], op=ALU.subtract),
                     reads=[rt_b[0], rt_b[1]], writes=[dst_b])
                P.op("pool", lambda e, dstT=dstT: e.tensor_tensor(out=dstT[:, 1, :], in0=rt[:, 2, :], in1=rt[:, 3, :], op=ALU.add),
                     reads=[rt_b[2], rt_b[3]], writes=[dst_b])
                if which == "q":
                    for fc in range(2):
                        P.op("pool", lambda e, fc=fc: e.tensor_tensor(out=qdT[:, fc, :], in0=qT[:, fc, :], in1=qdec[:, 0, :], op=ALU.mult),
                             reads=[qT_b, qdec_b], writes=[qdT_b])
        def ph0(c, hd=hd, cd=cd, qk_proj=qk_proj):
            if True:
                tg, cl = c // 4, c % 4
                if cl == 0:
                    qk_proj(tg)
                s = c % 2
                tsl = slice(cl * 128, (cl + 1) * 128)
                csl = slice(c * 128, (c + 1) * 128)
                for kc in range(KC):
                    P.op("pe", lambda e, kc=kc, csl=csl: e.matmul(bk[4][:], lhsT=self.aT[:, kc, csl], rhs=W[:, kc, 512:1024],
                                                               start=(kc == 0), stop=(kc == KC - 1)),
                         reads=[W_b["v"], self.aT_b[c]], writes=[bb[4]])
                P.op("act", lambda e, s=s: e.copy(out=v_sb[s][:, 0, :], in_=bk[4][:]), reads=[bb[4]], writes=[v_b[s]])
                if not pre:
                    for kc in range(KC):
                        P.op("pe", lambda e, kc=kc, csl=csl: e.matmul(bk[5][:], lhsT=self.aT[:, kc, csl], rhs=W[:, kc, 1024:1536],
                                                                   start=(kc == 0), stop=(kc == KC - 1)),
                             reads=[W_b["g"], self.aT_b[c]], writes=[bb[5]])
                    P.op("act", lambda e, s=s: e.activation(out=sg[s][:, 0, :], in_=bk[5][:], func=AF.Silu), reads=[bb[5]], writes=[sg_b[s]])
                    for fc in range(2):
                        P.op("pe", lambda e, fc=fc, tsl=tsl: e.matmul(sT_ps, lhsT=kT[:, fc, tsl], rhs=qT[:, fc, tsl], start=(fc == 0), stop=(fc == 1)),
                             reads=[kT_b, qT_b], writes=[sT_pb])
                    P.op("dve", lambda e, s=s, hd=hd: e.tensor_tensor(out=sm[s][:, 0, :], in0=sT_ps, in1=maskT[:, hd, :], op=ALU.mult),
                         reads=[sT_pb, maskT_b], writes=[sm_b[s]])
                for fc in range(2):
                    P.op("pe", lambda e, fc=fc, tsl=tsl: e.transpose(out=kdT_ps[:, fc, :], in_=kT[:, fc, tsl], identity=self.ident[:]),
                         reads=[kT_b, self.ident_b], writes=[kdT_pb])
                P.op("dve", lambda e, s=s, hd=hd: e.tensor_scalar(out=kd[s][:, 0, :], in0=kdT_ps.rearrange("p a b -> p (a b)"),
                                                                 scalar1=kdec[:, 0, hd:hd + 1], scalar2=None, op0=ALU.mult),
                     reads=[kdT_pb, kdec_b], writes=[kd_b[s]])
        def ph1(c, hd=hd, cd=cd):
            if True:
                tg, cl = c // 4, c % 4
                s = c % 2
                tsl = slice(cl * 128, (cl + 1) * 128)
                csl = slice(c * 128, (c + 1) * 128)
                if not pre:
                    P.op("pe", lambda e, s=s: e.matmul(bk[7][:], lhsT=sm[s][:, 0, :], rhs=v_sb[s][:, 0, :], start=True, stop=False),
                         reads=[sm_b[s], v_b[s]], writes=[bb[7]])
                    for fc in range(2):
                        P.op("pe", lambda e, fc=fc, tsl=tsl: e.matmul(bk[7][:], lhsT=qdT[:, fc, tsl], rhs=sbf[:, fc, :], start=False, stop=(fc == 1)),
                             reads=[qdT_b, sbf_b], writes=[bb[7]])
                for fc in range(2):
                    P.op("pe", lambda e, fc=fc, s=s: e.matmul(bk[fc][:], lhsT=kd[s][:, 0, fc * 128:(fc + 1) * 128], rhs=v_sb[s][:, 0, :], start=True, stop=True),
                         reads=[kd_b[s], v_b[s]], writes=[bb[fc]])
                for fc in range(2):
                    P.op("dve", lambda e, fc=fc, cd=cd: e.scalar_tensor_tensor(out=state[:, fc, :], in0=state[:, fc, :], scalar=float(cd), in1=bk[fc][:],
                                                                           op0=ALU.mult, op1=ALU.add),
                         reads=[bb[fc]], writes=[state_b])
                if not pre:
                    P.op("act", lambda e: e.copy(out=sbf, in_=state), reads=[state_b], writes=[sbf_b])
                if not pre:
                    r, r_b = self.rstd_of(bk[7][:], [bb[7]], 512)
                    P.op("dve", lambda e, s=s, r=r: e.scalar_tensor_tensor(out=gy[s][:, 0, :], in0=bk[7][:], scalar=r, in1=sg[s][:, 0, :],
                                                                         op0=ALU.mult, op1=ALU.mult),
                         reads=[bb[7], r_b, sg_b[s]], writes=[gy_b[s]])
        def ph2(c, hd=hd, cd=cd):
            if True:
                tg, cl = c // 4, c % 4
                s = c % 2
                if not pre:
                    for q in range(4):
                        P.op("pe", lambda e, q=q, s=s: e.transpose(out=gyT_ps[:, q, :], in_=gy[s][:, 0, q * 128:(q + 1) * 128], identity=self.ident[:]),
                             reads=[gy_b[s], self.ident_b], writes=[gyT_pb])
                    P.op("act", lambda e, s=s: e.copy(out=gyT[s], in_=gyT_ps), reads=[gyT_pb], writes=[gyT_b[s]])
                    for half in range(2):
                        for q in range(4):
                            P.op("pe", lambda e, q=q, s=s, half=half: e.matmul(bk[2 + half][:], lhsT=gyT[s][:, q, :], rhs=wo[:, q, half * 512:(half + 1) * 512],
                                                                              start=(q == 0), stop=(q == 3)),
                                 reads=[gyT_b[s], wo_b], writes=[bb[2 + half]])
                        dst = self.acc[:, c, half * 512:(half + 1) * 512]
                        if hd == 0:
                            P.op("act", lambda e, dst=dst, half=half: e.copy(out=dst, in_=bk[2 + half][:]), reads=[bb[2 + half]], writes=[self.acc_b[c]])
                        else:
                            P.op("dve", lambda e, dst=dst, half=half: e.tensor_tensor(out=dst, in0=dst, in1=bk[2 + half][:], op=ALU.add),
                                 reads=[bb[2 + half]], writes=[self.acc_b[c]])
        phases = [ph0, ph1] if pre else [ph0, ph1, ph2]
        if boundary and (not pre) and hd == RET_H - 1:
            phases = phases + self.boundary_phases_f(lambda t: (self.acc[:, t, :], [self.acc_b[t]]), lambda t: (self.acc[:, t, :], self.acc_b[t]),
                                                   toks=btoks, tbanks=(5,), merge_b=True)
        pipeline(NT, phases)
        if pre:
            P.dma("sp", dr["L_out"][hd].rearrange("(fc p) n -> p fc n", p=128), state, reads=[state_b], chan=state_b)
    if pre:
        return [(state_b.dsem, state_b.dcount)]
    return btoks


Builder.ret_emit = _ret_emit2


def _mixer_boundaries(self, i):
    toks = []
    for t in range(NT):
        a = self.acc[:, t, :]
        toks.append(self.boundary(t, a, [self.acc_b[t]], a, self.acc_b[t]))
    return toks


Builder.mixer_boundaries = _mixer_boundaries


def ret_tables():
    gam = np.array([1.0 - 2.0 ** (-5.0 - h) for h in range(RET_H)], np.float64)
    idx = np.arange(128, dtype=np.float64)
    diff = idx[None, :] - idx[:, None]
    maskT = np.where(diff >= 0, gam[:, None, None] ** np.maximum(diff, 0.0), 0.0) / 16.0
    qdec = gam[:, None] ** (np.tile(idx, 4) + 1.0)[None, :]
    qdec = np.broadcast_to(qdec[:, None, :], (RET_H, 128, 512))
    kdec = (gam[None, :] ** (127.0 - idx[:, None])) / 16.0
    return (np.ascontiguousarray(maskT, dtype=np.float32), np.ascontiguousarray(qdec, dtype=np.float32),
            np.ascontiguousarray(kdec, dtype=np.float32))


def ret_coef(core):
    gam = np.array([1.0 - 2.0 ** (-5.0 - h) for h in range(RET_H)], np.float64)
    b, s = core // 4, core % 4
    c = np.zeros((L_RANKS, RET_H), np.float64)
    for rs in range(L_RANKS):
        if rs < s:
            c[rs] = gam ** (float(T) * (s - 1 - rs))
    return np.ascontiguousarray(np.broadcast_to(c.reshape(1, -1), (128, L_RANKS * RET_H)), dtype=np.float32)


def rope_tables(core):
    s = core % 4
    inv_freq = (np.float32(10000.0) ** (-(np.arange(0, RET_DK, 2, dtype=np.float32) / np.float32(RET_DK)))).astype(np.float32)
    pos = (np.arange(T, dtype=np.float32) + np.float32(s * T)).astype(np.float32)
    ang = (inv_freq[:, None] * pos[None, :]).astype(np.float32)
    return np.cos(ang).astype(np.float32), np.sin(ang).astype(np.float32)


def _gmlp_stage(self, i, j):
    P = self.P
    dr = self.dr
    P.barrier()
    bigb = self.big[:].bitcast(BF16)
    Wv = bigb[:, 0:16384].rearrange("p (k n) -> p k n", k=KC)
    Wu = bigb[:, 16384:32768].rearrange("p (k n) -> p k n", k=KC)
    Wv_b, Wu_b = Buf("gWv"), Buf("gWu")
    off = 0
    Wo, off = self.st_view(off, [16, D], BF16)
    Wo_b = Buf("gWo")
    wsT, off = self.st_view(off, [GM_G, 128], BF16)
    wsT_b = Buf("gwsT")
    gtab, off = self.st_view(off, [1, GM_W], F32)
    btab, off = self.st_view(off, [1, GM_W], F32)
    gtab_b, btab_b = Buf("ggtab"), Buf("gbtab")
    bsT, off = self.st_view(off, [1, GM_G], F32)
    bsT_b = Buf("gbsT")
    cmask, off = self.st_view(off, [1, 128], F32)
    cmask_b = Buf("gcmask")
    off_vf = off
    vf, off = self.st_view(off, [1, GM_W], F32)
    vf_b = Buf("gvf")
    vn, off = self.st_view(off, [1, GM_W], BF16)
    vn_b = Buf("gvn")
    ssb, off = self.st_view(off, [1, GM_W], F32)
    ssb_b = Buf("gssb")
    us, off = self.st_view(off, [1, GM_W], BF16)
    us_b = Buf("gus")
    usT, off = self.st_view(off, [16, 128], BF16)
    usT_b = Buf("gusT")
    gu, gu_b = [], []
    for s in range(2):
        a, off = self.st_view(off, [1, 512], F32)
        gu.append(a)
        gu_b.append(Buf(f"ggu{s}"))
    wsl, off2 = self.st_view(off_vf, [GM_G, 128], BF16)
    wsl_b = vf_b
    w_in = dr["gm_w_in"][j]
    for cb in range(4):
        self.P.dma("pool", Wv[:, :, cb * 512:(cb + 1) * 512], w_in[:, GM_W + cb * 512:GM_W + (cb + 1) * 512].rearrange("(kc p) n -> p kc n", p=128),
                   writes=[Wv_b] + self.acc_b[0:8])
    for cb in range(4):
        self.P.dma("pool", Wu[:, :, cb * 512:(cb + 1) * 512], w_in[:, cb * 512:(cb + 1) * 512].rearrange("(kc p) n -> p kc n", p=128),
                   writes=[Wu_b] + self.acc_b[8:16])
    for q in range(4):
        self.wload(Wo[:, q * 4:(q + 1) * 4, :], dr["gm_w_out"][j][q * 512:(q + 1) * 512, :].rearrange("(kc p) n -> p kc n", p=128), Wo_b)
    self.wload(wsl, dr["gm_w_s"][j].rearrange("g n m -> n g m"), wsl_b)
    P.dma("sp", gtab[:, 0, :], dr["gm_ln_g"][j, :].partition_broadcast(128), writes=[gtab_b])
    P.dma("sp", btab[:, 0, :], dr["gm_ln_b"][j, :].partition_broadcast(128), writes=[btab_b])
    P.dma("sp", bsT[:, 0, :], dr["gm_b_s"][j].rearrange("g n -> n g"), writes=[bsT_b], allow_slow_non_contiguous=True)
    P.dma("sp", cmask[:, 0, :], dr["causalT"][:, :], writes=[cmask_b])
    bk, bb = self.bank, self.bank_b
    tps = bk[0][:].bitcast(BF16)[:, 0:1024].rearrange("p (g n) -> p g n", g=GM_G)
    for g in range(GM_G):
        P.op("pe", lambda e, g=g: e.transpose(out=tps[:, g, :], in_=wsl[:, g, :], identity=self.ident[:]),
             reads=[wsl_b, self.ident_b], writes=[bb[0]])
    for g in range(GM_G):
        P.op("dve", lambda e, g=g: e.tensor_tensor(out=wsT[:, g, :], in0=tps[:, g, :], in1=cmask[:, 0, :], op=ALU.mult),
             reads=[bb[0], cmask_b], writes=[wsT_b])
    toks = []
    for t in range(NT):
        tsl = slice(t * 128, (t + 1) * 128)
        sv, sv_b = [], []
        for cb in range(4):
            for kc in range(KC):
                P.op("pe", lambda e, cb=cb, kc=kc, tsl=tsl: e.matmul(bk[cb][:], lhsT=self.aT[:, kc, tsl], rhs=Wv[:, kc, cb * 512:(cb + 1) * 512],
                                                                     start=(kc == 0), stop=(kc == KC - 1)),
                     reads=[Wv_b, self.aT_b[t]], writes=[bb[cb]])
            a, a_b = self.new_stat()
            sv.append(a)
            sv_b.append(a_b)
            P.op("act", lambda e, cb=cb, a=a: e.activation(out=vf[:, 0, cb * 512:(cb + 1) * 512], in_=bk[cb][:], func=AF.Gelu, accum_out=a),
                 reads=[bb[cb]], writes=[vf_b, a_b])
        sq, sq_b = [], []
        for hh in range(2):
            a, a_b = self.new_stat()
            sq.append(a)
            sq_b.append(a_b)
            P.op("act", lambda e, hh=hh, a=a: e.activation(out=self.junk[:, :], in_=vf[:, 0, hh * 1024:(hh + 1) * 1024], func=AF.Square, accum_out=a),
                 reads=[vf_b], writes=[self.junk_b, a_b])
        m1, m1_b = self.new_stat()
        P.op("pool", lambda e, sv=sv, m1=m1: e.tensor_tensor(out=m1, in0=sv[0], in1=sv[1], op=ALU.add), reads=[sv_b[0], sv_b[1]], writes=[m1_b])
        m2, m2_b = self.new_stat()
        P.op("pool", lambda e, sv=sv, m2=m2: e.tensor_tensor(out=m2, in0=sv[2], in1=sv[3], op=ALU.add), reads=[sv_b[2], sv_b[3]], writes=[m2_b])
        mu, mu_b = self.new_stat()
        P.op("pool", lambda e, m1=m1, m2=m2, mu=mu: e.tensor_tensor(out=mu, in0=m1, in1=m2, op=ALU.add), reads=[m1_b, m2_b], writes=[mu_b])
        P.op("pool", lambda e, mu=mu: e.tensor_scalar(out=mu, in0=mu, scalar1=1.0 / GM_W, scalar2=None, op0=ALU.mult), reads=[], writes=[mu_b])
        e2, e2_b = self.new_stat()
        P.op("pool", lambda e, sq=sq, e2=e2: e.tensor_tensor(out=e2, in0=sq[0], in1=sq[1], op=ALU.add), reads=[sq_b[0], sq_b[1]], writes=[e2_b])
        mm, mm_b = self.new_stat()
        P.op("pool", lambda e, mu=mu, mm=mm: e.tensor_tensor(out=mm, in0=mu, in1=mu, op=ALU.mult), reads=[mu_b], writes=[mm_b])
        var, var_b = self.new_stat()
        P.op("pool", lambda e, e2=e2, var=var: e.tensor_scalar(out=var, in0=e2, scalar1=1.0 / GM_W, scalar2=EPS, op0=ALU.mult, op1=ALU.add),
             reads=[e2_b], writes=[var_b])
        P.op("pool", lambda e, var=var, mm=mm: e.tensor_tensor(out=var, in0=var, in1=mm, op=ALU.subtract), reads=[mm_b], writes=[var_b])
        rs, rs_b = self.new_stat()
        P.op("pool", lambda e, var=var, rs=rs: e.tensor_tensor(out=rs, in0=var, in1=self.cm05[:], op=ALU.pow), reads=[var_b, self.cm05_b], writes=[rs_b])
        P.op("dve", lambda e, mu=mu, rs=rs: e.tensor_scalar(out=vf[:, 0, :], in0=vf[:, 0, :], scalar1=mu, scalar2=rs, op0=ALU.subtract, op1=ALU.mult),
             reads=[mu_b, rs_b], writes=[vf_b])
        P.op("pool", lambda e: e.tensor_tensor(out=vf[:, 0, :], in0=vf[:, 0, :], in1=gtab[:, 0, :], op=ALU.mult), reads=[gtab_b], writes=[vf_b])
        P.op("dve", lambda e: e.tensor_tensor(out=vn[:, 0, :], in0=vf[:, 0, :], in1=btab[:, 0, :], op=ALU.add), reads=[vf_b, btab_b], writes=[vn_b])
        for g in range(GM_G):
            bkx = 4 + g // 2
            P.op("pe", lambda e, g=g, bkx=bkx: e.matmul(bk[bkx][:, (g % 2) * 256:(g % 2 + 1) * 256], lhsT=wsT[:, g, :], rhs=vn[:, 0, g * 256:(g + 1) * 256],
                                                        start=True, stop=True),
                 reads=[wsT_b, vn_b], writes=[bb[bkx]])
        for g in range(GM_G):
            bkx = 4 + g // 2
            P.op("dve", lambda e, g=g, bkx=bkx: e.tensor_scalar(out=ssb[:, 0, g * 256:(g + 1) * 256], in0=bk[bkx][:, (g % 2) * 256:(g % 2 + 1) * 256],
                                                                scalar1=bsT[:, 0, g:g + 1], scalar2=None, op0=ALU.add),
                 reads=[bb[bkx], bsT_b], writes=[ssb_b])
        for cb in range(4):
            for kc in range(KC):
                P.op("pe", lambda e, cb=cb, kc=kc, tsl=tsl: e.matmul(bk[cb][:], lhsT=self.aT[:, kc, tsl], rhs=Wu[:, kc, cb * 512:(cb + 1) * 512],
                                                                     start=(kc == 0), stop=(kc == KC - 1)),
                     reads=[Wu_b, self.aT_b[t]], writes=[bb[cb]])
            s = cb % 2
            P.op("act", lambda e, cb=cb, s=s: e.activation(out=gu[s][:, 0, :], in_=bk[cb][:], func=AF.Gelu), reads=[bb[cb]], writes=[gu_b[s]])
            P.op("dve", lambda e, cb=cb, s=s: e.tensor_tensor(out=us[:, 0, cb * 512:(cb + 1) * 512], in0=gu[s][:, 0, :], in1=ssb[:, 0, cb * 512:(cb + 1) * 512], op=ALU.mult),
                 reads=[gu_b[s], ssb_b], writes=[us_b])
        for half in range(2):
            tp = bk[4 + half][:].bitcast(BF16).rearrange("p (k n) -> p k n", k=8)
            for q in range(8):
                kc = half * 8 + q
                P.op("pe", lambda e, tp=tp, q=q, kc=kc: e.transpose(out=tp[:, q, :], in_=us[:, 0, kc * 128:(kc + 1) * 128], identity=self.ident[:]),
                     reads=[us_b, self.ident_b], writes=[bb[4 + half]])
            P.op("act", lambda e, tp=tp, half=half: e.copy(out=usT[:, half * 8:(half + 1) * 8, :], in_=tp), reads=[bb[4 + half]], writes=[usT_b])
        for half in range(2):
            for kc in range(16):
                P.op("pe", lambda e, half=half, kc=kc: e.matmul(bk[6 + half][:], lhsT=usT[:, kc, :], rhs=Wo[:, kc, half * 512:(half + 1) * 512],
                                                                start=(kc == 0), stop=(kc == 15)),
                     reads=[usT_b, Wo_b], writes=[bb[6 + half]])
        src = self.psum_all[:, 6:8, :].rearrange("p a b -> p (a b)")
        toks.append(self.boundary(t, src, [bb[6], bb[7]], vf[:, 0, 0:D], vf_b, tbank=5))
    for t in range(NT):
        src_b = Wv_b if t < 8 else Wu_b
        self.acc_b[t].r = list(self.acc_b[t].r) + list(src_b.r) + ([src_b.w] if src_b.w else [])
    return toks


Builder.gmlp_stage = _gmlp_stage


def att_tables(rel_bias):
    p = np.arange(128)[:, None, None]
    kb = np.arange(2)[None, :, None]
    q = np.arange(128)[None, None, :]
    dist = 128 * (1 - kb) + q - p
    mask01 = ((dist >= 0) & (dist <= 128)).astype(np.float32).reshape(128, 256)
    dcl = np.maximum(dist, 0)
    out = np.zeros((3, ATT_H, 128, 256), np.float32)
    for g, (w, r) in enumerate(DIL):
        dd = (dcl * r).astype(np.int32)
        df = np.maximum(dd, 1).astype(np.float32)
        large = 16 + (np.log(df / np.float32(16)) / np.float32(math.log(2048 / 16)) * np.float32(16)).astype(np.int32)
        large = np.minimum(large, 31)
        bucket = np.where(dd < 16, dd, large).reshape(128, 256)
        tab = rel_bias[:, g * ATT_H:(g + 1) * ATT_H]
        out[g] = np.transpose(tab[bucket], (2, 0, 1))
    return out, mask01


def att_prevmask(core, mask01):
    m = mask01.copy()
    if core % 4 == 0:
        m[:, 0:128] = 0.0
    return m


def _att_kv_pass(self, j):
    P = self.P
    dr = self.dr
    P.barrier()
    w_in = dr["attn_w_in"][j]
    off = 0
    ring, ring_b = [], []
    for s in range(3):
        v, off = self.st_view(off, [KC, 512], BF16)
        ring.append(v)
        ring_b.append(Buf(f"akv_w{s}"))
    kst, kst_b = [], []
    for s in range(2):
        v, off = self.st_view(off, [1, T], BF16)
        kst.append(v)
        kst_b.append(Buf(f"akv_k{s}"))
    vst, vst_b = [], []
    for s in range(2):
        v, off = self.st_view(off, [1, D], BF16)
        vst.append(v)
        vst_b.append(Buf(f"akv_v{s}"))
    bk, bb = self.bank, self.bank_b
    self.KT_b = [[Buf(f"KT{g}_{fb}") for fb in range(8)] for g in range(3)]
    self.V_b = [[Buf(f"V{g}_{blk}") for blk in range(NT)] for g in range(3)]
    ri = 0
    ui = 0
    for g, (w, r) in enumerate(DIL):
        nbc = 16 // r
        for hf in range(2):
            ws, wb = ring[ri % 3], ring_b[ri % 3]
            ri += 1
            c0 = g * 3072 + 1024 + hf * 512
            self.wload(ws, w_in[:, c0:c0 + 512].rearrange("(kc p) n -> p kc n", p=128), wb)
            for fq in range(4):
                fb = hf * 4 + fq
                ks, ksb = kst[fb % 2], kst_b[fb % 2]
                for tg in range(4):
                    b = ui % 4
                    ui += 1
                    for kc in range(KC):
                        P.op("pe", lambda e, b=b, kc=kc, fq=fq, tg=tg, ws=ws: e.matmul(
                            bk[b][:], lhsT=ws[:, kc, fq * 128:(fq + 1) * 128], rhs=self.aT[:, kc, tg * 512:(tg + 1) * 512],
                            start=(kc == 0), stop=(kc == KC - 1)), reads=[wb] + self.aT_b[tg * 4:(tg + 1) * 4], writes=[bb[b]])
                    jn = 512 // r
                    dst = ks[:, 0, :].rearrange("p (c j) -> p c j", c=r)[:, :, tg * jn:(tg + 1) * jn]
                    src = bk[b][:].rearrange("p (j c) -> p c j", c=r)
                    eng = "act" if (ui % 2 == 0) else "dve"
                    if eng == "act":
                        P.op("act", lambda e, dst=dst, src=src: e.copy(out=dst, in_=src), reads=[bb[b]], writes=[ksb])
                    else:
                        P.op("dve", lambda e, dst=dst, src=src: e.tensor_copy(out=dst, in_=src), reads=[bb[b]], writes=[ksb])
                P.dma("sp", dr["KT_own"][g, fb * 128:(fb + 1) * 128, :], ks[:, 0, :], reads=[ksb], writes=[self.KT_b[g][fb]], chan=ksb)
        wsl = []
        for hf in range(2):
            ws, wb = ring[ri % 3], ring_b[ri % 3]
            ri += 1
            c0 = g * 3072 + 2048 + hf * 512
            self.wload(ws, w_in[:, c0:c0 + 512].rearrange("(kc p) n -> p kc n", p=128), wb)
            wsl.append((ws, wb))
        for blk in range(NT):
            c, n = blk // nbc, blk % nbc
            start = n * 128 * r + c
            vs, vsb = vst[blk % 2], vst_b[blk % 2]
            toks_t = sorted(set((start + i * r) // 128 for i in range(128)))
            for hf in range(2):
                ws, wb = wsl[hf]
                b = ui % 4
                ui += 1
                for kc in range(KC):
                    P.op("pe", lambda e, b=b, kc=kc, ws=ws, start=start, r=r: e.matmul(
                        bk[b][:], lhsT=self.aT[:, kc, start:start + 127 * r + 1:r], rhs=ws[:, kc, :],
                        start=(kc == 0), stop=(kc == KC - 1)), reads=[wb] + [self.aT_b[t] for t in toks_t], writes=[bb[b]])
                if hf == 0:
                    P.op("act", lambda e, b=b, vs=vs: e.copy(out=vs[:, 0, 0:512], in_=bk[b][:]), reads=[bb[b]], writes=[vsb])
                else:
                    P.op("dve", lambda e, b=b, vs=vs: e.tensor_copy(out=vs[:, 0, 512:1024], in_=bk[b][:]), reads=[bb[b]], writes=[vsb])
            P.dma("sp", dr["V_own"][g, blk * 128:(blk + 1) * 128, :], vs[:, 0, :], reads=[vsb], writes=[self.V_b[g][blk]], chan=vsb)


Builder.att_kv_pass = _att_kv_pass


def _att_main(self, i, j):
    P = self.P
    dr = self.dr
    P.barrier()
    w_in = dr["attn_w_in"][j]
    off = 0
    Wq, off = self.st_view(off, [KC, 384], BF16)
    Wq_b = Buf("aWq")
    QT, off = self.st_view(off, [1, T], BF16)
    QT_b = Buf("aQT")
    KT, off = self.st_view(off, [1, T], BF16)
    KT_sb = Buf("aKT")
    KH, off = self.st_view(off, [1, T], BF16)
    KH_b = Buf("aKH")
    Vt, off = self.st_view(off, [NT, 128], BF16)
    Vt_b = Buf("aVt")
    VH, off = self.st_view(off, [NT, 128], BF16)
    VH_b = Buf("aVH")
    Nacc, off = self.st_view(off, [2, T], F32)
    Dacc, off = self.st_view(off, [2, T], F32)
    ND_b = [Buf("aND0"), Buf("aND1")]
    bst, off = self.st_view(off, [2, 256], F32)
    bst_b = Buf("abst")
    Et, off = self.st_view(off, [2, 256], F32)
    Et0, off = self.st_view(off, [2, 256], F32)
    Et_b = Buf("aEt")
    m01, off = self.st_view(off, [1, 256], F32)
    m01p, off = self.st_view(off, [1, 256], F32)
    m_b = Buf("am01")
    ones, off = self.st_view(off, [1, 64], BF16)
    ones_b = Buf("aones")
    ex, ex_b, pt, pt_b = [], [], [], []
    for s in range(2):
        a, off = self.st_view(off, [1, 256], F32); ex.append(a); ex_b.append(Buf(f"aex{s}"))
        a, off = self.st_view(off, [2, 128], BF16); pt.append(a); pt_b.append(Buf(f"apt{s}"))
    rc, off = self.st_view(off, [1, T], F32)
    rc_b = Buf("arc")
    bk, bb = self.bank, self.bank_b
    if getattr(self, "KT_b", None) is None:
        self.KT_b = [[Buf(f"KT{g}_{fb}") for fb in range(8)] for g in range(3)]
        self.V_b = [[Buf(f"V{g}_{blk}") for blk in range(NT)] for g in range(3)]
    oT = self.big[:].bitcast(BF16).rearrange("p (h t) -> p h t", h=ATT_H)
    oT_b = [Buf(f"aoT{h}") for h in range(ATT_H)]
    P.dma("sp", m01[:, 0, :], dr["att_mask01"][:, :], writes=[m_b])
    P.dma("sp", m01p[:, 0, :], dr["att_prevmask"][:, :], writes=[m_b])
    P.op("pool", lambda e: e.memset(ones, 1.0), writes=[ones_b])
    sT_ps = [self.psum_all[:, 2 + s // 2, (s % 2) * 256:(s % 2 + 1) * 256].rearrange("p (a b) -> p a b", a=2) for s in range(4)]
    sT_pb = [Buf(f"asT{s}") for s in range(4)]
    nd_ps = [self.psum_all[0:64, 4, s * 256:(s + 1) * 256].rearrange("p (a b) -> p a b", a=2) for s in range(2)]
    nd_pb = [Buf(f"and{s}") for s in range(2)]
    ui = 0
    si = 0
    for hp in range(8):
        for g in range(3):
            c0 = g * 3072 + hp * 128
            self.wload(Wq[:, :, g * 128:(g + 1) * 128], w_in[:, c0:c0 + 128].rearrange("(kc p) n -> p kc n", p=128), Wq_b)
        for g, (w, r) in enumerate(DIL):
            nbc = 16 // r
            P.dma("sp", KT[:, 0, :], dr["KT_own"][g, hp * 128:(hp + 1) * 128, :], reads=self.KT_b[g], writes=[KT_sb])
            for c in range(r):
                p0 = c * (T // r) + (nbc - 1) * 128
                P.dma("sp", KH[:, 0, c * 128:(c + 1) * 128], dr["KT_prev"][g, hp * 128:(hp + 1) * 128, p0:p0 + 128], writes=[KH_b])
            P.dma("sp", Vt, dr["V_own"][g, :, hp * 128:(hp + 1) * 128].rearrange("(b p) f -> p b f", p=128), reads=self.V_b[g], writes=[Vt_b])
            for c in range(r):
                p0 = c * (T // r) + (nbc - 1) * 128
                P.dma("sp", VH[:, c, :], dr["V_prev"][g, p0:p0 + 128, hp * 128:(hp + 1) * 128], writes=[VH_b])
            P.dma("sp", bst, dr["att_bias"][g, 2 * hp:2 * hp + 2].rearrange("h p n -> p h n"), writes=[bst_b])
            P.op("act", lambda e: e.activation(out=bst, in_=bst, func=AF.Exp), reads=[], writes=[bst_b])
            for hh in range(2):
                P.op("dve", lambda e, hh=hh: e.tensor_tensor(out=Et[:, hh, :], in0=bst[:, hh, :], in1=m01[:, 0, :], op=ALU.mult),
                     reads=[bst_b, m_b], writes=[Et_b])
                P.op("dve", lambda e, hh=hh: e.tensor_tensor(out=Et0[:, hh, :], in0=bst[:, hh, :], in1=m01p[:, 0, :], op=ALU.mult),
                     reads=[bst_b, m_b], writes=[Et_b])
            for tg in range(4):
                b = ui % 2
                ui += 1
                for kc in range(KC):
                    P.op("pe", lambda e, b=b, kc=kc, g=g, tg=tg: e.matmul(
                        bk[b][:], lhsT=Wq[:, kc, g * 128:(g + 1) * 128], rhs=self.aT[:, kc, tg * 512:(tg + 1) * 512],
                        start=(kc == 0), stop=(kc == KC - 1)), reads=[Wq_b] + self.aT_b[tg * 4:(tg + 1) * 4], writes=[bb[b]])
                jn = 512 // r
                dst = QT[:, 0, :].rearrange("p (c j) -> p c j", c=r)[:, :, tg * jn:(tg + 1) * jn]
                src = bk[b][:].rearrange("p (j c) -> p c j", c=r)
                P.op("act", lambda e, dst=dst, src=src: e.copy(out=dst, in_=src), reads=[bb[b]], writes=[QT_b])
            for blk in range(NT):
                c, n = blk // nbc, blk % nbc
                qs = slice(blk * 128, (blk + 1) * 128)
                first = (n == 0)
                for hh in range(2):
                    ps = slice(hh * 64, (hh + 1) * 64)
                    s4 = si % 4
                    s2 = si % 2
                    si += 1
                    kprev = KH[ps, 0, c * 128:(c + 1) * 128] if first else KT[ps, 0, (blk - 1) * 128:blk * 128]
                    P.op("pe", lambda e, s4=s4, kprev=kprev, ps=ps, qs=qs: e.matmul(sT_ps[s4][:, 0, :], lhsT=kprev, rhs=QT[ps, 0, qs], start=True, stop=True),
                         reads=[KH_b if first else KT_sb, QT_b], writes=[sT_pb[s4]])
                    P.op("pe", lambda e, s4=s4, ps=ps, qs=qs: e.matmul(sT_ps[s4][:, 1, :], lhsT=KT[ps, 0, qs], rhs=QT[ps, 0, qs], start=True, stop=True),
                         reads=[KT_sb, QT_b], writes=[sT_pb[s4]])
                    P.op("act", lambda e, s4=s4, s2=s2: e.activation(out=ex[s2][:, 0, :], in_=sT_ps[s4].rearrange("p a b -> p (a b)"), func=AF.Exp, scale=0.125),
                         reads=[sT_pb[s4]], writes=[ex_b[s2]])
                    tab = Et0 if first else Et
                    P.op("dve", lambda e, s2=s2, tab=tab, hh=hh: e.tensor_tensor(out=pt[s2].rearrange("p a b -> p (a b)"), in0=ex[s2][:, 0, :], in1=tab[:, hh, :], op=ALU.mult),
                         reads=[ex_b[s2], Et_b], writes=[pt_b[s2]])
                    vprev = VH[:, c, ps] if first else Vt[:, blk - 1, ps]
                    P.op("pe", lambda e, s2=s2, vprev=vprev: e.matmul(nd_ps[s2][:, 0, :], lhsT=vprev, rhs=pt[s2][:, 0, :], start=True, stop=False),
                         reads=[VH_b if first else Vt_b, pt_b[s2]], writes=[nd_pb[s2]])
                    P.op("pe", lambda e, s2=s2, blk=blk, ps=ps: e.matmul(nd_ps[s2][:, 0, :], lhsT=Vt[:, blk, ps], rhs=pt[s2][:, 1, :], start=False, stop=True),
                         reads=[Vt_b, pt_b[s2]], writes=[nd_pb[s2]])
                    P.op("pe", lambda e, s2=s2: e.matmul(nd_ps[s2][:, 1, :], lhsT=ones[:, 0, :], rhs=pt[s2][:, 0, :], start=True, stop=False),
                         reads=[ones_b, pt_b[s2]], writes=[nd_pb[s2]])
                    P.op("pe", lambda e, s2=s2: e.matmul(nd_ps[s2][:, 1, :], lhsT=ones[:, 0, :], rhs=pt[s2][:, 1, :], start=False, stop=True),
                         reads=[ones_b, pt_b[s2]], writes=[nd_pb[s2]])
                    t0 = n * 128 * r + c
                    tsl = slice(t0, t0 + 127 * r + 1, r)
                    if g == 0:
                        P.op("act", lambda e, s2=s2, hh=hh, tsl=tsl: e.copy(out=Nacc[0:64, hh, tsl], in_=nd_ps[s2][:, 0, :]), reads=[nd_pb[s2]], writes=[ND_b[hh]])
                        P.op("dve", lambda e, s2=s2, hh=hh, tsl=tsl: e.tensor_copy(out=Dacc[0:64, hh, tsl], in_=nd_ps[s2][:, 1, :]), reads=[nd_pb[s2]], writes=[ND_b[hh]])
                    else:
                        P.op("dve", lambda e, s2=s2, hh=hh, tsl=tsl: e.tensor_tensor(out=Nacc[0:64, hh, tsl], in0=Nacc[0:64, hh, tsl], in1=nd_ps[s2][:, 0, :], op=ALU.add),
                             reads=[nd_pb[s2]], writes=[ND_b[hh]])
                        P.op("dve", lambda e, s2=s2, hh=hh, tsl=tsl: e.tensor_tensor(out=Dacc[0:64, hh, tsl], in0=Dacc[0:64, hh, tsl], in1=nd_ps[s2][:, 1, :], op=ALU.add),
                             reads=[nd_pb[s2]], writes=[ND_b[hh]])
        for hh in range(2):
            h = 2 * hp + hh
            P.op("dve", lambda e, hh=hh: e.reciprocal(out=rc[0:64, 0, :], in_=Dacc[0:64, hh, :]), reads=[ND_b[hh]], writes=[rc_b])
            P.op("pool", lambda e, hh=hh, h=h: e.tensor_tensor(out=oT[0:64, h, :], in0=Nacc[0:64, hh, :], in1=rc[0:64, 0, :], op=ALU.mult),
                 reads=[ND_b[hh], rc_b], writes=[oT_b[h]] + self.acc_b)
    P.barrier()
    off = 0
    Wo, off = self.st_view(off, [ATT_H, D], BF16)
    Wo_b = Buf("aWo")
    nrm, off = self.st_view(off, [2, D], F32)
    nrm_b = [Buf("anrm0"), Buf("anrm1")]
    self.wload(Wo[0:64], dr["attn_w_out"][j].rearrange("(h p) n -> p h n", p=64), Wo_b)
    toks = []
    for t in range(NT):
        tsl = slice(t * 128, (t + 1) * 128)
        for half in range(2):
            for h in range(ATT_H):
                P.op("pe", lambda e, h=h, half=half, tsl=tsl: e.matmul(bk[6 + half][:], lhsT=oT[0:64, h, tsl], rhs=Wo[0:64, h, half * 512:(half + 1) * 512],
                                                                        start=(h == 0), stop=(h == ATT_H - 1)),
                     reads=[oT_b[h], Wo_b], writes=[bb[6 + half]])
        src = self.psum_all[:, 6:8, :].rearrange("p a b -> p (a b)")
        toks.append(self.boundary(t, src, [bb[6], bb[7]], nrm[:, t % 2, :], nrm_b[t % 2], tbank=5))
    for t in range(NT):
        for h in range(ATT_H):
            self.acc_b[t].r = list(self.acc_b[t].r) + list(oT_b[h].r)
    return toks


Builder.att_main = _att_main


W_SHAPES = {
    "norm_g": [DEPTH, 5, D], "ret_w_in": [2, D, 6144], "ret_w_out": [2, 2048, D],
    "attn_w_in": [1, D, 9216], "attn_w_out": [1, D, D],
    "gm_w_in": [1, D, 4096], "gm_ln_g": [1, 2048], "gm_ln_b": [1, 2048], "gm_w_s": [1, 8, 128, 128], "gm_b_s": [1, 8, 128],
    "gm_w_out": [1, 2048, D], "ffn_w_in": [DEPTH, D, 2 * FFN_H], "ffn_w_out": [DEPTH, FFN_H, D],
    "ple_w_proj": [DEPTH, PLE_DIM, D], "ple_w_gate": [DEPTH, D, D],
}
T_SHAPES = {
    "ident": [128, 128], "causalT": [128, 128],
    "ret_maskT": [RET_H, 128, 128], "ret_qdec": [RET_H, 128, 512], "ret_kdec": [128, RET_H], "ret_coef": [128, L_RANKS * RET_H],
    "rope_cos": [128, T], "rope_sin": [128, T],
    "att_bias": [3, ATT_H, 128, 256], "att_mask01": [128, 256], "att_prevmask": [128, 256],
}
RET_IN = ["ret_w_in", "ret_w_out", "ret_maskT", "ret_qdec", "ret_kdec", "ret_coef", "rope_cos", "rope_sin"]
FFN_IN = ["ffn_w_in", "ffn_w_out", "ple_w_proj", "ple_w_gate", "p"]
ATT_IN = ["attn_w_in", "attn_w_out", "att_bias", "att_mask01", "att_prevmask"]
GM_IN = ["gm_w_in", "gm_ln_g", "gm_ln_b", "gm_w_s", "gm_b_s", "gm_w_out", "causalT"]
SEG_IN = {
    1: ["hin", "norm_g", "ident"] + RET_IN,
    2: ["hin", "norm_g", "ident", "L_all"] + RET_IN + FFN_IN + ["attn_w_in"],
    3: ["hin", "norm_g", "ident", "KT_own", "V_own", "KT_prev", "V_prev"] + ATT_IN + FFN_IN + GM_IN + RET_IN,
    4: ["hin", "norm_g", "ident", "L_all"] + RET_IN + FFN_IN,
}
SEG_OUT = {1: ["L_out"], 2: ["hout", "KT_own", "V_own"], 3: ["hout", "L_out"], 4: ["hout"]}
IO_SHAPES = {
    "hin": ([T, D], F32), "hout": ([T, D], F32), "p": ([DEPTH, T, PLE_DIM], F32),
    "L_all": ([L_RANKS, RET_H, RET_DK, RET_DV], F32), "L_out": ([RET_H, RET_DK, RET_DV], F32),
    "KT_own": ([3, D, T], BF16), "V_own": ([3, T, D], BF16), "KT_prev": ([3, D, T], BF16), "V_prev": ([3, T, D], BF16),
}


def build_segment(seg):
    nc = bass.Bass("TRN2", target_bir_lowering=False)
    dr = {}
    for name in dict.fromkeys(SEG_IN[seg]):
        if name in SEG_OUT[seg]:
            continue
        if name in W_SHAPES:
            shape, dt = W_SHAPES[name], F32
        elif name in T_SHAPES:
            shape, dt = T_SHAPES[name], F32
        else:
            shape, dt = IO_SHAPES[name]
        dr[name] = nc.dram_tensor(name, list(shape), dt, kind="ExternalInput").ap()
    for name in SEG_OUT[seg]:
        shape, dt = IO_SHAPES[name]
        dr[name] = nc.dram_tensor(name, list(shape), dt, kind="ExternalOutput").ap()
    hD = nc.dram_tensor("hD", [T, D], F32, kind="Internal").ap()
    B = Builder(nc, dr)
    P = B.P
    fin = []
    if seg == 1:
        B.h_rd = dr["hin"]
        B.load_g("pre", (0, 0))
        B.first_norm()
        fin += B.ret_emit(0, True)
    elif seg == 2:
        B.h_rd, B.h_wr = dr["hin"], hD
        B.load_g("pre", (0, 0))
        B.first_norm()
        B.load_g("post", (0, 1))
        B.load_g("pre", (0, 2))
        B.ret_emit(0, False, boundary=True)
        B.h_rd = hD
        B.ffn_stage(0)
        B.h_wr = dr["hout"]
        fin += B.ple_stage(0, last=False)
        B.att_kv_pass(0)
        for g in range(3):
            for b in B.KT_b[g] + B.V_b[g]:
                fin.append(b.w)
    elif seg == 3:
        B.h_rd, B.h_wr = dr["hin"], hD
        B.load_g("pre", (1, 0))
        B.first_norm()
        B.load_g("post", (1, 1))
        B.load_g("pre", (1, 2))
        B.att_main(1, 0)
        B.h_rd = hD
        B.ffn_stage(1)
        B.ple_stage(1, last=False)
        B.load_g("post", (2, 1))
        B.load_g("pre", (2, 2))
        B.gmlp_stage(2, 0)
        B.ffn_stage(2)
        B.h_wr = dr["hout"]
        fin += B.ple_stage(2, last=False)
        fin += B.ret_emit(1, True)
    elif seg == 4:
        B.h_rd, B.h_wr = dr["hin"], hD
        B.load_g("pre", (3, 0))
        B.first_norm()
        B.load_g("post", (3, 1))
        B.load_g("pre", (3, 2))
        B.ret_emit(1, False, boundary=True)
        B.h_rd = hD
        B.ffn_stage(3)
        B.h_wr = dr["hout"]
        fin += B.ple_stage(3, last=True)
    P.finish_on("sp", [t for t in fin if t is not None])
    P.emit()
    return nc


_PROG_CACHE = {}


def _prog(seg):
    if seg not in _PROG_CACHE:
        _PROG_CACHE[seg] = build_segment(seg)
    return _PROG_CACHE[seg]


def _run(seg, maps):
    nc = build_segment(seg)
    res = run_bass_kernel_spmd(nc, maps, core_ids=list(range(NCORES)))
    return res.results


def kernel(x, p, norm_g, ret_w_in, ret_w_out, attn_w_in, attn_w_out, rel_bias,
           gm_w_in, gm_ln_g, gm_ln_b, gm_w_s, gm_b_s, gm_w_out,
           ffn_w_in, ffn_w_out, ple_w_proj, ple_w_gate):
    import ml_dtypes
    f32 = lambda a: np.ascontiguousarray(np.asarray(a), dtype=np.float32)
    wts = {"norm_g": f32(norm_g), "ret_w_in": f32(ret_w_in), "ret_w_out": f32(ret_w_out), "attn_w_in": f32(attn_w_in),
           "attn_w_out": f32(attn_w_out), "gm_w_in": f32(gm_w_in), "gm_ln_g": f32(gm_ln_g), "gm_ln_b": f32(gm_ln_b),
           "gm_w_s": f32(gm_w_s), "gm_b_s": f32(gm_b_s), "gm_w_out": f32(gm_w_out), "ffn_w_in": f32(ffn_w_in),
           "ffn_w_out": f32(ffn_w_out), "ple_w_proj": f32(ple_w_proj), "ple_w_gate": f32(ple_w_gate)}
    x = f32(x)
    p = f32(p)
    maskT, qdec, kdec = ret_tables()
    bias_full, mask01 = att_tables(f32(rel_bias))
    idx = np.arange(128)
    consts = {"ident": np.eye(128, dtype=np.float32), "causalT": (idx[None, :] >= idx[:, None]).astype(np.float32),
              "ret_maskT": maskT, "ret_qdec": qdec, "ret_kdec": kdec, "att_bias": bias_full, "att_mask01": mask01}
    per_core = []
    for c in range(NCORES):
        b, s = c // 4, c % 4
        cos, sin = rope_tables(c)
        per_core.append({"ret_coef": ret_coef(c), "rope_cos": cos, "rope_sin": sin, "att_prevmask": att_prevmask(c, mask01),
                         "p": np.ascontiguousarray(p[:, b, s * T:(s + 1) * T, :])})

    def maps(seg, extra):
        out = []
        for c in range(NCORES):
            m = {}
            for name in dict.fromkeys(SEG_IN[seg]):
                if name in SEG_OUT[seg]:
                    continue
                if name in wts:
                    m[name] = wts[name]
                elif name in consts:
                    m[name] = consts[name]
                elif name in per_core[c]:
                    m[name] = per_core[c][name]
                else:
                    m[name] = extra[c][name]
            out.append(m)
        return out

    h = [np.ascontiguousarray(x[c // 4, (c % 4) * T:(c % 4 + 1) * T, :]) for c in range(NCORES)]
    r1 = _run(1, maps(1, [{"hin": h[c]} for c in range(NCORES)]))
    L_all = [np.ascontiguousarray(np.stack([r1[(c // 4) * 4 + rs]["L_out"] for rs in range(L_RANKS)], 0)) for c in range(NCORES)]
    r2 = _run(2, maps(2, [{"hin": h[c], "L_all": L_all[c]} for c in range(NCORES)]))
    h = [r2[c]["hout"] for c in range(NCORES)]
    zk = np.zeros((3, D, T), ml_dtypes.bfloat16)
    zv = np.zeros((3, T, D), ml_dtypes.bfloat16)
    ex3 = []
    for c in range(NCORES):
        prev = c - 1 if c % 4 > 0 else None
        ex3.append({"hin": h[c], "KT_own": r2[c]["KT_own"], "V_own": r2[c]["V_own"],
                    "KT_prev": r2[prev]["KT_own"] if prev is not None else zk,
                    "V_prev": r2[prev]["V_own"] if prev is not None else zv})
    r3 = _run(3, maps(3, ex3))
    h = [r3[c]["hout"] for c in range(NCORES)]
    L_all = [np.ascontiguousarray(np.stack([r3[(c // 4) * 4 + rs]["L_out"] for rs in range(L_RANKS)], 0)) for c in range(NCORES)]
    r4 = _run(4, maps(4, [{"hin": h[c], "L_all": L_all[c]} for c in range(NCORES)]))
    out = np.empty((BATCH, SEQ, D), np.float32)
    for c in range(NCORES):
        out[c // 4, (c % 4) * T:(c % 4 + 1) * T, :] = r4[c]["hout"]
    return out


def pipeline(n, phases):
    nph = len(phases)
    if _SEQ_DEBUG:
        for t in range(n):
            for ph in phases:
                ph(t)
        return
    for step in range(n + nph - 1):
        for ph in reversed(range(nph)):
            t = step - ph
            if 0 <= t < n:
                phases[ph](t)


def _pre_norm_phases(self, get_h, tbanks=(7,), gain=None):
    P = self.P
    stt = {}
    gt, gt_b = gain if gain is not None else (self.gpre, self.gpre_b)
    slots = (0, 1) if gain is None else (2, 3)

    def pa(t):
        hs, hb = get_h(t)
        r, r_b = self.rstd_of(hs, [hb], D)
        s = slots[t % 2] if self.abf_n == 4 else t % 2
        ab, ab_b = self.abf[:, s, :], self.abf_b[s]
        P.op("dve", lambda e: e.scalar_tensor_tensor(out=ab, in0=hs, scalar=r, in1=gt[:], op0=ALU.mult, op1=ALU.mult),
             reads=[hb, r_b, gt_b], writes=[ab_b])
        stt[t] = (s, ab_b)

    def pb(t):
        s, ab_b = stt.pop(t)
        tbank = tbanks[t % len(tbanks)]
        pst = self.bank[tbank][:].bitcast(BF16).rearrange("p (k n) -> p k n", k=KC)
        for kc in range(KC):
            P.op("pe", lambda e, kc=kc: e.transpose(out=pst[:, kc, :], in_=self.abf[:, s, kc * 128:(kc + 1) * 128], identity=self.ident[:]),
                 reads=[ab_b, self.ident_b], writes=[self.bank_b[tbank]])
        P.op("act", lambda e: e.copy(out=self.aT[:, :, t * 128:(t + 1) * 128], in_=pst),
             reads=[self.bank_b[tbank]], writes=[self.aT_b[t]])

    return [pa, pb]


def _boundary_phases(self, src_fn, dst_fn, do_pre=True, tbanks=(7,), toks=None):
    P = self.P
    st1 = {}
    st2 = {}

    def p0(t):
        src_ap, src_b = src_fn(t)
        dst_ap, dst_b = dst_fn(t)
        r, r_b = self.rstd_of(src_ap, src_b, D)
        P.op("dve", lambda e: e.scalar_tensor_tensor(out=dst_ap, in0=src_ap, scalar=r, in1=self.gpost[:], op0=ALU.mult, op1=ALU.mult),
             reads=list(src_b) + [r_b, self.gpost_b], writes=[dst_b])
        hs, hb = self.h_load(t)
        st1[t] = (dst_ap, dst_b, hs, hb)

    def p1(t):
        dst_ap, dst_b, hs, hb = st1.pop(t)
        P.op("pool", lambda e: e.tensor_tensor(out=hs, in0=hs, in1=dst_ap, op=ALU.add), reads=[dst_b], writes=[hb])
        tok = self.h_store(t, hs, hb)
        if toks is not None:
            toks.append(tok)
        st2[t] = (hs, hb)

    phases = [p0, p1]
    if do_pre:
        pn = self.pre_norm_phases(lambda t: st2.pop(t), tbanks=tbanks)
        def p1b(t, p1=p1, pa=pn[0]):
            p1(t)
            pa(t)
        phases = [p0, p1b, pn[1]]
    return phases


def _first_norm(self):
    self.set_hextra(0, 5)
    pipeline(NT, self.pre_norm_phases(lambda t: self.h_load(t), tbanks=(6, 7)))


def _mixer_boundaries2(self, i):
    toks = []
    ph = self.boundary_phases(lambda t: (self.acc[:, t, :], [self.acc_b[t]]), lambda t: (self.acc[:, t, :], self.acc_b[t]), toks=toks, tbanks=(6, 7))
    pipeline(NT, ph)
    return toks


Builder.pre_norm_phases = _pre_norm_phases
Builder.boundary_phases = _boundary_phases
Builder.first_norm = _first_norm
Builder.mixer_boundaries = _mixer_boundaries2


def _ffn_stage2(self, i):
    P = self.P
    self.load_g("post", (i, 3))
    self.load_g("pre", (i, 4))
    P.barrier()
    w_in = self.dr["ffn_w_in"][i]
    w_out = self.dr["ffn_w_out"][i]
    off = 0
    winA, winA_b = [], []
    for s in range(3):
        v, off = self.st_view(off, [KC, 512], BF16)
        winA.append(v)
        winA_b.append(Buf(f"f{i}winA{s}"))
    woutB, woutB_b = [], []
    for s in range(2):
        v, off = self.st_view(off, [4, D], BF16)
        woutB.append(v)
        woutB_b.append(Buf(f"f{i}woutB{s}"))
    hidT, hidT_b = [], []
    for s in range(2):
        v, off = self.st_view(off, [4, T], BF16)
        hidT.append(v)
        hidT_b.append([[Buf(f"f{i}hid{s}_{jl}_{tg}") for tg in range(4)] for jl in range(4)])
    sil, sil_b = [], []
    for s in range(2):
        v, off = self.st_view(off, [1, 512], F32)
        sil.append(v)
        sil_b.append(Buf(f"f{i}sil{s}"))
    nchunk = FFN_H // 128
    ngrp = (nchunk + 3) // 4
    unit = [0]
    self.set_hextra(off, (self.ST_BYTES - off) // 4096)

    def in_phase(pi):
        ws, wb = winA[pi % 3], winA_b[pi % 3]
        self.wload(ws[:, :, 0:256], w_in[:, pi * 256:(pi + 1) * 256].rearrange("(kc p) n -> p kc n", p=128), wb)
        self.wload(ws[:, :, 256:512], w_in[:, FFN_H + pi * 256:FFN_H + (pi + 1) * 256].rearrange("(kc p) n -> p kc n", p=128), wb)
        for jj in range(2):
            j = 2 * pi + jj
            grp, jl = j // 4, j % 4
            hs_, hs_b = hidT[grp % 2], hidT_b[grp % 2]
            for tg in range(4):
                u = unit[0]
                unit[0] += 1
                bg, bu = 2 * (u % 2), 2 * (u % 2) + 1
                sl, slb = sil[u % 2], sil_b[u % 2]
                for (bk, coff) in ((bg, jj * 128), (bu, 256 + jj * 128)):
                    for kc in range(KC):
                        P.op("pe", lambda e, bk=bk, coff=coff, kc=kc, tg=tg, ws=ws: e.matmul(
                            self.bank[bk][:], lhsT=ws[:, kc, coff:coff + 128], rhs=self.aT[:, kc, tg * 512:(tg + 1) * 512],
                            start=(kc == 0), stop=(kc == KC - 1)),
                            reads=[wb] + self.aT_b[tg * 4:(tg + 1) * 4], writes=[self.bank_b[bk]])
                P.op("act", lambda e, bg=bg, sl=sl: e.activation(out=sl[:, 0, :], in_=self.bank[bg][:], func=AF.Silu),
                     reads=[self.bank_b[bg]], writes=[slb])
                P.op("dve", lambda e, bu=bu, sl=sl, hs_=hs_, jl=jl, tg=tg: e.tensor_tensor(
                    out=hs_[:, jl, tg * 512:(tg + 1) * 512], in0=sl[:, 0, :], in1=self.bank[bu][:], op=ALU.mult),
                    reads=[slb, self.bank_b[bu]], writes=[hs_b[jl][tg]])

    def out_load(grp):
        n = min(4, nchunk - grp * 4)
        wo, wob = woutB[grp % 2], woutB_b[grp % 2]
        self.wload(wo[:, 0:n, :], w_out[grp * 512:grp * 512 + n * 128, :].rearrange("(jl p) n -> p jl n", p=128), wob)

    def out_tile(grp, t):
        n = min(4, nchunk - grp * 4)
        wo, wob = woutB[grp % 2], woutB_b[grp % 2]
        hs_, hs_b = hidT[grp % 2], hidT_b[grp % 2]
        for half in range(2):
            bk = 4 + 2 * (t % 2) + half
            for q in range(n):
                P.op("pe", lambda e, bk=bk, q=q, half=half: e.matmul(
                    self.bank[bk][:], lhsT=hs_[:, q, t * 128:(t + 1) * 128], rhs=wo[:, q, half * 512:(half + 1) * 512],
                    start=(q == 0), stop=(q == n - 1)),
                    reads=[wob, hs_b[q][t // 4]], writes=[self.bank_b[bk]])
            dst = self.acc[:, t, half * 512:(half + 1) * 512]
            if grp == 0:
                P.op("act", lambda e, bk=bk, dst=dst: e.copy(out=dst, in_=self.bank[bk][:]),
                     reads=[self.bank_b[bk]], writes=[self.acc_b[t]])
            else:
                P.op("dve", lambda e, bk=bk, dst=dst: e.tensor_tensor(out=dst, in0=dst, in1=self.bank[bk][:], op=ALU.add),
                     reads=[self.bank_b[bk]], writes=[self.acc_b[t]])

    toks = []
    pairs_of = lambda grp: range(grp * 2, min(grp * 2 + 2, nchunk // 2))
    for pi in pairs_of(0):
        in_phase(pi)
    for grp in range(ngrp):
        out_load(grp)
        if grp + 1 < ngrp:
            for pi in pairs_of(grp + 1):
                in_phase(pi)
            for t in range(NT):
                out_tile(grp, t)
        else:
            ph = self.boundary_phases_f(lambda t: (self.acc[:, t, :], [self.acc_b[t]]), lambda t: (self.acc[:, t, :], self.acc_b[t]),
                                        toks=toks, tbanks=(0, 1))
            pipeline(NT, [lambda t, grp=grp: out_tile(grp, t)] + ph)
    return toks


Builder.ffn_stage = _ffn_stage2


def _ple2(self, i, do_pre=True):
    P = self.P
    P.barrier()
    off = 0
    Wg, off = self.st_view(off, [KC, D], BF16)
    Wg_b = Buf(f"p{i}Wg")
    Wp, off = self.st_view(off, [2, D], BF16)
    Wp_b = Buf(f"p{i}Wp")
    pb, pb_b, pT, pT_b, sg, sg_b = [], [], [], [], [], []
    for s in range(3):
        v, off = self.st_view(off, [1, 256], BF16)
        pb.append(v)
        pb_b.append(Buf(f"p{i}pb{s}"))
        v, off = self.st_view(off, [2, 128], BF16)
        pT.append(v)
        pT_b.append(Buf(f"p{i}pT{s}"))
        v, off = self.st_view(off, [2, 512], F32)
        sg.append(v)
        sg_b.append(Buf(f"p{i}sg{s}"))
    self.set_hextra(off, 6)
    self.wload(Wg[:, :, 0:512], self.dr["ple_w_gate"][i][:, 0:512].rearrange("(kc p) n -> p kc n", p=128), Wg_b)
    self.wload(Wp, self.dr["ple_w_proj"][i].rearrange("(kc p) n -> p kc n", p=128), Wp_b)
    Wg2_b = Buf(f"p{i}Wg2")
    self.wload(Wg[:, :, 512:1024], self.dr["ple_w_gate"][i][:, 512:1024].rearrange("(kc p) n -> p kc n", p=128), Wg2_b)
    toks = []
    hst = {}

    def mm(t):
        s = t % 3
        self.wload(pb[s][:, 0, :], self.dr["p"][i, t * 128:(t + 1) * 128, :], pb_b[s])
        ptb = 6
        pst = self.bank[ptb][:].bitcast(BF16)[:, (t % 2) * 256:(t % 2) * 256 + 256].rearrange("p (k n) -> p k n", k=2)
        for kc in range(2):
            P.op("pe", lambda e, kc=kc: e.transpose(out=pst[:, kc, :], in_=pb[s][:, 0, kc * 128:(kc + 1) * 128], identity=self.ident[:]),
                 reads=[pb_b[s], self.ident_b], writes=[self.bank_b[ptb]])
        P.op("dve", lambda e: e.tensor_copy(out=pT[s], in_=pst), reads=[self.bank_b[ptb]], writes=[pT_b[s]])
        for half in range(2):
            u = 2 * t + half
            bG, bE = u % 3, 3 + u % 3
            wgb = Wg_b if half == 0 else Wg2_b
            for kc in range(KC):
                P.op("pe", lambda e, kc=kc, half=half, bG=bG: e.matmul(
                    self.bank[bG][:], lhsT=self.aT[:, kc, t * 128:(t + 1) * 128], rhs=Wg[:, kc, half * 512:(half + 1) * 512],
                    start=(kc == 0), stop=(kc == KC - 1)), reads=[wgb, self.aT_b[t]], writes=[self.bank_b[bG]])
            for kc in range(2):
                P.op("pe", lambda e, kc=kc, half=half, bE=bE: e.matmul(
                    self.bank[bE][:], lhsT=pT[s][:, kc, :], rhs=Wp[:, kc, half * 512:(half + 1) * 512],
                    start=(kc == 0), stop=(kc == 1)), reads=[Wp_b, pT_b[s]], writes=[self.bank_b[bE]])

    def post(t):
        s = t % 3
        for half in range(2):
            u = 2 * t + half
            bG, bE = u % 3, 3 + u % 3
            P.op("act", lambda e, half=half, bG=bG: e.activation(out=sg[s][:, half, :], in_=self.bank[bG][:], func=AF.Sigmoid),
                 reads=[self.bank_b[bG]], writes=[sg_b[s]])
            P.op("dve", lambda e, half=half, bE=bE: e.tensor_tensor(out=sg[s][:, half, :], in0=sg[s][:, half, :], in1=self.bank[bE][:], op=ALU.mult),
                 reads=[self.bank_b[bE]], writes=[sg_b[s]])
        hs, hb = self.h_load(t)
        hst[t] = (hs, hb)

    def resid(t):
        s = t % 3
        hs, hb = hst[t]
        P.op("dve", lambda e: e.tensor_tensor(out=hs, in0=hs, in1=sg[s].rearrange("p a b -> p (a b)"), op=ALU.add),
             reads=[sg_b[s]], writes=[hb])
        toks.append(self.h_store(t, hs, hb))

    phases = [mm, post]
    if do_pre:
        pn = self.pre_norm_phases_f(lambda t: hst.pop(t), tbanks=(7,))

        def resid_pa(t):
            resid(t)
            pn[0](t)
        phases += [resid_pa] + pn[1:]
    else:
        phases += [resid]
    pipeline(NT, phases)
    return toks


Builder.ple = _ple2


def _att_main2(self, i, j):
    P = self.P
    dr = self.dr
    P.barrier()
    w_in = dr["attn_w_in"][j]
    if getattr(self, "KT_b", None) is None:
        self.KT_b = [[Buf(f"KT{g}_{fb}") for fb in range(8)] for g in range(3)]
        self.V_b = [[Buf(f"V{g}_{blk}") for blk in range(NT)] for g in range(3)]
    off = 0
    Wq, off = self.st_view(off, [KC, 384], BF16)
    Wq_b = [Buf(f"aWq{g}") for g in range(3)]
    QT, KT, Vt, KH, VH, Et, Et0 = [], [], [], [], [], [], []
    QT_b, KT_sb, Vt_b, KH_b, VH_b, Et_b = [], [], [], [], [], []
    for g, (w, r) in enumerate(DIL):
        a, off = self.st_view(off, [1, T], BF16); QT.append(a); QT_b.append(Buf(f"aQT{g}"))
        a, off = self.st_view(off, [1, T], BF16); KT.append(a); KT_sb.append(Buf(f"aKT{g}"))
        a, off = self.st_view(off, [NT, 128], BF16); Vt.append(a); Vt_b.append(Buf(f"aVt{g}"))
        a, off = self.st_view(off, [1, r * 128], BF16); KH.append(a); KH_b.append(Buf(f"aKH{g}"))
        a, off = self.st_view(off, [r, 128], BF16); VH.append(a); VH_b.append(Buf(f"aVH{g}"))
        a, off = self.st_view(off, [2, 256], F32); Et.append(a)
        a, off = self.st_view(off, [2, 256], F32); Et0.append(a); Et_b.append(Buf(f"aEt{g}"))
    bst, off = self.st_view(off, [2, 256], F32)
    bst_b = Buf("abst")
    m01, off = self.st_view(off, [1, 256], F32)
    m01p, off = self.st_view(off, [1, 256], F32)
    m_b = Buf("am01")
    ones, off = self.st_view(off, [1, 64], BF16)
    ones_b = Buf("aones")
    ex, ex_b, pt, pt_b = [], [], [], []
    for s in range(2):
        a, off = self.st_view(off, [1, 1024], F32); ex.append(a); ex_b.append(Buf(f"aex{s}"))
        a, off = self.st_view(off, [8, 128], BF16); pt.append(a); pt_b.append(Buf(f"apt{s}"))
    bk, bb = self.bank, self.bank_b
    bigb = self.big[:].bitcast(BF16)
    oT = bigb[:, 0:16384].rearrange("p (h t) -> p h t", h=8)
    oT_b = [Buf(f"aoT{h}") for h in range(8)]
    ND2 = [self.big[:, 8192:12288].rearrange("p (a t) -> p a t", a=2), self.big[:, 12288:16384].rearrange("p (a t) -> p a t", a=2)]
    ND2_b = [Buf("aND0"), Buf("aND1")]
    ND2_acc = [self.acc_b[8:12], self.acc_b[12:16]]
    pending = []
    P.dma("sp", m01[:, 0, :], dr["att_mask01"][:, :], writes=[m_b])
    P.dma("sp", m01p[:, 0, :], dr["att_prevmask"][:, :], writes=[m_b])
    P.op("pool", lambda e: e.memset(ones, 1.0), writes=[ones_b])
    uq = [0]
    def make_fin(hp, nd, nd_b, n):
        out = []
        for k in range(n):
            csl = slice(k * (T // n), (k + 1) * (T // n))

            def f(csl=csl):
                P.op("dve", lambda e: e.reciprocal(out=nd[:, 1, csl], in_=nd[:, 1, csl]), reads=[], writes=[nd_b])
                P.op("pool", lambda e: e.tensor_tensor(out=oT[:, hp, csl], in0=nd[:, 0, csl], in1=nd[:, 1, csl], op=ALU.mult),
                     reads=[nd_b], writes=[oT_b[hp]] + self.acc_b[hp:hp + 1])
            out.append(f)
        return out

    for hp in range(8):
        NDacc, ND_b = ND2[hp % 2], ND2_b[hp % 2]
        for g, (w, r) in enumerate(DIL):
            nbc = 16 // r
            c0 = g * 3072 + hp * 128
            self.wload(Wq[:, :, g * 128:(g + 1) * 128], w_in[:, c0:c0 + 128].rearrange("(kc p) n -> p kc n", p=128), Wq_b[g])
            P.dma("sp", KT[g][:, 0, :], dr["KT_own"][g, hp * 128:(hp + 1) * 128, :], reads=[self.KT_b[g][hp]], writes=[KT_sb[g]])
            P.dma("sp", Vt[g], dr["V_own"][g, :, hp * 128:(hp + 1) * 128].rearrange("(b p) f -> p b f", p=128), reads=self.V_b[g], writes=[Vt_b[g]])
            P.dma("sp", KH[g][:, 0, :].rearrange("p (c j) -> p c j", c=r),
                  dr["KT_prev"][g, hp * 128:(hp + 1) * 128, :].rearrange("f (c n j) -> f c n j", c=r, n=nbc)[:, :, nbc - 1, :], writes=[KH_b[g]])
            P.dma("sp", VH[g],
                  dr["V_prev"][g, :, hp * 128:(hp + 1) * 128].rearrange("(c n p) f -> p c n f", c=r, n=nbc)[:, :, nbc - 1, :], writes=[VH_b[g]])
            P.dma("sp", bst, dr["att_bias"][g, 2 * hp:2 * hp + 2].rearrange("h p n -> p h n"), writes=[bst_b])
            P.op("act", lambda e: e.activation(out=bst, in_=bst, func=AF.Exp), reads=[], writes=[bst_b])
            for hh in range(2):
                P.op("pool", lambda e, hh=hh, g=g: e.tensor_tensor(out=Et[g][:, hh, :], in0=bst[:, hh, :], in1=m01[:, 0, :], op=ALU.mult),
                     reads=[bst_b, m_b], writes=[Et_b[g]])
                P.op("pool", lambda e, hh=hh, g=g: e.tensor_tensor(out=Et0[g][:, hh, :], in0=bst[:, hh, :], in1=m01p[:, 0, :], op=ALU.mult),
                     reads=[bst_b, m_b], writes=[Et_b[g]])

        def q_proj(g):
            r = DIL[g][1]
            for tg in range(4):
                b = uq[0] % 2
                uq[0] += 1
                for kc in range(KC):
                    P.op("pe", lambda e, b=b, kc=kc, tg=tg: e.matmul(
                        bk[b][:], lhsT=Wq[:, kc, g * 128:(g + 1) * 128], rhs=self.aT[:, kc, tg * 512:(tg + 1) * 512],
                        start=(kc == 0), stop=(kc == KC - 1)), reads=[Wq_b[g]] + self.aT_b[tg * 4:(tg + 1) * 4], writes=[bb[b]])
                jn = 512 // r
                dst = QT[g][:, 0, :].rearrange("p (c j) -> p c j", c=r)[:, :, tg * jn:(tg + 1) * jn]
                src = bk[b][:].rearrange("p (j c) -> p c j", c=r)
                P.op("act", lambda e, dst=dst, src=src: e.copy(out=dst, in_=src), reads=[bb[b]], writes=[QT_b[g]])

        def geom(it, d):
            g, bp = it // 8, it % 8
            blk = 2 * bp + d
            r = DIL[g][1]
            nbc = 16 // r
            c, n = blk // nbc, blk % nbc
            return g, blk, r, c, n

        def ph0(it):
            if it % 8 == 0:
                q_proj(it // 8)
            sb0 = 2 if it % 2 == 0 else 6
            for d in range(2):
                g, blk, r, c, n = geom(it, d)
                qs = slice(blk * 128, (blk + 1) * 128)
                first = (n == 0)
                for hh in range(2):
                    ps = slice(hh * 64, (hh + 1) * 64)
                    sT = bk[sb0 + hh][:].rearrange("p (d k q) -> p d k q", d=2, k=2)
                    kprev = KH[g][ps, 0, c * 128:(c + 1) * 128] if first else KT[g][ps, 0, (blk - 1) * 128:blk * 128]
                    P.op("pe", lambda e, sT=sT, kprev=kprev, ps=ps, d=d, qs=qs, g=g: e.matmul(sT[:, d, 0, :], lhsT=kprev, rhs=QT[g][ps, 0, qs], start=True, stop=True),
                         reads=[KH_b[g] if first else KT_sb[g], QT_b[g]], writes=[bb[sb0 + hh]])
                    P.op("pe", lambda e, sT=sT, ps=ps, d=d, qs=qs, g=g: e.matmul(sT[:, d, 1, :], lhsT=KT[g][ps, 0, qs], rhs=QT[g][ps, 0, qs], start=True, stop=True),
                         reads=[KT_sb[g], QT_b[g]], writes=[bb[sb0 + hh]])

        def ph1(it):
            sb0 = 2 if it % 2 == 0 else 6
            s2 = it % 2
            P.op("act", lambda e: e.activation(out=ex[s2][:, 0, :].rearrange("p (h x) -> p h x", h=2), in_=self.psum_all[:, sb0:sb0 + 2, :], func=AF.Exp, scale=0.125),
                 reads=[bb[sb0], bb[sb0 + 1]], writes=[ex_b[s2]])
            exv = ex[s2][:, 0, :].rearrange("p (h d x) -> p h d x", h=2, d=2)
            ptv = pt[s2].rearrange("p (h d k) q -> p h d (k q)", h=2, d=2)
            for d in range(2):
                g, blk, r, c, n = geom(it, d)
                tab = Et0[g] if n == 0 else Et[g]
                P.op("dve", lambda e, d=d, tab=tab: e.tensor_tensor(out=ptv[:, :, d, :], in0=exv[:, :, d, :], in1=tab, op=ALU.mult),
                     reads=[ex_b[s2], Et_b[g]], writes=[pt_b[s2]])

        def ph2(it):
            s2 = it % 2
            nb_ = 4 + it % 2
            ndb = bk[nb_][:].rearrange("p (d a q) -> p d a q", d=2, a=2)
            for d in range(2):
                g, blk, r, c, n = geom(it, d)
                first = (n == 0)
                for hh in range(2):
                    ps = slice(hh * 64, (hh + 1) * 64)
                    i0 = hh * 4 + d * 2
                    vprev = VH[g][:, c, ps] if first else Vt[g][:, blk - 1, ps]
                    P.op("pe", lambda e, d=d, vprev=vprev, ps=ps, i0=i0: e.matmul(ndb[ps, d, 0, :], lhsT=vprev, rhs=pt[s2][:, i0, :], start=True, stop=False),
                         reads=[VH_b[g] if first else Vt_b[g], pt_b[s2]], writes=[bb[nb_]])
                    P.op("pe", lambda e, d=d, ps=ps, i0=i0, g=g, blk=blk: e.matmul(ndb[ps, d, 0, :], lhsT=Vt[g][:, blk, ps], rhs=pt[s2][:, i0 + 1, :], start=False, stop=True),
                         reads=[Vt_b[g], pt_b[s2]], writes=[bb[nb_]])
                    P.op("pe", lambda e, d=d, ps=ps, i0=i0: e.matmul(ndb[ps, d, 1, :], lhsT=ones[:, 0, :], rhs=pt[s2][:, i0, :], start=True, stop=False),
                         reads=[ones_b, pt_b[s2]], writes=[bb[nb_]])
                    P.op("pe", lambda e, d=d, ps=ps, i0=i0: e.matmul(ndb[ps, d, 1, :], lhsT=ones[:, 0, :], rhs=pt[s2][:, i0 + 1, :], start=False, stop=True),
                         reads=[ones_b, pt_b[s2]], writes=[bb[nb_]])

        def ph3(it):
            nb_ = 4 + it % 2
            ndb = bk[nb_][:].rearrange("p (d a q) -> p d a q", d=2, a=2)
            for d in range(2):
                g, blk, r, c, n = geom(it, d)
                t0 = n * 128 * r + c
                tsl = slice(t0, t0 + 127 * r + 1, r)
                if g == 0:
                    P.op("dve", lambda e, d=d, tsl=tsl, nd=NDacc: e.tensor_copy(out=nd[:, :, tsl], in_=ndb[:, d, :, :]), reads=[bb[nb_]], writes=[ND_b] + ND2_acc[hp % 2])
                else:
                    P.op("dve", lambda e, d=d, tsl=tsl, nd=NDacc: e.tensor_tensor(out=nd[:, :, tsl], in0=nd[:, :, tsl], in1=ndb[:, d, :, :], op=ALU.add), reads=[bb[nb_]], writes=[ND_b])

        def ph3w(it):
            ph3(it)
            if pending and it < len(pending[0]):
                pending[0][it]()

        pipeline(3 * 8, [ph0, ph1, ph2, ph3w])
        pending.clear()
        fin = make_fin(hp, NDacc, ND_b, 8)
        if hp < 7:
            pending.append(fin)
        else:
            for f in fin:
                f()
    P.barrier()
    off = 0
    Wo, off = self.st_view(off, [KC, D], BF16)
    Wo_b = Buf("aWo")
    nrm, off = self.st_view(off, [2, D], F32)
    nrm_b = [Buf("anrm0"), Buf("anrm1")]
    self.wload(Wo, dr["attn_w_out"][j].rearrange("(kc p) n -> p kc n", p=128), Wo_b)
    self.set_hextra(off, 6)
    toks = []

    def outp(t):
        tsl = slice(t * 128, (t + 1) * 128)
        for half in range(2):
            for h in range(8):
                P.op("pe", lambda e, h=h, half=half: e.matmul(bk[6 + half][:], lhsT=oT[:, h, tsl], rhs=Wo[:, h, half * 512:(half + 1) * 512],
                                                             start=(h == 0), stop=(h == 7)),
                     reads=[oT_b[h], Wo_b], writes=[bb[6 + half]])

    src = self.psum_all[:, 6:8, :].rearrange("p a b -> p (a b)")
    ph = self.boundary_phases(lambda t: (src, [bb[6], bb[7]]), lambda t: (nrm[:, t % 2, :], nrm_b[t % 2]), toks=toks, tbanks=(0, 1))
    pipeline(NT, [outp] + ph)
    for t in range(NT):
        for h in range(8):
            self.acc_b[t].r = list(self.acc_b[t].r) + list(oT_b[h].r)
        for nb2 in ND2_b:
            self.acc_b[t].r = list(self.acc_b[t].r) + list(nb2.r) + ([nb2.w] if nb2.w else [])
    return toks


Builder.att_main = _att_main2


def _gmlp_stage2(self, i, j):
    P = self.P
    dr = self.dr
    P.barrier()
    bigb = self.big[:].bitcast(BF16)
    Wv = bigb[:, 0:16384].rearrange("p (k n) -> p k n", k=KC)
    Wu = bigb[:, 16384:32768].rearrange("p (k n) -> p k n", k=KC)
    Wv_b = [Buf(f"gWv{c}") for c in range(4)]
    Wu_b = [Buf(f"gWu{c}") for c in range(4)]
    off = 0
    Wo, off = self.st_view(off, [16, D], BF16)
    Wo_b = [Buf(f"gWo{q}") for q in range(4)]
    wsT, off = self.st_view(off, [GM_G, 128], BF16)
    wsT_b = Buf("gwsT")
    gtab, off = self.st_view(off, [1, GM_W], F32)
    btab, off = self.st_view(off, [1, GM_W], BF16)
    gtab_b, btab_b = Buf("ggtab"), Buf("gbtab")
    bsT, off = self.st_view(off, [1, GM_G], F32)
    bsT_b = Buf("gbsT")
    cmask, off = self.st_view(off, [1, 128], F32)
    cmask_b = Buf("gcmask")
    off_vf = off
    vf, off = self.st_view(off, [1, GM_W], F32)
    vf_b = Buf("gvf")
    vn, off = self.st_view(off, [1, GM_W], BF16)
    vn_b = Buf("gvn")
    ssb, off = self.st_view(off, [1, GM_W], F32)
    ssb_b = Buf("gssb")
    us, off = self.st_view(off, [1, GM_W], BF16)
    us_b = Buf("gus")
    usT, off = self.st_view(off, [16, 128], BF16)
    usT_b = Buf("gusT")
    gu, gu_b = [], []
    for s in range(2):
        a, off = self.st_view(off, [1, 512], F32)
        gu.append(a)
        gu_b.append(Buf(f"ggu{s}"))
    nrm, off = self.st_view(off, [1, D], F32)
    nrm_b = Buf("gnrm")
    wsl, off2 = self.st_view(off_vf, [GM_G, 128], BF16)
    wsl_b = vf_b
    w_in = dr["gm_w_in"][j]
    self.wload(wsl, dr["gm_w_s"][j].rearrange("g n m -> n g m"), wsl_b)
    self.wload(btab[:, 0, :], dr["gm_ln_b"][j, :].partition_broadcast(128), btab_b)
    P.dma("sp", gtab[:, 0, :], dr["gm_ln_g"][j, :].partition_broadcast(128), writes=[gtab_b])
    for cb in range(4):
        self.P.dma("pool", Wv[:, :, cb * 512:(cb + 1) * 512], w_in[:, GM_W + cb * 512:GM_W + (cb + 1) * 512].rearrange("(kc p) n -> p kc n", p=128),
                   writes=[Wv_b[cb]])
    for cb in range(4):
        self.P.dma("pool", Wu[:, :, cb * 512:(cb + 1) * 512], w_in[:, cb * 512:(cb + 1) * 512].rearrange("(kc p) n -> p kc n", p=128),
                   writes=[Wu_b[cb]])
    for q in range(4):
        self.wload(Wo[:, q * 4:(q + 1) * 4, :], dr["gm_w_out"][j][q * 512:(q + 1) * 512, :].rearrange("(kc p) n -> p kc n", p=128), Wo_b[q])
    P.dma("sp", bsT[:, 0, :], dr["gm_b_s"][j].rearrange("g n -> n g"), writes=[bsT_b], allow_slow_non_contiguous=True)
    P.dma("sp", cmask[:, 0, :], dr["causalT"][:, :], writes=[cmask_b])
    bk, bb = self.bank, self.bank_b
    tps = bk[0][:].bitcast(BF16)[:, 0:1024].rearrange("p (g n) -> p g n", g=GM_G)
    for g in range(GM_G):
        P.op("pe", lambda e, g=g: e.transpose(out=tps[:, g, :], in_=wsl[:, g, :], identity=self.ident[:]),
             reads=[wsl_b, self.ident_b], writes=[bb[0]])
    for g in range(GM_G):
        P.op("dve", lambda e, g=g: e.tensor_tensor(out=wsT[:, g, :], in0=tps[:, g, :], in1=cmask[:, 0, :], op=ALU.mult),
             reads=[bb[0], cmask_b], writes=[wsT_b])
    toks = []
    stt = {}
    uu = [0]

    def ph0(t):
        tsl = slice(t * 128, (t + 1) * 128)
        sv, sv_b = [], []
        for cb in range(4):
            b = uu[0] % 2
            uu[0] += 1
            for kc in range(KC):
                P.op("pe", lambda e, cb=cb, kc=kc, b=b: e.matmul(bk[b][:], lhsT=self.aT[:, kc, tsl], rhs=Wv[:, kc, cb * 512:(cb + 1) * 512],
                                                               start=(kc == 0), stop=(kc == KC - 1)),
                     reads=[Wv_b[cb], self.aT_b[t]], writes=[bb[b]])
            a, a_b = self.new_stat()
            sv.append(a)
            sv_b.append(a_b)
            P.op("act", lambda e, cb=cb, a=a, b=b: e.activation(out=vf[:, 0, cb * 512:(cb + 1) * 512], in_=bk[b][:], func=AF.Gelu, accum_out=a),
                 reads=[bb[b]], writes=[vf_b, a_b])
        stt[t] = (sv, sv_b)

    def ph1(t):
        sv, sv_b = stt.pop(t)
        sq, sq_b = [], []
        for hh in range(2):
            a, a_b = self.new_stat()
            sq.append(a)
            sq_b.append(a_b)
            P.op("act", lambda e, hh=hh, a=a: e.activation(out=self.junk[:, :], in_=vf[:, 0, hh * 1024:(hh + 1) * 1024], func=AF.Square, accum_out=a),
                 reads=[vf_b], writes=[self.junk_b, a_b])
        m1, m1_b = self.new_stat()
        P.op("pool", lambda e: e.tensor_tensor(out=m1, in0=sv[0], in1=sv[1], op=ALU.add), reads=[sv_b[0], sv_b[1]], writes=[m1_b])
        m2, m2_b = self.new_stat()
        P.op("pool", lambda e: e.tensor_tensor(out=m2, in0=sv[2], in1=sv[3], op=ALU.add), reads=[sv_b[2], sv_b[3]], writes=[m2_b])
        mu, mu_b = self.new_stat()
        P.op("pool", lambda e: e.tensor_tensor(out=mu, in0=m1, in1=m2, op=ALU.add), reads=[m1_b, m2_b], writes=[mu_b])
        P.op("pool", lambda e: e.tensor_scalar(out=mu, in0=mu, scalar1=1.0 / GM_W, scalar2=None, op0=ALU.mult), reads=[], writes=[mu_b])
        e2, e2_b = self.new_stat()
        P.op("pool", lambda e: e.tensor_tensor(out=e2, in0=sq[0], in1=sq[1], op=ALU.add), reads=[sq_b[0], sq_b[1]], writes=[e2_b])
        mm, mm_b = self.new_stat()
        P.op("pool", lambda e: e.tensor_tensor(out=mm, in0=mu, in1=mu, op=ALU.mult), reads=[mu_b], writes=[mm_b])
        var, var_b = self.new_stat()
        P.op("pool", lambda e: e.tensor_scalar(out=var, in0=e2, scalar1=1.0 / GM_W, scalar2=EPS, op0=ALU.mult, op1=ALU.add),
             reads=[e2_b], writes=[var_b])
        P.op("pool", lambda e: e.tensor_tensor(out=var, in0=var, in1=mm, op=ALU.subtract), reads=[mm_b], writes=[var_b])
        rs, rs_b = self.new_stat()
        P.op("pool", lambda e: e.tensor_tensor(out=rs, in0=var, in1=self.cm05[:], op=ALU.pow), reads=[var_b, self.cm05_b], writes=[rs_b])
        P.op("dve", lambda e: e.scalar_tensor_tensor(out=vf[:, 0, :], in0=vf[:, 0, :], scalar=mu, in1=gtab[:, 0, :], op0=ALU.subtract, op1=ALU.mult),
             reads=[mu_b, gtab_b], writes=[vf_b])
        P.op("dve", lambda e: e.scalar_tensor_tensor(out=vn[:, 0, :], in0=vf[:, 0, :], scalar=rs, in1=btab[:, 0, :], op0=ALU.mult, op1=ALU.add),
             reads=[vf_b, rs_b, btab_b], writes=[vn_b])

    def ph2(t):
        tsl = slice(t * 128, (t + 1) * 128)
        for un in range(4):
            bkx = 2 + un % 2
            for gg in range(2):
                g = un * 2 + gg
                P.op("pe", lambda e, g=g, gg=gg, bkx=bkx: e.matmul(bk[bkx][:, gg * 256:(gg + 1) * 256], lhsT=wsT[:, g, :], rhs=vn[:, 0, g * 256:(g + 1) * 256],
                                                                   start=True, stop=True),
                     reads=[wsT_b, vn_b], writes=[bb[bkx]])
            for gg in range(2):
                g = un * 2 + gg
                P.op("dve", lambda e, g=g, gg=gg, bkx=bkx: e.tensor_scalar(out=ssb[:, 0, g * 256:(g + 1) * 256], in0=bk[bkx][:, gg * 256:(gg + 1) * 256],
                                                                            scalar1=bsT[:, 0, g:g + 1], scalar2=None, op0=ALU.add),
                     reads=[bb[bkx], bsT_b], writes=[ssb_b])
        for cb in range(4):
            b = uu[0] % 2
            uu[0] += 1
            for kc in range(KC):
                P.op("pe", lambda e, cb=cb, kc=kc, b=b: e.matmul(bk[b][:], lhsT=self.aT[:, kc, tsl], rhs=Wu[:, kc, cb * 512:(cb + 1) * 512],
                                                               start=(kc == 0), stop=(kc == KC - 1)),
                     reads=[Wu_b[cb], self.aT_b[t]], writes=[bb[b]])
            s = cb % 2
            P.op("act", lambda e, b=b, s=s: e.activation(out=gu[s][:, 0, :], in_=bk[b][:], func=AF.Gelu), reads=[bb[b]], writes=[gu_b[s]])
            P.op("dve", lambda e, cb=cb, s=s: e.tensor_tensor(out=us[:, 0, cb * 512:(cb + 1) * 512], in0=gu[s][:, 0, :], in1=ssb[:, 0, cb * 512:(cb + 1) * 512], op=ALU.mult),
                 reads=[gu_b[s], ssb_b], writes=[us_b])

    def ph3(t):
        for half in range(2):
            tp = bk[4 + half][:].bitcast(BF16).rearrange("p (k n) -> p k n", k=8)
            for q in range(8):
                kc = half * 8 + q
                P.op("pe", lambda e, tp=tp, q=q, kc=kc: e.transpose(out=tp[:, q, :], in_=us[:, 0, kc * 128:(kc + 1) * 128], identity=self.ident[:]),
                     reads=[us_b, self.ident_b], writes=[bb[4 + half]])
            P.op("act", lambda e, tp=tp, half=half: e.copy(out=usT[:, half * 8:(half + 1) * 8, :], in_=tp), reads=[bb[4 + half]], writes=[usT_b])
        for half in range(2):
            for kc in range(16):
                P.op("pe", lambda e, half=half, kc=kc: e.matmul(bk[6 + half][:], lhsT=usT[:, kc, :], rhs=Wo[:, kc, half * 512:(half + 1) * 512],
                                                                start=(kc == 0), stop=(kc == 15)),
                     reads=[usT_b, Wo_b[kc // 4]], writes=[bb[6 + half]])

    src = self.psum_all[:, 6:8, :].rearrange("p a b -> p (a b)")
    bph = self.boundary_phases(lambda t: (src, [bb[6], bb[7]]), lambda t: (nrm[:, 0, :], nrm_b), toks=toks, tbanks=(4, 5))
    def ph3b(t):
        ph3(t)
        bph[0](t)
    pipeline(NT, [ph0, ph1, ph2, ph3b] + bph[1:])
    for t in range(NT):
        for src_b in (Wv_b if t < 8 else Wu_b):
            self.acc_b[t].r = list(self.acc_b[t].r) + list(src_b.r) + ([src_b.w] if src_b.w else [])
    return toks


Builder.gmlp_stage = _gmlp_stage2


def _ffn_ple_stage(self, i, last):
    P = self.P
    self.load_g("post", (i, 3))
    self.load_g("pre", (i, 4))
    P.barrier()
    w_in = self.dr["ffn_w_in"][i]
    w_out = self.dr["ffn_w_out"][i]
    off = 0
    winA, winA_b = [], []
    for s in range(3):
        v, off = self.st_view(off, [KC, 512], BF16)
        winA.append(v)
        winA_b.append(Buf(f"f{i}winA{s}"))
    woutB, woutB_b, woutB_off = [], [], []
    for s in range(2):
        woutB_off.append(off)
        v, off = self.st_view(off, [4, D], BF16)
        woutB.append(v)
        woutB_b.append(Buf(f"f{i}woutB{s}"))
    hidT, hidT_b, hidT_off = [], [], []
    for s in range(2):
        hidT_off.append(off)
        v, off = self.st_view(off, [4, T], BF16)
        hidT.append(v)
        hidT_b.append([[Buf(f"f{i}hid{s}_{jl}_{tg}") for tg in range(4)] for jl in range(4)])
    sil, sil_b = [], []
    for s in range(2):
        v, off = self.st_view(off, [1, 512], F32)
        sil.append(v)
        sil_b.append(Buf(f"f{i}sil{s}"))
    nchunk = FFN_H // 128
    ngrp = (nchunk + 3) // 4
    unit = [0]

    def in_phase(pi):
        ws, wb = winA[pi % 3], winA_b[pi % 3]
        self.wload(ws[:, :, 0:256], w_in[:, pi * 256:(pi + 1) * 256].rearrange("(kc p) n -> p kc n", p=128), wb)
        self.wload(ws[:, :, 256:512], w_in[:, FFN_H + pi * 256:FFN_H + (pi + 1) * 256].rearrange("(kc p) n -> p kc n", p=128), wb)
        for jj in range(2):
            j = 2 * pi + jj
            grp, jl = j // 4, j % 4
            hs_, hs_b = hidT[grp % 2], hidT_b[grp % 2]
            for tg in range(4):
                u = unit[0]
                unit[0] += 1
                bg, bu = 2 * (u % 2), 2 * (u % 2) + 1
                sl, slb = sil[u % 2], sil_b[u % 2]
                for (bk, coff) in ((bg, jj * 128), (bu, 256 + jj * 128)):
                    for kc in range(KC):
                        P.op("pe", lambda e, bk=bk, coff=coff, kc=kc, tg=tg, ws=ws: e.matmul(
                            self.bank[bk][:], lhsT=ws[:, kc, coff:coff + 128], rhs=self.aT[:, kc, tg * 512:(tg + 1) * 512],
                            start=(kc == 0), stop=(kc == KC - 1)),
                            reads=[wb] + self.aT_b[tg * 4:(tg + 1) * 4], writes=[self.bank_b[bk]])
                P.op("act", lambda e, bg=bg, sl=sl: e.activation(out=sl[:, 0, :], in_=self.bank[bg][:], func=AF.Silu),
                     reads=[self.bank_b[bg]], writes=[slb])
                P.op("dve", lambda e, bu=bu, sl=sl, hs_=hs_, jl=jl, tg=tg: e.tensor_tensor(
                    out=hs_[:, jl, tg * 512:(tg + 1) * 512], in0=sl[:, 0, :], in1=self.bank[bu][:], op=ALU.mult),
                    reads=[slb, self.bank_b[bu]], writes=[hs_b[jl][tg]])

    def out_load(grp):
        n = min(4, nchunk - grp * 4)
        wo, wob = woutB[grp % 2], woutB_b[grp % 2]
        self.wload(wo[:, 0:n, :], w_out[grp * 512:grp * 512 + n * 128, :].rearrange("(jl p) n -> p jl n", p=128), wob)

    def out_tile(grp, t, banks):
        n = min(4, nchunk - grp * 4)
        wo, wob = woutB[grp % 2], woutB_b[grp % 2]
        hs_, hs_b = hidT[grp % 2], hidT_b[grp % 2]
        for half in range(2):
            bk = banks[half]
            for q in range(n):
                P.op("pe", lambda e, bk=bk, q=q, half=half: e.matmul(
                    self.bank[bk][:], lhsT=hs_[:, q, t * 128:(t + 1) * 128], rhs=wo[:, q, half * 512:(half + 1) * 512],
                    start=(q == 0), stop=(q == n - 1)),
                    reads=[wob, hs_b[q][t // 4]], writes=[self.bank_b[bk]])
            dst = self.acc[:, t, half * 512:(half + 1) * 512]
            if grp == 0:
                P.op("act", lambda e, bk=bk, dst=dst: e.copy(out=dst, in_=self.bank[bk][:]),
                     reads=[self.bank_b[bk]], writes=[self.acc_b[t]])
            else:
                P.op("dve", lambda e, bk=bk, dst=dst: e.tensor_tensor(out=dst, in0=dst, in1=self.bank[bk][:], op=ALU.add),
                     reads=[self.bank_b[bk]], writes=[self.acc_b[t]])

    pairs_of = lambda grp: range(grp * 2, min(grp * 2 + 2, nchunk // 2))
    for pi in pairs_of(0):
        in_phase(pi)
    for grp in range(ngrp - 1):
        out_load(grp)
        for pi in pairs_of(grp + 1):
            in_phase(pi)
        for t in range(NT):
            out_tile(grp, t, (4 + 2 * (t % 2), 5 + 2 * (t % 2)))
    lg = ngrp - 1
    out_load(lg)
    oth = (lg + 1) % 2
    dead = list(winA_b) + [woutB_b[oth]] + [b for row in hidT_b[oth] for b in row] + list(sil_b)
    assert woutB_off[oth] == 24 * 1024 or woutB_off[oth] == 32 * 1024
    po = 0
    Wg, po = self.st_view(po, [KC, D], BF16)
    Wp, po = self.st_view(po, [2, D], BF16)
    Wg_b, Wg2_b, Wp_b = Buf(f"p{i}Wg"), Buf(f"p{i}Wg2"), Buf(f"p{i}Wp")
    pb, pb_b, pT, pT_b = [], [], [], []
    for s in range(3):
        v, po = self.st_view(po, [1, 256], BF16)
        pb.append(v)
        pb_b.append(Buf(f"p{i}pb{s}"))
        v, po = self.st_view(po, [2, 128], BF16)
        pT.append(v)
        pT_b.append(Buf(f"p{i}pT{s}"))
    lim = 24 * 1024 if woutB_off[oth] != 24 * 1024 else 32 * 1024
    assert po <= lim, (po, lim)
    so = hidT_off[oth]
    sg, sg_b = [], []
    for s in range(3):
        v, so = self.st_view(so, [2, 512], F32)
        sg.append(v)
        sg_b.append(Buf(f"p{i}sg{s}"))
    assert so <= hidT_off[oth] + 16 * 1024
    if lim == 24 * 1024:
        dead = list(winA_b) + [b for row in hidT_b[oth] for b in row]
    P.dma("pool", Wg[:, :, 0:512], self.dr["ple_w_gate"][i][:, 0:512].rearrange("(kc p) n -> p kc n", p=128), writes=[Wg_b] + dead)
    P.dma("pool", Wp, self.dr["ple_w_proj"][i].rearrange("(kc p) n -> p kc n", p=128), writes=[Wp_b])
    P.dma("pool", Wg[:, :, 512:1024], self.dr["ple_w_gate"][i][:, 512:1024].rearrange("(kc p) n -> p kc n", p=128), writes=[Wg2_b])
    if not last:
        self.load_g("pre2", (i + 1, 0))
    toks_b, toks = [], []
    hst = {}

    def mm(t):
        s = t % 3
        P.dma("pool", pb[s][:, 0, :], self.dr["p"][i, t * 128:(t + 1) * 128, :], writes=[pb_b[s]])
        pst = self.bank[7][:].bitcast(BF16)[:, 0:256].rearrange("p (k n) -> p k n", k=2)
        for kc in range(2):
            P.op("pe", lambda e, kc=kc: e.transpose(out=pst[:, kc, :], in_=pb[s][:, 0, kc * 128:(kc + 1) * 128], identity=self.ident[:]),
                 reads=[pb_b[s], self.ident_b], writes=[self.bank_b[7]])
        P.op("dve", lambda e: e.tensor_copy(out=pT[s], in_=pst), reads=[self.bank_b[7]], writes=[pT_b[s]])
        for half in range(2):
            u = 2 * t + half
            bG, bE = u % 2, 2 + u % 2
            wgb = Wg_b if half == 0 else Wg2_b
            for kc in range(KC):
                P.op("pe", lambda e, kc=kc, half=half, bG=bG: e.matmul(
                    self.bank[bG][:], lhsT=self.aT[:, kc, t * 128:(t + 1) * 128], rhs=Wg[:, kc, half * 512:(half + 1) * 512],
                    start=(kc == 0), stop=(kc == KC - 1)), reads=[wgb, self.aT_b[t]], writes=[self.bank_b[bG]])
            for kc in range(2):
                P.op("pe", lambda e, kc=kc, half=half, bE=bE: e.matmul(
                    self.bank[bE][:], lhsT=pT[s][:, kc, :], rhs=Wp[:, kc, half * 512:(half + 1) * 512],
                    start=(kc == 0), stop=(kc == 1)), reads=[Wp_b, pT_b[s]], writes=[self.bank_b[bE]])
            P.op("act", lambda e, half=half, bG=bG: e.activation(out=sg[s][:, half, :], in_=self.bank[bG][:], func=AF.Sigmoid),
                 reads=[self.bank_b[bG]], writes=[sg_b[s]])
            P.op("dve", lambda e, half=half, bE=bE: e.tensor_tensor(out=sg[s][:, half, :], in0=sg[s][:, half, :], in1=self.bank[bE][:], op=ALU.mult),
                 reads=[self.bank_b[bE]], writes=[sg_b[s]])

    def post(t):
        hs, hb = self.h_load(t, src=getattr(self, "h_rd_ple", None))
        hst[t] = (hs, hb)

    def resid(t):
        s = t % 3
        hs, hb = hst[t]
        P.op("pool", lambda e: e.tensor_tensor(out=hs, in0=hs, in1=sg[s].rearrange("p a b -> p (a b)"), op=ALU.add),
             reads=[sg_b[s]], writes=[hb])
        toks.append(self.h_store(t, hs, hb, dst=getattr(self, "h_wr_ple", None)))

    bph = self.boundary_phases(lambda t: (self.acc[:, t, :], [self.acc_b[t]]), lambda t: (self.acc[:, t, :], self.acc_b[t]),
                               toks=toks_b, tbanks=(6,))
    phases = [lambda t: out_tile(lg, t, (4, 5))] + bph + [mm, post]
    if not last:
        pn = self.pre_norm_phases(lambda t: hst.pop(t), tbanks=(7,), gain=(self.gpre2, self.gpre2_b))

        def resid_pa(t):
            resid(t)
            pn[0](t)
        phases += [resid_pa, pn[1]]
    else:
        phases += [resid]
    pipeline(NT, phases)
    return toks


Builder.ffn_ple_stage = _ffn_ple_stage


def _rstd_split(self, src_ap, src_bufs, n):
    P = self.P
    st = {}

    def sq():
        ss, ss_b = self.new_stat()
        P.op("act", lambda e: e.activation(out=self.junk[:, 0:n], in_=src_ap, func=AF.Square, scale=float(n) ** -0.5, accum_out=ss),
             reads=list(src_bufs), writes=[self.junk_b, ss_b])
        st["ss"] = (ss, ss_b)

    def pw():
        ss, ss_b = st["ss"]
        r, r_b = self.new_stat()
        P.op("pool", lambda e: e.tensor_scalar(out=r, in0=ss, scalar1=EPS, scalar2=None, op0=ALU.add), reads=[ss_b], writes=[r_b])
        r2, r2_b = self.new_stat()
        P.op("pool", lambda e: e.tensor_tensor(out=r2, in0=r, in1=self.cm05[:], op=ALU.pow), reads=[r_b, self.cm05_b], writes=[r2_b])
        st["r"] = (r2, r2_b)

    return sq, pw, (lambda: st["r"])


def _pre_norm_phases_f(self, get_h, tbanks=(6, 7), merge_b=False):
    P = self.P
    S = {}

    def a1(t):
        hs, hb = get_h(t)
        sq, pw, get = self.rstd_split(hs, [hb], D)
        sq()
        S[t] = [hs, hb, pw, get]

    def a2(t):
        S[t][2]()

    def a3(t):
        hs, hb, pw, get = S[t]
        r, r_b = get()
        s = t % 2
        ab, ab_b = self.abf[:, s, :], self.abf_b[s]
        P.op("dve", lambda e: e.scalar_tensor_tensor(out=ab, in0=hs, scalar=r, in1=self.gpre[:], op0=ALU.mult, op1=ALU.mult),
             reads=[hb, r_b, self.gpre_b], writes=[ab_b])
        S[t] = (s, ab_b)

    def b1(t):
        s, ab_b = S[t]
        tbank = tbanks[t % len(tbanks)]
        pst = self.bank[tbank][:].bitcast(BF16).rearrange("p (k n) -> p k n", k=KC)
        for kc in range(KC):
            P.op("pe", lambda e, kc=kc: e.transpose(out=pst[:, kc, :], in_=self.abf[:, s, kc * 128:(kc + 1) * 128], identity=self.ident[:]),
                 reads=[ab_b, self.ident_b], writes=[self.bank_b[tbank]])
        S[t] = (tbank, pst)

    def b2(t):
        tbank, pst = S.pop(t)
        P.op("act", lambda e: e.copy(out=self.aT[:, :, t * 128:(t + 1) * 128], in_=pst),
             reads=[self.bank_b[tbank]], writes=[self.aT_b[t]])

    if merge_b:
        def b12(t):
            b1(t)
            b2(t)
        return [a1, a2, a3, b12]
    return [a1, a2, a3, b1, b2]


def _boundary_phases_f(self, src_fn, dst_fn, do_pre=True, tbanks=(6, 7), toks=None, merge_b=False):
    P = self.P
    S = {}
    H = {}

    def p0a(t):
        src_ap, src_b = src_fn(t)
        sq, pw, get = self.rstd_split(src_ap, src_b, D)
        sq()
        S[t] = [src_ap, src_b, pw, get]

    def p0b(t):
        S[t][2]()

    def p0c(t):
        src_ap, src_b, pw, get = S[t]
        dst_ap, dst_b = dst_fn(t)
        r, r_b = get()
        P.op("dve", lambda e: e.scalar_tensor_tensor(out=dst_ap, in0=src_ap, scalar=r, in1=self.gpost[:], op0=ALU.mult, op1=ALU.mult),
             reads=list(src_b) + [r_b, self.gpost_b], writes=[dst_b])
        hs, hb = self.h_load(t)
        S[t] = (dst_ap, dst_b, hs, hb)

    def p1(t):
        dst_ap, dst_b, hs, hb = S.pop(t)
        P.op("dve", lambda e: e.tensor_tensor(out=hs, in0=hs, in1=dst_ap, op=ALU.add), reads=[dst_b], writes=[hb])
        tok = self.h_store(t, hs, hb)
        if toks is not None:
            toks.append(tok)
        H[t] = (hs, hb)

    phases = [p0a, p0b, p0c, p1]
    if do_pre:
        pn = self.pre_norm_phases_f(lambda t: H.pop(t), tbanks=tbanks, merge_b=merge_b)

        def p1a(t):
            p1(t)
            pn[0](t)
        phases = [p0a, p0b, p0c, p1a] + pn[1:]
    return phases


Builder.rstd_split = _rstd_split
Builder.pre_norm_phases_f = _pre_norm_phases_f
Builder.boundary_phases_f = _boundary_phases_f


def _first_norm_f(self):
    self.set_hextra(0, 5)
    pipeline(NT, self.pre_norm_phases_f(lambda t: self.h_load(t), tbanks=(6, 7)))


Builder.first_norm = _first_norm_f
```

```python
import contextlib
import math
import numpy as np
import concourse.bass as bass
import concourse.mybir as mybir
from concourse.bass_utils import run_bass_kernel_spmd

F32 = mybir.dt.float32
BF16 = mybir.dt.bfloat16
I32 = mybir.dt.int32
AF = mybir.ActivationFunctionType
ALU = mybir.AluOpType
AX = mybir.AxisListType

SEM_ROLL = 30000
import os as _os
_SEQ_DEBUG = bool(_os.environ.get('KSEQ'))

D = 1024
KC = 8
BATCH = 2
SEQ = 8192
DEPTH = 4
NCORES = 8
T = 2048
NT = 16
EPS = 1e-6
PLE_DIM = 256
FFN_H = 2816
RET_H, RET_DK, RET_DV = 4, 256, 512
L_RANKS = 3
ATT_H, ATT_DH = 16, 64
DIL = ((128, 1), (512, 4), (2048, 16))
GM_W = 2048
GM_G = 8


class Buf:
    __slots__ = ("name", "w", "r", "dsem", "dcount")

    def __init__(self, name):
        self.name = name
        self.w = None
        self.r = []
        self.dsem = None
        self.dcount = 0


class Prog:
    ENGS = ("pe", "act", "dve", "pool", "sp")

    def __init__(self, nc):
        self.nc = nc
        self.stack = contextlib.ExitStack()
        self.ops = {e: [] for e in self.ENGS}
        self.sems = {}
        self.esem = {}
        self.ecount = {e: 0 for e in self.ENGS}
        self.eroll = {e: 0 for e in self.ENGS}
        self.known = {e: {} for e in self.ENGS}
        self.nsem = 0
        self.final_tokens = []
        self.chans = []
        self.free_sems = []
        for e in self.ENGS:
            self._new_esem(e)

    def sem(self, name):
        h = self.stack.enter_context(self.nc.semaphore(name))
        self.sems[name] = h
        self.nsem += 1
        return name

    def _new_esem(self, e):
        k = self.sem(f"s_{e}_{self.eroll[e]}")
        self.eroll[e] += 1
        self.esem[e] = k
        self.ecount[e] = 0

    def sbuf(self, name, shape, dtype):
        return self.stack.enter_context(self.nc.sbuf_tensor("sb_" + name, list(shape), dtype))

    def psum(self, name, shape, dtype):
        return self.stack.enter_context(self.nc.psum_tensor("ps_" + name, list(shape), dtype))

    def _deps(self, eng, reads, writes, skip_same_pe=False):
        waits = {}

        def need(tok):
            if tok is None:
                return
            k, v = tok
            if skip_same_pe and eng == "pe" and k == self.esem["pe"]:
                return
            if self.known[eng].get(k, 0) >= v:
                return
            if waits.get(k, 0) < v:
                waits[k] = v

        for b in reads:
            need(b.w)
        for b in writes:
            need(b.w)
            for t in b.r:
                need(t)
        for k, v in waits.items():
            self.known[eng][k] = v
        return list(waits.items())

    def _commit(self, tok, reads, writes):
        for b in writes:
            b.w = tok
            b.r = []
        for b in reads:
            b.r.append(tok)
            if len(b.r) > 48:
                best = {}
                for k, v in b.r:
                    if best.get(k, 0) < v:
                        best[k] = v
                b.r = list(best.items())

    def op(self, eng, fn, reads=(), writes=()):
        if self.ecount[eng] >= SEM_ROLL:
            self._new_esem(eng)
        waits = self._deps(eng, reads, writes, skip_same_pe=True)
        self.ecount[eng] += 1
        tok = (self.esem[eng], self.ecount[eng])
        self.ops[eng].append((fn, waits, (self.esem[eng], 1)))
        self._commit(tok, reads, writes)
        return tok

    def dma(self, q, out_ap, in_ap, reads=(), writes=(), chan=None, **kw):
        if chan is None:
            chan = writes[0] if writes else reads[0]
        if chan.dsem is None:
            self.free_sems.sort(key=lambda x: x[1])
            if self.free_sems and self.free_sems[0][1] < 16000:
                chan.dsem, chan.dcount = self.free_sems.pop(0)
            else:
                chan.dsem = self.sem(f"d_{chan.name}")
                chan.dcount = 0
            self.chans.append(chan)
        waits = self._deps(q, reads, writes)
        if chan.dcount > 0 and self.known[q].get(chan.dsem, 0) < chan.dcount:
            waits = [w for w in waits if w[0] != chan.dsem] + [(chan.dsem, chan.dcount)]
            self.known[q][chan.dsem] = chan.dcount
        chan.dcount += 16
        tok = (chan.dsem, chan.dcount)

        def fn(e, out_ap=out_ap, in_ap=in_ap, kw=kw):
            return e.dma_start(out=out_ap, in_=in_ap, **kw)

        self.ops[q].append((fn, waits, (chan.dsem, 16)))
        self._commit(tok, reads, writes)
        return tok

    def barrier(self):
        toks = {}
        for e in self.ENGS:
            if self.ecount[e] > 0:
                toks[self.esem[e]] = self.ecount[e]
        for c in self.chans:
            if c.dcount > 0:
                toks[c.dsem] = c.dcount
        for e in self.ENGS:
            waits = []
            for k, v in toks.items():
                if self.known[e].get(k, 0) < v:
                    waits.append((k, v))
                    self.known[e][k] = v
            if waits:
                self.ops[e].append((None, waits, None))

    def retire(self, bufs):
        for c in bufs:
            if c.dsem is not None:
                self.free_sems.append((c.dsem, c.dcount))
                self.chans.remove(c)
                c.dsem = None
                c.dcount = 0

    def finish_on(self, eng, toks):
        self.final_tokens.append((eng, list(toks)))

    def emit(self):
        nc = self.nc
        for eng, toks in self.final_tokens:
            best = {}
            for k, v in toks:
                if best.get(k, 0) < v:
                    best[k] = v
            self.ops[eng].append((None, list(best.items()), None))
        handles = {"pe": "tensor", "act": "scalar", "dve": "vector", "pool": "gpsimd", "sp": "sync"}
        with nc.Block() as block:
            for eng in self.ENGS:
                ops = self.ops[eng]
                if not ops:
                    continue

                def body(e, ops=ops):
                    for fn, waits, inc in ops:
                        for k, v in waits:
                            e.wait_ge(self.sems[k], v)
                        if fn is not None:
                            ins = fn(e)
                            ins.then_inc(self.sems[inc[0]], inc[1])

                getattr(block, handles[eng])(body)
        self.stack.close()

    def stats(self):
        d = {e: len(v) for e, v in self.ops.items()}
        d["nsem"] = self.nsem
        return d


class Builder:
    ST_BYTES = 84 * 1024

    def __init__(self, nc, dr):
        self.nc = nc
        self.dr = dr
        P = self.P = Prog(nc)
        self.aT = P.sbuf("aT", [128, KC, T], BF16)
        self.aT_b = [Buf(f"aT{t}") for t in range(NT)]
        self.big = P.sbuf("big", [128, NT * D], F32)
        self.acc = self.big[:].rearrange("p (t n) -> p t n", t=NT)
        self.acc_b = [Buf(f"acc{t}") for t in range(NT)]
        self.st = P.sbuf("st", [128, self.ST_BYTES // 2], BF16)
        self.ident = P.sbuf("ident", [128, 128], BF16)
        self.ident_b = Buf("ident")
        self.hring = P.sbuf("hring", [128, 3, D], F32)
        self.hr_b = [Buf("hr0"), Buf("hr1"), Buf("hr2")]
        self.hextra = []
        self.gpost = P.sbuf("gpost", [128, D], F32)
        self.gpost_b = Buf("gpost")
        self.gpre = P.sbuf("gpre", [128, D], F32)
        self.gpre_b = Buf("gpre")
        self.gpre2 = None
        self.gpre2_b = None
        self.abf_n = 2
        self.abf = P.sbuf("abf", [128, 2, D], BF16)
        self.abf_b = [Buf("abf0"), Buf("abf1")]
        self.junk = P.sbuf("junk", [128, D], BF16)
        self.junk_b = Buf("junk")
        self.stat = P.sbuf("stat", [128, 64], F32)
        self.stat_b = [Buf(f"stat{i}") for i in range(64)]
        self.stat_i = 0
        self.psum_all = P.psum("all", [128, 8, 512], F32)
        self.bank = [self.psum_all[:, i, :] for i in range(8)]
        self.bank_b = [Buf(f"bank{i}") for i in range(8)]
        self.hD_b = [Buf(f"hD{t}") for t in range(NT)]
        self.h_rd = None
        self.h_wr = None
        self.hcnt = 0
        P.dma("pool", self.ident[:], dr["ident"][:, :], writes=[self.ident_b])
        _orig_barrier = P.barrier

        def _barrier():
            _orig_barrier()
            P.retire([b for _, b in self.hextra])
            self.hextra = []
        P.barrier = _barrier
        self.cm05 = P.sbuf("cm05", [128, 1], F32)
        self.cm05_b = Buf("cm05")
        P.op("pool", lambda e: e.memset(self.cm05[:], -0.5), writes=[self.cm05_b])

    def st_view(self, off_bytes, shape, dtype):
        n = int(np.prod(shape))
        if dtype == BF16:
            a = self.st[:, off_bytes // 2: off_bytes // 2 + n]
            nb = n * 2
        else:
            a = self.st[:, off_bytes // 2: off_bytes // 2 + 2 * n].bitcast(F32)
            nb = n * 4
        assert off_bytes + nb <= self.ST_BYTES, (off_bytes, nb)
        if len(shape) == 2:
            v = a.rearrange("p (a b) -> p a b", a=shape[0])
        elif len(shape) == 3:
            v = a.rearrange("p (a b c) -> p a b c", a=shape[0], b=shape[1])
        else:
            v = a
        return v, off_bytes + nb

    def set_hextra(self, off_bytes, n):
        self.hextra = []
        for k in range(n):
            v, off_bytes = self.st_view(off_bytes, [1, D], F32)
            self.hextra.append((v[:, 0, :], Buf(f"hx{self.hcnt}_{k}")))

    def new_stat(self):
        i = self.stat_i % 64
        self.stat_i += 1
        return self.stat[:, i:i + 1], self.stat_b[i]

    def load_g(self, which, idx):
        tile, b = {"post": (self.gpost, self.gpost_b), "pre": (self.gpre, self.gpre_b), "pre2": (self.gpre2, self.gpre2_b)}[which]
        i, j = idx
        self.P.dma("sp", tile[:], self.dr["norm_g"][i, j, :].partition_broadcast(128), writes=[b])

    def rstd_of(self, src_ap, src_bufs, n):
        P = self.P
        ss, ss_b = self.new_stat()
        P.op("act", lambda e: e.activation(out=self.junk[:, 0:n], in_=src_ap, func=AF.Square, scale=float(n) ** -0.5, accum_out=ss),
             reads=list(src_bufs), writes=[self.junk_b, ss_b])
        r, r_b = self.new_stat()
        P.op("pool", lambda e: e.tensor_scalar(out=r, in0=ss, scalar1=EPS, scalar2=None, op0=ALU.add),
             reads=[ss_b], writes=[r_b])
        r2, r2_b = self.new_stat()
        P.op("pool", lambda e: e.tensor_tensor(out=r2, in0=r, in1=self.cm05[:], op=ALU.pow),
             reads=[r_b, self.cm05_b], writes=[r2_b])
        return r2, r2_b

    def h_load(self, t, src=None):
        P = self.P
        slots = [(self.hring[:, k, :], self.hr_b[k]) for k in range(3)] + list(self.hextra)
        hs, hb = slots[self.hcnt % len(slots)]
        self.hcnt += 1
        src = self.h_rd if src is None else src
        P.dma("sp", hs, src[t * 128:(t + 1) * 128, :], reads=[self.hD_b[t]], writes=[hb])
        return hs, hb

    def h_store(self, t, hs, hb, dst=None):
        dst = self.h_wr if dst is None else dst
        return self.P.dma("sp", dst[t * 128:(t + 1) * 128, :], hs, reads=[hb], writes=[self.hD_b[t]], chan=hb)

    def pre_norm(self, t, hs, hb, tbank=7):
        P = self.P
        r, r_b = self.rstd_of(hs, [hb], D)
        s = t % 2
        ab, ab_b = self.abf[:, s, :], self.abf_b[s]
        P.op("dve", lambda e: e.scalar_tensor_tensor(out=ab, in0=hs, scalar=r, in1=self.gpre[:], op0=ALU.mult, op1=ALU.mult),
             reads=[hb, r_b, self.gpre_b], writes=[ab_b])
        pst = self.bank[tbank][:].bitcast(BF16).rearrange("p (k n) -> p k n", k=KC)
        for kc in range(KC):
            P.op("pe", lambda e, kc=kc: e.transpose(out=pst[:, kc, :], in_=self.abf[:, s, kc * 128:(kc + 1) * 128], identity=self.ident[:]),
                 reads=[ab_b, self.ident_b], writes=[self.bank_b[tbank]])
        P.op("act", lambda e: e.copy(out=self.aT[:, :, t * 128:(t + 1) * 128], in_=pst),
             reads=[self.bank_b[tbank]], writes=[self.aT_b[t]])

    def boundary(self, t, src_ap, src_b, dst_ap, dst_b, do_pre=True, tbank=7):
        P = self.P
        r, r_b = self.rstd_of(src_ap, src_b, D)
        P.op("dve", lambda e: e.scalar_tensor_tensor(out=dst_ap, in0=src_ap, scalar=r, in1=self.gpost[:], op0=ALU.mult, op1=ALU.mult),
             reads=list(src_b) + [r_b, self.gpost_b], writes=[dst_b])
        hs, hb = self.h_load(t)
        P.op("pool", lambda e: e.tensor_tensor(out=hs, in0=hs, in1=dst_ap, op=ALU.add), reads=[dst_b], writes=[hb])
        tok = self.h_store(t, hs, hb)
        if do_pre:
            self.pre_norm(t, hs, hb, tbank=tbank)
        return tok

    def stage_barrier(self):
        self.P.barrier()
        self.hextra = []

    def first_norm(self):
        for t in range(NT):
            hs, hb = self.h_load(t)
            self.pre_norm(t, hs, hb)

    def aT_load(self):
        for t in range(NT):
            self.P.dma("sp", self.aT[:, :, t * 128:(t + 1) * 128], self.dr["aT_in"][:, :, t * 128:(t + 1) * 128], writes=[self.aT_b[t]])

    def aT_store(self):
        toks = []
        for t in range(NT):
            toks.append(self.P.dma("sp", self.dr["aT_out"][:, :, t * 128:(t + 1) * 128], self.aT[:, :, t * 128:(t + 1) * 128],
                                   reads=[self.aT_b[t]], chan=self.aT_b[t]))
        return toks

    def wload(self, dst_ap, src_ap, buf):
        return self.P.dma("pool", dst_ap, src_ap, writes=[buf])

    def ffn(self, i):
        P = self.P
        P.barrier()
        w_in = self.dr["ffn_w_in"][i]
        w_out = self.dr["ffn_w_out"][i]
        off = 0
        winA, winA_b = [], []
        for s in range(3):
            v, off = self.st_view(off, [KC, 512], BF16)
            winA.append(v)
            winA_b.append(Buf(f"f{i}winA{s}"))
        woutB, woutB_b = [], []
        for s in range(2):
            v, off = self.st_view(off, [4, D], BF16)
            woutB.append(v)
            woutB_b.append(Buf(f"f{i}woutB{s}"))
        hidT, hidT_b = [], []
        for s in range(2):
            v, off = self.st_view(off, [4, T], BF16)
            hidT.append(v)
            hidT_b.append([[Buf(f"f{i}hid{s}_{jl}_{tg}") for tg in range(4)] for jl in range(4)])
        sil, sil_b = [], []
        for s in range(2):
            v, off = self.st_view(off, [1, 512], F32)
            sil.append(v)
            sil_b.append(Buf(f"f{i}sil{s}"))
        nchunk = FFN_H // 128
        ngrp = (nchunk + 3) // 4
        unit = 0
        for pi in range(nchunk // 2):
            ws, wb = winA[pi % 3], winA_b[pi % 3]
            self.wload(ws[:, :, 0:256], w_in[:, pi * 256:(pi + 1) * 256].rearrange("(kc p) n -> p kc n", p=128), wb)
            self.wload(ws[:, :, 256:512], w_in[:, FFN_H + pi * 256:FFN_H + (pi + 1) * 256].rearrange("(kc p) n -> p kc n", p=128), wb)
            for jj in range(2):
                j = 2 * pi + jj
                grp, jl = j // 4, j % 4
                hs_, hs_b = hidT[grp % 2], hidT_b[grp % 2]
                for tg in range(4):
                    bg, bu = 2 * (unit % 2), 2 * (unit % 2) + 1
                    sl, slb = sil[unit % 2], sil_b[unit % 2]
                    unit += 1
                    for (bk, coff) in ((bg, jj * 128), (bu, 256 + jj * 128)):
                        for kc in range(KC):
                            P.op("pe", lambda e, bk=bk, coff=coff, kc=kc, tg=tg, ws=ws: e.matmul(
                                self.bank[bk][:], lhsT=ws[:, kc, coff:coff + 128], rhs=self.aT[:, kc, tg * 512:(tg + 1) * 512],
                                start=(kc == 0), stop=(kc == KC - 1)),
                                reads=[wb] + self.aT_b[tg * 4:(tg + 1) * 4], writes=[self.bank_b[bk]])
                    P.op("act", lambda e, bg=bg, sl=sl: e.activation(out=sl[:, 0, :], in_=self.bank[bg][:], func=AF.Silu),
                         reads=[self.bank_b[bg]], writes=[slb])
                    P.op("dve", lambda e, bu=bu, sl=sl, hs_=hs_, jl=jl, tg=tg: e.tensor_tensor(
                        out=hs_[:, jl, tg * 512:(tg + 1) * 512], in0=sl[:, 0, :], in1=self.bank[bu][:], op=ALU.mult),
                        reads=[slb, self.bank_b[bu]], writes=[hs_b[jl][tg]])
                if jl == 3 or j == nchunk - 1:
                    n = jl + 1
                    wo, wob = woutB[grp % 2], woutB_b[grp % 2]
                    self.wload(wo[:, 0:n, :], w_out[grp * 512:grp * 512 + n * 128, :].rearrange("(jl p) n -> p jl n", p=128), wob)
                    for t in range(NT):
                        for half in range(2):
                            bk = 4 + 2 * (t % 2) + half
                            for q in range(n):
                                P.op("pe", lambda e, bk=bk, q=q, t=t, half=half, hs_=hs_, wo=wo, n=n: e.matmul(
                                    self.bank[bk][:], lhsT=hs_[:, q, t * 128:(t + 1) * 128], rhs=wo[:, q, half * 512:(half + 1) * 512],
                                    start=(q == 0), stop=(q == n - 1)),
                                    reads=[wob, hs_b[q][t // 4]], writes=[self.bank_b[bk]])
                            dst = self.acc[:, t, half * 512:(half + 1) * 512]
                            if grp == 0:
                                P.op("act", lambda e, bk=bk, dst=dst: e.copy(out=dst, in_=self.bank[bk][:]),
                                     reads=[self.bank_b[bk]], writes=[self.acc_b[t]])
                            else:
                                P.op("dve", lambda e, bk=bk, dst=dst: e.tensor_tensor(out=dst, in0=dst, in1=self.bank[bk][:], op=ALU.add),
                                     reads=[self.bank_b[bk]], writes=[self.acc_b[t]])
        assert grp == ngrp - 1

    def ple(self, i, do_pre=True):
        P = self.P
        P.barrier()
        off = 0
        Wg, off = self.st_view(off, [KC, D], BF16)
        Wg_b = Buf(f"p{i}Wg")
        Wp, off = self.st_view(off, [2, D], BF16)
        Wp_b = Buf(f"p{i}Wp")
        pb, pb_b, pT, pT_b, sg, sg_b = [], [], [], [], [], []
        for s in range(2):
            v, off = self.st_view(off, [1, 256], BF16)
            pb.append(v)
            pb_b.append(Buf(f"p{i}pb{s}"))
            v, off = self.st_view(off, [2, 128], BF16)
            pT.append(v)
            pT_b.append(Buf(f"p{i}pT{s}"))
            v, off = self.st_view(off, [2, 512], F32)
            sg.append(v)
            sg_b.append(Buf(f"p{i}sg{s}"))
        self.wload(Wg, self.dr["ple_w_gate"][i].rearrange("(kc p) n -> p kc n", p=128), Wg_b)
        self.wload(Wp, self.dr["ple_w_proj"][i].rearrange("(kc p) n -> p kc n", p=128), Wp_b)
        toks = []
        for t in range(NT):
            s = t % 2
            self.wload(pb[s][:, 0, :], self.dr["p"][i, t * 128:(t + 1) * 128, :], pb_b[s])
            ptb = 6
            pst = self.bank[ptb][:].bitcast(BF16)[:, 0:256].rearrange("p (k n) -> p k n", k=2)
            for kc in range(2):
                P.op("pe", lambda e, kc=kc, s=s, pst=pst: e.transpose(out=pst[:, kc, :], in_=pb[s][:, 0, kc * 128:(kc + 1) * 128], identity=self.ident[:]),
                     reads=[pb_b[s], self.ident_b], writes=[self.bank_b[ptb]])
            P.op("dve", lambda e, s=s, pst=pst: e.tensor_copy(out=pT[s], in_=pst), reads=[self.bank_b[ptb]], writes=[pT_b[s]])
            for half in range(2):
                u = 2 * t + half
                bG, bE = u % 3, 3 + u % 3
                for kc in range(KC):
                    P.op("pe", lambda e, kc=kc, half=half, bG=bG, t=t: e.matmul(
                        self.bank[bG][:], lhsT=self.aT[:, kc, t * 128:(t + 1) * 128], rhs=Wg[:, kc, half * 512:(half + 1) * 512],
                        start=(kc == 0), stop=(kc == KC - 1)), reads=[Wg_b, self.aT_b[t]], writes=[self.bank_b[bG]])
                for kc in range(2):
                    P.op("pe", lambda e, kc=kc, half=half, bE=bE, s=s: e.matmul(
                        self.bank[bE][:], lhsT=pT[s][:, kc, :], rhs=Wp[:, kc, half * 512:(half + 1) * 512],
                        start=(kc == 0), stop=(kc == 1)), reads=[Wp_b, pT_b[s]], writes=[self.bank_b[bE]])
                P.op("act", lambda e, half=half, bG=bG, s=s: e.activation(out=sg[s][:, half, :], in_=self.bank[bG][:], func=AF.Sigmoid),
                     reads=[self.bank_b[bG]], writes=[sg_b[s]])
                P.op("dve", lambda e, half=half, bE=bE, s=s: e.tensor_tensor(out=sg[s][:, half, :], in0=sg[s][:, half, :], in1=self.bank[bE][:], op=ALU.mult),
                     reads=[self.bank_b[bE]], writes=[sg_b[s]])
            hs, hb = self.h_load(t)
            P.op("pool", lambda e, hs=hs, s=s: e.tensor_tensor(out=hs, in0=hs, in1=sg[s].rearrange("p a b -> p (a b)"), op=ALU.add),
                 reads=[sg_b[s]], writes=[hb])
            toks.append(self.h_store(t, hs, hb))
            if do_pre:
                self.pre_norm(t, hs, hb)
        return toks

    def ffn_stage(self, i):
        self.load_g("post", (i, 3))
        self.load_g("pre", (i, 4))
        self.ffn(i)
        toks = []
        for t in range(NT):
            a = self.acc[:, t, :]
            toks.append(self.boundary(t, a, [self.acc_b[t]], a, self.acc_b[t]))
        return toks

    def ple_stage(self, i, last):
        if not last:
            self.load_g("pre", (i + 1, 0))
        return self.ple(i, do_pre=not last)


def _ret_emit(self, j, pre):
    P = self.P
    P.barrier()
    dr = self.dr
    w_in = dr["ret_w_in"][j]
    w_out = dr["ret_w_out"][j]
    off = 0
    W, off = self.st_view(off, [KC, 1536], BF16)
    W_b = {k: Buf(f"r{j}{pre}W{k}") for k in "qkvg"}
    wo, off = self.st_view(off, [4, D], BF16)
    wo_b = Buf(f"r{j}{pre}wo")
    cs, cs_b = [], []
    for s in range(2):
        v, off = self.st_view(off, [2, 512], F32)
        cs.append(v)
        cs_b.append(Buf(f"r{j}{pre}cs{s}"))
    qT, off = self.st_view(off, [2, 512], BF16)
    kT, off = self.st_view(off, [2, 512], BF16)
    qdT, off = self.st_view(off, [2, 512], BF16)
    qT_b, kT_b, qdT_b = Buf("qT"), Buf("kT"), Buf("qdT")
    rt, off = self.st_view(off, [4, 512], F32)
    rt_b = [Buf(f"rt{x}") for x in range(4)]
    qdec, off = self.st_view(off, [1, 512], F32)
    qdec_b = Buf("qdec")
    v_sb, sg, sm, kd, gy, gyT = [], [], [], [], [], []
    v_b, sg_b, sm_b, kd_b, gy_b, gyT_b = [], [], [], [], [], []
    for s in range(2):
        a, off = self.st_view(off, [1, 512], BF16); v_sb.append(a); v_b.append(Buf(f"v{s}"))
        a, off = self.st_view(off, [1, 512], BF16); sg.append(a); sg_b.append(Buf(f"sg{s}"))
        a, off = self.st_view(off, [1, 128], BF16); sm.append(a); sm_b.append(Buf(f"sm{s}"))
        a, off = self.st_view(off, [1, 256], BF16); kd.append(a); kd_b.append(Buf(f"kd{s}"))
        a, off = self.st_view(off, [1, 512], BF16); gy.append(a); gy_b.append(Buf(f"gy{s}"))
        a, off = self.st_view(off, [4, 128], BF16); gyT.append(a); gyT_b.append(Buf(f"gyT{s}"))
    maskT, off = self.st_view(off, [RET_H, 128], F32)
    maskT_b = Buf("maskT")
    kdec, off = self.st_view(off, [1, RET_H], F32)
    kdec_b = Buf("kdec")
    coef, off = self.st_view(off, [1, NCORES * RET_H], F32)
    coef_b = Buf("coef")
    state, off = self.st_view(off, [2, 512], F32)
    state_b = Buf("state")
    sbf, off = self.st_view(off, [2, 512], BF16)
    sbf_b = Buf("sbf")
    P.dma("sp", maskT, dr["ret_maskT"].rearrange("h m n -> m h n"), writes=[maskT_b])
    P.dma("sp", kdec[:, 0, :], dr["ret_kdec"][:, :], writes=[kdec_b])
    P.dma("sp", coef[:, 0, :], dr["ret_coef"][:, :], writes=[coef_b])
    bk = self.bank
    bb = self.bank_b
    b6 = bk[6][:]
    sT_ps = b6[:, 0:128]
    sT_pb = Buf("b6a")
    kdT_ps = b6[:, 128:256].bitcast(BF16).rearrange("p (a b) -> p a b", a=2)
    kdT_pb = Buf("b6b")
    gyT_ps = b6[:, 256:512].bitcast(BF16).rearrange("p (a b) -> p a b", a=4)
    gyT_pb = Buf("b6c")
    gam = [1.0 - 2.0 ** (-5.0 - h) for h in range(RET_H)]

    for hd in range(RET_H):
        cd = gam[hd] ** 128
        cols = {"q": (hd * 256, 256, 0), "k": (1024 + hd * 256, 256, 256), "v": (2048 + hd * 512, 512, 512), "g": (4096 + hd * 512, 512, 1024)}
        for k in ("k", "v") if pre else ("q", "k", "v", "g"):
            c0, n, o = cols[k]
            self.wload(W[:, :, o:o + n], w_in[:, c0:c0 + n].rearrange("(kc p) n -> p kc n", p=128), W_b[k])
        if not pre:
            self.wload(wo, w_out[hd * 512:(hd + 1) * 512, :].rearrange("(q p) n -> p q n", p=128), wo_b)
            P.dma("sp", qdec[:, 0, :], dr["ret_qdec"][hd], writes=[qdec_b])
        if pre:
            P.op("pool", lambda e: e.memset(state, 0.0), writes=[state_b])
        else:
            tmp = rt[:, 0:2, :]
            for r in range(NCORES):
                P.dma("sp", tmp, dr["L_all"][r, hd].rearrange("(fc p) n -> p fc n", p=128), writes=[rt_b[0], rt_b[1]])
                cf = coef[:, 0, r * RET_H + hd:r * RET_H + hd + 1]
                if r == 0:
                    P.op("dve", lambda e, cf=cf: e.tensor_scalar(out=state, in0=tmp, scalar1=cf, scalar2=None, op0=ALU.mult),
                         reads=[rt_b[0], rt_b[1], coef_b], writes=[state_b])
                else:
                    P.op("dve", lambda e, cf=cf: e.scalar_tensor_tensor(out=state, in0=tmp, scalar=cf, in1=state, op0=ALU.mult, op1=ALU.add),
                         reads=[rt_b[0], rt_b[1], coef_b], writes=[state_b])
        P.op("act", lambda e: e.copy(out=sbf, in_=state), reads=[state_b], writes=[sbf_b])

        for tg in range(4):
            cst, cst_b = cs[tg % 2], cs_b[tg % 2]
            P.dma("sp", cst[:, 0, :], dr["rope_cos"][:, tg * 512:(tg + 1) * 512], writes=[cst_b])
            P.dma("sp", cst[:, 1, :], dr["rope_sin"][:, tg * 512:(tg + 1) * 512], writes=[cst_b])
            for which in ("k",) if pre else ("q", "k"):
                o = cols[which][2]
                b0 = 0 if which == "q" else 2
                for fc in range(2):
                    for kc in range(KC):
                        P.op("pe", lambda e, fc=fc, kc=kc, o=o, b0=b0, tg=tg: e.matmul(
                            bk[b0 + fc][:], lhsT=W[:, kc, o + fc * 128:o + (fc + 1) * 128], rhs=self.aT[:, kc, tg * 512:(tg + 1) * 512],
                            start=(kc == 0), stop=(kc == KC - 1)),
                            reads=[W_b[which]] + self.aT_b[tg * 4:(tg + 1) * 4], writes=[bb[b0 + fc]])
                dstT, dst_b = (qT, qT_b) if which == "q" else (kT, kT_b)
                t1, t2 = bk[b0][:], bk[b0 + 1][:]
                P.op("dve", lambda e, t1=t1, cst=cst: e.tensor_tensor(out=rt[:, 0, :], in0=t1, in1=cst[:, 0, :], op=ALU.mult),
                     reads=[bb[b0], cst_b], writes=[rt_b[0]])
                P.op("dve", lambda e, t2=t2, cst=cst: e.tensor_tensor(out=rt[:, 1, :], in0=t2, in1=cst[:, 1, :], op=ALU.mult),
                     reads=[bb[b0 + 1], cst_b], writes=[rt_b[1]])
                P.op("dve", lambda e, t2=t2, cst=cst: e.tensor_tensor(out=rt[:, 2, :], in0=t2, in1=cst[:, 0, :], op=ALU.mult),
                     reads=[bb[b0 + 1], cst_b], writes=[rt_b[2]])
                P.op("dve", lambda e, t1=t1, cst=cst: e.tensor_tensor(out=rt[:, 3, :], in0=t1, in1=cst[:, 1, :], op=ALU.mult),
                     reads=[bb[b0], cst_b], writes=[rt_b[3]])
                P.op("pool", lambda e, dstT=dstT: e.tensor_tensor(out=dstT[:, 0, :], in0=rt[:, 0, :], in1=rt[:, 1, :], op=ALU.subtract),
                     reads=[rt_b[0], rt_b[1]], writes=[dst_b])
                P.op("pool", lambda e, dstT=dstT: e.tensor_tensor(out=dstT[:, 1, :], in0=rt[:, 2, :], in1=rt[:, 3, :], op=ALU.add),
                     reads=[rt_b[2], rt_b[3]], writes=[dst_b])
                if which == "q":
                    for fc in range(2):
                        P.op("pool", lambda e, fc=fc: e.tensor_tensor(out=qdT[:, fc, :], in0=qT[:, fc, :], in1=qdec[:, 0, :], op=ALU.mult),
                             reads=[qT_b, qdec_b], writes=[qdT_b])
            for cl in range(4):
                c = tg * 4 + cl
                s = c % 2
                tsl = slice(cl * 128, (cl + 1) * 128)
                csl = slice(c * 128, (c + 1) * 128)
                for kc in range(KC):
                    P.op("pe", lambda e, kc=kc, csl=csl: e.matmul(bk[4][:], lhsT=self.aT[:, kc, csl], rhs=W[:, kc, 512:1024],
                                                               start=(kc == 0), stop=(kc == KC - 1)),
                         reads=[W_b["v"], self.aT_b[c]], writes=[bb[4]])
                P.op("act", lambda e, s=s: e.copy(out=v_sb[s][:, 0, :], in_=bk[4][:]), reads=[bb[4]], writes=[v_b[s]])
                if not pre:
                    for kc in range(KC):
                        P.op("pe", lambda e, kc=kc, csl=csl: e.matmul(bk[5][:], lhsT=self.aT[:, kc, csl], rhs=W[:, kc, 1024:1536],
                                                                   start=(kc == 0), stop=(kc == KC - 1)),
                             reads=[W_b["g"], self.aT_b[c]], writes=[bb[5]])
                    P.op("act", lambda e, s=s: e.activation(out=sg[s][:, 0, :], in_=bk[5][:], func=AF.Silu), reads=[bb[5]], writes=[sg_b[s]])
                    for fc in range(2):
                        P.op("pe", lambda e, fc=fc, tsl=tsl: e.matmul(sT_ps, lhsT=kT[:, fc, tsl], rhs=qT[:, fc, tsl], start=(fc == 0), stop=(fc == 1)),
                             reads=[kT_b, qT_b], writes=[sT_pb])
                    P.op("dve", lambda e, s=s, hd=hd: e.tensor_tensor(out=sm[s][:, 0, :], in0=sT_ps, in1=maskT[:, hd, :], op=ALU.mult),
                         reads=[sT_pb, maskT_b], writes=[sm_b[s]])
                for fc in range(2):
                    P.op("pe", lambda e, fc=fc, tsl=tsl: e.transpose(out=kdT_ps[:, fc, :], in_=kT[:, fc, tsl], identity=self.ident[:]),
                         reads=[kT_b, self.ident_b], writes=[kdT_pb])
                P.op("dve", lambda e, s=s, hd=hd: e.tensor_scalar(out=kd[s][:, 0, :], in0=kdT_ps.rearrange("p a b -> p (a b)"),
                                                                 scalar1=kdec[:, 0, hd:hd + 1], scalar2=None, op0=ALU.mult),
                     reads=[kdT_pb, kdec_b], writes=[kd_b[s]])
                if not pre:
                    P.op("pe", lambda e, s=s: e.matmul(bk[7][:], lhsT=sm[s][:, 0, :], rhs=v_sb[s][:, 0, :], start=True, stop=False),
                         reads=[sm_b[s], v_b[s]], writes=[bb[7]])
                    for fc in range(2):
                        P.op("pe", lambda e, fc=fc, tsl=tsl: e.matmul(bk[7][:], lhsT=qdT[:, fc, tsl], rhs=sbf[:, fc, :], start=False, stop=(fc == 1)),
                             reads=[qdT_b, sbf_b], writes=[bb[7]])
                for fc in range(2):
                    P.op("pe", lambda e, fc=fc, s=s: e.matmul(bk[fc][:], lhsT=kd[s][:, 0, fc * 128:(fc + 1) * 128], rhs=v_sb[s][:, 0, :], start=True, stop=True),
                         reads=[kd_b[s], v_b[s]], writes=[bb[fc]])
                for fc in range(2):
                    P.op("dve", lambda e, fc=fc, cd=cd: e.scalar_tensor_tensor(out=state[:, fc, :], in0=state[:, fc, :], scalar=float(cd), in1=bk[fc][:],
                                                                           op0=ALU.mult, op1=ALU.add),
                         reads=[bb[fc]], writes=[state_b])
                P.op("act", lambda e: e.copy(out=sbf, in_=state), reads=[state_b], writes=[sbf_b])
                if not pre:
                    r, r_b = self.rstd_of(bk[7][:], [bb[7]], 512)
                    P.op("dve", lambda e, s=s, r=r: e.scalar_tensor_tensor(out=gy[s][:, 0, :], in0=bk[7][:], scalar=r, in1=sg[s][:, 0, :],
                                                                         op0=ALU.mult, op1=ALU.mult),
                         reads=[bb[7], r_b, sg_b[s]], writes=[gy_b[s]])
                    for q in range(4):
                        P.op("pe", lambda e, q=q, s=s: e.transpose(out=gyT_ps[:, q, :], in_=gy[s][:, 0, q * 128:(q + 1) * 128], identity=self.ident[:]),
                             reads=[gy_b[s], self.ident_b], writes=[gyT_pb])
                    P.op("act", lambda e, s=s: e.copy(out=gyT[s], in_=gyT_ps), reads=[gyT_pb], writes=[gyT_b[s]])
                    for half in range(2):
                        for q in range(4):
                            P.op("pe", lambda e, q=q, s=s, half=half: e.matmul(bk[2 + half][:], lhsT=gyT[s][:, q, :], rhs=wo[:, q, half * 512:(half + 1) * 512],
                                                                              start=(q == 0), stop=(q == 3)),
                                 reads=[gyT_b[s], wo_b], writes=[bb[2 + half]])
                        dst = self.acc[:, c, half * 512:(half + 1) * 512]
                        if hd == 0:
                            P.op("act", lambda e, dst=dst, half=half: e.copy(out=dst, in_=bk[2 + half][:]), reads=[bb[2 + half]], writes=[self.acc_b[c]])
                        else:
                            P.op("dve", lambda e, dst=dst, half=half: e.tensor_tensor(out=dst, in0=dst, in1=bk[2 + half][:], op=ALU.add),
                                 reads=[bb[2 + half]], writes=[self.acc_b[c]])
        if pre:
            P.dma("sp", dr["L_out"][hd].rearrange("(fc p) n -> p fc n", p=128), state, reads=[state_b], chan=state_b)
    if pre:
        return [(state_b.dsem, state_b.dcount)]
    return []


Builder.ret_emit = _ret_emit


def _ret_emit2(self, j, pre, boundary=False):
    P = self.P
    P.barrier()
    dr = self.dr
    w_in = dr["ret_w_in"][j]
    w_out = dr["ret_w_out"][j]
    off = 0
    W, off = self.st_view(off, [KC, 1536], BF16)
    W_b = {k: Buf(f"r{j}{pre}W{k}") for k in "qkvg"}
    wo, off = self.st_view(off, [4, D], BF16)
    wo_b = Buf(f"r{j}{pre}wo")
    cs, cs_b = [], []
    for s in range(2):
        v, off = self.st_view(off, [2, 512], F32)
        cs.append(v)
        cs_b.append(Buf(f"r{j}{pre}cs{s}"))
    qT, off = self.st_view(off, [2, 512], BF16)
    kT, off = self.st_view(off, [2, 512], BF16)
    qdT, off = self.st_view(off, [2, 512], BF16)
    qT_b, kT_b, qdT_b = Buf("qT"), Buf("kT"), Buf("qdT")
    rt, off = self.st_view(off, [4, 512], F32)
    rt_b = [Buf(f"rt{x}") for x in range(4)]
    qdec, off = self.st_view(off, [1, 512], F32)
    qdec_b = Buf("qdec")
    v_sb, sg, sm, kd, gy, gyT = [], [], [], [], [], []
    v_b, sg_b, sm_b, kd_b, gy_b, gyT_b = [], [], [], [], [], []
    for s in range(2):
        a, off = self.st_view(off, [1, 512], BF16); v_sb.append(a); v_b.append(Buf(f"v{s}"))
        a, off = self.st_view(off, [1, 512], BF16); sg.append(a); sg_b.append(Buf(f"sg{s}"))
        a, off = self.st_view(off, [1, 128], BF16); sm.append(a); sm_b.append(Buf(f"sm{s}"))
        a, off = self.st_view(off, [1, 256], BF16); kd.append(a); kd_b.append(Buf(f"kd{s}"))
        a, off = self.st_view(off, [1, 512], BF16); gy.append(a); gy_b.append(Buf(f"gy{s}"))
        a, off = self.st_view(off, [4, 128], BF16); gyT.append(a); gyT_b.append(Buf(f"gyT{s}"))
    maskT, off = self.st_view(off, [RET_H, 128], F32)
    maskT_b = Buf("maskT")
    kdec, off = self.st_view(off, [1, RET_H], F32)
    kdec_b = Buf("kdec")
    coef, off = self.st_view(off, [1, L_RANKS * RET_H], F32)
    coef_b = Buf("coef")
    state, off = self.st_view(off, [2, 512], F32)
    state_b = Buf("state")
    sbf, off = self.st_view(off, [2, 512], BF16)
    sbf_b = Buf("sbf")
    self.set_hextra(off, max(0, (self.ST_BYTES - off) // 4096))
    P.dma("sp", maskT, dr["ret_maskT"].rearrange("h m n -> m h n"), writes=[maskT_b])
    P.dma("sp", kdec[:, 0, :], dr["ret_kdec"][:, :], writes=[kdec_b])
    P.dma("sp", coef[:, 0, :], dr["ret_coef"][:, :], writes=[coef_b])
    bk = self.bank
    bb = self.bank_b
    b6 = bk[6][:]
    sT_ps = b6[:, 0:128]
    sT_pb = bb[6]
    kdT_ps = b6[:, 128:256].bitcast(BF16).rearrange("p (a b) -> p a b", a=2)
    kdT_pb = bb[6]
    gyT_ps = bk[7][:, 256:512].bitcast(BF16).rearrange("p (a b) -> p a b", a=4)
    gyT_pb = bb[7]
    gam = [1.0 - 2.0 ** (-5.0 - h) for h in range(RET_H)]
    btoks = []

    for hd in range(RET_H):
        cd = gam[hd] ** 128
        cols = {"q": (hd * 256, 256, 0), "k": (1024 + hd * 256, 256, 256), "v": (2048 + hd * 512, 512, 512), "g": (4096 + hd * 512, 512, 1024)}
        for k in ("k", "v") if pre else ("q", "k", "v", "g"):
            c0, n, o = cols[k]
            self.wload(W[:, :, o:o + n], w_in[:, c0:c0 + n].rearrange("(kc p) n -> p kc n", p=128), W_b[k])
        if not pre:
            self.wload(wo, w_out[hd * 512:(hd + 1) * 512, :].rearrange("(q p) n -> p q n", p=128), wo_b)
            P.dma("sp", qdec[:, 0, :], dr["ret_qdec"][hd], writes=[qdec_b])
        if pre:
            P.op("pool", lambda e: e.memset(state, 0.0), writes=[state_b])
        else:
            tmp = rt[:, 0:2, :]
            for r in range(L_RANKS):
                P.dma("sp", tmp, dr["L_all"][r, hd].rearrange("(fc p) n -> p fc n", p=128), writes=[rt_b[0], rt_b[1]])
                cf = coef[:, 0, r * RET_H + hd:r * RET_H + hd + 1]
                if r == 0:
                    P.op("dve", lambda e, cf=cf: e.tensor_scalar(out=state, in0=tmp, scalar1=cf, scalar2=None, op0=ALU.mult),
                         reads=[rt_b[0], rt_b[1], coef_b], writes=[state_b])
                else:
                    P.op("dve", lambda e, cf=cf: e.scalar_tensor_tensor(out=state, in0=tmp, scalar=cf, in1=state, op0=ALU.mult, op1=ALU.add),
                         reads=[rt_b[0], rt_b[1], coef_b], writes=[state_b])
        P.op("act", lambda e: e.copy(out=sbf, in_=state), reads=[state_b], writes=[sbf_b])

        def qk_proj(tg, hd=hd, cols=cols):
            cst, cst_b = cs[tg % 2], cs_b[tg % 2]
            P.dma("sp", cst[:, 0, :], dr["rope_cos"][:, tg * 512:(tg + 1) * 512], writes=[cst_b])
            P.dma("sp", cst[:, 1, :], dr["rope_sin"][:, tg * 512:(tg + 1) * 512], writes=[cst_b])
            for which in ("k",) if pre else ("q", "k"):
                o = cols[which][2]
                b0 = 0 if which == "q" else 2
                for fc in range(2):
                    for kc in range(KC):
                        P.op("pe", lambda e, fc=fc, kc=kc, o=o, b0=b0, tg=tg: e.matmul(
                            bk[b0 + fc][:], lhsT=W[:, kc, o + fc * 128:o + (fc + 1) * 128], rhs=self.aT[:, kc, tg * 512:(tg + 1) * 512],
                            start=(kc == 0), stop=(kc == KC - 1)),
                            reads=[W_b[which]] + self.aT_b[tg * 4:(tg + 1) * 4], writes=[bb[b0 + fc]])
                dstT, dst_b = (qT, qT_b) if which == "q" else (kT, kT_b)
                t1, t2 = bk[b0][:], bk[b0 + 1][:]
                P.op("dve", lambda e, t1=t1, cst=cst: e.tensor_tensor(out=rt[:, 0, :], in0=t1, in1=cst[:, 0, :], op=ALU.mult),
                     reads=[bb[b0], cst_b], writes=[rt_b[0]])
                P.op("dve", lambda e, t2=t2, cst=cst: e.tensor_tensor(out=rt[:, 1, :], in0=t2, in1=cst[:, 1, :], op=ALU.mult),
                     reads=[bb[b0 + 1], cst_b], writes=[rt_b[1]])
                P.op("dve", lambda e, t2=t2, cst=cst: e.tensor_tensor(out=rt[:, 2, :], in0=t2, in1=cst[:, 0, :], op=ALU.mult),
                     reads=[bb[b0 + 1], cst_b], writes=[rt_b[2]])
                P.op("dve", lambda e, t1=t1, cst=cst: e.tensor_tensor(out=rt[:, 3, :], in0=t1, in1=cst[:, 1, :], op=ALU.mult),
                     reads=[bb[b0], cst_b], writes=[rt_b[3]])
                P.op("pool", lambda e, dstT=dstT: e.tensor_tensor(out=dstT[:, 0, :], in0=rt[:, 0, :], in1=rt[:, 1, :], op=ALU.subtract),
                     reads=[rt_b[0], rt_b[1]], writes=[dst_b])
                P.op("pool", lambda e, dstT=dstT: e.tensor_tensor(out=dstT[:, 1, :], in0=rt[:, 2, :], in1=rt[:, 3, :], op=ALU.add),
                     reads=[rt_b[2], rt_b[3]], writes=[dst_b])
                if which == "q":
                    for fc in range(2):
                        P.op("pool", lambda e, fc=fc: e.tensor_tensor(out=qdT[:, fc, :], in0=qT[:, fc, :], in1=qdec[:, 0, :], op=ALU.mult),
                             reads=[qT_b, qdec_b], writes=[qdT_b])
        def ph0(c, hd=hd, cd=cd, qk_proj=qk_proj):
            if True:
                tg, cl = c // 4, c % 4
                if cl == 0:
                    qk_proj(tg)
                s = c % 2
                tsl = slice(cl * 128, (cl + 1) * 128)
                csl = slice(c * 128, (c + 1) * 128)
                for kc in range(KC):
                    P.op("pe", lambda e, kc=kc, csl=csl: e.matmul(bk[4][:], lhsT=self.aT[:, kc, csl], rhs=W[:, kc, 512:1024],
                                                               start=(kc == 0), stop=(kc == KC - 1)),
                         reads=[W_b["v"], self.aT_b[c]], writes=[bb[4]])
                P.op("act", lambda e, s=s: e.copy(out=v_sb[s][:, 0, :], in_=bk[4][:]), reads=[bb[4]], writes=[v_b[s]])
                if not pre:
                    for kc in range(KC):
                        P.op("pe", lambda e, kc=kc, csl=csl: e.matmul(bk[5][:], lhsT=self.aT[:, kc, csl], rhs=W[:, kc, 1024:1536],
                                                                   start=(kc == 0), stop=(kc == KC - 1)),
                             reads=[W_b["g"], self.aT_b[c]], writes=[bb[5]])
                    P.op("act", lambda e, s=s: e.activation(out=sg[s][:, 0, :], in_=bk[5][:], func=AF.Silu), reads=[bb[5]], writes=[sg_b[s]])
                    for fc in range(2):
                        P.op("pe", lambda e, fc=fc, tsl=tsl: e.matmul(sT_ps, lhsT=kT[:, fc, tsl], rhs=qT[:, fc, tsl], start=(fc == 0), stop=(fc == 1)),
                             reads=[kT_b, qT_b], writes=[sT_pb])
                    P.op("dve", lambda e, s=s, hd=hd: e.tensor_tensor(out=sm[s][:, 0, :], in0=sT_ps, in1=maskT[:, hd, :], op=ALU.mult),
                         reads=[sT_pb, maskT_b], writes=[sm_b[s]])
                for fc in range(2):
                    P.op("pe", lambda e, fc=fc, tsl=tsl: e.transpose(out=kdT_ps[:, fc, :], in_=kT[:, fc, tsl], identity=self.ident[:]),
                         reads=[kT_b, self.ident_b], writes=[kdT_pb])
                P.op("dve", lambda e, s=s, hd=hd: e.tensor_scalar(out=kd[s][:, 0, :], in0=kdT_ps.rearrange("p a b -> p (a b)"),
                                                                 scalar1=kdec[:, 0, hd:hd + 1], scalar2=None, op0=ALU.mult),
                     reads=[kdT_pb, kdec_b], writes=[kd_b[s]])
        def ph1(c, hd=hd, cd=cd):
            if True:
                tg, cl = c // 4, c % 4
                s = c % 2
                tsl = slice(cl * 128, (cl + 1) * 128)
                csl = slice(c * 128, (c + 1) * 128)
                if not pre:
                    P.op("pe", lambda e, s=s: e.matmul(bk[7][:], lhsT=sm[s][:, 0, :], rhs=v_sb[s][:, 0, :], start=True, stop=False),
                         reads=[sm_b[s], v_b[s]], writes=[bb[7]])
                    for fc in range(2):
                        P.op("pe", lambda e, fc=fc, tsl=tsl: e.matmul(bk[7][:], lhsT=qdT[:, fc, tsl], rhs=sbf[:, fc, :], start=False, stop=(fc == 1)),
                             reads=[qdT_b, sbf_b], writes=[bb[7]])
                for fc in range(2):
                    P.op("pe", lambda e, fc=fc, s=s: e.matmul(bk[fc][:], lhsT=kd[s][:, 0, fc * 128:(fc + 1) * 128], rhs=v_sb[s][:, 0, :], start=True, stop=True),
                         reads=[kd_b[s], v_b[s]], writes=[bb[fc]])
                for fc in range(2):
                    P.op("dve", lambda e, fc=fc, cd=cd: e.scalar_tensor_tensor(out=state[:, fc, :], in0=state[:, fc, :], scalar=float(cd), in1=bk[fc][:],
                                                                           op0=ALU.mult, op1=ALU.add),
                         reads=[bb[fc]], writes=[state_b])
                if not pre:
                    P.op("act", lambda e: e.copy(out=sbf, in_=state), reads=[state_b], writes=[sbf_b])
                if not pre:
                    r, r_b = self.rstd_of(bk[7][:], [bb[7]], 512)
                    P.op("dve", lambda e, s=s, r=r: e.scalar_tensor_tensor(out=gy[s][:, 0, :], in0=bk[7][:], scalar=r, in1=sg[s][:, 0, :],
                                                                         op0=ALU.mult, op1=ALU.mult),
                         reads=[bb[7], r_b, sg_b[s]], writes=[gy_b[s]])
        def ph2(c, hd=hd, cd=cd):
            if True:
                tg, cl = c // 4, c % 4
                s = c % 2
                if not pre:
                    for q in range(4):
                        P.op("pe", lambda e, q=q, s=s: e.transpose(out=gyT_ps[:, q, :], in_=gy[s][:, 0, q * 128:(q + 1) * 128], identity=self.ident[:]),
                             reads=[gy_b[s], self.ident_b], writes=[gyT_pb])
                    P.op("act", lambda e, s=s: e.copy(out=gyT[s], in_=gyT_ps), reads=[gyT_pb], writes=[gyT_b[s]])
                    for half in range(2):
                        for q in range(4):
                            P.op("pe", lambda e, q=q, s=s, half=half: e.matmul(bk[2 + half][:], lhsT=gyT[s][:, q, :], rhs=wo[:, q, half * 512:(half + 1) * 512],
                                                                              start=(q == 0), stop=(q == 3)),
                                 reads=[gyT_b[s], wo_b], writes=[bb[2 + half]])
                        dst = self.acc[:, c, half * 512:(half + 1) * 512]
                        if hd == 0:
                            P.op("act", lambda e, dst=dst, half=half: e.copy(out=dst, in_=bk[2 + half][:]), reads=[bb[2 + half]], writes=[self.acc_b[c]])
                        else:
                            P.op("dve", lambda e, dst=dst, half=half: e.tensor_tensor(out=dst, in0=dst, in1=bk[2 + half][:], op=ALU.add),
                                 reads=[bb[2 + half]], writes=[self.acc_b[c]])
        phases = [ph0, ph1] if pre else [ph0, ph1, ph2]
        if boundary and (not pre) and hd == RET_H - 1:
            phases = phases + self.boundary_phases_f(lambda t: (self.acc[:, t, :], [self.acc_b[t]]), lambda t: (self.acc[:, t, :], self.acc_b[t]),
                                                   toks=btoks, tbanks=(5,), merge_b=True)
        pipeline(NT, phases)
        if pre:
            P.dma("sp", dr["L_out"][hd].rearrange("(fc p) n -> p fc n", p=128), state, reads=[state_b], chan=state_b)
    if pre:
        return [(state_b.dsem, state_b.dcount)]
    return btoks


Builder.ret_emit = _ret_emit2


def _mixer_boundaries(self, i):
    toks = []
    for t in range(NT):
        a = self.acc[:, t, :]
        toks.append(self.boundary(t, a, [self.acc_b[t]], a, self.acc_b[t]))
    return toks


Builder.mixer_boundaries = _mixer_boundaries


def ret_tables():
    gam = np.array([1.0 - 2.0 ** (-5.0 - h) for h in range(RET_H)], np.float64)
    idx = np.arange(128, dtype=np.float64)
    diff = idx[None, :] - idx[:, None]
    maskT = np.where(diff >= 0, gam[:, None, None] ** np.maximum(diff, 0.0), 0.0) / 16.0
    qdec = gam[:, None] ** (np.tile(idx, 4) + 1.0)[None, :]
    qdec = np.broadcast_to(qdec[:, None, :], (RET_H, 128, 512))
    kdec = (gam[None, :] ** (127.0 - idx[:, None])) / 16.0
    return (np.ascontiguousarray(maskT, dtype=np.float32), np.ascontiguousarray(qdec, dtype=np.float32),
            np.ascontiguousarray(kdec, dtype=np.float32))


def ret_coef(core):
    gam = np.array([1.0 - 2.0 ** (-5.0 - h) for h in range(RET_H)], np.float64)
    b, s = core // 4, core % 4
    c = np.zeros((L_RANKS, RET_H), np.float64)
    for rs in range(L_RANKS):
        if rs < s:
            c[rs] = gam ** (float(T) * (s - 1 - rs))
    return np.ascontiguousarray(np.broadcast_to(c.reshape(1, -1), (128, L_RANKS * RET_H)), dtype=np.float32)


def rope_tables(core):
    s = core % 4
    inv_freq = (np.float32(10000.0) ** (-(np.arange(0, RET_DK, 2, dtype=np.float32) / np.float32(RET_DK)))).astype(np.float32)
    pos = (np.arange(T, dtype=np.float32) + np.float32(s * T)).astype(np.float32)
    ang = (inv_freq[:, None] * pos[None, :]).astype(np.float32)
    return np.cos(ang).astype(np.float32), np.sin(ang).astype(np.float32)


def _gmlp_stage(self, i, j):
    P = self.P
    dr = self.dr
    P.barrier()
    bigb = self.big[:].bitcast(BF16)
    Wv = bigb[:, 0:16384].rearrange("p (k n) -> p k n", k=KC)
    Wu = bigb[:, 16384:32768].rearrange("p (k n) -> p k n", k=KC)
    Wv_b, Wu_b = Buf("gWv"), Buf("gWu")
    off = 0
    Wo, off = self.st_view(off, [16, D], BF16)
    Wo_b = Buf("gWo")
    wsT, off = self.st_view(off, [GM_G, 128], BF16)
    wsT_b = Buf("gwsT")
    gtab, off = self.st_view(off, [1, GM_W], F32)
    btab, off = self.st_view(off, [1, GM_W], F32)
    gtab_b, btab_b = Buf("ggtab"), Buf("gbtab")
    bsT, off = self.st_view(off, [1, GM_G], F32)
    bsT_b = Buf("gbsT")
    cmask, off = self.st_view(off, [1, 128], F32)
    cmask_b = Buf("gcmask")
    off_vf = off
    vf, off = self.st_view(off, [1, GM_W], F32)
    vf_b = Buf("gvf")
    vn, off = self.st_view(off, [1, GM_W], BF16)
    vn_b = Buf("gvn")
    ssb, off = self.st_view(off, [1, GM_W], F32)
    ssb_b = Buf("gssb")
    us, off = self.st_view(off, [1, GM_W], BF16)
    us_b = Buf("gus")
    usT, off = self.st_view(off, [16, 128], BF16)
    usT_b = Buf("gusT")
    gu, gu_b = [], []
    for s in range(2):
        a, off = self.st_view(off, [1, 512], F32)
        gu.append(a)
        gu_b.append(Buf(f"ggu{s}"))
    wsl, off2 = self.st_view(off_vf, [GM_G, 128], BF16)
    wsl_b = vf_b
    w_in = dr["gm_w_in"][j]
    for cb in range(4):
        self.P.dma("pool", Wv[:, :, cb * 512:(cb + 1) * 512], w_in[:, GM_W + cb * 512:GM_W + (cb + 1) * 512].rearrange("(kc p) n -> p kc n", p=128),
                   writes=[Wv_b] + self.acc_b[0:8])
    for cb in range(4):
        self.P.dma("pool", Wu[:, :, cb * 512:(cb + 1) * 512], w_in[:, cb * 512:(cb + 1) * 512].rearrange("(kc p) n -> p kc n", p=128),
                   writes=[Wu_b] + self.acc_b[8:16])
    for q in range(4):
        self.wload(Wo[:, q * 4:(q + 1) * 4, :], dr["gm_w_out"][j][q * 512:(q + 1) * 512, :].rearrange("(kc p) n -> p kc n", p=128), Wo_b)
    self.wload(wsl, dr["gm_w_s"][j].rearrange("g n m -> n g m"), wsl_b)
    P.dma("sp", gtab[:, 0, :], dr["gm_ln_g"][j, :].partition_broadcast(128), writes=[gtab_b])
    P.dma("sp", btab[:, 0, :], dr["gm_ln_b"][j, :].partition_broadcast(128), writes=[btab_b])
    P.dma("sp", bsT[:, 0, :], dr["gm_b_s"][j].rearrange("g n -> n g"), writes=[bsT_b], allow_slow_non_contiguous=True)
    P.dma("sp", cmask[:, 0, :], dr["causalT"][:, :], writes=[cmask_b])
    bk, bb = self.bank, self.bank_b
    tps = bk[0][:].bitcast(BF16)[:, 0:1024].rearrange("p (g n) -> p g n", g=GM_G)
    for g in range(GM_G):
        P.op("pe", lambda e, g=g: e.transpose(out=tps[:, g, :], in_=wsl[:, g, :], identity=self.ident[:]),
             reads=[wsl_b, self.ident_b], writes=[bb[0]])
    for g in range(GM_G):
        P.op("dve", lambda e, g=g: e.tensor_tensor(out=wsT[:, g, :], in0=tps[:, g, :], in1=cmask[:, 0, :], op=ALU.mult),
             reads=[bb[0], cmask_b], writes=[wsT_b])
    toks = []
    for t in range(NT):
        tsl = slice(t * 128, (t + 1) * 128)
        sv, sv_b = [], []
        for cb in range(4):
            for kc in range(KC):
                P.op("pe", lambda e, cb=cb, kc=kc, tsl=tsl: e.matmul(bk[cb][:], lhsT=self.aT[:, kc, tsl], rhs=Wv[:, kc, cb * 512:(cb + 1) * 512],
                                                                     start=(kc == 0), stop=(kc == KC - 1)),
                     reads=[Wv_b, self.aT_b[t]], writes=[bb[cb]])
            a, a_b = self.new_stat()
            sv.append(a)
            sv_b.append(a_b)
            P.op("act", lambda e, cb=cb, a=a: e.activation(out=vf[:, 0, cb * 512:(cb + 1) * 512], in_=bk[cb][:], func=AF.Gelu, accum_out=a),
                 reads=[bb[cb]], writes=[vf_b, a_b])
        sq, sq_b = [], []
        for hh in range(2):
            a, a_b = self.new_stat()
            sq.append(a)
            sq_b.append(a_b)
            P.op("act", lambda e, hh=hh, a=a: e.activation(out=self.junk[:, :], in_=vf[:, 0, hh * 1024:(hh + 1) * 1024], func=AF.Square, accum_out=a),
                 reads=[vf_b], writes=[self.junk_b, a_b])
        m1, m1_b = self.new_stat()
        P.op("pool", lambda e, sv=sv, m1=m1: e.tensor_tensor(out=m1, in0=sv[0], in1=sv[1], op=ALU.add), reads=[sv_b[0], sv_b[1]], writes=[m1_b])
        m2, m2_b = self.new_stat()
        P.op("pool", lambda e, sv=sv, m2=m2: e.tensor_tensor(out=m2, in0=sv[2], in1=sv[3], op=ALU.add), reads=[sv_b[2], sv_b[3]], writes=[m2_b])
        mu, mu_b = self.new_stat()
        P.op("pool", lambda e, m1=m1, m2=m2, mu=mu: e.tensor_tensor(out=mu, in0=m1, in1=m2, op=ALU.add), reads=[m1_b, m2_b], writes=[mu_b])
        P.op("pool", lambda e, mu=mu: e.tensor_scalar(out=mu, in0=mu, scalar1=1.0 / GM_W, scalar2=None, op0=ALU.mult), reads=[], writes=[mu_b])
        e2, e2_b = self.new_stat()
        P.op("pool", lambda e, sq=sq, e2=e2: e.tensor_tensor(out=e2, in0=sq[0], in1=sq[1], op=ALU.add), reads=[sq_b[0], sq_b[1]], writes=[e2_b])
        mm, mm_b = self.new_stat()
        P.op("pool", lambda e, mu=mu, mm=mm: e.tensor_tensor(out=mm, in0=mu, in1=mu, op=ALU.mult), reads=[mu_b], writes=[mm_b])
        var, var_b = self.new_stat()
        P.op("pool", lambda e, e2=e2, var=var: e.tensor_scalar(out=var, in0=e2, scalar1=1.0 / GM_W, scalar2=EPS, op0=ALU.mult, op1=ALU.add),
             reads=[e2_b], writes=[var_b])
        P.op("pool", lambda e, var=var, mm=mm: e.tensor_tensor(out=var, in0=var, in1=mm, op=ALU.subtract), reads=[mm_b], writes=[var_b])
        rs, rs_b = self.new_stat()
        P.op("pool", lambda e, var=var, rs=rs: e.tensor_tensor(out=rs, in0=var, in1=self.cm05[:], op=ALU.pow), reads=[var_b, self.cm05_b], writes=[rs_b])
        P.op("dve", lambda e, mu=mu, rs=rs: e.tensor_scalar(out=vf[:, 0, :], in0=vf[:, 0, :], scalar1=mu, scalar2=rs, op0=ALU.subtract, op1=ALU.mult),
             reads=[mu_b, rs_b], writes=[vf_b])
        P.op("pool", lambda e: e.tensor_tensor(out=vf[:, 0, :], in0=vf[:, 0, :], in1=gtab[:, 0, :], op=ALU.mult), reads=[gtab_b], writes=[vf_b])
        P.op("dve", lambda e: e.tensor_tensor(out=vn[:, 0, :], in0=vf[:, 0, :], in1=btab[:, 0, :], op=ALU.add), reads=[vf_b, btab_b], writes=[vn_b])
        for g in range(GM_G):
            bkx = 4 + g // 2
            P.op("pe", lambda e, g=g, bkx=bkx: e.matmul(bk[bkx][:, (g % 2) * 256:(g % 2 + 1) * 256], lhsT=wsT[:, g, :], rhs=vn[:, 0, g * 256:(g + 1) * 256],
                                                        start=True, stop=True),
                 reads=[wsT_b, vn_b], writes=[bb[bkx]])
        for g in range(GM_G):
            bkx = 4 + g // 2
            P.op("dve", lambda e, g=g, bkx=bkx: e.tensor_scalar(out=ssb[:, 0, g * 256:(g + 1) * 256], in0=bk[bkx][:, (g % 2) * 256:(g % 2 + 1) * 256],
                                                                scalar1=bsT[:, 0, g:g + 1], scalar2=None, op0=ALU.add),
                 reads=[bb[bkx], bsT_b], writes=[ssb_b])
        for cb in range(4):
            for kc in range(KC):
                P.op("pe", lambda e, cb=cb, kc=kc, tsl=tsl: e.matmul(bk[cb][:], lhsT=self.aT[:, kc, tsl], rhs=Wu[:, kc, cb * 512:(cb + 1) * 512],
                                                                     start=(kc == 0), stop=(kc == KC - 1)),
                     reads=[Wu_b, self.aT_b[t]], writes=[bb[cb]])
            s = cb % 2
            P.op("act", lambda e, cb=cb, s=s: e.activation(out=gu[s][:, 0, :], in_=bk[cb][:], func=AF.Gelu), reads=[bb[cb]], writes=[gu_b[s]])
            P.op("dve", lambda e, cb=cb, s=s: e.tensor_tensor(out=us[:, 0, cb * 512:(cb + 1) * 512], in0=gu[s][:, 0, :], in1=ssb[:, 0, cb * 512:(cb + 1) * 512], op=ALU.mult),
                 reads=[gu_b[s], ssb_b], writes=[us_b])
        for half in range(2):
            tp = bk[4 + half][:].bitcast(BF16).rearrange("p (k n) -> p k n", k=8)
            for q in range(8):
                kc = half * 8 + q
                P.op("pe", lambda e, tp=tp, q=q, kc=kc: e.transpose(out=tp[:, q, :], in_=us[:, 0, kc * 128:(kc + 1) * 128], identity=self.ident[:]),
                     reads=[us_b, self.ident_b], writes=[bb[4 + half]])
            P.op("act", lambda e, tp=tp, half=half: e.copy(out=usT[:, half * 8:(half + 1) * 8, :], in_=tp), reads=[bb[4 + half]], writes=[usT_b])
        for half in range(2):
            for kc in range(16):
                P.op("pe", lambda e, half=half, kc=kc: e.matmul(bk[6 + half][:], lhsT=usT[:, kc, :], rhs=Wo[:, kc, half * 512:(half + 1) * 512],
                                                                start=(kc == 0), stop=(kc == 15)),
                     reads=[usT_b, Wo_b], writes=[bb[6 + half]])
        src = self.psum_all[:, 6:8, :].rearrange("p a b -> p (a b)")
        toks.append(self.boundary(t, src, [bb[6], bb[7]], vf[:, 0, 0:D], vf_b, tbank=5))
    for t in range(NT):
        src_b = Wv_b if t < 8 else Wu_b
        self.acc_b[t].r = list(self.acc_b[t].r) + list(src_b.r) + ([src_b.w] if src_b.w else [])
    return toks


Builder.gmlp_stage = _gmlp_stage


def att_tables(rel_bias):
    p = np.arange(128)[:, None, None]
    kb = np.arange(2)[None, :, None]
    q = np.arange(128)[None, None, :]
    dist = 128 * (1 - kb) + q - p
    mask01 = ((dist >= 0) & (dist <= 128)).astype(np.float32).reshape(128, 256)
    dcl = np.maximum(dist, 0)
    out = np.zeros((3, ATT_H, 128, 256), np.float32)
    for g, (w, r) in enumerate(DIL):
        dd = (dcl * r).astype(np.int32)
        df = np.maximum(dd, 1).astype(np.float32)
        large = 16 + (np.log(df / np.float32(16)) / np.float32(math.log(2048 / 16)) * np.float32(16)).astype(np.int32)
        large = np.minimum(large, 31)
        bucket = np.where(dd < 16, dd, large).reshape(128, 256)
        tab = rel_bias[:, g * ATT_H:(g + 1) * ATT_H]
        out[g] = np.transpose(tab[bucket], (2, 0, 1))
    return out, mask01


def att_prevmask(core, mask01):
    m = mask01.copy()
    if core % 4 == 0:
        m[:, 0:128] = 0.0
    return m


def _att_kv_pass(self, j):
    P = self.P
    dr = self.dr
    P.barrier()
    w_in = dr["attn_w_in"][j]
    off = 0
    ring, ring_b = [], []
    for s in range(3):
        v, off = self.st_view(off, [KC, 512], BF16)
        ring.append(v)
        ring_b.append(Buf(f"akv_w{s}"))
    kst, kst_b = [], []
    for s in range(2):
        v, off = self.st_view(off, [1, T], BF16)
        kst.append(v)
        kst_b.append(Buf(f"akv_k{s}"))
    vst, vst_b = [], []
    for s in range(2):
        v, off = self.st_view(off, [1, D], BF16)
        vst.append(v)
        vst_b.append(Buf(f"akv_v{s}"))
    bk, bb = self.bank, self.bank_b
    self.KT_b = [[Buf(f"KT{g}_{fb}") for fb in range(8)] for g in range(3)]
    self.V_b = [[Buf(f"V{g}_{blk}") for blk in range(NT)] for g in range(3)]
    ri = 0
    ui = 0
    for g, (w, r) in enumerate(DIL):
        nbc = 16 // r
        for hf in range(2):
            ws, wb = ring[ri % 3], ring_b[ri % 3]
            ri += 1
            c0 = g * 3072 + 1024 + hf * 512
            self.wload(ws, w_in[:, c0:c0 + 512].rearrange("(kc p) n -> p kc n", p=128), wb)
            for fq in range(4):
                fb = hf * 4 + fq
                ks, ksb = kst[fb % 2], kst_b[fb % 2]
                for tg in range(4):
                    b = ui % 4
                    ui += 1
                    for kc in range(KC):
                        P.op("pe", lambda e, b=b, kc=kc, fq=fq, tg=tg, ws=ws: e.matmul(
                            bk[b][:], lhsT=ws[:, kc, fq * 128:(fq + 1) * 128], rhs=self.aT[:, kc, tg * 512:(tg + 1) * 512],
                            start=(kc == 0), stop=(kc == KC - 1)), reads=[wb] + self.aT_b[tg * 4:(tg + 1) * 4], writes=[bb[b]])
                    jn = 512 // r
                    dst = ks[:, 0, :].rearrange("p (c j) -> p c j", c=r)[:, :, tg * jn:(tg + 1) * jn]
                    src = bk[b][:].rearrange("p (j c) -> p c j", c=r)
                    eng = "act" if (ui % 2 == 0) else "dve"
                    if eng == "act":
                        P.op("act", lambda e, dst=dst, src=src: e.copy(out=dst, in_=src), reads=[bb[b]], writes=[ksb])
                    else:
                        P.op("dve", lambda e, dst=dst, src=src: e.tensor_copy(out=dst, in_=src), reads=[bb[b]], writes=[ksb])
                P.dma("sp", dr["KT_own"][g, fb * 128:(fb + 1) * 128, :], ks[:, 0, :], reads=[ksb], writes=[self.KT_b[g][fb]], chan=ksb)
        wsl = []
        for hf in range(2):
            ws, wb = ring[ri % 3], ring_b[ri % 3]
            ri += 1
            c0 = g * 3072 + 2048 + hf * 512
            self.wload(ws, w_in[:, c0:c0 + 512].rearrange("(kc p) n -> p kc n", p=128), wb)
            wsl.append((ws, wb))
        for blk in range(NT):
            c, n = blk // nbc, blk % nbc
            start = n * 128 * r + c
            vs, vsb = vst[blk % 2], vst_b[blk % 2]
            toks_t = sorted(set((start + i * r) // 128 for i in range(128)))
            for hf in range(2):
                ws, wb = wsl[hf]
                b = ui % 4
                ui += 1
                for kc in range(KC):
                    P.op("pe", lambda e, b=b, kc=kc, ws=ws, start=start, r=r: e.matmul(
                        bk[b][:], lhsT=self.aT[:, kc, start:start + 127 * r + 1:r], rhs=ws[:, kc, :],
                        start=(kc == 0), stop=(kc == KC - 1)), reads=[wb] + [self.aT_b[t] for t in toks_t], writes=[bb[b]])
                if hf == 0:
                    P.op("act", lambda e, b=b, vs=vs: e.copy(out=vs[:, 0, 0:512], in_=bk[b][:]), reads=[bb[b]], writes=[vsb])
                else:
                    P.op("dve", lambda e, b=b, vs=vs: e.tensor_copy(out=vs[:, 0, 512:1024], in_=bk[b][:]), reads=[bb[b]], writes=[vsb])
            P.dma("sp", dr["V_own"][g, blk * 128:(blk + 1) * 128, :], vs[:, 0, :], reads=[vsb], writes=[self.V_b[g][blk]], chan=vsb)


Builder.att_kv_pass = _att_kv_pass


def _att_main(self, i, j):
    P = self.P
    dr = self.dr
    P.barrier()
    w_in = dr["attn_w_in"][j]
    off = 0
    Wq, off = self.st_view(off, [KC, 384], BF16)
    Wq_b = Buf("aWq")
    QT, off = self.st_view(off, [1, T], BF16)
    QT_b = Buf("aQT")
    KT, off = self.st_view(off, [1, T], BF16)
    KT_sb = Buf("aKT")
    KH, off = self.st_view(off, [1, T], BF16)
    KH_b = Buf("aKH")
    Vt, off = self.st_view(off, [NT, 128], BF16)
    Vt_b = Buf("aVt")
    VH, off = self.st_view(off, [NT, 128], BF16)
    VH_b = Buf("aVH")
    Nacc, off = self.st_view(off, [2, T], F32)
    Dacc, off = self.st_view(off, [2, T], F32)
    ND_b = [Buf("aND0"), Buf("aND1")]
    bst, off = self.st_view(off, [2, 256], F32)
    bst_b = Buf("abst")
    Et, off = self.st_view(off, [2, 256], F32)
    Et0, off = self.st_view(off, [2, 256], F32)
    Et_b = Buf("aEt")
    m01, off = self.st_view(off, [1, 256], F32)
    m01p, off = self.st_view(off, [1, 256], F32)
    m_b = Buf("am01")
    ones, off = self.st_view(off, [1, 64], BF16)
    ones_b = Buf("aones")
    ex, ex_b, pt, pt_b = [], [], [], []
    for s in range(2):
        a, off = self.st_view(off, [1, 256], F32); ex.append(a); ex_b.append(Buf(f"aex{s}"))
        a, off = self.st_view(off, [2, 128], BF16); pt.append(a); pt_b.append(Buf(f"apt{s}"))
    rc, off = self.st_view(off, [1, T], F32)
    rc_b = Buf("arc")
    bk, bb = self.bank, self.bank_b
    if getattr(self, "KT_b", None) is None:
        self.KT_b = [[Buf(f"KT{g}_{fb}") for fb in range(8)] for g in range(3)]
        self.V_b = [[Buf(f"V{g}_{blk}") for blk in range(NT)] for g in range(3)]
    oT = self.big[:].bitcast(BF16).rearrange("p (h t) -> p h t", h=ATT_H)
    oT_b = [Buf(f"aoT{h}") for h in range(ATT_H)]
    P.dma("sp", m01[:, 0, :], dr["att_mask01"][:, :], writes=[m_b])
    P.dma("sp", m01p[:, 0, :], dr["att_prevmask"][:, :], writes=[m_b])
    P.op("pool", lambda e: e.memset(ones, 1.0), writes=[ones_b])
    sT_ps = [self.psum_all[:, 2 + s // 2, (s % 2) * 256:(s % 2 + 1) * 256].rearrange("p (a b) -> p a b", a=2) for s in range(4)]
    sT_pb = [Buf(f"asT{s}") for s in range(4)]
    nd_ps = [self.psum_all[0:64, 4, s * 256:(s + 1) * 256].rearrange("p (a b) -> p a b", a=2) for s in range(2)]
    nd_pb = [Buf(f"and{s}") for s in range(2)]
    ui = 0
    si = 0
    for hp in range(8):
        for g in range(3):
            c0 = g * 3072 + hp * 128
            self.wload(Wq[:, :, g * 128:(g + 1) * 128], w_in[:, c0:c0 + 128].rearrange("(kc p) n -> p kc n", p=128), Wq_b)
        for g, (w, r) in enumerate(DIL):
            nbc = 16 // r
            P.dma("sp", KT[:, 0, :], dr["KT_own"][g, hp * 128:(hp + 1) * 128, :], reads=self.KT_b[g], writes=[KT_sb])
            for c in range(r):
                p0 = c * (T // r) + (nbc - 1) * 128
                P.dma("sp", KH[:, 0, c * 128:(c + 1) * 128], dr["KT_prev"][g, hp * 128:(hp + 1) * 128, p0:p0 + 128], writes=[KH_b])
            P.dma("sp", Vt, dr["V_own"][g, :, hp * 128:(hp + 1) * 128].rearrange("(b p) f -> p b f", p=128), reads=self.V_b[g], writes=[Vt_b])
            for c in range(r):
                p0 = c * (T // r) + (nbc - 1) * 128
                P.dma("sp", VH[:, c, :], dr["V_prev"][g, p0:p0 + 128, hp * 128:(hp + 1) * 128], writes=[VH_b])
            P.dma("sp", bst, dr["att_bias"][g, 2 * hp:2 * hp + 2].rearrange("h p n -> p h n"), writes=[bst_b])
            P.op("act", lambda e: e.activation(out=bst, in_=bst, func=AF.Exp), reads=[], writes=[bst_b])
            for hh in range(2):
                P.op("dve", lambda e, hh=hh: e.tensor_tensor(out=Et[:, hh, :], in0=bst[:, hh, :], in1=m01[:, 0, :], op=ALU.mult),
                     reads=[bst_b, m_b], writes=[Et_b])
                P.op("dve", lambda e, hh=hh: e.tensor_tensor(out=Et0[:, hh, :], in0=bst[:, hh, :], in1=m01p[:, 0, :], op=ALU.mult),
                     reads=[bst_b, m_b], writes=[Et_b])
            for tg in range(4):
                b = ui % 2
                ui += 1
                for kc in range(KC):
                    P.op("pe", lambda e, b=b, kc=kc, g=g, tg=tg: e.matmul(
                        bk[b][:], lhsT=Wq[:, kc, g * 128:(g + 1) * 128], rhs=self.aT[:, kc, tg * 512:(tg + 1) * 512],
                        start=(kc == 0), stop=(kc == KC - 1)), reads=[Wq_b] + self.aT_b[tg * 4:(tg + 1) * 4], writes=[bb[b]])
                jn = 512 // r
                dst = QT[:, 0, :].rearrange("p (c j) -> p c j", c=r)[:, :, tg * jn:(tg + 1) * jn]
                src = bk[b][:].rearrange("p (j c) -> p c j", c=r)
                P.op("act", lambda e, dst=dst, src=src: e.copy(out=dst, in_=src), reads=[bb[b]], writes=[QT_b])
            for blk in range(NT):
                c, n = blk // nbc, blk % nbc
                qs = slice(blk * 128, (blk + 1) * 128)
                first = (n == 0)
                for hh in range(2):
                    ps = slice(hh * 64, (hh + 1) * 64)
                    s4 = si % 4
                    s2 = si % 2
                    si += 1
                    kprev = KH[ps, 0, c * 128:(c + 1) * 128] if first else KT[ps, 0, (blk - 1) * 128:blk * 128]
                    P.op("pe", lambda e, s4=s4, kprev=kprev, ps=ps, qs=qs: e.matmul(sT_ps[s4][:, 0, :], lhsT=kprev, rhs=QT[ps, 0, qs], start=True, stop=True),
                         reads=[KH_b if first else KT_sb, QT_b], writes=[sT_pb[s4]])
                    P.op("pe", lambda e, s4=s4, ps=ps, qs=qs: e.matmul(sT_ps[s4][:, 1, :], lhsT=KT[ps, 0, qs], rhs=QT[ps, 0, qs], start=True, stop=True),
                         reads=[KT_sb, QT_b], writes=[sT_pb[s4]])
                    P.op("act", lambda e, s4=s4, s2=s2: e.activation(out=ex[s2][:, 0, :], in_=sT_ps[s4].rearrange("p a b -> p (a b)"), func=AF.Exp, scale=0.125),
                         reads=[sT_pb[s4]], writes=[ex_b[s2]])
                    tab = Et0 if first else Et
                    P.op("dve", lambda e, s2=s2, tab=tab, hh=hh: e.tensor_tensor(out=pt[s2].rearrange("p a b -> p (a b)"), in0=ex[s2][:, 0, :], in1=tab[:, hh, :], op=ALU.mult),
                         reads=[ex_b[s2], Et_b], writes=[pt_b[s2]])
                    vprev = VH[:, c, ps] if first else Vt[:, blk - 1, ps]
                    P.op("pe", lambda e, s2=s2, vprev=vprev: e.matmul(nd_ps[s2][:, 0, :], lhsT=vprev, rhs=pt[s2][:, 0, :], start=True, stop=False),
                         reads=[VH_b if first else Vt_b, pt_b[s2]], writes=[nd_pb[s2]])
                    P.op("pe", lambda e, s2=s2, blk=blk, ps=ps: e.matmul(nd_ps[s2][:, 0, :], lhsT=Vt[:, blk, ps], rhs=pt[s2][:, 1, :], start=False, stop=True),
                         reads=[Vt_b, pt_b[s2]], writes=[nd_pb[s2]])
                    P.op("pe", lambda e, s2=s2: e.matmul(nd_ps[s2][:, 1, :], lhsT=ones[:, 0, :], rhs=pt[s2][:, 0, :], start=True, stop=False),
                         reads=[ones_b, pt_b[s2]], writes=[nd_pb[s2]])
                    P.op("pe", lambda e, s2=s2: e.matmul(nd_ps[s2][:, 1, :], lhsT=ones[:, 0, :], rhs=pt[s2][:, 1, :], start=False, stop=True),
                         reads=[ones_b, pt_b[s2]], writes=[nd_pb[s2]])
                    t0 = n * 128 * r + c
                    tsl = slice(t0, t0 + 127 * r + 1, r)
                    if g == 0:
                        P.op("act", lambda e, s2=s2, hh=hh, tsl=tsl: e.copy(out=Nacc[0:64, hh, tsl], in_=nd_ps[s2][:, 0, :]), reads=[nd_pb[s2]], writes=[ND_b[hh]])
                        P.op("dve", lambda e, s2=s2, hh=hh, tsl=tsl: e.tensor_copy(out=Dacc[0:64, hh, tsl], in_=nd_ps[s2][:, 1, :]), reads=[nd_pb[s2]], writes=[ND_b[hh]])
                    else:
                        P.op("dve", lambda e, s2=s2, hh=hh, tsl=tsl: e.tensor_tensor(out=Nacc[0:64, hh, tsl], in0=Nacc[0:64, hh, tsl], in1=nd_ps[s2][:, 0, :], op=ALU.add),
                             reads=[nd_pb[s2]], writes=[ND_b[hh]])
                        P.op("dve", lambda e, s2=s2, hh=hh, tsl=tsl: e.tensor_tensor(out=Dacc[0:64, hh, tsl], in0=Dacc[0:64, hh, tsl], in1=nd_ps[s2][:, 1, :], op=ALU.add),
                             reads=[nd_pb[s2]], writes=[ND_b[hh]])
        for hh in range(2):
            h = 2 * hp + hh
            P.op("dve", lambda e, hh=hh: e.reciprocal(out=rc[0:64, 0, :], in_=Dacc[0:64, hh, :]), reads=[ND_b[hh]], writes=[rc_b])
            P.op("pool", lambda e, hh=hh, h=h: e.tensor_tensor(out=oT[0:64, h, :], in0=Nacc[0:64, hh, :], in1=rc[0:64, 0, :], op=ALU.mult),
                 reads=[ND_b[hh], rc_b], writes=[oT_b[h]] + self.acc_b)
    P.barrier()
    off = 0
    Wo, off = self.st_view(off, [ATT_H, D], BF16)
    Wo_b = Buf("aWo")
    nrm, off = self.st_view(off, [2, D], F32)
    nrm_b = [Buf("anrm0"), Buf("anrm1")]
    self.wload(Wo[0:64], dr["attn_w_out"][j].rearrange("(h p) n -> p h n", p=64), Wo_b)
    toks = []
    for t in range(NT):
        tsl = slice(t * 128, (t + 1) * 128)
        for half in range(2):
            for h in range(ATT_H):
                P.op("pe", lambda e, h=h, half=half, tsl=tsl: e.matmul(bk[6 + half][:], lhsT=oT[0:64, h, tsl], rhs=Wo[0:64, h, half * 512:(half + 1) * 512],
                                                                        start=(h == 0), stop=(h == ATT_H - 1)),
                     reads=[oT_b[h], Wo_b], writes=[bb[6 + half]])
        src = self.psum_all[:, 6:8, :].rearrange("p a b -> p (a b)")
        toks.append(self.boundary(t, src, [bb[6], bb[7]], nrm[:, t % 2, :], nrm_b[t % 2], tbank=5))
    for t in range(NT):
        for h in range(ATT_H):
            self.acc_b[t].r = list(self.acc_b[t].r) + list(oT_b[h].r)
    return toks


Builder.att_main = _att_main


W_SHAPES = {
    "norm_g": [DEPTH, 5, D], "ret_w_in": [2, D, 6144], "ret_w_out": [2, 2048, D],
    "attn_w_in": [1, D, 9216], "attn_w_out": [1, D, D],
    "gm_w_in": [1, D, 4096], "gm_ln_g": [1, 2048], "gm_ln_b": [1, 2048], "gm_w_s": [1, 8, 128, 128], "gm_b_s": [1, 8, 128],
    "gm_w_out": [1, 2048, D], "ffn_w_in": [DEPTH, D, 2 * FFN_H], "ffn_w_out": [DEPTH, FFN_H, D],
    "ple_w_proj": [DEPTH, PLE_DIM, D], "ple_w_gate": [DEPTH, D, D],
}
T_SHAPES = {
    "ident": [128, 128], "causalT": [128, 128],
    "ret_maskT": [RET_H, 128, 128], "ret_qdec": [RET_H, 128, 512], "ret_kdec": [128, RET_H], "ret_coef": [128, L_RANKS * RET_H],
    "rope_cos": [128, T], "rope_sin": [128, T],
    "att_bias": [3, ATT_H, 128, 256], "att_mask01": [128, 256], "att_prevmask": [128, 256],
}
RET_IN = ["ret_w_in", "ret_w_out", "ret_maskT", "ret_qdec", "ret_kdec", "ret_coef", "rope_cos", "rope_sin"]
FFN_IN = ["ffn_w_in", "ffn_w_out", "ple_w_proj", "ple_w_gate", "p"]
ATT_IN = ["attn_w_in", "attn_w_out", "att_bias", "att_mask01", "att_prevmask"]
GM_IN = ["gm_w_in", "gm_ln_g", "gm_ln_b", "gm_w_s", "gm_b_s", "gm_w_out", "causalT"]
SEG_IN = {
    1: ["hin", "norm_g", "ident"] + RET_IN,
    2: ["hin", "norm_g", "ident", "L_all"] + RET_IN + FFN_IN + ["attn_w_in"],
    3: ["hin", "norm_g", "ident", "KT_own", "V_own", "KT_prev", "V_prev"] + ATT_IN + FFN_IN + GM_IN + RET_IN,
    4: ["hin", "norm_g", "ident", "L_all"] + RET_IN + FFN_IN,
}
SEG_OUT = {1: ["L_out", "aT_out"], 2: ["hout", "KT_own", "V_own", "aT_out"], 3: ["hout", "L_out", "aT_out"], 4: ["hout"]}
for _s in (2, 3, 4):
    SEG_IN[_s] = SEG_IN[_s] + ["aT_in"]
IO_SHAPES = {
    "hin": ([T, D], F32), "hout": ([T, D], F32), "p": ([DEPTH, T, PLE_DIM], F32),
    "L_all": ([L_RANKS, RET_H, RET_DK, RET_DV], F32), "L_out": ([RET_H, RET_DK, RET_DV], F32),
    "KT_own": ([3, D, T], BF16), "V_own": ([3, T, D], BF16), "KT_prev": ([3, D, T], BF16), "V_prev": ([3, T, D], BF16),
    "aT_in": ([128, KC, T], BF16), "aT_out": ([128, KC, T], BF16),
}


def build_segment(seg):
    nc = bass.Bass("TRN2", target_bir_lowering=False)
    dr = {}
    for name in dict.fromkeys(SEG_IN[seg]):
        if name in SEG_OUT[seg]:
            continue
        if name in W_SHAPES:
            shape, dt = W_SHAPES[name], F32
        elif name in T_SHAPES:
            shape, dt = T_SHAPES[name], F32
        else:
            shape, dt = IO_SHAPES[name]
        dr[name] = nc.dram_tensor(name, list(shape), dt, kind="ExternalInput").ap()
    for name in SEG_OUT[seg]:
        shape, dt = IO_SHAPES[name]
        dr[name] = nc.dram_tensor(name, list(shape), dt, kind="ExternalOutput").ap()
    hD = nc.dram_tensor("hD", [T, D], F32, kind="Internal").ap()
    B = Builder(nc, dr)
    P = B.P
    fin = []
    if seg == 1:
        B.h_rd = dr["hin"]
        B.load_g("pre", (0, 0))
        B.first_norm()
        fin += B.aT_store()
        fin += B.ret_emit(0, True)
    elif seg == 2:
        B.h_rd, B.h_wr = dr["hin"], hD
        B.aT_load()
        B.load_g("post", (0, 1))
        B.load_g("pre", (0, 2))
        B.ret_emit(0, False, boundary=True)
        B.h_rd = hD
        B.ffn_stage(0)
        B.h_wr = dr["hout"]
        fin += B.ple_stage(0, last=False)
        fin += B.aT_store()
        B.att_kv_pass(0)
        for g in range(3):
            for b in B.KT_b[g] + B.V_b[g]:
                fin.append(b.w)
    elif seg == 3:
        B.h_rd, B.h_wr = dr["hin"], hD
        B.aT_load()
        B.load_g("post", (1, 1))
        B.load_g("pre", (1, 2))
        B.att_main(1, 0)
        B.h_rd = hD
        B.ffn_stage(1)
        B.ple_stage(1, last=False)
        B.load_g("post", (2, 1))
        B.load_g("pre", (2, 2))
        B.gmlp_stage(2, 0)
        B.ffn_stage(2)
        B.h_wr = dr["hout"]
        fin += B.ple_stage(2, last=False)
        fin += B.aT_store()
        fin += B.ret_emit(1, True)
    elif seg == 4:
        B.h_rd, B.h_wr = dr["hin"], hD
        B.aT_load()
        B.load_g("post", (3, 1))
        B.load_g("pre", (3, 2))
        B.ret_emit(1, False, boundary=True)
        B.h_rd = hD
        B.ffn_stage(3)
        B.h_wr = dr["hout"]
        fin += B.ple_stage(3, last=True)
    P.finish_on("sp", [t for t in fin if t is not None])
    P.emit()
    return nc


_PROG_CACHE = {}


def _prog(seg):
    if seg not in _PROG_CACHE:
        _PROG_CACHE[seg] = build_segment(seg)
    return _PROG_CACHE[seg]


def _run(seg, maps):
    nc = build_segment(seg)
    res = run_bass_kernel_spmd(nc, maps, core_ids=list(range(NCORES)))
    return res.results


def kernel(x, p, norm_g, ret_w_in, ret_w_out, attn_w_in, attn_w_out, rel_bias,
           gm_w_in, gm_ln_g, gm_ln_b, gm_w_s, gm_b_s, gm_w_out,
           ffn_w_in, ffn_w_out, ple_w_proj, ple_w_gate):
    import ml_dtypes
    f32 = lambda a: np.ascontiguousarray(np.asarray(a), dtype=np.float32)
    wts = {"norm_g": f32(norm_g), "ret_w_in": f32(ret_w_in), "ret_w_out": f32(ret_w_out), "attn_w_in": f32(attn_w_in),
           "attn_w_out": f32(attn_w_out), "gm_w_in": f32(gm_w_in), "gm_ln_g": f32(gm_ln_g), "gm_ln_b": f32(gm_ln_b),
           "gm_w_s": f32(gm_w_s), "gm_b_s": f32(gm_b_s), "gm_w_out": f32(gm_w_out), "ffn_w_in": f32(ffn_w_in),
           "ffn_w_out": f32(ffn_w_out), "ple_w_proj": f32(ple_w_proj), "ple_w_gate": f32(ple_w_gate)}
    x = f32(x)
    p = f32(p)
    maskT, qdec, kdec = ret_tables()
    bias_full, mask01 = att_tables(f32(rel_bias))
    idx = np.arange(128)
    consts = {"ident": np.eye(128, dtype=np.float32), "causalT": (idx[None, :] >= idx[:, None]).astype(np.float32),
              "ret_maskT": maskT, "ret_qdec": qdec, "ret_kdec": kdec, "att_bias": bias_full, "att_mask01": mask01}
    per_core = []
    for c in range(NCORES):
        b, s = c // 4, c % 4
        cos, sin = rope_tables(c)
        per_core.append({"ret_coef": ret_coef(c), "rope_cos": cos, "rope_sin": sin, "att_prevmask": att_prevmask(c, mask01),
                         "p": np.ascontiguousarray(p[:, b, s * T:(s + 1) * T, :])})

    def maps(seg, extra):
        out = []
        for c in range(NCORES):
            m = {}
            for name in dict.fromkeys(SEG_IN[seg]):
                if name in SEG_OUT[seg]:
                    continue
                if name in wts:
                    m[name] = wts[name]
                elif name in consts:
                    m[name] = consts[name]
                elif name in per_core[c]:
                    m[name] = per_core[c][name]
                else:
                    m[name] = extra[c][name]
            out.append(m)
        return out

    h = [np.ascontiguousarray(x[c // 4, (c % 4) * T:(c % 4 + 1) * T, :]) for c in range(NCORES)]
    r1 = _run(1, maps(1, [{"hin": h[c]} for c in range(NCORES)]))
    L_all = [np.ascontiguousarray(np.stack([r1[(c // 4) * 4 + rs]["L_out"] for rs in range(L_RANKS)], 0)) for c in range(NCORES)]
    r2 = _run(2, maps(2, [{"hin": h[c], "L_all": L_all[c], "aT_in": r1[c]["aT_out"]} for c in range(NCORES)]))
    h = [r2[c]["hout"] for c in range(NCORES)]
    zk = np.zeros((3, D, T), ml_dtypes.bfloat16)
    zv = np.zeros((3, T, D), ml_dtypes.bfloat16)
    ex3 = []
    for c in range(NCORES):
        prev = c - 1 if c % 4 > 0 else None
        ex3.append({"hin": h[c], "aT_in": r2[c]["aT_out"], "KT_own": r2[c]["KT_own"], "V_own": r2[c]["V_own"],
                    "KT_prev": r2[prev]["KT_own"] if prev is not None else zk,
                    "V_prev": r2[prev]["V_own"] if prev is not None else zv})
    r3 = _run(3, maps(3, ex3))
    h = [r3[c]["hout"] for c in range(NCORES)]
    L_all = [np.ascontiguousarray(np.stack([r3[(c // 4) * 4 + rs]["L_out"] for rs in range(L_RANKS)], 0)) for c in range(NCORES)]
    r4 = _run(4, maps(4, [{"hin": h[c], "L_all": L_all[c], "aT_in": r3[c]["aT_out"]} for c in range(NCORES)]))
    out = np.empty((BATCH, SEQ, D), np.float32)
    for c in range(NCORES):
        out[c // 4, (c % 4) * T:(c % 4 + 1) * T, :] = r4[c]["hout"]
    return out


def pipeline(n, phases):
    nph = len(phases)
    if _SEQ_DEBUG:
        for t in range(n):
            for ph in phases:
                ph(t)
        return
    for step in range(n + nph - 1):
        for ph in reversed(range(nph)):
            t = step - ph
            if 0 <= t < n:
                phases[ph](t)


def _pre_norm_phases(self, get_h, tbanks=(7,), gain=None):
    P = self.P
    stt = {}
    gt, gt_b = gain if gain is not None else (self.gpre, self.gpre_b)
    slots = (0, 1) if gain is None else (2, 3)

    def pa(t):
        hs, hb = get_h(t)
        r, r_b = self.rstd_of(hs, [hb], D)
        s = slots[t % 2] if self.abf_n == 4 else t % 2
        ab, ab_b = self.abf[:, s, :], self.abf_b[s]
        P.op("dve", lambda e: e.scalar_tensor_tensor(out=ab, in0=hs, scalar=r, in1=gt[:], op0=ALU.mult, op1=ALU.mult),
             reads=[hb, r_b, gt_b], writes=[ab_b])
        stt[t] = (s, ab_b)

    def pb(t):
        s, ab_b = stt.pop(t)
        tbank = tbanks[t % len(tbanks)]
        pst = self.bank[tbank][:].bitcast(BF16).rearrange("p (k n) -> p k n", k=KC)
        for kc in range(KC):
            P.op("pe", lambda e, kc=kc: e.transpose(out=pst[:, kc, :], in_=self.abf[:, s, kc * 128:(kc + 1) * 128], identity=self.ident[:]),
                 reads=[ab_b, self.ident_b], writes=[self.bank_b[tbank]])
        P.op("act", lambda e: e.copy(out=self.aT[:, :, t * 128:(t + 1) * 128], in_=pst),
             reads=[self.bank_b[tbank]], writes=[self.aT_b[t]])

    return [pa, pb]


def _boundary_phases(self, src_fn, dst_fn, do_pre=True, tbanks=(7,), toks=None):
    P = self.P
    st1 = {}
    st2 = {}

    def p0(t):
        src_ap, src_b = src_fn(t)
        dst_ap, dst_b = dst_fn(t)
        r, r_b = self.rstd_of(src_ap, src_b, D)
        P.op("dve", lambda e: e.scalar_tensor_tensor(out=dst_ap, in0=src_ap, scalar=r, in1=self.gpost[:], op0=ALU.mult, op1=ALU.mult),
             reads=list(src_b) + [r_b, self.gpost_b], writes=[dst_b])
        hs, hb = self.h_load(t)
        st1[t] = (dst_ap, dst_b, hs, hb)

    def p1(t):
        dst_ap, dst_b, hs, hb = st1.pop(t)
        P.op("pool", lambda e: e.tensor_tensor(out=hs, in0=hs, in1=dst_ap, op=ALU.add), reads=[dst_b], writes=[hb])
        tok = self.h_store(t, hs, hb)
        if toks is not None:
            toks.append(tok)
        st2[t] = (hs, hb)

    phases = [p0, p1]
    if do_pre:
        pn = self.pre_norm_phases(lambda t: st2.pop(t), tbanks=tbanks)
        def p1b(t, p1=p1, pa=pn[0]):
            p1(t)
            pa(t)
        phases = [p0, p1b, pn[1]]
    return phases


def _first_norm(self):
    self.set_hextra(0, 5)
    pipeline(NT, self.pre_norm_phases(lambda t: self.h_load(t), tbanks=(6, 7)))


def _mixer_boundaries2(self, i):
    toks = []
    ph = self.boundary_phases(lambda t: (self.acc[:, t, :], [self.acc_b[t]]), lambda t: (self.acc[:, t, :], self.acc_b[t]), toks=toks, tbanks=(6, 7))
    pipeline(NT, ph)
    return toks


Builder.pre_norm_phases = _pre_norm_phases
Builder.boundary_phases = _boundary_phases
Builder.first_norm = _first_norm
Builder.mixer_boundaries = _mixer_boundaries2


def _ffn_stage2(self, i):
    P = self.P
    self.load_g("post", (i, 3))
    self.load_g("pre", (i, 4))
    P.barrier()
    w_in = self.dr["ffn_w_in"][i]
    w_out = self.dr["ffn_w_out"][i]
    off = 0
    winA, winA_b = [], []
    for s in range(3):
        v, off = self.st_view(off, [KC, 512], BF16)
        winA.append(v)
        winA_b.append(Buf(f"f{i}winA{s}"))
    woutB, woutB_b = [], []
    for s in range(2):
        v, off = self.st_view(off, [4, D], BF16)
        woutB.append(v)
        woutB_b.append(Buf(f"f{i}woutB{s}"))
    hidT, hidT_b = [], []
    for s in range(2):
        v, off = self.st_view(off, [4, T], BF16)
        hidT.append(v)
        hidT_b.append([[Buf(f"f{i}hid{s}_{jl}_{tg}") for tg in range(4)] for jl in range(4)])
    sil, sil_b = [], []
    for s in range(2):
        v, off = self.st_view(off, [1, 512], F32)
        sil.append(v)
        sil_b.append(Buf(f"f{i}sil{s}"))
    nchunk = FFN_H // 128
    ngrp = (nchunk + 3) // 4
    unit = [0]
    self.set_hextra(off, (self.ST_BYTES - off) // 4096)

    def in_phase(pi):
        ws, wb = winA[pi % 3], winA_b[pi % 3]
        self.wload(ws[:, :, 0:256], w_in[:, pi * 256:(pi + 1) * 256].rearrange("(kc p) n -> p kc n", p=128), wb)
        self.wload(ws[:, :, 256:512], w_in[:, FFN_H + pi * 256:FFN_H + (pi + 1) * 256].rearrange("(kc p) n -> p kc n", p=128), wb)
        for jj in range(2):
            j = 2 * pi + jj
            grp, jl = j // 4, j % 4
            hs_, hs_b = hidT[grp % 2], hidT_b[grp % 2]
            for tg in range(4):
                u = unit[0]
                unit[0] += 1
                bg, bu = 2 * (u % 2), 2 * (u % 2) + 1
                sl, slb = sil[u % 2], sil_b[u % 2]
                for (bk, coff) in ((bg, jj * 128), (bu, 256 + jj * 128)):
                    for kc in range(KC):
                        P.op("pe", lambda e, bk=bk, coff=coff, kc=kc, tg=tg, ws=ws: e.matmul(
                            self.bank[bk][:], lhsT=ws[:, kc, coff:coff + 128], rhs=self.aT[:, kc, tg * 512:(tg + 1) * 512],
                            start=(kc == 0), stop=(kc == KC - 1)),
                            reads=[wb] + self.aT_b[tg * 4:(tg + 1) * 4], writes=[self.bank_b[bk]])
                P.op("act", lambda e, bg=bg, sl=sl: e.activation(out=sl[:, 0, :], in_=self.bank[bg][:], func=AF.Silu),
                     reads=[self.bank_b[bg]], writes=[slb])
                P.op("dve", lambda e, bu=bu, sl=sl, hs_=hs_, jl=jl, tg=tg: e.tensor_tensor(
                    out=hs_[:, jl, tg * 512:(tg + 1) * 512], in0=sl[:, 0, :], in1=self.bank[bu][:], op=ALU.mult),
                    reads=[slb, self.bank_b[bu]], writes=[hs_b[jl][tg]])

    def out_load(grp):
        n = min(4, nchunk - grp * 4)
        wo, wob = woutB[grp % 2], woutB_b[grp % 2]
        self.wload(wo[:, 0:n, :], w_out[grp * 512:grp * 512 + n * 128, :].rearrange("(jl p) n -> p jl n", p=128), wob)

    def out_tile(grp, t):
        n = min(4, nchunk - grp * 4)
        wo, wob = woutB[grp % 2], woutB_b[grp % 2]
        hs_, hs_b = hidT[grp % 2], hidT_b[grp % 2]
        for half in range(2):
            bk = 4 + 2 * (t % 2) + half
            for q in range(n):
                P.op("pe", lambda e, bk=bk, q=q, half=half: e.matmul(
                    self.bank[bk][:], lhsT=hs_[:, q, t * 128:(t + 1) * 128], rhs=wo[:, q, half * 512:(half + 1) * 512],
                    start=(q == 0), stop=(q == n - 1)),
                    reads=[wob, hs_b[q][t // 4]], writes=[self.bank_b[bk]])
            dst = self.acc[:, t, half * 512:(half + 1) * 512]
            if grp == 0:
                P.op("act", lambda e, bk=bk, dst=dst: e.copy(out=dst, in_=self.bank[bk][:]),
                     reads=[self.bank_b[bk]], writes=[self.acc_b[t]])
            else:
                P.op("dve", lambda e, bk=bk, dst=dst: e.tensor_tensor(out=dst, in0=dst, in1=self.bank[bk][:], op=ALU.add),
                     reads=[self.bank_b[bk]], writes=[self.acc_b[t]])

    toks = []
    pairs_of = lambda grp: range(grp * 2, min(grp * 2 + 2, nchunk // 2))
    for pi in pairs_of(0):
        in_phase(pi)
    for grp in range(ngrp):
        out_load(grp)
        if grp + 1 < ngrp:
            for pi in pairs_of(grp + 1):
                in_phase(pi)
            for t in range(NT):
                out_tile(grp, t)
        else:
            ph = self.boundary_phases_f(lambda t: (self.acc[:, t, :], [self.acc_b[t]]), lambda t: (self.acc[:, t, :], self.acc_b[t]),
                                        toks=toks, tbanks=(0, 1))
            pipeline(NT, [lambda t, grp=grp: out_tile(grp, t)] + ph)
    return toks


Builder.ffn_stage = _ffn_stage2


def _ple2(self, i, do_pre=True):
    P = self.P
    P.barrier()
    off = 0
    Wg, off = self.st_view(off, [KC, D], BF16)
    Wg_b = Buf(f"p{i}Wg")
    Wp, off = self.st_view(off, [2, D], BF16)
    Wp_b = Buf(f"p{i}Wp")
    pb, pb_b, pT, pT_b, sg, sg_b = [], [], [], [], [], []
    for s in range(3):
        v, off = self.st_view(off, [1, 256], BF16)
        pb.append(v)
        pb_b.append(Buf(f"p{i}pb{s}"))
        v, off = self.st_view(off, [2, 128], BF16)
        pT.append(v)
        pT_b.append(Buf(f"p{i}pT{s}"))
        v, off = self.st_view(off, [2, 512], F32)
        sg.append(v)
        sg_b.append(Buf(f"p{i}sg{s}"))
    self.set_hextra(off, 6)
    self.wload(Wg[:, :, 0:512], self.dr["ple_w_gate"][i][:, 0:512].rearrange("(kc p) n -> p kc n", p=128), Wg_b)
    self.wload(Wp, self.dr["ple_w_proj"][i].rearrange("(kc p) n -> p kc n", p=128), Wp_b)
    Wg2_b = Buf(f"p{i}Wg2")
    self.wload(Wg[:, :, 512:1024], self.dr["ple_w_gate"][i][:, 512:1024].rearrange("(kc p) n -> p kc n", p=128), Wg2_b)
    toks = []
    hst = {}

    def mm(t):
        s = t % 3
        self.wload(pb[s][:, 0, :], self.dr["p"][i, t * 128:(t + 1) * 128, :], pb_b[s])
        ptb = 6
        pst = self.bank[ptb][:].bitcast(BF16)[:, (t % 2) * 256:(t % 2) * 256 + 256].rearrange("p (k n) -> p k n", k=2)
        for kc in range(2):
            P.op("pe", lambda e, kc=kc: e.transpose(out=pst[:, kc, :], in_=pb[s][:, 0, kc * 128:(kc + 1) * 128], identity=self.ident[:]),
                 reads=[pb_b[s], self.ident_b], writes=[self.bank_b[ptb]])
        P.op("dve", lambda e: e.tensor_copy(out=pT[s], in_=pst), reads=[self.bank_b[ptb]], writes=[pT_b[s]])
        for half in range(2):
            u = 2 * t + half
            bG, bE = u % 3, 3 + u % 3
            wgb = Wg_b if half == 0 else Wg2_b
            for kc in range(KC):
                P.op("pe", lambda e, kc=kc, half=half, bG=bG: e.matmul(
                    self.bank[bG][:], lhsT=self.aT[:, kc, t * 128:(t + 1) * 128], rhs=Wg[:, kc, half * 512:(half + 1) * 512],
                    start=(kc == 0), stop=(kc == KC - 1)), reads=[wgb, self.aT_b[t]], writes=[self.bank_b[bG]])
            for kc in range(2):
                P.op("pe", lambda e, kc=kc, half=half, bE=bE: e.matmul(
                    self.bank[bE][:], lhsT=pT[s][:, kc, :], rhs=Wp[:, kc, half * 512:(half + 1) * 512],
                    start=(kc == 0), stop=(kc == 1)), reads=[Wp_b, pT_b[s]], writes=[self.bank_b[bE]])

    def post(t):
        s = t % 3
        for half in range(2):
            u = 2 * t + half
            bG, bE = u % 3, 3 + u % 3
            P.op("act", lambda e, half=half, bG=bG: e.activation(out=sg[s][:, half, :], in_=self.bank[bG][:], func=AF.Sigmoid),
                 reads=[self.bank_b[bG]], writes=[sg_b[s]])
            P.op("dve", lambda e, half=half, bE=bE: e.tensor_tensor(out=sg[s][:, half, :], in0=sg[s][:, half, :], in1=self.bank[bE][:], op=ALU.mult),
                 reads=[self.bank_b[bE]], writes=[sg_b[s]])
        hs, hb = self.h_load(t)
        hst[t] = (hs, hb)

    def resid(t):
        s = t % 3
        hs, hb = hst[t]
        P.op("dve", lambda e: e.tensor_tensor(out=hs, in0=hs, in1=sg[s].rearrange("p a b -> p (a b)"), op=ALU.add),
             reads=[sg_b[s]], writes=[hb])
        toks.append(self.h_store(t, hs, hb))

    phases = [mm, post]
    if do_pre:
        pn = self.pre_norm_phases_f(lambda t: hst.pop(t), tbanks=(7,))

        def resid_pa(t):
            resid(t)
            pn[0](t)
        phases += [resid_pa] + pn[1:]
    else:
        phases += [resid]
    pipeline(NT, phases)
    return toks


Builder.ple = _ple2


def _att_main2(self, i, j):
    P = self.P
    dr = self.dr
    P.barrier()
    w_in = dr["attn_w_in"][j]
    if getattr(self, "KT_b", None) is None:
        self.KT_b = [[Buf(f"KT{g}_{fb}") for fb in range(8)] for g in range(3)]
        self.V_b = [[Buf(f"V{g}_{blk}") for blk in range(NT)] for g in range(3)]
    off = 0
    Wq, off = self.st_view(off, [KC, 384], BF16)
    Wq_b = [Buf(f"aWq{g}") for g in range(3)]
    QT, KT, Vt, KH, VH, Et, Et0 = [], [], [], [], [], [], []
    QT_b, KT_sb, Vt_b, KH_b, VH_b, Et_b = [], [], [], [], [], []
    for g, (w, r) in enumerate(DIL):
        a, off = self.st_view(off, [1, T], BF16); QT.append(a); QT_b.append(Buf(f"aQT{g}"))
        a, off = self.st_view(off, [1, T], BF16); KT.append(a); KT_sb.append(Buf(f"aKT{g}"))
        a, off = self.st_view(off, [NT, 128], BF16); Vt.append(a); Vt_b.append(Buf(f"aVt{g}"))
        a, off = self.st_view(off, [1, r * 128], BF16); KH.append(a); KH_b.append(Buf(f"aKH{g}"))
        a, off = self.st_view(off, [r, 128], BF16); VH.append(a); VH_b.append(Buf(f"aVH{g}"))
        a, off = self.st_view(off, [2, 256], F32); Et.append(a)
        a, off = self.st_view(off, [2, 256], F32); Et0.append(a); Et_b.append(Buf(f"aEt{g}"))
    bst, off = self.st_view(off, [2, 256], F32)
    bst_b = Buf("abst")
    m01, off = self.st_view(off, [1, 256], F32)
    m01p, off = self.st_view(off, [1, 256], F32)
    m_b = Buf("am01")
    ones, off = self.st_view(off, [1, 64], BF16)
    ones_b = Buf("aones")
    ex, ex_b, pt, pt_b = [], [], [], []
    for s in range(2):
        a, off = self.st_view(off, [1, 1024], F32); ex.append(a); ex_b.append(Buf(f"aex{s}"))
        a, off = self.st_view(off, [8, 128], BF16); pt.append(a); pt_b.append(Buf(f"apt{s}"))
    bk, bb = self.bank, self.bank_b
    bigb = self.big[:].bitcast(BF16)
    oT = bigb[:, 0:16384].rearrange("p (h t) -> p h t", h=8)
    oT_b = [Buf(f"aoT{h}") for h in range(8)]
    ND2 = [self.big[:, 8192:12288].rearrange("p (a t) -> p a t", a=2), self.big[:, 12288:16384].rearrange("p (a t) -> p a t", a=2)]
    ND2_b = [Buf("aND0"), Buf("aND1")]
    ND2_acc = [self.acc_b[8:12], self.acc_b[12:16]]
    pending = []
    P.dma("sp", m01[:, 0, :], dr["att_mask01"][:, :], writes=[m_b])
    P.dma("sp", m01p[:, 0, :], dr["att_prevmask"][:, :], writes=[m_b])
    P.op("pool", lambda e: e.memset(ones, 1.0), writes=[ones_b])
    uq = [0]
    def make_fin(hp, nd, nd_b, n):
        out = []
        for k in range(n):
            csl = slice(k * (T // n), (k + 1) * (T // n))

            def f(csl=csl):
                P.op("dve", lambda e: e.reciprocal(out=nd[:, 1, csl], in_=nd[:, 1, csl]), reads=[], writes=[nd_b])
                P.op("pool", lambda e: e.tensor_tensor(out=oT[:, hp, csl], in0=nd[:, 0, csl], in1=nd[:, 1, csl], op=ALU.mult),
                     reads=[nd_b], writes=[oT_b[hp]] + self.acc_b[hp:hp + 1])
            out.append(f)
        return out

    for hp in range(8):
        NDacc, ND_b = ND2[hp % 2], ND2_b[hp % 2]
        for g, (w, r) in enumerate(DIL):
            nbc = 16 // r
            c0 = g * 3072 + hp * 128
            self.wload(Wq[:, :, g * 128:(g + 1) * 128], w_in[:, c0:c0 + 128].rearrange("(kc p) n -> p kc n", p=128), Wq_b[g])
            P.dma("sp", KT[g][:, 0, :], dr["KT_own"][g, hp * 128:(hp + 1) * 128, :], reads=[self.KT_b[g][hp]], writes=[KT_sb[g]])
            P.dma("sp", Vt[g], dr["V_own"][g, :, hp * 128:(hp + 1) * 128].rearrange("(b p) f -> p b f", p=128), reads=self.V_b[g], writes=[Vt_b[g]])
            P.dma("sp", KH[g][:, 0, :].rearrange("p (c j) -> p c j", c=r),
                  dr["KT_prev"][g, hp * 128:(hp + 1) * 128, :].rearrange("f (c n j) -> f c n j", c=r, n=nbc)[:, :, nbc - 1, :], writes=[KH_b[g]])
            P.dma("sp", VH[g],
                  dr["V_prev"][g, :, hp * 128:(hp + 1) * 128].rearrange("(c n p) f -> p c n f", c=r, n=nbc)[:, :, nbc - 1, :], writes=[VH_b[g]])
            P.dma("sp", bst, dr["att_bias"][g, 2 * hp:2 * hp + 2].rearrange("h p n -> p h n"), writes=[bst_b])
            P.op("act", lambda e: e.activation(out=bst, in_=bst, func=AF.Exp), reads=[], writes=[bst_b])
            for hh in range(2):
                P.op("pool", lambda e, hh=hh, g=g: e.tensor_tensor(out=Et[g][:, hh, :], in0=bst[:, hh, :], in1=m01[:, 0, :], op=ALU.mult),
                     reads=[bst_b, m_b], writes=[Et_b[g]])
                P.op("pool", lambda e, hh=hh, g=g: e.tensor_tensor(out=Et0[g][:, hh, :], in0=bst[:, hh, :], in1=m01p[:, 0, :], op=ALU.mult),
                     reads=[bst_b, m_b], writes=[Et_b[g]])

        def q_proj(g):
            r = DIL[g][1]
            for tg in range(4):
                b = uq[0] % 2
                uq[0] += 1
                for kc in range(KC):
                    P.op("pe", lambda e, b=b, kc=kc, tg=tg: e.matmul(
                        bk[b][:], lhsT=Wq[:, kc, g * 128:(g + 1) * 128], rhs=self.aT[:, kc, tg * 512:(tg + 1) * 512],
                        start=(kc == 0), stop=(kc == KC - 1)), reads=[Wq_b[g]] + self.aT_b[tg * 4:(tg + 1) * 4], writes=[bb[b]])
                jn = 512 // r
                dst = QT[g][:, 0, :].rearrange("p (c j) -> p c j", c=r)[:, :, tg * jn:(tg + 1) * jn]
                src = bk[b][:].rearrange("p (j c) -> p c j", c=r)
                P.op("act", lambda e, dst=dst, src=src: e.copy(out=dst, in_=src), reads=[bb[b]], writes=[QT_b[g]])

        def geom(it, d):
            g, bp = it // 8, it % 8
            blk = 2 * bp + d
            r = DIL[g][1]
            nbc = 16 // r
            c, n = blk // nbc, blk % nbc
            return g, blk, r, c, n

        def ph0(it):
            if it % 8 == 0:
                q_proj(it // 8)
            sb0 = 2 if it % 2 == 0 else 6
            for d in range(2):
                g, blk, r, c, n = geom(it, d)
                qs = slice(blk * 128, (blk + 1) * 128)
                first = (n == 0)
                for hh in range(2):
                    ps = slice(hh * 64, (hh + 1) * 64)
                    sT = bk[sb0 + hh][:].rearrange("p (d k q) -> p d k q", d=2, k=2)
                    kprev = KH[g][ps, 0, c * 128:(c + 1) * 128] if first else KT[g][ps, 0, (blk - 1) * 128:blk * 128]
                    P.op("pe", lambda e, sT=sT, kprev=kprev, ps=ps, d=d, qs=qs, g=g: e.matmul(sT[:, d, 0, :], lhsT=kprev, rhs=QT[g][ps, 0, qs], start=True, stop=True),
                         reads=[KH_b[g] if first else KT_sb[g], QT_b[g]], writes=[bb[sb0 + hh]])
                    P.op("pe", lambda e, sT=sT, ps=ps, d=d, qs=qs, g=g: e.matmul(sT[:, d, 1, :], lhsT=KT[g][ps, 0, qs], rhs=QT[g][ps, 0, qs], start=True, stop=True),
                         reads=[KT_sb[g], QT_b[g]], writes=[bb[sb0 + hh]])

        def ph1(it):
            sb0 = 2 if it % 2 == 0 else 6
            s2 = it % 2
            P.op("act", lambda e: e.activation(out=ex[s2][:, 0, :].rearrange("p (h x) -> p h x", h=2), in_=self.psum_all[:, sb0:sb0 + 2, :], func=AF.Exp, scale=0.125),
                 reads=[bb[sb0], bb[sb0 + 1]], writes=[ex_b[s2]])
            exv = ex[s2][:, 0, :].rearrange("p (h d x) -> p h d x", h=2, d=2)
            ptv = pt[s2].rearrange("p (h d k) q -> p h d (k q)", h=2, d=2)
            for d in range(2):
                g, blk, r, c, n = geom(it, d)
                tab = Et0[g] if n == 0 else Et[g]
                P.op("dve", lambda e, d=d, tab=tab: e.tensor_tensor(out=ptv[:, :, d, :], in0=exv[:, :, d, :], in1=tab, op=ALU.mult),
                     reads=[ex_b[s2], Et_b[g]], writes=[pt_b[s2]])

        def ph2(it):
            s2 = it % 2
            nb_ = 4 + it % 2
            ndb = bk[nb_][:].rearrange("p (d a q) -> p d a q", d=2, a=2)
            for d in range(2):
                g, blk, r, c, n = geom(it, d)
                first = (n == 0)
                for hh in range(2):
                    ps = slice(hh * 64, (hh + 1) * 64)
                    i0 = hh * 4 + d * 2
                    vprev = VH[g][:, c, ps] if first else Vt[g][:, blk - 1, ps]
                    P.op("pe", lambda e, d=d, vprev=vprev, ps=ps, i0=i0: e.matmul(ndb[ps, d, 0, :], lhsT=vprev, rhs=pt[s2][:, i0, :], start=True, stop=False),
                         reads=[VH_b[g] if first else Vt_b[g], pt_b[s2]], writes=[bb[nb_]])
                    P.op("pe", lambda e, d=d, ps=ps, i0=i0, g=g, blk=blk: e.matmul(ndb[ps, d, 0, :], lhsT=Vt[g][:, blk, ps], rhs=pt[s2][:, i0 + 1, :], start=False, stop=True),
                         reads=[Vt_b[g], pt_b[s2]], writes=[bb[nb_]])
                    P.op("pe", lambda e, d=d, ps=ps, i0=i0: e.matmul(ndb[ps, d, 1, :], lhsT=ones[:, 0, :], rhs=pt[s2][:, i0, :], start=True, stop=False),
                         reads=[ones_b, pt_b[s2]], writes=[bb[nb_]])
                    P.op("pe", lambda e, d=d, ps=ps, i0=i0: e.matmul(ndb[ps, d, 1, :], lhsT=ones[:, 0, :], rhs=pt[s2][:, i0 + 1, :], start=False, stop=True),
                         reads=[ones_b, pt_b[s2]], writes=[bb[nb_]])

        def ph3(it):
            nb_ = 4 + it % 2
            ndb = bk[nb_][:].rearrange("p (d a q) -> p d a q", d=2, a=2)
            for d in range(2):
                g, blk, r, c, n = geom(it, d)
                t0 = n * 128 * r + c
                tsl = slice(t0, t0 + 127 * r + 1, r)
                if g == 0:
                    P.op("dve", lambda e, d=d, tsl=tsl, nd=NDacc: e.tensor_copy(out=nd[:, :, tsl], in_=ndb[:, d, :, :]), reads=[bb[nb_]], writes=[ND_b] + ND2_acc[hp % 2])
                else:
                    P.op("dve", lambda e, d=d, tsl=tsl, nd=NDacc: e.tensor_tensor(out=nd[:, :, tsl], in0=nd[:, :, tsl], in1=ndb[:, d, :, :], op=ALU.add), reads=[bb[nb_]], writes=[ND_b])

        def ph3w(it):
            ph3(it)
            if pending and it < len(pending[0]):
                pending[0][it]()

        pipeline(3 * 8, [ph0, ph1, ph2, ph3w])
        pending.clear()
        fin = make_fin(hp, NDacc, ND_b, 8)
        if hp < 7:
            pending.append(fin)
        else:
            for f in fin:
                f()
    P.barrier()
    off = 0
    Wo, off = self.st_view(off, [KC, D], BF16)
    Wo_b = Buf("aWo")
    nrm, off = self.st_view(off, [2, D], F32)
    nrm_b = [Buf("anrm0"), Buf("anrm1")]
    self.wload(Wo, dr["attn_w_out"][j].rearrange("(kc p) n -> p kc n", p=128), Wo_b)
    self.set_hextra(off, 6)
    toks = []

    def outp(t):
        tsl = slice(t * 128, (t + 1) * 128)
        for half in range(2):
            for h in range(8):
                P.op("pe", lambda e, h=h, half=half: e.matmul(bk[6 + half][:], lhsT=oT[:, h, tsl], rhs=Wo[:, h, half * 512:(half + 1) * 512],
                                                             start=(h == 0), stop=(h == 7)),
                     reads=[oT_b[h], Wo_b], writes=[bb[6 + half]])

    src = self.psum_all[:, 6:8, :].rearrange("p a b -> p (a b)")
    ph = self.boundary_phases(lambda t: (src, [bb[6], bb[7]]), lambda t: (nrm[:, t % 2, :], nrm_b[t % 2]), toks=toks, tbanks=(0, 1))
    pipeline(NT, [outp] + ph)
    for t in range(NT):
        for h in range(8):
            self.acc_b[t].r = list(self.acc_b[t].r) + list(oT_b[h].r)
        for nb2 in ND2_b:
            self.acc_b[t].r = list(self.acc_b[t].r) + list(nb2.r) + ([nb2.w] if nb2.w else [])
    return toks


Builder.att_main = _att_main2


def _gmlp_stage2(self, i, j):
    P = self.P
    dr = self.dr
    P.barrier()
    bigb = self.big[:].bitcast(BF16)
    Wv = bigb[:, 0:16384].rearrange("p (k n) -> p k n", k=KC)
    Wu = bigb[:, 16384:32768].rearrange("p (k n) -> p k n", k=KC)
    Wv_b = [Buf(f"gWv{c}") for c in range(4)]
    Wu_b = [Buf(f"gWu{c}") for c in range(4)]
    off = 0
    Wo, off = self.st_view(off, [16, D], BF16)
    Wo_b = [Buf(f"gWo{q}") for q in range(4)]
    wsT, off = self.st_view(off, [GM_G, 128], BF16)
    wsT_b = Buf("gwsT")
    gtab, off = self.st_view(off, [1, GM_W], F32)
    btab, off = self.st_view(off, [1, GM_W], BF16)
    gtab_b, btab_b = Buf("ggtab"), Buf("gbtab")
    bsT, off = self.st_view(off, [1, GM_G], F32)
    bsT_b = Buf("gbsT")
    cmask, off = self.st_view(off, [1, 128], F32)
    cmask_b = Buf("gcmask")
    off_vf = off
    vf, off = self.st_view(off, [1, GM_W], F32)
    vf_b = Buf("gvf")
    vn, off = self.st_view(off, [1, GM_W], BF16)
    vn_b = Buf("gvn")
    ssb, off = self.st_view(off, [1, GM_W], F32)
    ssb_b = Buf("gssb")
    us, off = self.st_view(off, [1, GM_W], BF16)
    us_b = Buf("gus")
    usT, off = self.st_view(off, [16, 128], BF16)
    usT_b = Buf("gusT")
    gu, gu_b = [], []
    for s in range(2):
        a, off = self.st_view(off, [1, 512], F32)
        gu.append(a)
        gu_b.append(Buf(f"ggu{s}"))
    nrm, off = self.st_view(off, [1, D], F32)
    nrm_b = Buf("gnrm")
    wsl, off2 = self.st_view(off_vf, [GM_G, 128], BF16)
    wsl_b = vf_b
    w_in = dr["gm_w_in"][j]
    self.wload(wsl, dr["gm_w_s"][j].rearrange("g n m -> n g m"), wsl_b)
    self.wload(btab[:, 0, :], dr["gm_ln_b"][j, :].partition_broadcast(128), btab_b)
    P.dma("sp", gtab[:, 0, :], dr["gm_ln_g"][j, :].partition_broadcast(128), writes=[gtab_b])
    for cb in range(4):
        self.P.dma("pool", Wv[:, :, cb * 512:(cb + 1) * 512], w_in[:, GM_W + cb * 512:GM_W + (cb + 1) * 512].rearrange("(kc p) n -> p kc n", p=128),
                   writes=[Wv_b[cb]])
    for cb in range(4):
        self.P.dma("pool", Wu[:, :, cb * 512:(cb + 1) * 512], w_in[:, cb * 512:(cb + 1) * 512].rearrange("(kc p) n -> p kc n", p=128),
                   writes=[Wu_b[cb]])
    for q in range(4):
        self.wload(Wo[:, q * 4:(q + 1) * 4, :], dr["gm_w_out"][j][q * 512:(q + 1) * 512, :].rearrange("(kc p) n -> p kc n", p=128), Wo_b[q])
    P.dma("sp", bsT[:, 0, :], dr["gm_b_s"][j].rearrange("g n -> n g"), writes=[bsT_b], allow_slow_non_contiguous=True)
    P.dma("sp", cmask[:, 0, :], dr["causalT"][:, :], writes=[cmask_b])
    bk, bb = self.bank, self.bank_b
    tps = bk[0][:].bitcast(BF16)[:, 0:1024].rearrange("p (g n) -> p g n", g=GM_G)
    for g in range(GM_G):
        P.op("pe", lambda e, g=g: e.transpose(out=tps[:, g, :], in_=wsl[:, g, :], identity=self.ident[:]),
             reads=[wsl_b, self.ident_b], writes=[bb[0]])
    for g in range(GM_G):
        P.op("dve", lambda e, g=g: e.tensor_tensor(out=wsT[:, g, :], in0=tps[:, g, :], in1=cmask[:, 0, :], op=ALU.mult),
             reads=[bb[0], cmask_b], writes=[wsT_b])
    toks = []
    stt = {}
    uu = [0]

    def ph0(t):
        tsl = slice(t * 128, (t + 1) * 128)
        sv, sv_b = [], []
        for cb in range(4):
            b = uu[0] % 2
            uu[0] += 1
            for kc in range(KC):
                P.op("pe", lambda e, cb=cb, kc=kc, b=b: e.matmul(bk[b][:], lhsT=self.aT[:, kc, tsl], rhs=Wv[:, kc, cb * 512:(cb + 1) * 512],
                                                               start=(kc == 0), stop=(kc == KC - 1)),
                     reads=[Wv_b[cb], self.aT_b[t]], writes=[bb[b]])
            a, a_b = self.new_stat()
            sv.append(a)
            sv_b.append(a_b)
            P.op("act", lambda e, cb=cb, a=a, b=b: e.activation(out=vf[:, 0, cb * 512:(cb + 1) * 512], in_=bk[b][:], func=AF.Gelu, accum_out=a),
                 reads=[bb[b]], writes=[vf_b, a_b])
        stt[t] = (sv, sv_b)

    def ph1(t):
        sv, sv_b = stt.pop(t)
        sq, sq_b = [], []
        for hh in range(2):
            a, a_b = self.new_stat()
            sq.append(a)
            sq_b.append(a_b)
            P.op("act", lambda e, hh=hh, a=a: e.activation(out=self.junk[:, :], in_=vf[:, 0, hh * 1024:(hh + 1) * 1024], func=AF.Square, accum_out=a),
                 reads=[vf_b], writes=[self.junk_b, a_b])
        m1, m1_b = self.new_stat()
        P.op("pool", lambda e: e.tensor_tensor(out=m1, in0=sv[0], in1=sv[1], op=ALU.add), reads=[sv_b[0], sv_b[1]], writes=[m1_b])
        m2, m2_b = self.new_stat()
        P.op("pool", lambda e: e.tensor_tensor(out=m2, in0=sv[2], in1=sv[3], op=ALU.add), reads=[sv_b[2], sv_b[3]], writes=[m2_b])
        mu, mu_b = self.new_stat()
        P.op("pool", lambda e: e.tensor_tensor(out=mu, in0=m1, in1=m2, op=ALU.add), reads=[m1_b, m2_b], writes=[mu_b])
        P.op("pool", lambda e: e.tensor_scalar(out=mu, in0=mu, scalar1=1.0 / GM_W, scalar2=None, op0=ALU.mult), reads=[], writes=[mu_b])
        e2, e2_b = self.new_stat()
        P.op("pool", lambda e: e.tensor_tensor(out=e2, in0=sq[0], in1=sq[1], op=ALU.add), reads=[sq_b[0], sq_b[1]], writes=[e2_b])
        mm, mm_b = self.new_stat()
        P.op("pool", lambda e: e.tensor_tensor(out=mm, in0=mu, in1=mu, op=ALU.mult), reads=[mu_b], writes=[mm_b])
        var, var_b = self.new_stat()
        P.op("pool", lambda e: e.tensor_scalar(out=var, in0=e2, scalar1=1.0 / GM_W, scalar2=EPS, op0=ALU.mult, op1=ALU.add),
             reads=[e2_b], writes=[var_b])
        P.op("pool", lambda e: e.tensor_tensor(out=var, in0=var, in1=mm, op=ALU.subtract), reads=[mm_b], writes=[var_b])
        rs, rs_b = self.new_stat()
        P.op("pool", lambda e: e.tensor_tensor(out=rs, in0=var, in1=self.cm05[:], op=ALU.pow), reads=[var_b, self.cm05_b], writes=[rs_b])
        P.op("dve", lambda e: e.scalar_tensor_tensor(out=vf[:, 0, :], in0=vf[:, 0, :], scalar=mu, in1=gtab[:, 0, :], op0=ALU.subtract, op1=ALU.mult),
             reads=[mu_b, gtab_b], writes=[vf_b])
        P.op("dve", lambda e: e.scalar_tensor_tensor(out=vn[:, 0, :], in0=vf[:, 0, :], scalar=rs, in1=btab[:, 0, :], op0=ALU.mult, op1=ALU.add),
             reads=[vf_b, rs_b, btab_b], writes=[vn_b])

    def ph2(t):
        tsl = slice(t * 128, (t + 1) * 128)
        for un in range(4):
            bkx = 2 + un % 2
            for gg in range(2):
                g = un * 2 + gg
                P.op("pe", lambda e, g=g, gg=gg, bkx=bkx: e.matmul(bk[bkx][:, gg * 256:(gg + 1) * 256], lhsT=wsT[:, g, :], rhs=vn[:, 0, g * 256:(g + 1) * 256],
                                                                   start=True, stop=True),
                     reads=[wsT_b, vn_b], writes=[bb[bkx]])
            for gg in range(2):
                g = un * 2 + gg
                P.op("dve", lambda e, g=g, gg=gg, bkx=bkx: e.tensor_scalar(out=ssb[:, 0, g * 256:(g + 1) * 256], in0=bk[bkx][:, gg * 256:(gg + 1) * 256],
                                                                            scalar1=bsT[:, 0, g:g + 1], scalar2=None, op0=ALU.add),
                     reads=[bb[bkx], bsT_b], writes=[ssb_b])
        for cb in range(4):
            b = uu[0] % 2
            uu[0] += 1
            for kc in range(KC):
                P.op("pe", lambda e, cb=cb, kc=kc, b=b: e.matmul(bk[b][:], lhsT=self.aT[:, kc, tsl], rhs=Wu[:, kc, cb * 512:(cb + 1) * 512],
                                                               start=(kc == 0), stop=(kc == KC - 1)),
                     reads=[Wu_b[cb], self.aT_b[t]], writes=[bb[b]])
            s = cb % 2
            P.op("act", lambda e, b=b, s=s: e.activation(out=gu[s][:, 0, :], in_=bk[b][:], func=AF.Gelu), reads=[bb[b]], writes=[gu_b[s]])
            P.op("dve", lambda e, cb=cb, s=s: e.tensor_tensor(out=us[:, 0, cb * 512:(cb + 1) * 512], in0=gu[s][:, 0, :], in1=ssb[:, 0, cb * 512:(cb + 1) * 512], op=ALU.mult),
                 reads=[gu_b[s], ssb_b], writes=[us_b])

    def ph3(t):
        for half in range(2):
            tp = bk[4 + half][:].bitcast(BF16).rearrange("p (k n) -> p k n", k=8)
            for q in range(8):
                kc = half * 8 + q
                P.op("pe", lambda e, tp=tp, q=q, kc=kc: e.transpose(out=tp[:, q, :], in_=us[:, 0, kc * 128:(kc + 1) * 128], identity=self.ident[:]),
                     reads=[us_b, self.ident_b], writes=[bb[4 + half]])
            P.op("act", lambda e, tp=tp, half=half: e.copy(out=usT[:, half * 8:(half + 1) * 8, :], in_=tp), reads=[bb[4 + half]], writes=[usT_b])
        for half in range(2):
            for kc in range(16):
                P.op("pe", lambda e, half=half, kc=kc: e.matmul(bk[6 + half][:], lhsT=usT[:, kc, :], rhs=Wo[:, kc, half * 512:(half + 1) * 512],
                                                                start=(kc == 0), stop=(kc == 15)),
                     reads=[usT_b, Wo_b[kc // 4]], writes=[bb[6 + half]])

    src = self.psum_all[:, 6:8, :].rearrange("p a b -> p (a b)")
    bph = self.boundary_phases(lambda t: (src, [bb[6], bb[7]]), lambda t: (nrm[:, 0, :], nrm_b), toks=toks, tbanks=(4, 5))
    def ph3b(t):
        ph3(t)
        bph[0](t)
    pipeline(NT, [ph0, ph1, ph2, ph3b] + bph[1:])
    for t in range(NT):
        for src_b in (Wv_b if t < 8 else Wu_b):
            self.acc_b[t].r = list(self.acc_b[t].r) + list(src_b.r) + ([src_b.w] if src_b.w else [])
    return toks


Builder.gmlp_stage = _gmlp_stage2


def _ffn_ple_stage(self, i, last):
    P = self.P
    self.load_g("post", (i, 3))
    self.load_g("pre", (i, 4))
    P.barrier()
    w_in = self.dr["ffn_w_in"][i]
    w_out = self.dr["ffn_w_out"][i]
    off = 0
    winA, winA_b = [], []
    for s in range(3):
        v, off = self.st_view(off, [KC, 512], BF16)
        winA.append(v)
        winA_b.append(Buf(f"f{i}winA{s}"))
    woutB, woutB_b, woutB_off = [], [], []
    for s in range(2):
        woutB_off.append(off)
        v, off = self.st_view(off, [4, D], BF16)
        woutB.append(v)
        woutB_b.append(Buf(f"f{i}woutB{s}"))
    hidT, hidT_b, hidT_off = [], [], []
    for s in range(2):
        hidT_off.append(off)
        v, off = self.st_view(off, [4, T], BF16)
        hidT.append(v)
        hidT_b.append([[Buf(f"f{i}hid{s}_{jl}_{tg}") for tg in range(4)] for jl in range(4)])
    sil, sil_b = [], []
    for s in range(2):
        v, off = self.st_view(off, [1, 512], F32)
        sil.append(v)
        sil_b.append(Buf(f"f{i}sil{s}"))
    nchunk = FFN_H // 128
    ngrp = (nchunk + 3) // 4
    unit = [0]

    def in_phase(pi):
        ws, wb = winA[pi % 3], winA_b[pi % 3]
        self.wload(ws[:, :, 0:256], w_in[:, pi * 256:(pi + 1) * 256].rearrange("(kc p) n -> p kc n", p=128), wb)
        self.wload(ws[:, :, 256:512], w_in[:, FFN_H + pi * 256:FFN_H + (pi + 1) * 256].rearrange("(kc p) n -> p kc n", p=128), wb)
        for jj in range(2):
            j = 2 * pi + jj
            grp, jl = j // 4, j % 4
            hs_, hs_b = hidT[grp % 2], hidT_b[grp % 2]
            for tg in range(4):
                u = unit[0]
                unit[0] += 1
                bg, bu = 2 * (u % 2), 2 * (u % 2) + 1
                sl, slb = sil[u % 2], sil_b[u % 2]
                for (bk, coff) in ((bg, jj * 128), (bu, 256 + jj * 128)):
                    for kc in range(KC):
                        P.op("pe", lambda e, bk=bk, coff=coff, kc=kc, tg=tg, ws=ws: e.matmul(
                            self.bank[bk][:], lhsT=ws[:, kc, coff:coff + 128], rhs=self.aT[:, kc, tg * 512:(tg + 1) * 512],
                            start=(kc == 0), stop=(kc == KC - 1)),
                            reads=[wb] + self.aT_b[tg * 4:(tg + 1) * 4], writes=[self.bank_b[bk]])
                P.op("act", lambda e, bg=bg, sl=sl: e.activation(out=sl[:, 0, :], in_=self.bank[bg][:], func=AF.Silu),
                     reads=[self.bank_b[bg]], writes=[slb])
                P.op("dve", lambda e, bu=bu, sl=sl, hs_=hs_, jl=jl, tg=tg: e.tensor_tensor(
                    out=hs_[:, jl, tg * 512:(tg + 1) * 512], in0=sl[:, 0, :], in1=self.bank[bu][:], op=ALU.mult),
                    reads=[slb, self.bank_b[bu]], writes=[hs_b[jl][tg]])

    def out_load(grp):
        n = min(4, nchunk - grp * 4)
        wo, wob = woutB[grp % 2], woutB_b[grp % 2]
        self.wload(wo[:, 0:n, :], w_out[grp * 512:grp * 512 + n * 128, :].rearrange("(jl p) n -> p jl n", p=128), wob)

    def out_tile(grp, t, banks):
        n = min(4, nchunk - grp * 4)
        wo, wob = woutB[grp % 2], woutB_b[grp % 2]
        hs_, hs_b = hidT[grp % 2], hidT_b[grp % 2]
        for half in range(2):
            bk = banks[half]
            for q in range(n):
                P.op("pe", lambda e, bk=bk, q=q, half=half: e.matmul(
                    self.bank[bk][:], lhsT=hs_[:, q, t * 128:(t + 1) * 128], rhs=wo[:, q, half * 512:(half + 1) * 512],
                    start=(q == 0), stop=(q == n - 1)),
                    reads=[wob, hs_b[q][t // 4]], writes=[self.bank_b[bk]])
            dst = self.acc[:, t, half * 512:(half + 1) * 512]
            if grp == 0:
                P.op("act", lambda e, bk=bk, dst=dst: e.copy(out=dst, in_=self.bank[bk][:]),
                     reads=[self.bank_b[bk]], writes=[self.acc_b[t]])
            else:
                P.op("dve", lambda e, bk=bk, dst=dst: e.tensor_tensor(out=dst, in0=dst, in1=self.bank[bk][:], op=ALU.add),
                     reads=[self.bank_b[bk]], writes=[self.acc_b[t]])

    pairs_of = lambda grp: range(grp * 2, min(grp * 2 + 2, nchunk // 2))
    for pi in pairs_of(0):
        in_phase(pi)
    for grp in range(ngrp - 1):
        out_load(grp)
        for pi in pairs_of(grp + 1):
            in_phase(pi)
        for t in range(NT):
            out_tile(grp, t, (4 + 2 * (t % 2), 5 + 2 * (t % 2)))
    lg = ngrp - 1
    out_load(lg)
    oth = (lg + 1) % 2
    dead = list(winA_b) + [woutB_b[oth]] + [b for row in hidT_b[oth] for b in row] + list(sil_b)
    assert woutB_off[oth] == 24 * 1024 or woutB_off[oth] == 32 * 1024
    po = 0
    Wg, po = self.st_view(po, [KC, D], BF16)
    Wp, po = self.st_view(po, [2, D], BF16)
    Wg_b, Wg2_b, Wp_b = Buf(f"p{i}Wg"), Buf(f"p{i}Wg2"), Buf(f"p{i}Wp")
    pb, pb_b, pT, pT_b = [], [], [], []
    for s in range(3):
        v, po = self.st_view(po, [1, 256], BF16)
        pb.append(v)
        pb_b.append(Buf(f"p{i}pb{s}"))
        v, po = self.st_view(po, [2, 128], BF16)
        pT.append(v)
        pT_b.append(Buf(f"p{i}pT{s}"))
    lim = 24 * 1024 if woutB_off[oth] != 24 * 1024 else 32 * 1024
    assert po <= lim, (po, lim)
    so = hidT_off[oth]
    sg, sg_b = [], []
    for s in range(3):
        v, so = self.st_view(so, [2, 512], F32)
        sg.append(v)
        sg_b.append(Buf(f"p{i}sg{s}"))
    assert so <= hidT_off[oth] + 16 * 1024
    if lim == 24 * 1024:
        dead = list(winA_b) + [b for row in hidT_b[oth] for b in row]
    P.dma("pool", Wg[:, :, 0:512], self.dr["ple_w_gate"][i][:, 0:512].rearrange("(kc p) n -> p kc n", p=128), writes=[Wg_b] + dead)
    P.dma("pool", Wp, self.dr["ple_w_proj"][i].rearrange("(kc p) n -> p kc n", p=128), writes=[Wp_b])
    P.dma("pool", Wg[:, :, 512:1024], self.dr["ple_w_gate"][i][:, 512:1024].rearrange("(kc p) n -> p kc n", p=128), writes=[Wg2_b])
    if not last:
        self.load_g("pre2", (i + 1, 0))
    toks_b, toks = [], []
    hst = {}

    def mm(t):
        s = t % 3
        P.dma("pool", pb[s][:, 0, :], self.dr["p"][i, t * 128:(t + 1) * 128, :], writes=[pb_b[s]])
        pst = self.bank[7][:].bitcast(BF16)[:, 0:256].rearrange("p (k n) -> p k n", k=2)
        for kc in range(2):
            P.op("pe", lambda e, kc=kc: e.transpose(out=pst[:, kc, :], in_=pb[s][:, 0, kc * 128:(kc + 1) * 128], identity=self.ident[:]),
                 reads=[pb_b[s], self.ident_b], writes=[self.bank_b[7]])
        P.op("dve", lambda e: e.tensor_copy(out=pT[s], in_=pst), reads=[self.bank_b[7]], writes=[pT_b[s]])
        for half in range(2):
            u = 2 * t + half
            bG, bE = u % 2, 2 + u % 2
            wgb = Wg_b if half == 0 else Wg2_b
            for kc in range(KC):
                P.op("pe", lambda e, kc=kc, half=half, bG=bG: e.matmul(
                    self.bank[bG][:], lhsT=self.aT[:, kc, t * 128:(t + 1) * 128], rhs=Wg[:, kc, half * 512:(half + 1) * 512],
                    start=(kc == 0), stop=(kc == KC - 1)), reads=[wgb, self.aT_b[t]], writes=[self.bank_b[bG]])
            for kc in range(2):
                P.op("pe", lambda e, kc=kc, half=half, bE=bE: e.matmul(
                    self.bank[bE][:], lhsT=pT[s][:, kc, :], rhs=Wp[:, kc, half * 512:(half + 1) * 512],
                    start=(kc == 0), stop=(kc == 1)), reads=[Wp_b, pT_b[s]], writes=[self.bank_b[bE]])
            P.op("act", lambda e, half=half, bG=bG: e.activation(out=sg[s][:, half, :], in_=self.bank[bG][:], func=AF.Sigmoid),
                 reads=[self.bank_b[bG]], writes=[sg_b[s]])
            P.op("dve", lambda e, half=half, bE=bE: e.tensor_tensor(out=sg[s][:, half, :], in0=sg[s][:, half, :], in1=self.bank[bE][:], op=ALU.mult),
                 reads=[self.bank_b[bE]], writes=[sg_b[s]])

    def post(t):
        hs, hb = self.h_load(t, src=getattr(self, "h_rd_ple", None))
        hst[t] = (hs, hb)

    def resid(t):
        s = t % 3
        hs, hb = hst[t]
        P.op("pool", lambda e: e.tensor_tensor(out=hs, in0=hs, in1=sg[s].rearrange("p a b -> p (a b)"), op=ALU.add),
             reads=[sg_b[s]], writes=[hb])
        toks.append(self.h_store(t, hs, hb, dst=getattr(self, "h_wr_ple", None)))

    bph = self.boundary_phases(lambda t: (self.acc[:, t, :], [self.acc_b[t]]), lambda t: (self.acc[:, t, :], self.acc_b[t]),
                               toks=toks_b, tbanks=(6,))
    phases = [lambda t: out_tile(lg, t, (4, 5))] + bph + [mm, post]
    if not last:
        pn = self.pre_norm_phases(lambda t: hst.pop(t), tbanks=(7,), gain=(self.gpre2, self.gpre2_b))

        def resid_pa(t):
            resid(t)
            pn[0](t)
        phases += [resid_pa, pn[1]]
    else:
        phases += [resid]
    pipeline(NT, phases)
    return toks


Builder.ffn_ple_stage = _ffn_ple_stage


def _rstd_split(self, src_ap, src_bufs, n):
    P = self.P
    st = {}

    def sq():
        ss, ss_b = self.new_stat()
        P.op("act", lambda e: e.activation(out=self.junk[:, 0:n], in_=src_ap, func=AF.Square, scale=float(n) ** -0.5, accum_out=ss),
             reads=list(src_bufs), writes=[self.junk_b, ss_b])
        st["ss"] = (ss, ss_b)

    def pw():
        ss, ss_b = st["ss"]
        r, r_b = self.new_stat()
        P.op("pool", lambda e: e.tensor_scalar(out=r, in0=ss, scalar1=EPS, scalar2=None, op0=ALU.add), reads=[ss_b], writes=[r_b])
        r2, r2_b = self.new_stat()
        P.op("pool", lambda e: e.tensor_tensor(out=r2, in0=r, in1=self.cm05[:], op=ALU.pow), reads=[r_b, self.cm05_b], writes=[r2_b])
        st["r"] = (r2, r2_b)

    return sq, pw, (lambda: st["r"])


def _pre_norm_phases_f(self, get_h, tbanks=(6, 7), merge_b=False):
    P = self.P
    S = {}

    def a1(t):
        hs, hb = get_h(t)
        sq, pw, get = self.rstd_split(hs, [hb], D)
        sq()
        S[t] = [hs, hb, pw, get]

    def a2(t):
        S[t][2]()

    def a3(t):
        hs, hb, pw, get = S[t]
        r, r_b = get()
        s = t % 2
        ab, ab_b = self.abf[:, s, :], self.abf_b[s]
        P.op("dve", lambda e: e.scalar_tensor_tensor(out=ab, in0=hs, scalar=r, in1=self.gpre[:], op0=ALU.mult, op1=ALU.mult),
             reads=[hb, r_b, self.gpre_b], writes=[ab_b])
        S[t] = (s, ab_b)

    def b1(t):
        s, ab_b = S[t]
        tbank = tbanks[t % len(tbanks)]
        pst = self.bank[tbank][:].bitcast(BF16).rearrange("p (k n) -> p k n", k=KC)
        for kc in range(KC):
            P.op("pe", lambda e, kc=kc: e.transpose(out=pst[:, kc, :], in_=self.abf[:, s, kc * 128:(kc + 1) * 128], identity=self.ident[:]),
                 reads=[ab_b, self.ident_b], writes=[self.bank_b[tbank]])
        S[t] = (tbank, pst)

    def b2(t):
        tbank, pst = S.pop(t)
        P.op("act", lambda e: e.copy(out=self.aT[:, :, t * 128:(t + 1) * 128], in_=pst),
             reads=[self.bank_b[tbank]], writes=[self.aT_b[t]])

    if merge_b:
        def b12(t):
            b1(t)
            b2(t)
        return [a1, a2, a3, b12]
    return [a1, a2, a3, b1, b2]


def _boundary_phases_f(self, src_fn, dst_fn, do_pre=True, tbanks=(6, 7), toks=None, merge_b=False):
    P = self.P
    S = {}
    H = {}

    def p0a(t):
        src_ap, src_b = src_fn(t)
        sq, pw, get = self.rstd_split(src_ap, src_b, D)
        sq()
        S[t] = [src_ap, src_b, pw, get]

    def p0b(t):
        S[t][2]()

    def p0c(t):
        src_ap, src_b, pw, get = S[t]
        dst_ap, dst_b = dst_fn(t)
        r, r_b = get()
        P.op("dve", lambda e: e.scalar_tensor_tensor(out=dst_ap, in0=src_ap, scalar=r, in1=self.gpost[:], op0=ALU.mult, op1=ALU.mult),
             reads=list(src_b) + [r_b, self.gpost_b], writes=[dst_b])
        hs, hb = self.h_load(t)
        S[t] = (dst_ap, dst_b, hs, hb)

    def p1(t):
        dst_ap, dst_b, hs, hb = S.pop(t)
        P.op("dve", lambda e: e.tensor_tensor(out=hs, in0=hs, in1=dst_ap, op=ALU.add), reads=[dst_b], writes=[hb])
        tok = self.h_store(t, hs, hb)
        if toks is not None:
            toks.append(tok)
        H[t] = (hs, hb)

    phases = [p0a, p0b, p0c, p1]
    if do_pre:
        pn = self.pre_norm_phases_f(lambda t: H.pop(t), tbanks=tbanks, merge_b=merge_b)

        def p1a(t):
            p1(t)
            pn[0](t)
        phases = [p0a, p0b, p0c, p1a] + pn[1:]
    return phases


Builder.rstd_split = _rstd_split
Builder.pre_norm_phases_f = _pre_norm_phases_f
Builder.boundary_phases_f = _boundary_phases_f


def _first_norm_f(self):
    self.set_hextra(0, 5)
    pipeline(NT, self.pre_norm_phases_f(lambda t: self.h_load(t), tbanks=(6, 7)))


Builder.first_norm = _first_norm_f
```
